# Optimizing a Trainium2 kernel written in Bass

```python
import math
import jax, jax.numpy as jnp
from jax import lax
import numpy as np

D_MODEL = 2048
BATCH = 2
SEQ = 4096
DEPTH = 2

GRID_W = 64
CTX_LEN = 256
D_MIX = D_MODEL
W_GRP = D_MIX // 4
EPS = 1e-6
CHUNK = 128
HD_A = 128
H_A = W_GRP // HD_A
HD_V = 128
H_B = W_GRP // HD_V
HD_QK = HD_V // 2
ROPE_BASE = 10000.0
Q_BLOCK = 128
HD_C = 64
H_C = W_GRP // HD_C
LORA_W = 64
LORA_A = 32
RWKV_F = 3 * W_GRP + LORA_W + LORA_A
GN_EPS = 64e-5
H_D = 4
CONV_W = 3

PROJ_SIZES = (W_GRP, W_GRP, W_GRP,
              W_GRP, W_GRP, W_GRP, W_GRP,
              3 * W_GRP, LORA_W + LORA_A, LORA_W + LORA_A, W_GRP,
              W_GRP, W_GRP, W_GRP, W_GRP)
N_IN = sum(PROJ_SIZES)

kernel_name = 'hybrid_dit_parallel_groups_ctx_prefix'


def _f32(a):
    return a.astype(jnp.float32)


def rmsnorm(x, g):
    xf = _f32(x)
    y = xf * lax.rsqrt(jnp.mean(xf * xf, axis=-1, keepdims=True) + EPS)
    return (y * _f32(g)).astype(x.dtype)


def split_proj(p):
    idx = np.cumsum(PROJ_SIZES)[:-1].tolist()
    return jnp.split(p, idx, axis=-1)


def rope_1d(x, pos):
    d = x.shape[-1]
    inv = ROPE_BASE ** (-jnp.arange(0, d, 2, dtype=jnp.float32) / d)
    ang = pos.astype(jnp.float32)[:, None] * inv[None, :]
    cos = jnp.concatenate([jnp.cos(ang), jnp.cos(ang)], -1)[:, None, None, :]
    sin = jnp.concatenate([jnp.sin(ang), jnp.sin(ang)], -1)[:, None, None, :]
    x1, x2 = jnp.split(x, 2, axis=-1)
    rot = jnp.concatenate([-x2, x1], axis=-1)
    return (_f32(x) * cos + _f32(rot) * sin).astype(x.dtype)


def rope_2d(x, row, col):
    half = x.shape[-1] // 2
    return jnp.concatenate([rope_1d(x[..., :half], row), rope_1d(x[..., half:], col)], axis=-1)


def chunk_sgu(u, v, w_s, b_s):
    b, t, _ = u.shape
    n = t // CHUNK
    vf = _f32(v).reshape(b, n, CHUNK, H_A, HD_A)
    mean = jnp.mean(vf, -1, keepdims=True)
    var = jnp.mean(jnp.square(vf - mean), -1, keepdims=True)
    vn = (vf - mean) * lax.rsqrt(var + EPS)
    mixed = jnp.einsum('hpq,bnqhd->bnphd', _f32(w_s), vn) + _f32(b_s).T[None, None, :, :, None]
    return u * mixed.reshape(b, t, W_GRP).astype(u.dtype)


def diff_attn_core(q, k, v, lam):
    s = _f32(jnp.einsum('bqhmd,bkhmd->bhmqk', q, k)) * (HD_QK ** -0.5)
    p = jax.nn.softmax(s, axis=-1)
    a = p[:, :, 0] - lam * p[:, :, 1]
    return jnp.einsum('bhqk,bkhd->bqhd', a.astype(v.dtype), v)


def diff_attention(qc, kc, vc, ql, kl, vl, lam, lam_init, subln_g, need_ctx):
    b, n = ql.shape[:2]
    k_all = jnp.concatenate([kc, kl], axis=1)
    v_all = jnp.concatenate([vc, vl], axis=1)
    nb = n // Q_BLOCK
    qb = ql.reshape(b, nb, Q_BLOCK, H_B, 2, HD_QK).transpose(1, 0, 2, 3, 4, 5)
    ob = lax.map(lambda q: diff_attn_core(q, k_all, v_all, lam), qb)
    o_lat = ob.transpose(1, 0, 2, 3, 4).reshape(b, n, H_B, HD_V)

    def post(o):
        return (rmsnorm(o, subln_g) * (1.0 - lam_init)).reshape(o.shape[0], o.shape[1], W_GRP)

    o_ctx = post(diff_attn_core(qc, kc, vc, lam)) if need_ctx else None
    return o_ctx, post(o_lat)


def rwkv_scan(r, k, v, decay, kk, a, s0):
    def step(S, inp):
        r_t, k_t, v_t, w_t, kk_t, a_t = inp
        sa = jnp.einsum('bhvk,bhk->bhv', S, -kk_t)
        S = (S * w_t[:, :, None, :] + sa[..., None] * (kk_t * a_t)[:, :, None, :]
             + v_t[..., None] * k_t[:, :, None, :])
        y = jnp.einsum('bhvk,bhk->bhv', S, r_t)
        return S, y

    xs = tuple(jnp.moveaxis(z, 1, 0) for z in (r, k, v, decay, kk, a))
    s_final, ys = lax.scan(step, s0, xs)
    return jnp.moveaxis(ys, 0, 1), s_final


def rwkv_direction(feats, s0, mu, w0, w2, a0, a2, k_k, k_a, r_k):
    b, t, _ = feats.shape
    f = _f32(feats)
    prev = jnp.pad(f, ((0, 0), (1, 0), (0, 0)))[:, :t]
    z = f + (prev - f) * _f32(mu)
    r, k, v, zw, za = jnp.split(z, [W_GRP, 2 * W_GRP, 3 * W_GRP, 3 * W_GRP + LORA_W], axis=-1)
    w = -jax.nn.softplus(-(_f32(w0) + jnp.tanh(zw) @ _f32(w2))) - 0.5
    decay = jnp.exp(-jnp.exp(w))
    a = jax.nn.sigmoid(_f32(a0) + za @ _f32(a2))
    heads = lambda y: y.reshape(b, t, H_C, HD_C)
    kk = heads(k * _f32(k_k))
    kk = kk / jnp.maximum(jnp.sqrt(jnp.sum(kk * kk, -1, keepdims=True)), 1e-12)
    k = k * (1.0 + (a - 1.0) * _f32(k_a))
    r, k, v, decay, a = heads(r), heads(k), heads(v), heads(decay), heads(a)
    y, s_final = rwkv_scan(r, k, v, decay, kk, a, s0)
    bonus = jnp.sum(r * k * _f32(r_k), -1, keepdims=True) * v
    return y, bonus, s_final


def rwkv_mixer(rkv_c, lf_c, lb_c, rkv_l, lf_l, lb_l, p_fwd, p_bwd, ln_w, ln_b, need_ctx):
    s0 = jnp.zeros((rkv_l.shape[0], H_C, HD_C, HD_C), jnp.float32)
    flip = lambda y: jnp.flip(y, axis=1)
    cat = lambda p, q: jnp.concatenate([p, q], axis=-1)
    yc_f, bc_f, sc_f = rwkv_direction(cat(rkv_c, lf_c), s0, *p_fwd)
    yl_f, bl_f, _ = rwkv_direction(cat(rkv_l, lf_l), sc_f, *p_fwd)
    yc_b, bc_b, sc_b = rwkv_direction(flip(cat(rkv_c, lb_c)), s0, *p_bwd)
    yl_b, bl_b, _ = rwkv_direction(flip(cat(rkv_l, lb_l)), sc_b, *p_bwd)

    def merge(yf, bf, yb, bb):
        y = yf + flip(yb)
        mean = jnp.mean(y, -1, keepdims=True)
        var = jnp.mean(jnp.square(y - mean), -1, keepdims=True)
        yn = ((y - mean) * lax.rsqrt(var + GN_EPS)).reshape(y.shape[0], y.shape[1], W_GRP)
        yn = yn * _f32(ln_w) + _f32(ln_b)
        return (yn + (bf + flip(bb)).reshape(yn.shape)).astype(rkv_l.dtype)

    o_ctx = merge(yc_f, bc_f, yc_b, bc_b) if need_ctx else None
    return o_ctx, merge(yl_f, bl_f, yl_b, bl_b)


def short_conv(bg, cg, xin, w):
    z = cg * xin
    y = lax.conv_general_dilated(z, w[:, None, :].astype(z.dtype), window_strides=(1,),
                                 padding=[(CONV_W // 2, CONV_W // 2)],
                                 dimension_numbers=('NWC', 'WIO', 'NWC'),
                                 feature_group_count=W_GRP)
    return bg * y


def setup_inputs(seed: int = 0) -> dict:
    key = jax.random.key(seed)
    ks = jax.random.split(key, 32)
    nrm = lambda k, s: jax.random.normal(k, s, jnp.float32)
    return {
        'x': nrm(ks[0], (BATCH, SEQ, D_MODEL)),
        'c': nrm(ks[1], (BATCH, D_MODEL)),
        'ctx': nrm(ks[2], (BATCH, CTX_LEN, D_MODEL)),
        'c_ctx': nrm(ks[3], (D_MODEL,)),
        'w_mod': nrm(ks[4], (DEPTH, D_MODEL, 3 * D_MODEL)) * (0.5 * D_MODEL ** -0.5),
        'b_mod': nrm(ks[5], (DEPTH, 3 * D_MODEL)) * 0.02,
        'g_pre': 1.0 + 0.02 * nrm(ks[6], (DEPTH, D_MODEL)),
        'g_post': 1.0 + 0.02 * nrm(ks[7], (DEPTH, D_MODEL)),
        'w_in': nrm(ks[8], (DEPTH, D_MODEL, N_IN)) * D_MODEL ** -0.5,
        'w_out': nrm(ks[9], (DEPTH, D_MIX, D_MODEL)) * D_MIX ** -0.5,
        'sgu_w': nrm(ks[10], (DEPTH, H_A, CHUNK, CHUNK)) * CHUNK ** -0.5,
        'sgu_b': 1.0 + 0.1 * nrm(ks[11], (DEPTH, H_A, CHUNK)),
        'lam_q1': 0.1 * nrm(ks[12], (DEPTH, HD_QK)),
        'lam_k1': 0.1 * nrm(ks[13], (DEPTH, HD_QK)),
        'lam_q2': 0.1 * nrm(ks[14], (DEPTH, HD_QK)),
        'lam_k2': 0.1 * nrm(ks[15], (DEPTH, HD_QK)),
        'subln_g': 1.0 + 0.02 * nrm(ks[16], (DEPTH, HD_V)),
        'rwkv_mu': jax.random.uniform(ks[17], (DEPTH, 2, RWKV_F), jnp.float32),
        'rwkv_w0': jax.random.uniform(ks[18], (DEPTH, 2, W_GRP), jnp.float32, -4.0, -0.5),
        'rwkv_w2': nrm(ks[19], (DEPTH, 2, LORA_W, W_GRP)) * (0.5 * LORA_W ** -0.5),
        'rwkv_a0': 0.5 * nrm(ks[20], (DEPTH, 2, W_GRP)),
        'rwkv_a2': nrm(ks[21], (DEPTH, 2, LORA_A, W_GRP)) * (0.5 * LORA_A ** -0.5),
        'rwkv_kk': 0.85 + 0.05 * nrm(ks[22], (DEPTH, 2, W_GRP)),
        'rwkv_ka': 1.0 + 0.05 * nrm(ks[23], (DEPTH, 2, W_GRP)),
        'rwkv_rk': 0.1 * nrm(ks[24], (DEPTH, 2, H_C, HD_C)),
        'rwkv_ln_w': 1.0 + 0.02 * nrm(ks[25], (DEPTH, W_GRP)),
        'rwkv_ln_b': 0.02 * nrm(ks[26], (DEPTH, W_GRP)),
        'conv_w': nrm(ks[27], (DEPTH, CONV_W, W_GRP)) * CONV_W ** -0.5,
    }


def reference(x, c, ctx, c_ctx, w_mod, b_mod, g_pre, g_post, w_in, w_out, sgu_w, sgu_b,
              lam_q1, lam_k1, lam_q2, lam_k2, subln_g, rwkv_mu, rwkv_w0, rwkv_w2, rwkv_a0,
              rwkv_a2, rwkv_kk, rwkv_ka, rwkv_rk, rwkv_ln_w, rwkv_ln_b, conv_w):
    b, n, _ = x.shape
    rows = n // GRID_W
    row = jnp.repeat(jnp.arange(rows), GRID_W)
    col = jnp.tile(jnp.arange(GRID_W), rows)
    s_c = jax.nn.silu(c)
    s_cc = jax.nn.silu(c_ctx)
    xc = ctx
    for l in range(DEPTH):
        need_ctx = l < DEPTH - 1
        shift, scale, gate = jnp.split((s_c @ w_mod[l] + b_mod[l])[:, None, :], 3, axis=-1)
        shift_c, scale_c, gate_c = jnp.split(s_cc @ w_mod[l] + b_mod[l], 3, axis=-1)
        hl = rmsnorm(x, g_pre[l]) * (1.0 + scale) + shift
        hc = rmsnorm(xc, g_pre[l]) * (1.0 + scale_c) + shift_c
        pl = split_proj(hl @ w_in[l])
        pc = split_proj(hc @ w_in[l])

        oA_l = chunk_sgu(pl[0], pl[1], sgu_w[l], sgu_b[l]) * jax.nn.silu(pl[2])
        oA_c = chunk_sgu(pc[0], pc[1], sgu_w[l], sgu_b[l]) * jax.nn.silu(pc[2]) if need_ctx else None

        ql = rope_2d(pl[3].reshape(b, n, H_B, 2, HD_QK), row, col)
        kl = rope_2d(pl[4].reshape(b, n, H_B, 2, HD_QK), row, col)
        vl = pl[5].reshape(b, n, H_B, HD_V)
        qc = pc[3].reshape(b, CTX_LEN, H_B, 2, HD_QK)
        kc = pc[4].reshape(b, CTX_LEN, H_B, 2, HD_QK)
        vc = pc[5].reshape(b, CTX_LEN, H_B, HD_V)
        lam_init = 0.8 - 0.6 * math.exp(-0.3 * l)
        lam = (jnp.exp(jnp.sum(_f32(lam_q1[l]) * _f32(lam_k1[l])))
               - jnp.exp(jnp.sum(_f32(lam_q2[l]) * _f32(lam_k2[l]))) + lam_init)
        oB_c, oB_l = diff_attention(qc, kc, vc, ql, kl, vl, lam, lam_init, subln_g[l], need_ctx)
        oB_l = oB_l * jax.nn.silu(pl[6])
        oB_c = oB_c * jax.nn.silu(pc[6]) if need_ctx else None

        p_fwd = (rwkv_mu[l, 0], rwkv_w0[l, 0], rwkv_w2[l, 0], rwkv_a0[l, 0], rwkv_a2[l, 0],
                 rwkv_kk[l, 0], rwkv_ka[l, 0], rwkv_rk[l, 0])
        p_bwd = (rwkv_mu[l, 1], rwkv_w0[l, 1], rwkv_w2[l, 1], rwkv_a0[l, 1], rwkv_a2[l, 1],
                 rwkv_kk[l, 1], rwkv_ka[l, 1], rwkv_rk[l, 1])
        oC_c, oC_l = rwkv_mixer(pc[7], pc[8], pc[9], pl[7], pl[8], pl[9], p_fwd, p_bwd,
                                rwkv_ln_w[l], rwkv_ln_b[l], need_ctx)
        oC_l = oC_l * jax.nn.silu(pl[10])
        oC_c = oC_c * jax.nn.silu(pc[10]) if need_ctx else None

        oD_l = short_conv(pl[11], pl[12], pl[13], conv_w[l]) * jax.nn.silu(pl[14])
        oD_c = short_conv(pc[11], pc[12], pc[13], conv_w[l]) * jax.nn.silu(pc[14]) if need_ctx else None

        o_l = jnp.concatenate([oA_l, oB_l, oC_l, oD_l], axis=-1) @ w_out[l]
        x = x + gate * rmsnorm(o_l, g_post[l])
        if need_ctx:
            o_c = jnp.concatenate([oA_c, oB_c, oC_c, oD_c], axis=-1) @ w_out[l]
            xc = xc + gate_c * rmsnorm(o_c, g_post[l])
    return x
```

```python
import math
import numpy as np
import concourse.bass as bass
import concourse.mybir as mybir
from concourse.bass_utils import run_bass_kernel_spmd

F32 = mybir.dt.float32
BF16 = mybir.dt.bfloat16
AF = mybir.ActivationFunctionType
ALU = mybir.AluOpType
AX = mybir.AxisListType

D = 2048
SEQ = 4096
CTX = 256
NB = 2
WG = 512
N_IN = 7872
EPS = 1e-6
GN_EPS = 64e-5
NCORES = 8

ENGS = ("pe", "act", "dve", "pool", "sp")
SAME_ENGINE_SYNC = {"pe": False, "act": True, "dve": True, "pool": True, "sp": False}


class Tok:
    __slots__ = ("name", "lastw", "readers", "sem", "semcnt", "psum", "multi", "wlist")

    def __init__(self, name="", psum=False, multi=False):
        self.name = name
        self.psum = psum
        self.multi = multi
        self.wlist = []
        self.lastw = None
        self.readers = []
        self.sem = None
        self.semcnt = 0


NSEM_POOL = 92


class Prog:
    def __init__(self):
        self.nc = bass.Bass("TRN2", target_bir_lowering=False)
        nc = self.nc
        self.eng = {"pe": nc.tensor, "act": nc.scalar, "dve": nc.vector, "pool": nc.gpsimd, "sp": nc.sync}
        self.cnt = {e: 0 for e in ENGS}
        self.seen = {e: {} for e in ENGS}
        self.esem = {}
        self._ctx = []
        for e in ENGS:
            cm = nc.semaphore("s_" + e)
            self.esem[e] = cm.__enter__()
            self._ctx.append(cm)
        cm = nc.semaphore("s_cc")
        self.ccsem = cm.__enter__(); self._ctx.append(cm)
        self.cccnt = 0
        self.sem_free = []
        self.sem_base = {}
        for i in range(NSEM_POOL):
            cm = nc.semaphore("d_%d" % i)
            sm = cm.__enter__(); self._ctx.append(cm)
            self.sem_free.append(sm)
            self.sem_base[id(sm)] = 0
        self.live = []
        self.scopes = []
        self.out_events = []
        self.nalloc = 0

    def _push(self, cm):
        t = cm.__enter__()
        (self.scopes[-1]["ctx"] if self.scopes else self._ctx).append(cm)
        return t

    def sbuf(self, name, shape, dt):
        self.nalloc += 1
        return self._push(self.nc.sbuf_tensor("sb%d_%s" % (self.nalloc, name), list(shape), dt))

    def psum(self, name, shape, dt=F32):
        self.nalloc += 1
        return self._push(self.nc.psum_tensor("pp%d_%s" % (self.nalloc, name), list(shape), dt))

    def dram_in(self, name, shape, dt):
        return self.nc.dram_tensor(name, list(shape), dt, kind="ExternalInput").ap()

    def dram_out(self, name, shape, dt):
        return self.nc.dram_tensor(name, list(shape), dt, kind="ExternalOutput").ap()

    def dram_tmp(self, name, shape, dt):
        return self.nc.dram_tensor(name, list(shape), dt, kind="Internal").ap()

    def pid4(self, eng):
        if not hasattr(self, "_pid4"):
            self._pid4 = {}
        if eng not in self._pid4:
            self._pid4[eng] = self.eng[eng].partition_id() % 4
        return self._pid4[eng]

    def open_scope(self):
        self.scopes.append({"ctx": [], "toks": []})

    def close_scope(self):
        self.barrier()
        sc = self.scopes.pop()
        for cm in reversed(sc["ctx"]):
            cm.__exit__(None, None, None)
        for t in sc["toks"]:
            self.sem_base[id(t.sem)] = t.semcnt
            self.sem_free.append(t.sem)
            self.live.remove(t)
            t.sem = None

    def _tsem(self, tok):
        if tok.sem is None:
            tok.sem = self.sem_free.pop(0)
            tok.semcnt = self.sem_base[id(tok.sem)]
            self.live.append(tok)
            if self.scopes:
                self.scopes[-1]["toks"].append(tok)
        return tok.sem

    def _need(self, eng, deps):
        waits = []
        best = {}
        for d in deps:
            if d is None:
                continue
            key, val = d
            if key == eng and not SAME_ENGINE_SYNC[eng]:
                continue
            kk = id(key) if not isinstance(key, str) else key
            if best.get(kk, (None, 0))[1] < val:
                best[kk] = (key, val)
        for kk, (key, val) in best.items():
            if self.seen[eng].get(kk, 0) >= val:
                continue
            self.seen[eng][kk] = val
            sem = key if not isinstance(key, str) else self.esem[key]
            waits.append((sem, val))
        return waits

    @staticmethod
    def _deps(eng, reads, writes):
        deps = []
        for r in reads:
            deps.append(r.lastw)
            if r.multi:
                deps.extend(r.wlist)
            if r.psum:
                deps.extend(x for x in r.readers if x[0] != eng)
        for w in writes:
            if w.multi:
                continue
            deps.append(w.lastw)
            deps.extend(w.readers)
        return deps

    @staticmethod
    def _commit(ev, reads, writes):
        for r in reads:
            r.readers.append(ev)
        for w in writes:
            if w.multi:
                w.wlist.append(ev)
            else:
                w.lastw = ev
                w.readers = []

    def op(self, eng, fn, reads=(), writes=()):
        waits = self._need(eng, self._deps(eng, reads, writes))
        e = self.eng[eng]
        for (s, v) in waits:
            e.wait_ge(s, v)
        self.cnt[eng] += 1
        ev = (eng, self.cnt[eng])
        fn(e).then_inc(self.esem[eng], 1)
        self._commit(ev, reads, writes)
        return ev

    def dma(self, eng, out, in_, reads=(), writes=(), semtok=None, is_output=False, **kw):
        if semtok is None:
            semtok = writes[0] if writes else reads[0]
        sem = self._tsem(semtok)
        waits = self._need(eng, self._deps(eng, reads, writes))
        e = self.eng[eng]
        for (s, v) in waits:
            e.wait_ge(s, v)
        semtok.semcnt += 16
        ev = (sem, semtok.semcnt)
        e.dma_start(out=out, in_=in_, **kw).then_inc(sem, 16)
        self._commit(ev, reads, writes)
        return ev

    def barrier(self):
        for e in ENGS:
            eo = self.eng[e]
            for e2 in ENGS:
                if e2 != e and self.seen[e].get(e2, 0) < self.cnt[e2]:
                    eo.wait_ge(self.esem[e2], self.cnt[e2])
                    self.seen[e][e2] = self.cnt[e2]
            for t in self.live:
                if self.seen[e].get(id(t.sem), 0) < t.semcnt:
                    eo.wait_ge(t.sem, t.semcnt)
                    self.seen[e][id(t.sem)] = t.semcnt
            if self.cccnt and self.seen[e].get(id(self.ccsem), 0) < self.cccnt:
                eo.wait_ge(self.ccsem, self.cccnt)
                self.seen[e][id(self.ccsem)] = self.cccnt

    def allgather(self, in_ap, out_ap, groups):
        self.barrier()
        self.cccnt += 1
        self.eng["pool"].collective_compute("AllGather", ALU.bypass, replica_groups=groups, ins=[in_ap], outs=[out_ap]).then_inc(self.ccsem, 1)
        self.barrier()

    def allgather_async(self, in_ap, out_ap, groups, reads=(), out_tok=None):
        waits = self._need("pool", self._deps("pool", reads, []))
        e = self.eng["pool"]
        for (s_, v) in waits:
            e.wait_ge(s_, v)
        if self.cccnt and self.seen["pool"].get(id(self.ccsem), 0) < self.cccnt:
            e.wait_ge(self.ccsem, self.cccnt)
            self.seen["pool"][id(self.ccsem)] = self.cccnt
        self.cccnt += 1
        e.collective_compute("AllGather", ALU.bypass, replica_groups=groups, ins=[in_ap], outs=[out_ap]).then_inc(self.ccsem, 1)
        if out_tok is not None:
            out_tok.lastw = (self.ccsem, self.cccnt)
            out_tok.readers = []

    def finish(self):
        self.barrier()
        for cm in reversed(self._ctx):
            cm.__exit__(None, None, None)
        return self.nc


class BankPool:
    def __init__(self, bufs):
        self.items = [(b, Tok(psum=True)) for b in bufs]
        self.free = list(range(len(bufs)))

    def try_get(self):
        if not self.free:
            return None
        i = self.free.pop(0)
        return i

    def put(self, i):
        self.free.append(i)


class Rot:
    def __init__(self, bufs, psum=False):
        self.bufs = bufs
        self.toks = [Tok(psum=psum) for _ in bufs]
        self.i = 0

    def next(self):
        j = self.i % len(self.bufs)
        self.i += 1
        return self.bufs[j], self.toks[j]


MC = 1536
GROUPS = [[0, 1, 2, 3], [4, 5, 6, 7]]


def emit_M(P, cT_d, wm_d, bm_d, out_d):
    P.open_scope()
    cs = P.sbuf("cs", [128, 16, 2], F32); cs_t = Tok()
    sb = P.sbuf("sb", [128, 16, 2], BF16); sb_t = Tok()
    bs = P.sbuf("bs", [2, 2, MC], F32); bs_t = Tok()
    os_ = P.sbuf("os", [2, 2, MC], F32); os_t = Tok()
    ps = [P.psum("ps%d" % i, [2, 512], F32) for i in range(6)]
    ps_t = [Tok(psum=True) for _ in range(6)]
    wb = [P.sbuf("wb%d" % i, [128, 16, 512], BF16) for i in range(3)]
    wb_t = [Tok() for _ in range(3)]
    P.dma("sp", cs[:], cT_d.rearrange("(kc p) j -> p kc j", p=128), writes=[cs_t])
    for l in range(2):
        P.dma("sp", bs[:, l, :], bm_d[l:l + 1, :].broadcast_to([2, MC]), writes=[bs_t])
    P.op("act", lambda e: e.activation(sb[:], cs[:], AF.Silu), reads=[cs_t], writes=[sb_t])
    for l in range(2):
        for j in range(3):
            pi = l * 3 + j
            w, w_t = wb[pi % 3], wb_t[pi % 3]
            P.dma("pool", w[:], wm_d[l][:, j * 512:(j + 1) * 512].rearrange("(kc p) n -> p kc n", p=128), writes=[w_t])
            for kc in range(16):
                P.op("pe", lambda e, pi=pi, w=w, kc=kc: e.matmul(ps[pi][:, :], sb[:, kc, :], w[:, kc, :], start=(kc == 0), stop=(kc == 15)),
                     reads=[sb_t, w_t], writes=[ps_t[pi]])
            P.op("dve", lambda e, pi=pi, l=l, j=j: e.tensor_tensor(os_[:, l, j * 512:(j + 1) * 512], ps[pi][:, :], bs[:, l, j * 512:(j + 1) * 512], ALU.add),
                 reads=[ps_t[pi], bs_t], writes=[os_t])
    P.dma("sp", out_d.rearrange("l j n -> j l n"), os_[:], reads=[os_t])
    P.close_scope()


def mod_pieces(c0, n=D):
    out = []
    c = c0
    while c < c0 + n:
        r, off = c // MC, c % MC
        ln = min(MC - off, c0 + n - c)
        out.append((r, off, ln, c - c0))
        c += ln
    return out


NT = 9
NTT = 10
OA, OD, OQ, OK_, OV, OGB, ORKV, OLF, OLB, OGC = 0, 512, 1024, 1536, 2048, 2560, 3072, 4608, 4704, 4800
NOUT_A = 5312
CBS = [
    (0, 512, "A_u", None), (512, 512, "A_v", None), (1024, 512, "A_g", OA),
    (1536, 512, "rope", OQ), (2048, 512, "rope", OK_), (2560, 512, "copy", OV), (3072, 512, "silu", OGB),
    (3584, 512, "copy", ORKV), (4096, 512, "copy", ORKV + 512), (4608, 512, "copy", ORKV + 1024),
    (5120, 192, "copy", OLF), (5312, 512, "silu", OGC),
    (5824, 512, "D_b", None), (6336, 512, "D_c", None), (6848, 512, "D_x", None), (7360, 512, "D_g", None),
]


DEBUG = {"ncb": 16, "conv": True, "halo": True}


def emit_A(P, l, x_d, xh_d, hmask_d, mod_all, gpre_d, win_d, sguw_d, sgub_d, convw_d, cos_d, sin_d, ident_d, dst_fn, hook=None):
    P.open_scope()
    ident = P.sbuf("ident", [128, 128], F32); ident_t = Tok()
    P.dma("sp", ident[:], ident_d[:, :], writes=[ident_t])
    gpre = P.sbuf("gpre", [128, 16], F32); gpre_t = Tok()
    P.dma("sp", gpre[:], gpre_d, writes=[gpre_t])
    MV = P.sbuf("MV", [64, 128], F32); MV_t = Tok()
    for v, (row, c0) in enumerate(((0, 2048), (0, 0), (1, 2048), (1, 0))):
        for (r, off, ln, dst) in mod_pieces(c0):
            P.dma("sp", MV[v * 16 + dst // 128:v * 16 + (dst + ln) // 128, :],
                  mod_all[r, l, row, off:off + ln].rearrange("(k c) -> k c", c=128), writes=[MV_t], semtok=MV_t)
    pmv = P.psum("pmv", [128, 512], F32); pmv_t = Tok(psum=True)
    P.op("pe", lambda e: e.transpose(pmv[:, 0:64], MV[:, :], ident[0:64, 0:64]), reads=[MV_t, ident_t], writes=[pmv_t])
    modT = P.sbuf("modT", [128, 4, 16], F32); modT_t = Tok()
    P.op("dve", lambda e: e.tensor_copy(modT[:].rearrange("p a b -> p (a b)"), pmv[:, 0:64]), reads=[pmv_t], writes=[modT_t])
    sc = P.sbuf("sc", [128, 2, 16], F32); sc_t = Tok()
    sh = P.sbuf("sh", [128, 2, 16], F32); sh_t = Tok()
    for j in range(2):
        P.op("dve", lambda e, j=j: e.scalar_tensor_tensor(sc[:, j, :], modT[:, 2 * j, :], 1.0, gpre[:], ALU.add, ALU.mult),
             reads=[modT_t, gpre_t], writes=[sc_t])
        P.op("dve", lambda e, j=j: e.tensor_copy(sh[:, j, :], modT[:, 2 * j + 1, :]), reads=[modT_t], writes=[sh_t])
    sguw32 = P.sbuf("sguw32", [128, 4, 128], F32); sguw32_t = Tok()
    P.dma("sp", sguw32[:], sguw_d, writes=[sguw32_t])
    sguw = P.sbuf("sguw", [128, 4, 128], BF16); sguw_t = Tok()
    P.op("dve", lambda e: e.tensor_copy(sguw[:], sguw32[:]), reads=[sguw32_t], writes=[sguw_t])
    sgub = P.sbuf("sgub", [128, 4], F32); sgub_t = Tok()
    P.dma("sp", sgub[:], sgub_d, writes=[sgub_t])
    convw = P.sbuf("convw", [128, 3, WG], F32); convw_t = Tok()
    P.dma("sp", convw[:].rearrange("p a b -> p (a b)"), convw_d.broadcast_to([128, 3 * WG]), writes=[convw_t])
    rcos = P.sbuf("rcos", [128, NT, 64], F32); rcos_t = Tok()
    rsin = P.sbuf("rsin", [128, NT, 64], F32); rsin_t = Tok()
    P.dma("sp", rcos[:], cos_d[:, :, :], writes=[rcos_t])
    P.dma("sp", rsin[:], sin_d[:, :, :], writes=[rsin_t])
    hmask = P.sbuf("hmask", [4, 1], F32); hmask_t = Tok()
    P.dma("sp", hmask[:], hmask_d[:, :], writes=[hmask_t])

    hT = P.sbuf("hT", [128, 16, NTT * 128], BF16)
    hT_t = [Tok() for _ in range(NTT)]
    xs = Rot([P.sbuf("xs%d" % i, [128, D], F32) for i in range(2)])
    junk = P.sbuf("junk", [128, D], BF16); junk_t = Tok()
    small = P.sbuf("small", [128, NTT, 4], F32)
    small_t = [Tok() for _ in range(NTT)]
    wbuf = [P.sbuf("wbuf%d" % i, [128, 16, 512], BF16) for i in range(2)]
    wbuf_t = [[Tok(), Tok()] for _ in range(2)]
    UB = P.sbuf("UB", [128, NT, WG], F32); UB_t = [Tok() for _ in range(NT)]
    VN = P.sbuf("VN", [128, NT, WG], BF16); VN_t = [Tok() for _ in range(NT)]
    CG = P.sbuf("CG", [128, NT, WG], F32); CG_t = [Tok() for _ in range(NT)]
    ZH = P.sbuf("ZH", [4, WG], F32); ZH_t = Tok()
    st = Rot([P.sbuf("st%d" % i, [128, 512], F32) for i in range(4)])
    tA = Rot([P.sbuf("tA%d" % i, [128, 512], F32) for i in range(2)])
    tB = Rot([P.sbuf("tB%d" % i, [128, 512], F32) for i in range(2)])
    lnst = Rot([P.sbuf("lnst%d" % i, [128, 16], F32) for i in range(2)])
    pt = Rot([P.psum("pt%d" % i, [128, 4, 128], F32) for i in range(2)], psum=True)
    pm = Rot([P.psum("pm%d" % i, [128, 512], F32) for i in range(3)], psum=True)
    pmix = Rot([P.psum("pmix%d" % i, [128, 512], F32) for i in range(2)], psum=True)

    P.op("pool", lambda e: e.memset(hT[:, :, NT * 128:NTT * 128], 0.0), writes=[hT_t[NT]])

    for t in range(NTT):
        rows = 128 if t < NT else 4
        xb, xb_t = xs.next()
        src = x_d[t] if t < NT else xh_d[:, :]
        P.dma("sp", xb[:rows, :], src, writes=[xb_t])
        sm = small[:rows, t, :]
        P.op("act", lambda e, xb=xb, rows=rows, sm=sm: e.activation(junk[:rows, :], xb[:rows, :], AF.Square, accum_out=sm[:, 0:1]),
             reads=[xb_t], writes=[junk_t, small_t[t]])
        P.op("act", lambda e, sm=sm: e.activation(sm[:, 1:2], sm[:, 0:1], AF.Ln, bias=EPS, scale=1.0 / D),
             reads=[small_t[t]], writes=[small_t[t]])
        P.op("act", lambda e, sm=sm: e.activation(sm[:, 2:3], sm[:, 1:2], AF.Exp, scale=-0.5),
             reads=[small_t[t]], writes=[small_t[t]])
        P.op("dve", lambda e, xb=xb, rows=rows, sm=sm: e.tensor_scalar(xb[:rows, :], xb[:rows, :], sm[:, 2:3], None, ALU.mult),
             reads=[xb_t, small_t[t]], writes=[xb_t])
        for g in range(4):
            pb, pb_t = pt.next()
            for j in range(4):
                kc = g * 4 + j
                P.op("pe", lambda e, pb=pb, j=j, kc=kc, xb=xb, rows=rows: e.transpose(
                    pb[:, j, :rows], xb[:rows, kc * 128:(kc + 1) * 128], ident[:rows, :rows]),
                    reads=[xb_t, ident_t], writes=[pb_t])
            for j in range(4):
                kc = g * 4 + j
                if t < NT:
                    m = 0 if t < NT - 1 else 1
                    parts = [(0, 128, m)]
                else:
                    parts = [(0, 2, 0), (2, 4, 1)]
                for (a, b, m) in parts:
                    P.op("act", lambda e, pb=pb, j=j, kc=kc, t=t, a=a, b=b, m=m: e.activation(
                        hT[:, kc, t * 128 + a:t * 128 + b], pb[:, j, a:b], AF.Identity,
                        scale=sc[:, m, kc:kc + 1], bias=sh[:, m, kc:kc + 1]),
                        reads=[pb_t, sc_t, sh_t], writes=[hT_t[t]])

    oq = ["sp"]

    def store(t, col, n, buf, buf_t):
        dap, dtok = dst_fn(t, col, n)
        P.dma("sp", dap, buf[:, :n], reads=[buf_t], writes=[dtok] if dtok is not None else [], semtok=buf_t)

    order = [3, 4, 5, 6, 7, 8, 9, 10, 11, 0, 1, 2, 12, 13, 14, 15]
    for ci, (c0, ncol, kind, ocol) in enumerate([CBS[i] for i in order]):
        wb = wbuf[ci % 2]; wb_t = wbuf_t[ci % 2]
        for hf in range(2):
            P.dma("pool", wb[:, hf * 8:(hf + 1) * 8, :ncol],
                  win_d[hf * 1024:(hf + 1) * 1024, c0:c0 + ncol].rearrange("(kc p) n -> p kc n", p=128),
                  writes=[wb_t[hf]])
        tiles = list(range(NT)) + ([NT] if kind in ("D_c", "D_x") else [])
        for t in tiles:
            rows = 128 if t < NT else 4
            ps, ps_t = pm.next()
            for kc in range(16):
                P.op("pe", lambda e, ps=ps, rows=rows, ncol=ncol, kc=kc, t=t, wb=wb: e.matmul(
                    ps[:rows, :ncol], hT[:, kc, t * 128:t * 128 + rows], wb[:, kc, :ncol], start=(kc == 0), stop=(kc == 15)),
                    reads=[hT_t[t], wb_t[kc // 8]], writes=[ps_t])
            if kind == "A_u":
                P.op("act", lambda e, ps=ps, t=t: e.copy(UB[:, t, :], ps[:, :]), reads=[ps_t], writes=[UB_t[t]])
            elif kind == "A_v":
                ls, ls_t = lnst.next()
                sq, sq_t = tA.next()
                P.op("act", lambda e, ps=ps, sq=sq: e.activation(sq[:], ps[:, :], AF.Square), reads=[ps_t], writes=[sq_t])
                P.op("dve", lambda e, ps=ps, ls=ls: e.tensor_reduce(ls[:, 0:4], ps[:, :].rearrange("p (h d) -> p h d", h=4), AX.X, ALU.add),
                     reads=[ps_t], writes=[ls_t])
                P.op("dve", lambda e, sq=sq, ls=ls: e.tensor_reduce(ls[:, 4:8], sq[:].rearrange("p (h d) -> p h d", h=4), AX.X, ALU.add),
                     reads=[sq_t, ls_t], writes=[ls_t])
                P.op("dve", lambda e, ls=ls: e.tensor_scalar(ls[:, 0:4], ls[:, 0:4], 1.0 / 128, None, ALU.mult), reads=[ls_t], writes=[ls_t])
                P.op("dve", lambda e, ls=ls: e.tensor_tensor(ls[:, 8:12], ls[:, 0:4], ls[:, 0:4], ALU.mult), reads=[ls_t], writes=[ls_t])
                P.op("dve", lambda e, ls=ls: e.scalar_tensor_tensor(ls[:, 8:12], ls[:, 4:8], 1.0 / 128, ls[:, 8:12], ALU.mult, ALU.subtract),
                     reads=[ls_t], writes=[ls_t])
                P.op("act", lambda e, ls=ls: e.activation(ls[:, 8:12], ls[:, 8:12], AF.Ln, bias=EPS), reads=[ls_t], writes=[ls_t])
                P.op("act", lambda e, ls=ls: e.activation(ls[:, 12:16], ls[:, 8:12], AF.Exp, scale=-0.5), reads=[ls_t], writes=[ls_t])
                for h in range(4):
                    P.op("dve", lambda e, ps=ps, ls=ls, h=h, t=t: e.tensor_scalar(
                        VN[:, t, h * 128:(h + 1) * 128], ps[:, h * 128:(h + 1) * 128], ls[:, h:h + 1], ls[:, 12 + h:13 + h],
                        ALU.subtract, ALU.mult), reads=[ps_t, ls_t], writes=[VN_t[t]])
            elif kind == "A_g":
                sg, sg_t = tA.next()
                P.op("act", lambda e, ps=ps, sg=sg: e.activation(sg[:], ps[:, :], AF.Silu), reads=[ps_t], writes=[sg_t])
                px, px_t = pmix.next()
                for h in range(4):
                    P.op("pe", lambda e, px=px, h=h, t=t: e.matmul(
                        px[:, h * 128:(h + 1) * 128], sguw[:, h, :], VN[:, t, h * 128:(h + 1) * 128], start=True, stop=True),
                        reads=[sguw_t, VN_t[t]], writes=[px_t])
                tb, tb_t = tB.next()
                for h in range(4):
                    P.op("dve", lambda e, px=px, h=h, t=t, tb=tb: e.scalar_tensor_tensor(
                        tb[:, h * 128:(h + 1) * 128], px[:, h * 128:(h + 1) * 128], sgub[:, h:h + 1], UB[:, t, h * 128:(h + 1) * 128],
                        ALU.add, ALU.mult), reads=[px_t, sgub_t, UB_t[t]], writes=[tb_t])
                sb_, sb_t = st.next()
                P.op("pool", lambda e, tb=tb, sg=sg, sb_=sb_: e.tensor_tensor(sb_[:], tb[:], sg[:], ALU.mult),
                     reads=[tb_t, sg_t], writes=[sb_t])
                store(t, ocol, 512, sb_, sb_t)
            elif kind == "copy":
                sb_, sb_t = st.next()
                if t % 2 == 0:
                    P.op("act", lambda e, ps=ps, sb_=sb_, ncol=ncol: e.copy(sb_[:, :ncol], ps[:, :ncol]), reads=[ps_t], writes=[sb_t])
                else:
                    P.op("dve", lambda e, ps=ps, sb_=sb_, ncol=ncol: e.tensor_copy(sb_[:, :ncol], ps[:, :ncol]), reads=[ps_t], writes=[sb_t])
                store(t, ocol, ncol, sb_, sb_t)
            elif kind == "silu":
                sb_, sb_t = st.next()
                P.op("act", lambda e, ps=ps, sb_=sb_: e.activation(sb_[:], ps[:, :], AF.Silu), reads=[ps_t], writes=[sb_t])
                store(t, ocol, 512, sb_, sb_t)
            elif kind == "rope":
                t1, t1_t = tA.next()
                t2, t2_t = tB.next()
                cosb = rcos[:, t, :].unsqueeze(1).to_broadcast([128, 8, 64])
                P.op("dve", lambda e, ps=ps, t1=t1, cosb=cosb: e.tensor_tensor(
                    t1[:].rearrange("p (g d) -> p g d", g=8), ps[:, :].rearrange("p (g d) -> p g d", g=8), cosb, ALU.mult),
                    reads=[ps_t, rcos_t], writes=[t1_t])
                psv = ps[:, :].rearrange("p (g a b d) -> p g a b d", g=8, a=2, b=2)
                t2v = t2[:].rearrange("p (g a b d) -> p g a b d", g=8, a=2, b=2)
                snv = rsin[:, t, :].rearrange("p (a b d) -> p a b d", a=2, b=2)
                for a in range(2):
                    for b in range(2):
                        sn = snv[:, a, b, :].unsqueeze(1).to_broadcast([128, 8, 16])
                        P.op("dve", lambda e, a=a, b=b, psv=psv, t2v=t2v, sn=sn: e.tensor_tensor(
                            t2v[:, :, a, b, :], psv[:, :, a, 1 - b, :], sn, ALU.mult),
                            reads=[ps_t, rsin_t], writes=[t2_t])
                sb_, sb_t = st.next()
                P.op("pool", lambda e, t1=t1, t2=t2, sb_=sb_: e.tensor_tensor(sb_[:], t1[:], t2[:], ALU.add),
                     reads=[t1_t, t2_t], writes=[sb_t])
                store(t, ocol, 512, sb_, sb_t)
            elif kind == "D_b":
                P.op("act", lambda e, ps=ps, t=t: e.copy(UB[:, t, :], ps[:, :]), reads=[ps_t], writes=[UB_t[t]])
            elif kind == "D_c":
                if t < NT:
                    P.op("act", lambda e, ps=ps, t=t: e.copy(CG[:, t, :], ps[:, :]), reads=[ps_t], writes=[CG_t[t]])
                else:
                    P.op("act", lambda e, ps=ps: e.copy(ZH[:, :], ps[:4, :]), reads=[ps_t], writes=[ZH_t])
            elif kind == "D_x":
                if t < NT:
                    P.op("dve", lambda e, ps=ps, t=t: e.tensor_tensor(CG[:, t, :], CG[:, t, :], ps[:, :], ALU.mult),
                         reads=[ps_t, CG_t[t]], writes=[CG_t[t]])
                else:
                    P.op("dve", lambda e, ps=ps: e.scalar_tensor_tensor(ZH[:, :], ps[:4, :], hmask[:, 0:1], ZH[:, :], ALU.mult, ALU.mult),
                         reads=[ps_t, ZH_t, hmask_t], writes=[ZH_t])
            elif kind == "D_g":
                sg, sg_t = tA.next()
                P.op("act", lambda e, ps=ps, sg=sg: e.activation(sg[:], ps[:, :], AF.Silu), reads=[ps_t], writes=[sg_t])
                P.op("dve", lambda e, sg=sg, t=t: e.tensor_tensor(UB[:, t, :], UB[:, t, :], sg[:], ALU.mult),
                     reads=[sg_t, UB_t[t]], writes=[UB_t[t]])

            if hook is not None:
                hook(ci, t)

    hTf = hT[:].rearrange("p a b -> p (a b)").bitcast(F32)
    ZMv = hTf[:, 0:NT * WG].rearrange("p (t c) -> p t c", t=NT)
    ZPv = hTf[:, NT * WG:2 * NT * WG].rearrange("p (t c) -> p t c", t=NT)
    zm_t = Tok(); zp_t = Tok()
    P.dma("sp", ZMv[1:128, :, :], CG[0:127, :, :], reads=CG_t, writes=[zm_t] + hT_t)
    P.dma("sp", ZMv[0:1, 1:NT - 1, :], CG[127:128, 0:NT - 2, :], reads=CG_t, writes=[zm_t], semtok=zm_t)
    P.dma("sp", ZMv[0:1, 0, :], ZH[0:1, :], reads=[ZH_t], writes=[zm_t], semtok=zm_t)
    P.dma("sp", ZMv[0:1, NT - 1, :], ZH[2:3, :], reads=[ZH_t], writes=[zm_t], semtok=zm_t)
    P.dma("sp", ZPv[0:127, :, :], CG[1:128, :, :], reads=CG_t, writes=[zp_t] + hT_t)
    P.dma("sp", ZPv[127:128, 0:NT - 2, :], CG[0:1, 1:NT - 1, :], reads=CG_t, writes=[zp_t], semtok=zp_t)
    P.dma("sp", ZPv[127:128, NT - 2, :], ZH[1:2, :], reads=[ZH_t], writes=[zp_t], semtok=zp_t)
    P.dma("sp", ZPv[127:128, NT - 1, :], ZH[3:4, :], reads=[ZH_t], writes=[zp_t], semtok=zp_t)
    for t in range(NT):
        t1, t1_t = tA.next()
        t2, t2_t = tB.next()
        P.op("dve", lambda e, t=t, t1=t1: e.tensor_tensor(t1[:], CG[:, t, :], convw[:, 1, :], ALU.mult),
             reads=[CG_t[t], convw_t], writes=[t1_t])
        P.op("pool", lambda e, t=t, t2=t2: e.tensor_tensor(t2[:], ZMv[:, t, :], convw[:, 0, :], ALU.mult),
             reads=[zm_t, convw_t], writes=[t2_t])
        P.op("dve", lambda e, t1=t1, t2=t2: e.tensor_tensor(t1[:], t1[:], t2[:], ALU.add), reads=[t1_t, t2_t], writes=[t1_t])
        P.op("pool", lambda e, t=t, t2=t2: e.tensor_tensor(t2[:], ZPv[:, t, :], convw[:, 2, :], ALU.mult),
             reads=[zp_t, convw_t], writes=[t2_t])
        P.op("dve", lambda e, t1=t1, t2=t2: e.tensor_tensor(t1[:], t1[:], t2[:], ALU.add), reads=[t1_t, t2_t], writes=[t1_t])
        sb_, sb_t = st.next()
        P.op("dve", lambda e, t=t, t1=t1, sb_=sb_: e.tensor_tensor(sb_[:], t1[:], UB[:, t, :], ALU.mult),
             reads=[t1_t, UB_t[t]], writes=[sb_t])
        store(t, OD, 512, sb_, sb_t)
    if hook is not None:
        hook(None, None)
    P.close_scope()


def emit_C(P, l, ntile, pa_l, obc_loc, x_d, mod_all, gpost_d, wo_d, ident_d, out_d, edge_d=None, stage=None):
    P.open_scope()
    ident = P.sbuf("ident", [128, 128], F32); ident_t = Tok()
    P.dma("sp", ident[:], ident_d[:, :], writes=[ident_t])
    wo = P.sbuf("wo", [128, 16, D], BF16)
    wo_t = [Tok() for _ in range(4)]
    for j in range(4):
        P.dma("pool", wo[:, j * 4:(j + 1) * 4, :], wo_d[j * 512:(j + 1) * 512, :].rearrange("(kc p) n -> p kc n", p=128),
              writes=[wo_t[j]])
    stoks = stage() if stage is not None else [[Tok(), Tok()], [Tok(), Tok()]]
    gp = P.sbuf("gp", [128, D], F32); gp_t = Tok()
    P.dma("sp", gp[:], gpost_d.broadcast_to([128, D]), writes=[gp_t])
    GG = P.sbuf("GG", [128, 2, D], F32); GG_t = Tok()
    for j in range(2):
        for (r, off, ln, dst) in mod_pieces(4096):
            P.dma("sp", GG[:, j, dst:dst + ln], mod_all[r, l, j, off:off + ln].unsqueeze(0).broadcast_to([128, ln]), writes=[GG_t], semtok=GG_t)
    for j in range(2):
        P.op("pool", lambda e, j=j: e.tensor_tensor(GG[:, j, :], GG[:, j, :], gp[:], ALU.mult), reads=[GG_t, gp_t], writes=[GG_t])
    os_ = Rot([P.sbuf("os%d" % i, [128, D], F32) for i in range(2)])
    xs = Rot([P.sbuf("xs%d" % i, [128, D], F32) for i in range(2)])
    oT = Rot([P.sbuf("oT%d" % i, [128, 16, 128], BF16) for i in range(2)])
    ol = Rot([P.sbuf("ol%d" % i, [128, D], F32) for i in range(2)])
    junk = P.sbuf("junk", [128, D], BF16); junk_t = Tok()
    sm = P.sbuf("sm", [128, ntile, 4], F32); sm_t = [Tok() for _ in range(ntile)]
    pt = Rot([P.psum("pt%d" % i, [128, 4, 128], F32) for i in range(3)], psum=True)
    pm = Rot([P.psum("pm%d" % i, [128, 512], F32) for i in range(4)], psum=True)
    for t in range(ntile):
        ob, ob_t = os_.next()
        P.dma("sp", ob[:, 0:512], pa_l[t, :, 0:512], writes=[ob_t])
        P.dma("sp", ob[:, 1536:2048], pa_l[t, :, 512:1024], writes=[ob_t], semtok=ob_t)
        for j in range(2):
            P.dma("sp", ob[:, 512 + j * 512:1024 + j * 512].rearrange("p (h c) -> p h c", h=4),
                  obc_loc[:, j, t * 128:(t + 1) * 128, :].rearrange("h p c -> p h c"), reads=[stoks[j][0 if t < 8 else 1]],
                  writes=[ob_t], semtok=ob_t)
        xb, xb_t = xs.next()
        P.dma("sp", xb[:], x_d[t], writes=[xb_t])
        ot, ot_t = oT.next()
        for g in range(4):
            pb, pb_t = pt.next()
            for j in range(4):
                kc = g * 4 + j
                P.op("pe", lambda e, pb=pb, j=j, kc=kc, ob=ob: e.transpose(pb[:, j, :], ob[:, kc * 128:(kc + 1) * 128], ident[:]),
                     reads=[ob_t, ident_t], writes=[pb_t])
            if g % 2 == 0:
                P.op("act", lambda e, pb=pb, g=g, ot=ot: e.copy(ot[:, g * 4:(g + 1) * 4, :], pb[:]), reads=[pb_t], writes=[ot_t])
            else:
                P.op("dve", lambda e, pb=pb, g=g, ot=ot: e.tensor_copy(ot[:, g * 4:(g + 1) * 4, :], pb[:]), reads=[pb_t], writes=[ot_t])
        olb, olb_t = ol.next()
        for cb in range(4):
            ps, ps_t = pm.next()
            for kc in range(16):
                P.op("pe", lambda e, ps=ps, kc=kc, cb=cb, ot=ot: e.matmul(
                    ps[:, :], ot[:, kc, :], wo[:, kc, cb * 512:(cb + 1) * 512], start=(kc == 0), stop=(kc == 15)),
                    reads=[ot_t, wo_t[kc // 4]], writes=[ps_t])
            if cb % 2 == 0:
                P.op("dve", lambda e, ps=ps, cb=cb, olb=olb: e.tensor_copy(olb[:, cb * 512:(cb + 1) * 512], ps[:, :]), reads=[ps_t], writes=[olb_t])
            else:
                P.op("act", lambda e, ps=ps, cb=cb, olb=olb: e.copy(olb[:, cb * 512:(cb + 1) * 512], ps[:, :]), reads=[ps_t], writes=[olb_t])
        s_ = sm[:, t, :]
        P.op("act", lambda e, olb=olb, s_=s_: e.activation(junk[:], olb[:], AF.Square, accum_out=s_[:, 0:1]),
             reads=[olb_t], writes=[junk_t, sm_t[t]])
        P.op("act", lambda e, s_=s_: e.activation(s_[:, 1:2], s_[:, 0:1], AF.Ln, bias=EPS, scale=1.0 / D), reads=[sm_t[t]], writes=[sm_t[t]])
        P.op("act", lambda e, s_=s_: e.activation(s_[:, 2:3], s_[:, 1:2], AF.Exp, scale=-0.5), reads=[sm_t[t]], writes=[sm_t[t]])
        gi = 0 if t < 8 else 1
        P.op("dve", lambda e, olb=olb, s_=s_, gi=gi: e.scalar_tensor_tensor(olb[:], olb[:], s_[:, 2:3], GG[:, gi, :], ALU.mult, ALU.mult),
             reads=[olb_t, sm_t[t], GG_t], writes=[olb_t])
        P.op("pool", lambda e, olb=olb, xb=xb: e.tensor_tensor(xb[:], xb[:], olb[:], ALU.add), reads=[olb_t, xb_t], writes=[xb_t])
        P.dma("sp", out_d[t], xb[:], reads=[xb_t], semtok=xb_t)
        if edge_d is not None:
            for (tt, prow, er) in ((0, 0, 0), (7, 127, 1), (8, 0, 2), (8, 127, 3)):
                if tt == t:
                    P.dma("sp", edge_d[er:er + 1, :], xb[prow:prow + 1, :], reads=[xb_t], semtok=xb_t)
    P.close_scope()


NTOK = CTX + SEQ
NKB = NTOK // 128


def emit_attn(P, l, has_ctxq, at_loc, lam_d, subg_d, ident_d, out_d, pre=None):
    lam_init = 0.8 - 0.6 * math.exp(-0.3 * l)
    P.open_scope()
    ident = P.sbuf("ident", [128, 128], F32); ident_t = Tok()
    P.dma("sp", ident[:], ident_d[:, :], writes=[ident_t])
    qT = P.sbuf("qT", [128, NTOK], BF16); qT_t = Tok()
    KT = [P.sbuf("KT%d" % m, [128, NTOK], BF16) for m in range(2)]
    KT_t = [Tok(), Tok()]
    for m in range(2):
        P.op("dve", lambda e, m=m: e.memset(KT[m][(1 - m) * 64:(2 - m) * 64, :], 0.0), writes=[KT_t[m]])
    QK = P.sbuf("QKtm", [128, 2, NKB, 128], F32); QK_t = [Tok(), Tok()]
    for j in range(2):
        P.dma("sp", QK[:, j, :, :], at_loc[:, j * 128:(j + 1) * 128].rearrange("(t p) d -> p t d", p=128), writes=[QK_t[j]])
    ptr = P.psum("ptr", [128, 4, 128], F32); ptr_t = Tok(psum=True)
    for g in range(0, NKB, 2):
        for j in range(2):
            for i in range(2):
                P.op("pe", lambda e, j=j, i=i, g=g: e.transpose(ptr[:, j * 2 + i, :], QK[:, j, g + i, :], ident[:]),
                     reads=[QK_t[j], ident_t], writes=[ptr_t])
        P.op("act", lambda e, g=g: e.copy(qT[:, g * 128:(g + 2) * 128], ptr[:, 0:2, :].rearrange("p a b -> p (a b)")), reads=[ptr_t], writes=[qT_t])
        P.op("dve", lambda e, g=g: e.tensor_copy(KT[0][0:64, g * 128:(g + 2) * 128], ptr[0:64, 2:4, :].rearrange("p a b -> p (a b)")), reads=[ptr_t], writes=[KT_t[0]])
        P.op("act", lambda e, g=g: e.copy(KT[1][64:128, g * 128:(g + 2) * 128], ptr[64:128, 2:4, :].rearrange("p a b -> p (a b)")), reads=[ptr_t], writes=[KT_t[1]])
    V = P.sbuf("V", [128, NKB, 129], BF16); V_t = Tok()
    P.op("dve", lambda e: e.memset(V[:, :, 128:129], 1.0), writes=[V_t])
    P.dma("pool", V[:, :, 0:128], at_loc[:, 256:384].rearrange("(t p) d -> p t d", p=128), writes=[V_t])
    GS = P.sbuf("GS", [128, NKB, 128], F32); GS_t = Tok()
    P.dma("sp", GS[:], at_loc[:, 384:512].rearrange("(t p) d -> p t d", p=128), writes=[GS_t])
    subg = P.sbuf("subg", [128, 128], F32); subg_t = Tok()
    P.dma("sp", subg[:], subg_d.broadcast_to([128, 128]), writes=[subg_t])
    lamv = P.sbuf("lamv", [128, 256], F32); lamv_t = Tok()
    P.dma("sp", lamv[:], lam_d.broadcast_to([128, 256]), writes=[lamv_t])
    if pre is not None:
        pre()
    lsm = P.sbuf("lsm", [128, 8], F32); lsm_t = Tok()
    ljunk = P.sbuf("ljunk", [128, 64], F32); ljunk_t = Tok()
    for j in range(2):
        P.op("dve", lambda e, j=j: e.scalar_tensor_tensor(ljunk[:], lamv[:, j * 128:j * 128 + 64], 1.0, lamv[:, j * 128 + 64:j * 128 + 128],
                                                         ALU.mult, ALU.mult, accum_out=lsm[:, j:j + 1]),
             reads=[lamv_t], writes=[ljunk_t, lsm_t])
    P.op("act", lambda e: e.activation(lsm[:, 2:4], lsm[:, 0:2], AF.Exp), reads=[lsm_t], writes=[lsm_t])
    P.op("dve", lambda e: e.tensor_tensor(lsm[:, 4:5], lsm[:, 3:4], lsm[:, 2:3], ALU.subtract), reads=[lsm_t], writes=[lsm_t])
    P.op("dve", lambda e: e.tensor_scalar(lsm[:, 4:5], lsm[:, 4:5], -lam_init, None, ALU.add), reads=[lsm_t], writes=[lsm_t])
    P.op("dve", lambda e: e.scalar_tensor_tensor(GS[:], GS[:], 1.0 - lam_init, subg[:].unsqueeze(1).to_broadcast([128, NKB, 128]),
                                                ALU.mult, ALU.mult), reads=[GS_t, subg_t], writes=[GS_t])

    pss = Rot([P.psum("pss%d" % i, [128, 512], F32) for i in range(3)], psum=True)
    po = [P.psum("po%d" % i, [128, 512], F32) for i in range(4)]
    po_t = [Tok(psum=True) for _ in range(4)]
    pT = Rot([P.sbuf("pT%d" % i, [128, 512], BF16) for i in range(3)])
    o0 = P.sbuf("o0", [128, 4, 128], F32); o0_t = [Tok() for _ in range(4)]
    osb = Rot([P.sbuf("osb%d" % i, [128, 128], F32) for i in range(2)])
    ost = Rot([P.sbuf("ost%d" % i, [128, 128], F32) for i in range(3)])
    sjunk = P.sbuf("sjunk", [128, 128], F32); sjunk_t = Tok()
    rs = Rot([P.sbuf("rs%d" % i, [128, 8], F32) for i in range(4)])

    groups = []
    for i in range(8):
        for m in range(2):
            groups.append((CTX + i * 512, 512, list(range(NKB)), m))
    if has_ctxq:
        for m in range(2):
            groups.append((0, 256, [0, 1], m))
    steps = []
    for gi, (q0, nq, kbs, m) in enumerate(groups):
        for kb in kbs:
            steps.append((gi, kb))
    held = {}

    def emit_qk(si):
        gi, kb = steps[si]
        q0, nq, kbs, m = groups[gi]
        ps, ps_t = pss.next()
        P.op("pe", lambda e, ps=ps, nq=nq, m=m, kb=kb, q0=q0: e.matmul(
            ps[:, :nq], KT[m][:, kb * 128:(kb + 1) * 128], qT[:, q0:q0 + nq], start=True, stop=True),
            reads=[KT_t[m], qT_t], writes=[ps_t])
        held[si] = (ps, ps_t)

    LOOK = 2
    for si in range(min(LOOK, len(steps))):
        emit_qk(si)
    for si, (gi, kb) in enumerate(steps):
        q0, nq, kbs, m = groups[gi]
        if si + LOOK < len(steps):
            emit_qk(si + LOOK)
        ps, ps_t = held.pop(si)
        pt_, pt_t = pT.next()
        P.op("act", lambda e, ps=ps, pt_=pt_, nq=nq: e.activation(pt_[:, :nq], ps[:, :nq], AF.Exp, scale=0.125),
             reads=[ps_t], writes=[pt_t])
        nqs = nq // 128
        for qs in range(nqs):
            P.op("pe", lambda e, qs=qs, pt_=pt_, kb=kb, kbs=kbs: e.matmul(
                po[qs][:, 0:129], pt_[:, qs * 128:(qs + 1) * 128], V[:, kb, :], start=(kb == kbs[0]), stop=(kb == kbs[-1])),
                reads=[pt_t, V_t], writes=[po_t[qs]])
        if kb == kbs[-1]:
            for qs in range(nqs):
                r, r_t = rs.next()
                tile = (q0 // 128) + qs
                P.op("dve", lambda e, r=r, qs=qs: e.reciprocal(r[:, 0:1], po[qs][:, 128:129]), reads=[po_t[qs]], writes=[r_t])
                if m == 0:
                    P.op("dve", lambda e, r=r, qs=qs: e.tensor_scalar(o0[:, qs, :], po[qs][:, 0:128], r[:, 0:1], None, ALU.mult),
                         reads=[po_t[qs], r_t], writes=[o0_t[qs]])
                else:
                    ob, ob_t = osb.next()
                    P.op("dve", lambda e, r=r: e.tensor_tensor(r[:, 1:2], r[:, 0:1], lsm[:, 4:5], ALU.mult), reads=[r_t, lsm_t], writes=[r_t])
                    P.op("dve", lambda e, r=r, qs=qs, ob=ob: e.scalar_tensor_tensor(ob[:], po[qs][:, 0:128], r[:, 1:2], o0[:, qs, :], ALU.mult, ALU.add),
                         reads=[po_t[qs], r_t, o0_t[qs]], writes=[ob_t])
                    P.op("dve", lambda e, r=r, ob=ob: e.scalar_tensor_tensor(sjunk[:], ob[:], 1.0, ob[:], ALU.mult, ALU.mult, accum_out=r[:, 2:3]),
                         reads=[ob_t], writes=[sjunk_t, r_t])
                    P.op("act", lambda e, r=r: e.activation(r[:, 3:4], r[:, 2:3], AF.Ln, bias=EPS, scale=1.0 / 128), reads=[r_t], writes=[r_t])
                    P.op("act", lambda e, r=r: e.activation(r[:, 4:5], r[:, 3:4], AF.Exp, scale=-0.5), reads=[r_t], writes=[r_t])
                    st_, st_t = ost.next()
                    P.op("dve", lambda e, r=r, ob=ob, st_=st_, tile=tile: e.scalar_tensor_tensor(st_[:], ob[:], r[:, 4:5], GS[:, tile, :], ALU.mult, ALU.mult),
                         reads=[ob_t, r_t, GS_t], writes=[st_t])
                    P.dma("sp", out_d[tile, :, :], st_[:], reads=[st_t], semtok=st_t)
    P.close_scope()


RW_F = 480


def rw_row0(c):
    return 1 + c * 128 if c < 2 else 259 + (c - 2) * 128


RW_ROWS = NTOK + 4
RW_COLS = 704


def emit_rwkv(P, nch, order_f, order_b, rw_loc, mu_d, w2e_d, a2e_d, vecs_d, cst_d, out_d):
    ntok = nch * 128
    RB = BF16
    P.open_scope()
    cst = P.sbuf("cst", [128, 8, 128], F32); cst_t = Tok()
    P.dma("sp", cst[:], cst_d.rearrange("c p q -> p c q"), writes=[cst_t])
    ident = cst[:, 0, :]; ones = cst[:, 1, :]
    TRI = [cst[:, 2, :], cst[:, 3, :]]
    MS = [cst[:, 4, :], cst[:, 6, :]]
    MI = [cst[:, 5, :], cst[:, 7, :]]
    MST = [cst[:, 6, :], cst[:, 4, :]]
    identb = P.sbuf("identb", [128, 128], RB); identb_t = Tok()
    P.op("dve", lambda e: e.tensor_copy(identb[:], ident), reads=[cst_t], writes=[identb_t])
    mu = P.sbuf("mu", [128, 2, RW_F], F32); mu_t = Tok()
    P.dma("sp", mu[:].rearrange("p a b -> p (a b)"), mu_d.rearrange("a b -> (a b)").unsqueeze(0).broadcast_to([128, 2 * RW_F]), writes=[mu_t])
    vecs = P.sbuf("vecs", [128, 8, 128], F32); vecs_t = Tok()
    P.dma("sp", vecs[:].rearrange("p a b -> p (a b)"), vecs_d.rearrange("a b -> (a b)").unsqueeze(0).broadcast_to([128, 8 * 128]), writes=[vecs_t])
    w2e = P.sbuf("w2e", [65, 2, 128], F32); w2e_t = Tok()
    P.dma("sp", w2e[:], w2e_d.rearrange("d k n -> k d n"), writes=[w2e_t])
    a2e = P.sbuf("a2e", [33, 2, 128], F32); a2e_t = Tok()
    P.dma("sp", a2e[:], a2e_d.rearrange("d k n -> k d n"), writes=[a2e_t])
    G = P.sbuf("G", [128, nch, 128], F32); G_t = Tok()
    P.dma("sp", G[:, 0:2, :], rw_loc[1:257, 384:512].rearrange("(c p) n -> p c n", p=128), writes=[G_t])
    P.dma("sp", G[:, 2:nch, :], rw_loc[259:259 + (nch - 2) * 128, 384:512].rearrange("(c p) n -> p c n", p=128), writes=[G_t], semtok=G_t)
    Y = [P.sbuf("Y%d" % d, [128, nch, 128], F32) for d in range(2)]
    Y_t = [[Tok() for _ in range(nch)] for d in range(2)]
    BON1 = P.sbuf("BON", [128, nch, 128], F32)
    BON = [BON1, BON1]
    BON1_t = [Tok() for _ in range(nch)]
    BON_t = [BON1_t, BON1_t]
    P.op("pool", lambda e: e.memset(BON1[:], 0.0), writes=BON1_t)
    ST32 = [P.sbuf("ST32_%d" % d, [128, 64], F32) for d in range(2)]
    STb = [P.sbuf("STb_%d" % d, [128, 64], RB) for d in range(2)]
    ST_t = [Tok(), Tok()]; STb_t = [Tok(), Tok()]
    for d in range(2):
        P.op("pool", lambda e, d=d: e.memset(ST32[d][:], 0.0), writes=[ST_t[d]])
        P.op("pool", lambda e, d=d: e.memset(STb[d][:], 0.0), writes=[STb_t[d]])

    NB_ = 2

    def dbuf(name, shape, dt, zero=False, one_rows=None):
        bufs = []
        for d in range(2):
            lst = []
            for i in range(NB_):
                b = P.sbuf("%s_%d_%d" % (name, d, i), shape, dt)
                t = Tok()
                if zero:
                    P.op("pool", lambda e, b=b: e.memset(b[:], 0.0), writes=[t])
                if one_rows is not None:
                    P.op("pool", lambda e, b=b: e.memset(b[one_rows[0]:one_rows[1], :], 1.0), writes=[t])
                lst.append((b, t))
            bufs.append(lst)
        return bufs

    X = dbuf("X", [128, RW_F], F32); XP = dbuf("XP", [128, RW_F], F32); Z = dbuf("Z", [128, RW_F], F32)
    TW = dbuf("TW", [128, 64], F32)
    TWT = dbuf("TWT", [65, 128], F32, one_rows=(64, 65)); ZAT = dbuf("ZAT", [33, 128], F32, one_rows=(32, 33))
    LW = dbuf("LW", [128, 128], F32); ETA = dbuf("ETA", [128, 128], F32)
    KK = dbuf("KK", [128, 128], F32); KP = dbuf("KP", [128, 128], F32); BB = dbuf("BB", [128, 128], F32)
    TMP = dbuf("TMP", [128, 128], F32); TMP2 = dbuf("TMP2", [128, 128], F32)
    SM = dbuf("SM", [128, 16], F32)
    CSs = dbuf("CSs", [128, 128], F32); D3 = dbuf("D3", [128, 128], F32); D4 = dbuf("D4", [128, 128], F32)
    E1 = dbuf("E1", [128, 128], F32); E2 = dbuf("E2", [128, 128], F32); E3 = dbuf("E3", [128, 128], F32); E4 = dbuf("E4", [128, 128], F32)
    GCOL = dbuf("GCOL", [128, 1], F32)
    AH = dbuf("AH", [128, 128], RB); BH = dbuf("BH", [128, 128], RB); KH = dbuf("KH", [128, 128], RB); RH = dbuf("RH", [128, 128], RB)
    BTz = [dbuf("BTz%d" % h, [128, 128], RB, zero=True) for h in range(2)]
    KTz = [dbuf("KTz%d" % h, [128, 128], RB, zero=True) for h in range(2)]
    VB = dbuf("VB", [128, 128], RB)
    BHT = dbuf("BHT", [128, 128], RB); KHT = dbuf("KHT", [128, 128], RB)
    AHTz = [dbuf("AHTz%d" % h, [128, 128], RB, zero=True) for h in range(2)]
    RHTz = [dbuf("RHTz%d" % h, [128, 128], RB, zero=True) for h in range(2)]
    Nm = [[dbuf("Nm%d_%d" % (h, i), [128, 128], F32) for i in range(2)] for h in range(2)]
    Mm = [[dbuf("Mm%d_%d" % (h, i), [128, 128], F32) for i in range(2)] for h in range(2)]
    Pm = [[dbuf("Pm%d_%d" % (h, i), [128, 128], F32) for i in range(2)] for h in range(2)]
    AH32 = dbuf("AH32", [128, 128], F32)
    AAK = [dbuf("AAK%d" % h, [128, 128], RB) for h in range(2)]
    ARB = [dbuf("ARB%d" % h, [128, 128], RB) for h in range(2)]
    ARK = [dbuf("ARK%d" % h, [128, 128], RB) for h in range(2)]
    X2 = [dbuf("X2%d" % h, [128, 64], F32) for h in range(2)]
    WTz = [dbuf("WTz%d" % h, [128, 128], RB, zero=True) for h in range(2)]
    US = [dbuf("US%d" % h, [128, 64], RB) for h in range(2)]

    pa_ = Rot([P.psum("pa%d" % i, [128, 512], F32) for i in range(2)], psum=True)
    pb_ = Rot([P.psum("pb%d" % i, [128, 512], F32) for i in range(1)], psum=True)
    ptb = P.psum("ptb", [128, 4, 128], RB); ptb_t = Tok(psum=True)
    pdbl = BankPool([P.psum("pdbl%d" % i, [128, 512], F32) for i in range(4)])

    def vec(i):
        return vecs[:, i, :]

    def unit(d, c, it):
        par = it % NB_
        g = lambda buf: buf[d][par]
        x, x_t = g(X); xp, xp_t = g(XP); z, z_t = g(Z)
        r0_ = rw_row0(c)
        rp_ = r0_ - 1 if d == 0 else r0_ + 1
        P.dma("sp", x[:, 0:384], rw_loc[r0_:r0_ + 128, 0:384], writes=[x_t])
        yield
        P.dma("sp", x[:, 384:480], rw_loc[r0_:r0_ + 128, 512 + d * 96:608 + d * 96], writes=[x_t], semtok=x_t)
        yield
        P.dma("sp", xp[:, 0:384], rw_loc[rp_:rp_ + 128, 0:384], writes=[xp_t])
        yield
        P.dma("sp", xp[:, 384:480], rw_loc[rp_:rp_ + 128, 512 + d * 96:608 + d * 96], writes=[xp_t], semtok=xp_t)
        yield
        P.op("dve", lambda e: e.tensor_tensor(xp[:], xp[:], x[:], ALU.subtract), reads=[x_t, xp_t], writes=[xp_t])
        yield
        P.op("pool", lambda e: e.tensor_tensor(xp[:], xp[:], mu[:, d, :], ALU.mult), reads=[xp_t, mu_t], writes=[xp_t])
        yield
        P.op("dve", lambda e: e.tensor_tensor(z[:], x[:], xp[:], ALU.add), reads=[x_t, xp_t], writes=[z_t])
        yield
        zr = z[:, 0:128]; zk = z[:, 128:256]; zv = z[:, 256:384]; zw = z[:, 384:448]; za = z[:, 448:480]
        yield
        tw, tw_t = g(TW); twt, twt_t = g(TWT); zat, zat_t = g(ZAT)
        P.op("act", lambda e: e.activation(tw[:], zw, AF.Tanh), reads=[z_t], writes=[tw_t])
        yield
        p1, p1_t = pa_.next()
        P.op("pe", lambda e: e.transpose(p1[0:64, 0:128], tw[:], ident), reads=[tw_t, cst_t], writes=[p1_t])
        P.op("pe", lambda e: e.transpose(p1[0:32, 128:256], za, ident), reads=[z_t, cst_t], writes=[p1_t])
        P.op("act", lambda e: e.copy(twt[0:64, :], p1[0:64, 0:128]), reads=[p1_t], writes=[twt_t])
        P.op("act", lambda e: e.copy(zat[0:32, :], p1[0:32, 128:256]), reads=[p1_t], writes=[zat_t])
        p2, p2_t = pa_.next()
        P.op("pe", lambda e: e.matmul(p2[:, 0:128], twt[:, :], w2e[:, d, :], start=True, stop=True), reads=[twt_t, w2e_t], writes=[p2_t])
        P.op("pe", lambda e: e.matmul(p2[:, 128:256], zat[:, :], a2e[:, d, :], start=True, stop=True), reads=[zat_t, a2e_t], writes=[p2_t])
        lw, lw_t = g(LW); eta, eta_t = g(ETA)
        P.op("act", lambda e: e.activation(lw[:], p2[:, 0:128], AF.Tanh, scale=0.5), reads=[p2_t], writes=[lw_t])
        P.op("act", lambda e: e.activation(eta[:], p2[:, 128:256], AF.Tanh, scale=0.5), reads=[p2_t], writes=[eta_t])
        P.op("pool", lambda e: e.tensor_scalar(lw[:], lw[:], 1.0, -0.5 * math.exp(-0.5), ALU.add, ALU.mult), reads=[lw_t], writes=[lw_t])
        yield
        P.op("pool", lambda e: e.tensor_scalar(eta[:], eta[:], 1.0, 0.5, ALU.add, ALU.mult), reads=[eta_t], writes=[eta_t])
        yield
        kk, kk_t = g(KK); kp, kp_t = g(KP); bb, bb_t = g(BB); tmp, tmp_t = g(TMP); tmp2, tmp2_t = g(TMP2); sm, sm_t = g(SM)
        P.op("dve", lambda e: e.tensor_tensor(kk[:], zk, vec(3 * d + 0), ALU.mult), reads=[z_t, vecs_t], writes=[kk_t])
        yield
        P.op("dve", lambda e: e.tensor_tensor(tmp[:], kk[:], kk[:], ALU.mult), reads=[kk_t], writes=[tmp_t])
        yield
        P.op("dve", lambda e: e.tensor_reduce(sm[:, 0:2], tmp[:].rearrange("p (h k) -> p h k", h=2), AX.X, ALU.add), reads=[tmp_t], writes=[sm_t])
        yield
        P.op("dve", lambda e: e.tensor_scalar(sm[:, 0:2], sm[:, 0:2], 1e-19, None, ALU.max), reads=[sm_t], writes=[sm_t])
        yield
        P.op("act", lambda e: e.activation(sm[:, 2:4], sm[:, 0:2], AF.Ln), reads=[sm_t], writes=[sm_t])
        yield
        P.op("act", lambda e: e.activation(sm[:, 4:6], sm[:, 2:4], AF.Exp, scale=-0.5), reads=[sm_t], writes=[sm_t])
        yield
        for h in range(2):
            P.op("dve", lambda e, h=h: e.tensor_scalar(kk[:, h * 64:(h + 1) * 64], kk[:, h * 64:(h + 1) * 64], sm[:, 4 + h:5 + h], None, ALU.mult),
                 reads=[kk_t, sm_t], writes=[kk_t])
        P.op("dve", lambda e: e.scalar_tensor_tensor(tmp[:], eta[:], -1.0, vec(3 * d + 1), ALU.add, ALU.mult), reads=[eta_t, vecs_t, tmp_t], writes=[tmp_t])
        yield
        P.op("dve", lambda e: e.scalar_tensor_tensor(kp[:], tmp[:], 1.0, zk, ALU.add, ALU.mult), reads=[tmp_t, z_t], writes=[kp_t])
        yield
        P.op("pool", lambda e: e.tensor_tensor(bb[:], kk[:], eta[:], ALU.mult), reads=[kk_t, eta_t], writes=[bb_t])
        yield
        P.op("pool", lambda e: e.tensor_tensor(tmp2[:], zr, kp[:], ALU.mult), reads=[z_t, kp_t], writes=[tmp2_t])
        P.op("pool", lambda e: e.tensor_tensor(tmp2[:], tmp2[:], vec(3 * d + 2), ALU.mult), reads=[tmp2_t, vecs_t], writes=[tmp2_t])
        P.op("dve", lambda e: e.tensor_reduce(sm[:, 6:8], tmp2[:].rearrange("p (h k) -> p h k", h=2), AX.X, ALU.add), reads=[tmp2_t, sm_t], writes=[sm_t])
        for h in range(2):
            P.op("dve", lambda e, h=h: e.scalar_tensor_tensor(BON[d][:, c, h * 64:(h + 1) * 64], z[:, 256 + h * 64:256 + (h + 1) * 64], sm[:, 6 + h:7 + h],
                                                             BON[d][:, c, h * 64:(h + 1) * 64], ALU.mult, ALU.add),
                 reads=[z_t, sm_t, BON_t[d][c]], writes=[BON_t[d][c]])
        yield
        p3, p3_t = pa_.next()
        P.op("pe", lambda e: e.matmul(p3[:, 0:128], TRI[d], lw[:], start=True, stop=True), reads=[cst_t, lw_t], writes=[p3_t])
        P.op("pe", lambda e: e.matmul(p3[:, 128:256], ones, lw[:], start=True, stop=True), reads=[cst_t, lw_t], writes=[p3_t])
        P.op("pe", lambda e: e.matmul(p3[:, 256:257], lw[:], ones[:, 0:1], start=True, stop=True), reads=[cst_t, lw_t], writes=[p3_t])
        css, css_t = g(CSs); d3, d3_t = g(D3); d4, d4_t = g(D4)
        e1, e1_t = g(E1); e2, e2_t = g(E2); e3, e3_t = g(E3); e4, e4_t = g(E4); gcol, gcol_t = g(GCOL)
        P.op("act", lambda e: e.copy(css[:], p3[:, 0:128]), reads=[p3_t], writes=[css_t])
        P.op("act", lambda e: e.activation(gcol[:], p3[:, 256:257], AF.Exp), reads=[p3_t], writes=[gcol_t])
        P.op("dve", lambda e: e.tensor_tensor(d4[:], p3[:, 128:256], css[:], ALU.subtract), reads=[p3_t, css_t], writes=[d4_t])
        P.op("pool", lambda e: e.tensor_tensor(d3[:], css[:], lw[:], ALU.subtract), reads=[css_t, lw_t], writes=[d3_t])
        yield
        P.op("act", lambda e: e.activation(e1[:], css[:], AF.Exp), reads=[css_t], writes=[e1_t])
        yield
        P.op("act", lambda e: e.activation(e2[:], css[:], AF.Exp, scale=-1.0), reads=[css_t], writes=[e2_t])
        yield
        P.op("act", lambda e: e.activation(e3[:], d3[:], AF.Exp), reads=[d3_t], writes=[e3_t])
        yield
        P.op("act", lambda e: e.activation(e4[:], d4[:], AF.Exp), reads=[d4_t], writes=[e4_t])
        yield
        ah, ah_t = g(AH); bh, bh_t = g(BH); kh, kh_t = g(KH); rh, rh_t = g(RH); vb, vb_t = g(VB)
        ah32, ah32_t = g(AH32)
        P.op("dve", lambda e: e.scalar_tensor_tensor(ah32[:], kk[:], -1.0, e3[:], ALU.mult, ALU.mult), reads=[kk_t, e3_t], writes=[ah32_t])
        yield
        P.op("pool", lambda e: e.tensor_copy(ah[:], ah32[:]), reads=[ah32_t], writes=[ah_t])
        yield
        P.op("pool", lambda e: e.tensor_tensor(bh[:], bb[:], e2[:], ALU.mult), reads=[bb_t, e2_t], writes=[bh_t])
        yield
        P.op("dve", lambda e: e.tensor_tensor(kh[:], kp[:], e2[:], ALU.mult), reads=[kp_t, e2_t], writes=[kh_t])
        yield
        P.op("pool", lambda e: e.tensor_tensor(rh[:], zr, e1[:], ALU.mult), reads=[z_t, e1_t], writes=[rh_t])
        yield
        P.op("act", lambda e: e.copy(vb[:], zv), reads=[z_t], writes=[vb_t])
        yield
        btz = [g(BTz[h]) for h in range(2)]; ktz = [g(KTz[h]) for h in range(2)]
        for h in range(2):
            hs = slice(h * 64, (h + 1) * 64)
            P.op("dve", lambda e, h=h, hs=hs: e.tensor_tensor(btz[h][0][:, hs], bb[:, hs], e4[:, hs], ALU.mult), reads=[bb_t, e4_t], writes=[btz[h][1]])
            P.op("pool", lambda e, h=h, hs=hs: e.tensor_tensor(ktz[h][0][:, hs], kp[:, hs], e4[:, hs], ALU.mult), reads=[kp_t, e4_t], writes=[ktz[h][1]])
        yield
        bht, bht_t = g(BHT); kht, kht_t = g(KHT)
        ahtz = [g(AHTz[h]) for h in range(2)]; rhtz = [g(RHTz[h]) for h in range(2)]
        for j, (src, src_t) in enumerate(((ah, ah_t), (bh, bh_t), (kh, kh_t), (rh, rh_t))):
            P.op("pe", lambda e, j=j, src=src: e.transpose(ptb[:, j, :], src[:], identb[:]), reads=[src_t, identb_t], writes=[ptb_t])
        P.op("act", lambda e: e.copy(bht[:], ptb[:, 1, :]), reads=[ptb_t], writes=[bht_t])
        P.op("act", lambda e: e.copy(kht[:], ptb[:, 2, :]), reads=[ptb_t], writes=[kht_t])
        for h in range(2):
            hs = slice(h * 64, (h + 1) * 64)
            P.op("dve", lambda e, h=h, hs=hs: e.tensor_copy(ahtz[h][0][hs, :], ptb[hs, 0, :]), reads=[ptb_t], writes=[ahtz[h][1]])
            P.op("act", lambda e, h=h, hs=hs: e.copy(rhtz[h][0][hs, :], ptb[hs, 3, :]), reads=[ptb_t], writes=[rhtz[h][1]])
        yield
        hd = []
        for h in range(2):
            n0, n0_t = g(Nm[h][0]); m0, m0_t = g(Mm[h][0]); pm0, pm0_t = g(Pm[h][0])
            aak, aak_t = g(AAK[h]); arb, arb_t = g(ARB[h]); ark, ark_t = g(ARK[h])
            specs = [
                (bht, bht_t, ahtz[h][0], ahtz[h][1], MS[d], n0, n0_t),
                (ahtz[h][0], ahtz[h][1], bht, bht_t, MST[d], m0, m0_t),
                (kht, kht_t, ahtz[h][0], ahtz[h][1], MS[d], aak, aak_t),
                (bht, bht_t, rhtz[h][0], rhtz[h][1], MI[d], arb, arb_t),
                (kht, kht_t, rhtz[h][0], rhtz[h][1], MI[d], ark, ark_t),
            ]
            for (lt, lt_t, rt, rt_t, msk, dst, dst_t) in specs:
                pp, pp_t = pa_.next()
                P.op("pe", lambda e, pp=pp, lt=lt, rt=rt: e.matmul(pp[:, 0:128], lt[:], rt[:], start=True, stop=True), reads=[lt_t, rt_t], writes=[pp_t])
                P.op("dve", lambda e, pp=pp, msk=msk, dst=dst: e.tensor_tensor(dst[:], pp[:, 0:128], msk, ALU.mult), reads=[pp_t, cst_t], writes=[dst_t])
            P.op("pool", lambda e, pm0=pm0, n0=n0: e.tensor_tensor(pm0[:], n0[:], ident, ALU.add), reads=[n0_t, cst_t], writes=[pm0_t])
            hd.append(dict(n=(n0, n0_t), m=(m0, m0_t), p=(pm0, pm0_t), aak=(aak, aak_t), arb=(arb, arb_t), ark=(ark, ark_t)))
        yield
        for k in range(7):
            do_sq = k <= 5
            need_n = k <= 4
            do_p = k >= 1
            while len(pdbl.free) < 2:
                yield
            cur = []
            for h in range(2):
                st = hd[h]
                (n, n_t), (m, m_t), (pm, pm_t) = st["n"], st["m"], st["p"]
                m2, m2_t = g(Mm[h][(k + 1) % 2]); n2, n2_t = g(Nm[h][(k + 1) % 2]); p2_, p2__t = g(Pm[h][k % 2])
                bi = pdbl.try_get()
                pp, pp_t = pdbl.items[bi]
                if do_sq:
                    P.op("pe", lambda e, pp=pp, n=n, m=m: e.matmul(pp[:, 0:128], n[:], m[:], start=True, stop=True), reads=[n_t, m_t], writes=[pp_t])
                if need_n:
                    P.op("pe", lambda e, pp=pp, n=n, m=m: e.matmul(pp[:, 128:256], m[:], n[:], start=True, stop=True), reads=[n_t, m_t], writes=[pp_t])
                if do_p:
                    P.op("pe", lambda e, pp=pp, m=m, pm=pm: e.matmul(pp[:, 256:384], m[:], pm[:], start=True, stop=True), reads=[m_t, pm_t], writes=[pp_t])
                cur.append((pp, pp_t, m2, m2_t, n2, n2_t, p2_, p2__t, pm, pm_t, bi))
            yield
            for h in range(2):
                pp, pp_t, m2, m2_t, n2, n2_t, p2_, p2__t, pm, pm_t, bi = cur[h]
                if do_sq:
                    P.op("act", lambda e, pp=pp, m2=m2: e.copy(m2[:], pp[:, 0:128]), reads=[pp_t], writes=[m2_t])
                    hd[h]["m"] = (m2, m2_t)
                if do_p:
                    P.op("dve", lambda e, pp=pp, pm=pm, p2_=p2_: e.tensor_tensor(p2_[:], pp[:, 256:384], pm[:], ALU.add), reads=[pp_t, pm_t], writes=[p2__t])
                    hd[h]["p"] = (p2_, p2__t)
                if need_n:
                    P.op("act", lambda e, pp=pp, n2=n2: e.copy(n2[:], pp[:, 128:256]), reads=[pp_t], writes=[n2_t])
                    hd[h]["n"] = (n2, n2_t)
                pdbl.put(bi)
            yield
        for h in range(2):
            st = hd[h]
            tt, tt_t = st["p"]
            aak, aak_t = st["aak"]
            x2, x2_t = g(X2[h]); wtz, wtz_t = g(WTz[h])
            hs = slice(h * 64, (h + 1) * 64)
            pp, pp_t = pa_.next()
            P.op("pe", lambda e, pp=pp, aak=aak, hs=hs: e.matmul(pp[:, 0:64], aak[:], vb[:, hs], start=True, stop=True), reads=[aak_t, vb_t], writes=[pp_t])
            P.op("pe", lambda e, pp=pp, tt=tt: e.matmul(pp[:, 128:256], ah32[:], tt[:], start=True, stop=True), reads=[ah32_t, tt_t], writes=[pp_t])
            P.op("act", lambda e, pp=pp, x2=x2: e.copy(x2[:], pp[:, 0:64]), reads=[pp_t], writes=[x2_t])
            P.op("dve", lambda e, pp=pp, wtz=wtz, hs=hs: e.tensor_copy(wtz[hs, :], pp[hs, 128:256]), reads=[pp_t], writes=[wtz_t])
            st["x2"] = (x2, x2_t); st["wtz"] = (wtz, wtz_t)
        yield
        us = []
        for h in range(2):
            st = hd[h]
            tt, tt_t = st["p"]; x2, x2_t = st["x2"]; wtz, wtz_t = st["wtz"]
            u, u_t = g(US[h])
            pu, pu_t = pb_.next()
            P.op("pe", lambda e, pu=pu, tt=tt, x2=x2: e.matmul(pu[:, 0:64], tt[:], x2[:], start=True, stop=False), reads=[tt_t, x2_t], writes=[pu_t])
            P.op("pe", lambda e, pu=pu, wtz=wtz: e.matmul(pu[:, 0:64], wtz[:], STb[d][:], start=False, stop=True), reads=[wtz_t, STb_t[d]], writes=[pu_t])
            if h == 0:
                P.op("act", lambda e, pu=pu, u=u: e.copy(u[:], pu[:, 0:64]), reads=[pu_t], writes=[u_t])
            else:
                P.op("dve", lambda e, pu=pu, u=u: e.tensor_copy(u[:], pu[:, 0:64]), reads=[pu_t], writes=[u_t])
            us.append((u, u_t))
        py, py_t = pb_.next()
        for h in range(2):
            st = hd[h]
            arb, arb_t = st["arb"]; ark, ark_t = st["ark"]
            u, u_t = us[h]
            hs = slice(h * 64, (h + 1) * 64)
            P.op("pe", lambda e, h=h, hs=hs: e.matmul(py[:, hs], rhtz[h][0][:], STb[d][:], start=True, stop=False), reads=[rhtz[h][1], STb_t[d]], writes=[py_t])
            P.op("pe", lambda e, hs=hs, arb=arb, u=u: e.matmul(py[:, hs], arb[:], u[:], start=False, stop=False), reads=[arb_t, u_t], writes=[py_t])
            P.op("pe", lambda e, hs=hs, ark=ark: e.matmul(py[:, hs], ark[:], vb[:, hs], start=False, stop=True), reads=[ark_t, vb_t], writes=[py_t])
        P.op("act", lambda e: e.copy(Y[d][:, c, :], py[:, 0:128]), reads=[py_t], writes=[Y_t[d][c]])
        pd, pd_t = pb_.next()
        for h in range(2):
            u, u_t = us[h]
            hs = slice(h * 64, (h + 1) * 64)
            P.op("pe", lambda e, h=h, u=u: e.matmul(pd[:, 0:64], btz[h][0][:], u[:], start=(h == 0), stop=False), reads=[btz[h][1], u_t], writes=[pd_t])
            P.op("pe", lambda e, h=h, hs=hs: e.matmul(pd[:, 0:64], ktz[h][0][:], vb[:, hs], start=False, stop=(h == 1)), reads=[ktz[h][1], vb_t], writes=[pd_t])
        P.op("dve", lambda e: e.scalar_tensor_tensor(ST32[d][:], ST32[d][:], gcol[:, 0:1], pd[:, 0:64], ALU.mult, ALU.add),
             reads=[ST_t[d], gcol_t, pd_t], writes=[ST_t[d]])
        P.op("act", lambda e: e.copy(STb[d][:], ST32[d][:]), reads=[ST_t[d]], writes=[STb_t[d]])
        yield

    nsteps = len(order_f)
    active = []
    nxt = [0]

    def admit():
        it = nxt[0]
        nxt[0] += 1
        active.append([it, unit(0, order_f[it], it), 0])
        active.append([it, unit(1, order_b[it], it), 0])

    admit()
    while active:
        for ent in list(active):
            try:
                next(ent[1])
                ent[2] += 1
            except StopIteration:
                active.remove(ent)
        if nxt[0] < nsteps:
            live_steps = {e_[0] for e_ in active}
            newest = [e_ for e_ in active if e_[0] == nxt[0] - 1]
            if len(live_steps) < 2 and all(e_[2] >= 6 for e_ in newest):
                admit()
    ysum = Rot([P.sbuf("ysum%d" % i, [128, 128], F32) for i in range(2)])
    ysq = Rot([P.sbuf("ysq%d" % i, [128, 128], F32) for i in range(2)])
    msm = Rot([P.sbuf("msm%d" % i, [128, 16], F32) for i in range(2)])
    ost = Rot([P.sbuf("ost%d" % i, [128, 128], F32) for i in range(3)])
    for c in range(nch):
        ys, ys_t = ysum.next(); yq, yq_t = ysq.next(); ms, ms_t = msm.next(); o, o_t = ost.next()
        P.op("dve", lambda e, ys=ys, c=c: e.tensor_tensor(ys[:], Y[0][:, c, :], Y[1][:, c, :], ALU.add), reads=[Y_t[0][c], Y_t[1][c]], writes=[ys_t])
        P.op("act", lambda e, ys=ys, yq=yq: e.activation(yq[:], ys[:], AF.Square), reads=[ys_t], writes=[yq_t])
        P.op("dve", lambda e, ys=ys, ms=ms: e.tensor_reduce(ms[:, 0:2], ys[:].rearrange("p (h k) -> p h k", h=2), AX.X, ALU.add), reads=[ys_t], writes=[ms_t])
        P.op("dve", lambda e, yq=yq, ms=ms: e.tensor_reduce(ms[:, 2:4], yq[:].rearrange("p (h k) -> p h k", h=2), AX.X, ALU.add), reads=[yq_t, ms_t], writes=[ms_t])
        P.op("dve", lambda e, ms=ms: e.tensor_scalar(ms[:, 0:2], ms[:, 0:2], 1.0 / 64, None, ALU.mult), reads=[ms_t], writes=[ms_t])
        P.op("dve", lambda e, ms=ms: e.tensor_tensor(ms[:, 4:6], ms[:, 0:2], ms[:, 0:2], ALU.mult), reads=[ms_t], writes=[ms_t])
        P.op("dve", lambda e, ms=ms: e.scalar_tensor_tensor(ms[:, 4:6], ms[:, 2:4], 1.0 / 64, ms[:, 4:6], ALU.mult, ALU.subtract), reads=[ms_t], writes=[ms_t])
        P.op("act", lambda e, ms=ms: e.activation(ms[:, 6:8], ms[:, 4:6], AF.Ln, bias=GN_EPS), reads=[ms_t], writes=[ms_t])
        P.op("act", lambda e, ms=ms: e.activation(ms[:, 8:10], ms[:, 6:8], AF.Exp, scale=-0.5), reads=[ms_t], writes=[ms_t])
        for h in range(2):
            hs = slice(h * 64, (h + 1) * 64)
            P.op("dve", lambda e, ys=ys, ms=ms, h=h, hs=hs: e.tensor_scalar(ys[:, hs], ys[:, hs], ms[:, h:h + 1], ms[:, 8 + h:9 + h], ALU.subtract, ALU.mult),
                 reads=[ys_t, ms_t], writes=[ys_t])
        P.op("pool", lambda e, ys=ys: e.tensor_tensor(ys[:], ys[:], vec(6), ALU.mult), reads=[ys_t, vecs_t], writes=[ys_t])
        P.op("pool", lambda e, ys=ys: e.tensor_tensor(ys[:], ys[:], vec(7), ALU.add), reads=[ys_t, vecs_t], writes=[ys_t])
        P.op("dve", lambda e, ys=ys, c=c: e.tensor_tensor(ys[:], ys[:], BON[0][:, c, :], ALU.add), reads=[ys_t, BON_t[0][c]], writes=[ys_t])
        P.op("dve", lambda e, ys=ys, o=o, c=c: e.tensor_tensor(o[:], ys[:], G[:, c, :], ALU.mult), reads=[ys_t, G_t], writes=[o_t])
        P.dma("sp", out_d[c, :, :], o[:], reads=[o_t], semtok=o_t)
    P.close_scope()


def rwkv_orders(nch_ctx, nch_lat):
    order_f = list(range(nch_ctx + nch_lat))
    order_b = list(range(nch_ctx - 1, -1, -1)) + list(range(nch_ctx + nch_lat - 1, nch_ctx - 1, -1))
    return order_f, order_b


def rwkv_consts():
    i = np.arange(128)
    s, t = i[:, None], i[None, :]
    c = np.stack([np.eye(128), np.ones((128, 128)), s <= t, s >= t, s < t, s <= t, s > t, s >= t]).astype(np.float32)
    return c


EXG = {"A": 2048, "R": 2048, "L": 192}


def ex_dst(exl, exl_t, oad_l):
    def fn(t, col, n):
        if col < OQ:
            return oad_l[t, :, col:col + n], None
        if col < ORKV:
            return exl[t]["A"][:, col - OQ:col - OQ + n], exl_t[t]["A"]
        if col < OLF:
            return exl[t]["R"][:, col - ORKV:col - ORKV + n], exl_t[t]["R"]
        if col < OGC:
            return exl[t]["L"][:, col - OLF:col - OLF + n], exl_t[t]["L"]
        return exl[t]["R"][:, 1536 + col - OGC:1536 + col - OGC + n], exl_t[t]["R"]
    return fn


def emit_compact(P, exg, exg_t, at_loc, eng):
    P.open_scope()
    hv = P.pid4(eng) * 128
    toks = Rot([None] * 16)
    for t in range(NT):
        for s_ in range(4 if t < 8 else 2):
            tok0 = CTX + s_ * 1024 + t * 128 if t < 8 else s_ * 128
            rows = slice(s_ * 128, (s_ + 1) * 128)
            _, tk = toks.next()
            P.dma(eng, at_loc[tok0:tok0 + 128, :].rearrange("n (f c) -> n f c", f=4),
                  exg[t]["A"][rows, :].rearrange("n (f w) -> n f w", f=4)[:, :, bass.ds(hv, 128)], reads=[exg_t[t]["A"]], writes=[tk], semtok=tk)
    P.close_scope()


def compact_rw(P, exg, exg_t, rw_loc, eng):
    hv = P.pid4(eng) * 128
    zt = P.sbuf("zt", [4, RW_COLS], F32); zt_t = Tok()
    P.op("pool", lambda e: e.memset(zt[:], 0.0), writes=[zt_t])
    for r in (0, 257, 258, RW_ROWS - 1):
        P.dma(eng, rw_loc[r:r + 1, :], zt[0:1, :], reads=[zt_t], semtok=zt_t)
    toks = Rot([None] * 12)
    for t in range(NT):
        for s_ in range(4 if t < 8 else 2):
            tok0 = CTX + s_ * 1024 + t * 128 if t < 8 else s_ * 128
            rrow = 259 + (tok0 - CTX) if t < 8 else 1 + tok0
            rows = slice(s_ * 128, (s_ + 1) * 128)
            _, tk = toks.next()
            P.dma(eng, rw_loc[rrow:rrow + 128, 0:512].rearrange("n (f c) -> n f c", f=4),
                  exg[t]["R"][rows, :].rearrange("n (f w) -> n f w", f=4)[:, :, bass.ds(hv, 128)], reads=[exg_t[t]["R"]], writes=[tk], semtok=tk)
            _, tk = toks.next()
            P.dma(eng, rw_loc[rrow:rrow + 128, 512:704], exg[t]["L"][rows, :], reads=[exg_t[t]["L"]], writes=[tk], semtok=tk)


def emit_obc_stage(P, gath, gath_t, obc_loc, eng):
    q = P.pid4(eng)
    toks = []
    for j in range(2):
        g, g_c = gath[j]
        tk = [Tok(), Tok()]
        P.dma(eng, obc_loc[:, j, 0:1024, :],
              g.rearrange("j h n c -> (j h n) c")[bass.ds(q * 4096, 4096), :].rearrange("(h n) c -> h n c", h=4), reads=[gath_t[j]], writes=[tk[0]], semtok=tk[0])
        P.dma(eng, obc_loc[:, j, 1024:1152, :], g_c[:, bass.ds((q % 2) * 128, 128), :], reads=[gath_t[j]], writes=[tk[1]], semtok=tk[1])
        toks.append(tk)
    return toks


def build_fused():
    P = Prog()
    di = P.dram_in
    x_d = di("x", [NT, 128, D], F32); xh_d = di("xh", [4, D], F32); hmask_d = di("hmask", [4, 1], F32)
    cT_d = di("cT", [D, 2], F32); wm_d = di("wm", [2, D, MC], F32); bm_d = di("bm", [2, MC], F32)
    gpre_d = di("gpreT", [2, 128, 16], F32); win_d = di("w_in", [2, D, N_IN], F32)
    sguw_d = di("sgu_wT", [2, 128, 4, 128], F32); sgub_d = di("sgu_bT", [2, 128, 4], F32); convw_d = di("conv_w", [2, 1, 3 * WG], F32)
    cos_d = di("rcos", [128, NT, 64], F32); sin_d = di("rsin", [128, NT, 64], F32); ident_d = di("ident", [128, 128], F32)
    wo_d = di("w_out", [2, D, D], F32); gpost_d = di("gpost", [2, 1, D], F32)
    lam_d = di("lamv", [2, 1, 256], F32); subg_d = di("subg", [2, 1, 128], F32)
    mu_d = di("rw_mu", [2, 2, RW_F], F32); w2e_d = di("rw_w2e", [2, 2, 65, 128], F32); a2e_d = di("rw_a2e", [2, 2, 33, 128], F32)
    vecs_d = di("rw_vecs", [2, 8, 128], F32); cst_d = di("rw_cst", [8, 128, 128], F32)
    xo_d = P.dram_out("xo", [8, 128, D], F32)
    tmp = P.dram_tmp
    stage = [0]
    STOP = DEBUG.get("stop", 999)

    def done():
        stage[0] += 1
        return stage[0] >= STOP

    mod_l = tmp("mod_l", [2, 2, MC], F32); mod_all = tmp("mod_all", [4, 2, 2, MC], F32)
    emit_M(P, cT_d, wm_d, bm_d, mod_l)
    P.allgather(mod_l.rearrange("l j n -> (l j) n"), mod_all.rearrange("r l j n -> (r l j) n"), GROUPS)
    if done():
        return P.finish()
    of, ob = rwkv_orders(2, 32)
    xcur, xhcur = x_d, xh_d
    for l in range(2):
        oad_l = tmp("oad_l%d" % l, [NT, 128, 1024], F32)
        exl = [{g: tmp("exl%d_%d%s" % (l, t, g), [128, n], F32) for g, n in EXG.items()} for t in range(NT)]
        exg = [{g: tmp("exg%d_%d%s" % (l, t, g), [4 * 128, n], F32) for g, n in EXG.items()} for t in range(NT)]
        exl_t = [{g: Tok(multi=True) for g in EXG} for t in range(NT)]
        exg_t = [{g: Tok() for g in EXG} for t in range(NT)]

        pend = []
        st_ = {"step": 0, "last": -99}

        def hook(pos, t, exl=exl, exg=exg, exl_t=exl_t, exg_t=exg_t, pend=pend, st_=st_):
            def issue():
                tt, g = pend.pop(0)
                P.allgather_async(exl[tt][g], exg[tt][g], GROUPS, reads=[exl_t[tt][g]], out_tok=exg_t[tt][g])
            if pos is None:
                while pend:
                    issue()
                return
            if t == NT - 1 and pos == 4:
                pend.extend((tt, "A") for tt in range(NT))
            if t == NT - 1 and pos == 9:
                pend.extend((tt, "R") for tt in range(NT))
                pend.extend((tt, "L") for tt in range(NT))
            st_["step"] += 1
            if pend and st_["step"] - st_["last"] >= 5:
                issue()
                st_["last"] = st_["step"]

        emit_A(P, l, xcur, xhcur, hmask_d, mod_all, gpre_d[l], win_d[l], sguw_d[l], sgub_d[l], convw_d[l], cos_d, sin_d, ident_d,
               ex_dst(exl, exl_t, oad_l), hook)
        if done():
            return P.finish()
        at_loc = tmp("at_loc%d" % l, [NTOK, 512], F32); rw_loc = tmp("rw_loc%d" % l, [RW_ROWS, RW_COLS], F32)
        emit_compact(P, exg, exg_t, at_loc, "sp")
        if done():
            return P.finish()
        ob_l = tmp("ob_l%d" % l, [NKB, 128, 128], F32); oc_l = tmp("oc_l%d" % l, [NKB, 128, 128], F32)
        gath = [(tmp("obg%d_%d" % (l, j), [4, 4, 1024, 128], F32), tmp("obgc%d_%d" % (l, j), [4, 256, 128], F32)) for j in range(2)]
        gath_t = [Tok(), Tok()]
        emit_attn(P, l, l == 0, at_loc, lam_d[l], subg_d[l], ident_d, ob_l,
                  pre=lambda exg=exg, exg_t=exg_t, rw_loc=rw_loc: compact_rw(P, exg, exg_t, rw_loc, "pool"))
        if done():
            return P.finish()

        def gather_o(j, src):
            g, g_c = gath[j]
            for c in range(4):
                P.allgather_async(src[2 + 8 * c:2 + 8 * (c + 1)].rearrange("t p c -> (t p) c"), g[c].rearrange("h n c -> (h n) c"), GROUPS)
            P.allgather_async(src[0:2].rearrange("t p c -> (t p) c"), g_c.rearrange("h n c -> (h n) c"), GROUPS, out_tok=gath_t[j])

        gather_o(0, ob_l)
        emit_rwkv(P, NKB, of, ob, rw_loc, mu_d[l], w2e_d[l], a2e_d[l], vecs_d[l], cst_d, oc_l)
        if done():
            return P.finish()
        gather_o(1, oc_l)
        obc_loc = tmp("obc_loc%d" % l, [4, 2, NT * 128, 128], F32)
        stage_fn = lambda gath=gath, gath_t=gath_t, obc_loc=obc_loc: emit_obc_stage(P, gath, gath_t, obc_loc, "pool")
        if l == 0:
            x1 = tmp("x1", [NT, 128, D], F32); edge_l = tmp("edge_l", [4, D], F32); edge_all = tmp("edge_all", [16, D], F32)
            xh1 = tmp("xh1", [4, D], F32)
            emit_C(P, 0, NT, oad_l, obc_loc, xcur, mod_all, gpost_d[0], wo_d[0], ident_d, x1, edge_l, stage=stage_fn)
            P.allgather(edge_l, edge_all, GROUPS)
            q = P.pid4("pool")
            etk = [Tok() for _ in range(4)]
            for i, (dq, er) in enumerate(((3, 1), (1, 0), (3, 3), (1, 2))):
                P.dma("pool", xh1[i:i + 1, :], edge_all[bass.ds(((q + dq) % 4) * 4 + er, 1), :], writes=[etk[i]])
            P.barrier()
            xcur, xhcur = x1, xh1
            if done():
                return P.finish()
        else:
            emit_C(P, 1, 8, oad_l, obc_loc, xcur, mod_all, gpost_d[1], wo_d[1], ident_d, xo_d, stage=stage_fn)
    return P.finish()


_NC_CACHE = {}


def _f32(a):
    return np.ascontiguousarray(a, dtype=np.float32)


def _rope_tables():
    inv = (10000.0 ** (-np.arange(0, 32, 2, dtype=np.float32) / 32)).astype(np.float32)
    tabs = []
    for q in range(4):
        cos = np.ones((128, NT, 2, 2, 16), np.float32)
        sin = np.zeros((128, NT, 2, 2, 16), np.float32)
        for t in range(NT - 1):
            tok = q * 1024 + t * 128 + np.arange(128)
            for a, pos in enumerate((tok // 64, tok % 64)):
                ang = pos.astype(np.float32)[:, None] * inv[None, :]
                cos[:, t, a, 0, :] = np.cos(ang); cos[:, t, a, 1, :] = np.cos(ang)
                sin[:, t, a, 0, :] = -np.sin(ang); sin[:, t, a, 1, :] = np.sin(ang)
        tabs.append((cos.reshape(128, NT, 64), sin.reshape(128, NT, 64)))
    return tabs


def _to_pk(v):
    return np.ascontiguousarray(v.reshape(16, 128).T)


def rwkv_consts():
    i = np.arange(128)
    s, t = i[:, None], i[None, :]
    c = np.stack([np.eye(128), np.ones((128, 128)), s <= t, s >= t, s < t, s <= t, s > t, s >= t]).astype(np.float32)
    return c


def _core_inputs(p):
    tabs = _rope_tables()
    ident = np.eye(128, dtype=np.float32)
    cst = rwkv_consts()
    x, xc = p['x'], p['ctx']
    shared = {
        "gpreT": _f32(np.stack([_to_pk(p['g_pre'][l]) for l in range(2)])),
        "w_in": _f32(p['w_in']), "sgu_wT": _f32(p['sgu_w'].transpose(0, 3, 1, 2)), "sgu_bT": _f32(p['sgu_b'].transpose(0, 2, 1)),
        "conv_w": _f32(p['conv_w'].reshape(2, 1, 3 * WG)), "ident": ident, "w_out": _f32(p['w_out']),
        "gpost": _f32(p['g_post'][:, None, :]),
        "lamv": _f32(np.concatenate([p['lam_q1'], p['lam_k1'], p['lam_q2'], p['lam_k2']], 1)[:, None, :]),
        "subg": _f32(p['subln_g'][:, None, :]), "rw_cst": cst,
    }
    ins = []
    for k in range(NCORES):
        b, q = k // 4, k % 4
        ct = q % 2
        xt = np.concatenate([x[b, q * 1024:(q + 1) * 1024].reshape(8, 128, D), xc[b, ct * 128:(ct + 1) * 128][None]], 0)
        xh = np.zeros((4, D), np.float32); hm = np.zeros((4, 1), np.float32)
        if q > 0:
            xh[0] = x[b, q * 1024 - 1]; hm[0] = 1
        if q < 3:
            xh[1] = x[b, (q + 1) * 1024]; hm[1] = 1
        if ct > 0:
            xh[2] = xc[b, ct * 128 - 1]; hm[2] = 1
        if ct < 1:
            xh[3] = xc[b, (ct + 1) * 128]; hm[3] = 1
        cols = slice(q * 128, (q + 1) * 128)
        c0 = q * 128
        mus, w2e, a2e, vecs = [], [], [], []
        for l in range(2):
            mus.append([]); w2e.append([]); a2e.append([])
            for d in range(2):
                m = p['rwkv_mu'][l, d]
                mus[l].append(np.concatenate([m[c0:c0 + 128], m[512 + c0:512 + c0 + 128], m[1024 + c0:1024 + c0 + 128], m[1536 + d * 0:1632]]))
                w2e[l].append(np.concatenate([p['rwkv_w2'][l, d][:, cols], p['rwkv_w0'][l, d][None, cols]], 0))
                a2e[l].append(np.concatenate([p['rwkv_a2'][l, d][:, cols], p['rwkv_a0'][l, d][None, cols]], 0))
            vecs.append(np.stack([p['rwkv_kk'][l, 0][cols], p['rwkv_ka'][l, 0][cols], p['rwkv_rk'][l, 0].reshape(-1)[cols],
                                  p['rwkv_kk'][l, 1][cols], p['rwkv_ka'][l, 1][cols], p['rwkv_rk'][l, 1].reshape(-1)[cols],
                                  p['rwkv_ln_w'][l][cols], p['rwkv_ln_b'][l][cols]]))
        dct = dict(shared)
        dct.update({
            "x": _f32(xt), "xh": xh, "hmask": hm,
            "cT": _f32(np.stack([p['c'][b], p['c_ctx']], 1)),
            "wm": _f32(p['w_mod'][:, :, q * MC:(q + 1) * MC]), "bm": _f32(p['b_mod'][:, q * MC:(q + 1) * MC]),
            "rcos": tabs[q][0], "rsin": tabs[q][1],
            "rw_mu": _f32(np.array(mus)), "rw_w2e": _f32(np.array(w2e)), "rw_a2e": _f32(np.array(a2e)), "rw_vecs": _f32(np.array(vecs)),
        })
        ins.append(dct)
    return ins


def kernel(**inputs):
    p = {k: np.asarray(v) for k, v in inputs.items()}
    if "fused" not in _NC_CACHE:
        _NC_CACHE["fused"] = build_fused()
    ins = _core_inputs(p)
    res = run_bass_kernel_spmd(_NC_CACHE["fused"], ins, core_ids=list(range(NCORES))).results
    out = np.zeros((NB, SEQ, D), np.float32)
    for k in range(NCORES):
        b, q = k // 4, k % 4
        out[b, q * 1024:(q + 1) * 1024] = res[k]["xo"].reshape(1024, D)
    return out
```

```python
import math
import numpy as np
import concourse.bass as bass
import concourse.mybir as mybir
from concourse.bass_utils import run_bass_kernel_spmd

F32 = mybir.dt.float32
BF16 = mybir.dt.bfloat16
AF = mybir.ActivationFunctionType
ALU = mybir.AluOpType
AX = mybir.AxisListType

D = 2048
SEQ = 4096
CTX = 256
NB = 2
WG = 512
N_IN = 7872
EPS = 1e-6
GN_EPS = 64e-5
NCORES = 8

ENGS = ("pe", "act", "dve", "pool", "sp")
SAME_ENGINE_SYNC = {"pe": False, "act": True, "dve": True, "pool": True, "sp": False}


class Tok:
    __slots__ = ("name", "lastw", "readers", "sem", "semcnt", "psum", "multi", "wlist")

    def __init__(self, name="", psum=False, multi=False):
        self.name = name
        self.psum = psum
        self.multi = multi
        self.wlist = []
        self.lastw = None
        self.readers = []
        self.sem = None
        self.semcnt = 0


NSEM_POOL = 92


class Prog:
    def __init__(self):
        self.nc = bass.Bass("TRN2", target_bir_lowering=False)
        nc = self.nc
        self.eng = {"pe": nc.tensor, "act": nc.scalar, "dve": nc.vector, "pool": nc.gpsimd, "sp": nc.sync}
        self.cnt = {e: 0 for e in ENGS}
        self.seen = {e: {} for e in ENGS}
        self.esem = {}
        self._ctx = []
        for e in ENGS:
            cm = nc.semaphore("s_" + e)
            self.esem[e] = cm.__enter__()
            self._ctx.append(cm)
        cm = nc.semaphore("s_cc")
        self.ccsem = cm.__enter__(); self._ctx.append(cm)
        self.cccnt = 0
        self.sem_free = []
        self.sem_base = {}
        for i in range(NSEM_POOL):
            cm = nc.semaphore("d_%d" % i)
            sm = cm.__enter__(); self._ctx.append(cm)
            self.sem_free.append(sm)
            self.sem_base[id(sm)] = 0
        self.live = []
        self.scopes = []
        self.out_events = []
        self.nalloc = 0

    def _push(self, cm):
        t = cm.__enter__()
        (self.scopes[-1]["ctx"] if self.scopes else self._ctx).append(cm)
        return t

    def sbuf(self, name, shape, dt):
        self.nalloc += 1
        return self._push(self.nc.sbuf_tensor("sb%d_%s" % (self.nalloc, name), list(shape), dt))

    def psum(self, name, shape, dt=F32):
        self.nalloc += 1
        return self._push(self.nc.psum_tensor("pp%d_%s" % (self.nalloc, name), list(shape), dt))

    def dram_in(self, name, shape, dt):
        return self.nc.dram_tensor(name, list(shape), dt, kind="ExternalInput").ap()

    def dram_out(self, name, shape, dt):
        return self.nc.dram_tensor(name, list(shape), dt, kind="ExternalOutput").ap()

    def dram_tmp(self, name, shape, dt):
        return self.nc.dram_tensor(name, list(shape), dt, kind="Internal").ap()

    def pid4(self, eng):
        if not hasattr(self, "_pid4"):
            self._pid4 = {}
        if eng not in self._pid4:
            self._pid4[eng] = self.eng[eng].partition_id() % 4
        return self._pid4[eng]

    def open_scope(self):
        self.scopes.append({"ctx": [], "toks": []})

    def close_scope(self):
        self.barrier()
        sc = self.scopes.pop()
        for cm in reversed(sc["ctx"]):
            cm.__exit__(None, None, None)
        for t in sc["toks"]:
            self.sem_base[id(t.sem)] = t.semcnt
            self.sem_free.append(t.sem)
            self.live.remove(t)
            t.sem = None

    def _tsem(self, tok):
        if tok.sem is None:
            tok.sem = self.sem_free.pop(0)
            tok.semcnt = self.sem_base[id(tok.sem)]
            self.live.append(tok)
            if self.scopes:
                self.scopes[-1]["toks"].append(tok)
        return tok.sem

    def _need(self, eng, deps):
        waits = []
        best = {}
        for d in deps:
            if d is None:
                continue
            key, val = d
            if key == eng and not SAME_ENGINE_SYNC[eng]:
                continue
            kk = id(key) if not isinstance(key, str) else key
            if best.get(kk, (None, 0))[1] < val:
                best[kk] = (key, val)
        for kk, (key, val) in best.items():
            if self.seen[eng].get(kk, 0) >= val:
                continue
            self.seen[eng][kk] = val
            sem = key if not isinstance(key, str) else self.esem[key]
            waits.append((sem, val))
        return waits

    @staticmethod
    def _deps(eng, reads, writes):
        deps = []
        for r in reads:
            deps.append(r.lastw)
            if r.multi:
                deps.extend(r.wlist)
            if r.psum:
                deps.extend(x for x in r.readers if x[0] != eng)
        for w in writes:
            if w.multi:
                continue
            deps.append(w.lastw)
            deps.extend(w.readers)
        return deps

    @staticmethod
    def _commit(ev, reads, writes):
        for r in reads:
            r.readers.append(ev)
        for w in writes:
            if w.multi:
                w.wlist.append(ev)
            else:
                w.lastw = ev
                w.readers = []

    def op(self, eng, fn, reads=(), writes=()):
        waits = self._need(eng, self._deps(eng, reads, writes))
        e = self.eng[eng]
        for (s, v) in waits:
            e.wait_ge(s, v)
        self.cnt[eng] += 1
        ev = (eng, self.cnt[eng])
        fn(e).then_inc(self.esem[eng], 1)
        self._commit(ev, reads, writes)
        return ev

    def dma(self, eng, out, in_, reads=(), writes=(), semtok=None, is_output=False, **kw):
        if semtok is None:
            semtok = writes[0] if writes else reads[0]
        sem = self._tsem(semtok)
        waits = self._need(eng, self._deps(eng, reads, writes))
        e = self.eng[eng]
        for (s, v) in waits:
            e.wait_ge(s, v)
        semtok.semcnt += 16
        ev = (sem, semtok.semcnt)
        e.dma_start(out=out, in_=in_, **kw).then_inc(sem, 16)
        self._commit(ev, reads, writes)
        return ev

    def barrier(self):
        for e in ENGS:
            eo = self.eng[e]
            for e2 in ENGS:
                if e2 != e and self.seen[e].get(e2, 0) < self.cnt[e2]:
                    eo.wait_ge(self.esem[e2], self.cnt[e2])
                    self.seen[e][e2] = self.cnt[e2]
            for t in self.live:
                if self.seen[e].get(id(t.sem), 0) < t.semcnt:
                    eo.wait_ge(t.sem, t.semcnt)
                    self.seen[e][id(t.sem)] = t.semcnt
            if self.cccnt and self.seen[e].get(id(self.ccsem), 0) < self.cccnt:
                eo.wait_ge(self.ccsem, self.cccnt)
                self.seen[e][id(self.ccsem)] = self.cccnt

    def allgather(self, in_ap, out_ap, groups):
        self.barrier()
        self.cccnt += 1
        self.eng["pool"].collective_compute("AllGather", ALU.bypass, replica_groups=groups, ins=[in_ap], outs=[out_ap]).then_inc(self.ccsem, 1)
        self.barrier()

    def allgather_async(self, in_ap, out_ap, groups, reads=(), out_tok=None):
        waits = self._need("pool", self._deps("pool", reads, []))
        e = self.eng["pool"]
        for (s_, v) in waits:
            e.wait_ge(s_, v)
        if self.cccnt and self.seen["pool"].get(id(self.ccsem), 0) < self.cccnt:
            e.wait_ge(self.ccsem, self.cccnt)
            self.seen["pool"][id(self.ccsem)] = self.cccnt
        self.cccnt += 1
        e.collective_compute("AllGather", ALU.bypass, replica_groups=groups, ins=[in_ap], outs=[out_ap]).then_inc(self.ccsem, 1)
        if out_tok is not None:
            out_tok.lastw = (self.ccsem, self.cccnt)
            out_tok.readers = []

    def finish(self):
        self.barrier()
        for cm in reversed(self._ctx):
            cm.__exit__(None, None, None)
        return self.nc


class BankPool:
    def __init__(self, bufs):
        self.items = [(b, Tok(psum=True)) for b in bufs]
        self.free = list(range(len(bufs)))

    def try_get(self):
        if not self.free:
            return None
        i = self.free.pop(0)
        return i

    def put(self, i):
        self.free.append(i)


class Rot:
    def __init__(self, bufs, psum=False):
        self.bufs = bufs
        self.toks = [Tok(psum=psum) for _ in bufs]
        self.i = 0

    def next(self):
        j = self.i % len(self.bufs)
        self.i += 1
        return self.bufs[j], self.toks[j]


MC = 1536
GROUPS = [[0, 1, 2, 3], [4, 5, 6, 7]]


def emit_M(P, cT_d, wm_d, bm_d, out_d):
    P.open_scope()
    cs = P.sbuf("cs", [128, 16, 2], F32); cs_t = Tok()
    sb = P.sbuf("sb", [128, 16, 2], BF16); sb_t = Tok()
    bs = P.sbuf("bs", [2, 2, MC], F32); bs_t = Tok()
    os_ = P.sbuf("os", [2, 2, MC], F32); os_t = Tok()
    ps = [P.psum("ps%d" % i, [2, 512], F32) for i in range(6)]
    ps_t = [Tok(psum=True) for _ in range(6)]
    wb = [P.sbuf("wb%d" % i, [128, 16, 512], BF16) for i in range(3)]
    wb_t = [Tok() for _ in range(3)]
    P.dma("sp", cs[:], cT_d.rearrange("(kc p) j -> p kc j", p=128), writes=[cs_t])
    for l in range(2):
        P.dma("sp", bs[:, l, :], bm_d[l:l + 1, :].broadcast_to([2, MC]), writes=[bs_t])
    P.op("act", lambda e: e.activation(sb[:], cs[:], AF.Silu), reads=[cs_t], writes=[sb_t])
    for l in range(2):
        for j in range(3):
            pi = l * 3 + j
            w, w_t = wb[pi % 3], wb_t[pi % 3]
            P.dma("pool", w[:], wm_d[l][:, j * 512:(j + 1) * 512].rearrange("(kc p) n -> p kc n", p=128), writes=[w_t])
            for kc in range(16):
                P.op("pe", lambda e, pi=pi, w=w, kc=kc: e.matmul(ps[pi][:, :], sb[:, kc, :], w[:, kc, :], start=(kc == 0), stop=(kc == 15)),
                     reads=[sb_t, w_t], writes=[ps_t[pi]])
            P.op("dve", lambda e, pi=pi, l=l, j=j: e.tensor_tensor(os_[:, l, j * 512:(j + 1) * 512], ps[pi][:, :], bs[:, l, j * 512:(j + 1) * 512], ALU.add),
                 reads=[ps_t[pi], bs_t], writes=[os_t])
    P.dma("sp", out_d.rearrange("l j n -> j l n"), os_[:], reads=[os_t])
    P.close_scope()


def mod_pieces(c0, n=D):
    out = []
    c = c0
    while c < c0 + n:
        r, off = c // MC, c % MC
        ln = min(MC - off, c0 + n - c)
        out.append((r, off, ln, c - c0))
        c += ln
    return out


NT = 9
NTT = 10
OA, OD, OQ, OK_, OV, OGB, ORKV, OLF, OLB, OGC = 0, 512, 1024, 1536, 2048, 2560, 3072, 4608, 4704, 4800
NOUT_A = 5312
CBS = [
    (0, 512, "A_u", None), (512, 512, "A_v", None), (1024, 512, "A_g", OA),
    (1536, 512, "rope", OQ), (2048, 512, "rope", OK_), (2560, 512, "copy", OV), (3072, 512, "silu", OGB),
    (3584, 512, "copy", ORKV), (4096, 512, "copy", ORKV + 512), (4608, 512, "copy", ORKV + 1024),
    (5120, 192, "copy", OLF), (5312, 512, "silu", OGC),
    (5824, 512, "D_b", None), (6336, 512, "D_c", None), (6848, 512, "D_x", None), (7360, 512, "D_g", None),
]


DEBUG = {"ncb": 16, "conv": True, "halo": True}


def emit_A(P, l, x_d, xh_d, hmask_d, mod_all, gpre_d, win_d, sguw_d, sgub_d, convw_d, cos_d, sin_d, ident_d, dst_fn, hook=None):
    P.open_scope()
    ident = P.sbuf("ident", [128, 128], F32); ident_t = Tok()
    P.dma("sp", ident[:], ident_d[:, :], writes=[ident_t])
    gpre = P.sbuf("gpre", [128, 16], F32); gpre_t = Tok()
    P.dma("sp", gpre[:], gpre_d, writes=[gpre_t])
    MV = P.sbuf("MV", [64, 128], F32); MV_t = Tok()
    for v, (row, c0) in enumerate(((0, 2048), (0, 0), (1, 2048), (1, 0))):
        for (r, off, ln, dst) in mod_pieces(c0):
            P.dma("sp", MV[v * 16 + dst // 128:v * 16 + (dst + ln) // 128, :],
                  mod_all[r, l, row, off:off + ln].rearrange("(k c) -> k c", c=128), writes=[MV_t], semtok=MV_t)
    pmv = P.psum("pmv", [128, 512], F32); pmv_t = Tok(psum=True)
    P.op("pe", lambda e: e.transpose(pmv[:, 0:64], MV[:, :], ident[0:64, 0:64]), reads=[MV_t, ident_t], writes=[pmv_t])
    modT = P.sbuf("modT", [128, 4, 16], F32); modT_t = Tok()
    P.op("dve", lambda e: e.tensor_copy(modT[:].rearrange("p a b -> p (a b)"), pmv[:, 0:64]), reads=[pmv_t], writes=[modT_t])
    sc = P.sbuf("sc", [128, 2, 16], F32); sc_t = Tok()
    sh = P.sbuf("sh", [128, 2, 16], F32); sh_t = Tok()
    for j in range(2):
        P.op("dve", lambda e, j=j: e.scalar_tensor_tensor(sc[:, j, :], modT[:, 2 * j, :], 1.0, gpre[:], ALU.add, ALU.mult),
             reads=[modT_t, gpre_t], writes=[sc_t])
        P.op("dve", lambda e, j=j: e.tensor_copy(sh[:, j, :], modT[:, 2 * j + 1, :]), reads=[modT_t], writes=[sh_t])
    sguw32 = P.sbuf("sguw32", [128, 4, 128], F32); sguw32_t = Tok()
    P.dma("sp", sguw32[:], sguw_d, writes=[sguw32_t])
    sguw = P.sbuf("sguw", [128, 4, 128], BF16); sguw_t = Tok()
    P.op("dve", lambda e: e.tensor_copy(sguw[:], sguw32[:]), reads=[sguw32_t], writes=[sguw_t])
    sgub = P.sbuf("sgub", [128, 4], F32); sgub_t = Tok()
    P.dma("sp", sgub[:], sgub_d, writes=[sgub_t])
    convw = P.sbuf("convw", [128, 3, WG], F32); convw_t = Tok()
    P.dma("sp", convw[:].rearrange("p a b -> p (a b)"), convw_d.broadcast_to([128, 3 * WG]), writes=[convw_t])
    rcos = P.sbuf("rcos", [128, NT, 64], F32); rcos_t = Tok()
    rsin = P.sbuf("rsin", [128, NT, 64], F32); rsin_t = Tok()
    P.dma("sp", rcos[:], cos_d[:, :, :], writes=[rcos_t])
    P.dma("sp", rsin[:], sin_d[:, :, :], writes=[rsin_t])
    hmask = P.sbuf("hmask", [4, 1], F32); hmask_t = Tok()
    P.dma("sp", hmask[:], hmask_d[:, :], writes=[hmask_t])

    hT = P.sbuf("hT", [128, 16, NTT * 128], BF16)
    hT_t = [Tok() for _ in range(NTT)]
    xs = Rot([P.sbuf("xs%d" % i, [128, D], F32) for i in range(2)])
    junk = P.sbuf("junk", [128, D], BF16); junk_t = Tok()
    small = P.sbuf("small", [128, NTT, 4], F32)
    small_t = [Tok() for _ in range(NTT)]
    wbuf = [P.sbuf("wbuf%d" % i, [128, 16, 512], BF16) for i in range(2)]
    wbuf_t = [[Tok(), Tok()] for _ in range(2)]
    UB = P.sbuf("UB", [128, NT, WG], F32); UB_t = [Tok() for _ in range(NT)]
    VN = P.sbuf("VN", [128, NT, WG], BF16); VN_t = [Tok() for _ in range(NT)]
    CG = P.sbuf("CG", [128, NT, WG], F32); CG_t = [Tok() for _ in range(NT)]
    ZH = P.sbuf("ZH", [4, WG], F32); ZH_t = Tok()
    st = Rot([P.sbuf("st%d" % i, [128, 512], F32) for i in range(4)])
    tA = Rot([P.sbuf("tA%d" % i, [128, 512], F32) for i in range(2)])
    tB = Rot([P.sbuf("tB%d" % i, [128, 512], F32) for i in range(2)])
    lnst = Rot([P.sbuf("lnst%d" % i, [128, 16], F32) for i in range(2)])
    pt = Rot([P.psum("pt%d" % i, [128, 4, 128], F32) for i in range(2)], psum=True)
    pm = Rot([P.psum("pm%d" % i, [128, 512], F32) for i in range(3)], psum=True)
    pmix = Rot([P.psum("pmix%d" % i, [128, 512], F32) for i in range(2)], psum=True)

    P.op("pool", lambda e: e.memset(hT[:, :, NT * 128:NTT * 128], 0.0), writes=[hT_t[NT]])

    for t in range(NTT):
        rows = 128 if t < NT else 4
        xb, xb_t = xs.next()
        src = x_d[t] if t < NT else xh_d[:, :]
        P.dma("sp", xb[:rows, :], src, writes=[xb_t])
        sm = small[:rows, t, :]
        P.op("act", lambda e, xb=xb, rows=rows, sm=sm: e.activation(junk[:rows, :], xb[:rows, :], AF.Square, accum_out=sm[:, 0:1]),
             reads=[xb_t], writes=[junk_t, small_t[t]])
        P.op("act", lambda e, sm=sm: e.activation(sm[:, 1:2], sm[:, 0:1], AF.Ln, bias=EPS, scale=1.0 / D),
             reads=[small_t[t]], writes=[small_t[t]])
        P.op("act", lambda e, sm=sm: e.activation(sm[:, 2:3], sm[:, 1:2], AF.Exp, scale=-0.5),
             reads=[small_t[t]], writes=[small_t[t]])
        P.op("dve", lambda e, xb=xb, rows=rows, sm=sm: e.tensor_scalar(xb[:rows, :], xb[:rows, :], sm[:, 2:3], None, ALU.mult),
             reads=[xb_t, small_t[t]], writes=[xb_t])
        for g in range(4):
            pb, pb_t = pt.next()
            for j in range(4):
                kc = g * 4 + j
                P.op("pe", lambda e, pb=pb, j=j, kc=kc, xb=xb, rows=rows: e.transpose(
                    pb[:, j, :rows], xb[:rows, kc * 128:(kc + 1) * 128], ident[:rows, :rows]),
                    reads=[xb_t, ident_t], writes=[pb_t])
            for j in range(4):
                kc = g * 4 + j
                if t < NT:
                    m = 0 if t < NT - 1 else 1
                    parts = [(0, 128, m)]
                else:
                    parts = [(0, 2, 0), (2, 4, 1)]
                for (a, b, m) in parts:
                    P.op("act", lambda e, pb=pb, j=j, kc=kc, t=t, a=a, b=b, m=m: e.activation(
                        hT[:, kc, t * 128 + a:t * 128 + b], pb[:, j, a:b], AF.Identity,
                        scale=sc[:, m, kc:kc + 1], bias=sh[:, m, kc:kc + 1]),
                        reads=[pb_t, sc_t, sh_t], writes=[hT_t[t]])

    oq = ["sp"]

    def store(t, col, n, buf, buf_t):
        dap, dtok = dst_fn(t, col, n)
        P.dma("sp", dap, buf[:, :n], reads=[buf_t], writes=[dtok] if dtok is not None else [], semtok=buf_t)

    order = [3, 4, 5, 6, 7, 8, 9, 10, 11, 0, 1, 2, 12, 13, 14, 15]
    for ci, (c0, ncol, kind, ocol) in enumerate([CBS[i] for i in order]):
        wb = wbuf[ci % 2]; wb_t = wbuf_t[ci % 2]
        for hf in range(2):
            P.dma("pool", wb[:, hf * 8:(hf + 1) * 8, :ncol],
                  win_d[hf * 1024:(hf + 1) * 1024, c0:c0 + ncol].rearrange("(kc p) n -> p kc n", p=128),
                  writes=[wb_t[hf]])
        tiles = list(range(NT)) + ([NT] if kind in ("D_c", "D_x") else [])
        for t in tiles:
            rows = 128 if t < NT else 4
            ps, ps_t = pm.next()
            for kc in range(16):
                P.op("pe", lambda e, ps=ps, rows=rows, ncol=ncol, kc=kc, t=t, wb=wb: e.matmul(
                    ps[:rows, :ncol], hT[:, kc, t * 128:t * 128 + rows], wb[:, kc, :ncol], start=(kc == 0), stop=(kc == 15)),
                    reads=[hT_t[t], wb_t[kc // 8]], writes=[ps_t])
            if kind == "A_u":
                P.op("act", lambda e, ps=ps, t=t: e.copy(UB[:, t, :], ps[:, :]), reads=[ps_t], writes=[UB_t[t]])
            elif kind == "A_v":
                ls, ls_t = lnst.next()
                sq, sq_t = tA.next()
                P.op("act", lambda e, ps=ps, sq=sq: e.activation(sq[:], ps[:, :], AF.Square), reads=[ps_t], writes=[sq_t])
                P.op("dve", lambda e, ps=ps, ls=ls: e.tensor_reduce(ls[:, 0:4], ps[:, :].rearrange("p (h d) -> p h d", h=4), AX.X, ALU.add),
                     reads=[ps_t], writes=[ls_t])
                P.op("dve", lambda e, sq=sq, ls=ls: e.tensor_reduce(ls[:, 4:8], sq[:].rearrange("p (h d) -> p h d", h=4), AX.X, ALU.add),
                     reads=[sq_t, ls_t], writes=[ls_t])
                P.op("dve", lambda e, ls=ls: e.tensor_scalar(ls[:, 0:4], ls[:, 0:4], 1.0 / 128, None, ALU.mult), reads=[ls_t], writes=[ls_t])
                P.op("dve", lambda e, ls=ls: e.tensor_tensor(ls[:, 8:12], ls[:, 0:4], ls[:, 0:4], ALU.mult), reads=[ls_t], writes=[ls_t])
                P.op("dve", lambda e, ls=ls: e.scalar_tensor_tensor(ls[:, 8:12], ls[:, 4:8], 1.0 / 128, ls[:, 8:12], ALU.mult, ALU.subtract),
                     reads=[ls_t], writes=[ls_t])
                P.op("act", lambda e, ls=ls: e.activation(ls[:, 8:12], ls[:, 8:12], AF.Ln, bias=EPS), reads=[ls_t], writes=[ls_t])
                P.op("act", lambda e, ls=ls: e.activation(ls[:, 12:16], ls[:, 8:12], AF.Exp, scale=-0.5), reads=[ls_t], writes=[ls_t])
                for h in range(4):
                    P.op("dve", lambda e, ps=ps, ls=ls, h=h, t=t: e.tensor_scalar(
                        VN[:, t, h * 128:(h + 1) * 128], ps[:, h * 128:(h + 1) * 128], ls[:, h:h + 1], ls[:, 12 + h:13 + h],
                        ALU.subtract, ALU.mult), reads=[ps_t, ls_t], writes=[VN_t[t]])
            elif kind == "A_g":
                sg, sg_t = tA.next()
                P.op("act", lambda e, ps=ps, sg=sg: e.activation(sg[:], ps[:, :], AF.Silu), reads=[ps_t], writes=[sg_t])
                px, px_t = pmix.next()
                for h in range(4):
                    P.op("pe", lambda e, px=px, h=h, t=t: e.matmul(
                        px[:, h * 128:(h + 1) * 128], sguw[:, h, :], VN[:, t, h * 128:(h + 1) * 128], start=True, stop=True),
                        reads=[sguw_t, VN_t[t]], writes=[px_t])
                tb, tb_t = tB.next()
                for h in range(4):
                    P.op("dve", lambda e, px=px, h=h, t=t, tb=tb: e.scalar_tensor_tensor(
                        tb[:, h * 128:(h + 1) * 128], px[:, h * 128:(h + 1) * 128], sgub[:, h:h + 1], UB[:, t, h * 128:(h + 1) * 128],
                        ALU.add, ALU.mult), reads=[px_t, sgub_t, UB_t[t]], writes=[tb_t])
                sb_, sb_t = st.next()
                P.op("pool", lambda e, tb=tb, sg=sg, sb_=sb_: e.tensor_tensor(sb_[:], tb[:], sg[:], ALU.mult),
                     reads=[tb_t, sg_t], writes=[sb_t])
                store(t, ocol, 512, sb_, sb_t)
            elif kind == "copy":
                sb_, sb_t = st.next()
                if t % 2 == 0:
                    P.op("act", lambda e, ps=ps, sb_=sb_, ncol=ncol: e.copy(sb_[:, :ncol], ps[:, :ncol]), reads=[ps_t], writes=[sb_t])
                else:
                    P.op("dve", lambda e, ps=ps, sb_=sb_, ncol=ncol: e.tensor_copy(sb_[:, :ncol], ps[:, :ncol]), reads=[ps_t], writes=[sb_t])
                store(t, ocol, ncol, sb_, sb_t)
            elif kind == "silu":
                sb_, sb_t = st.next()
                P.op("act", lambda e, ps=ps, sb_=sb_: e.activation(sb_[:], ps[:, :], AF.Silu), reads=[ps_t], writes=[sb_t])
                store(t, ocol, 512, sb_, sb_t)
            elif kind == "rope":
                t1, t1_t = tA.next()
                t2, t2_t = tB.next()
                cosb = rcos[:, t, :].unsqueeze(1).to_broadcast([128, 8, 64])
                P.op("dve", lambda e, ps=ps, t1=t1, cosb=cosb: e.tensor_tensor(
                    t1[:].rearrange("p (g d) -> p g d", g=8), ps[:, :].rearrange("p (g d) -> p g d", g=8), cosb, ALU.mult),
                    reads=[ps_t, rcos_t], writes=[t1_t])
                psv = ps[:, :].rearrange("p (g a b d) -> p g a b d", g=8, a=2, b=2)
                t2v = t2[:].rearrange("p (g a b d) -> p g a b d", g=8, a=2, b=2)
                snv = rsin[:, t, :].rearrange("p (a b d) -> p a b d", a=2, b=2)
                for a in range(2):
                    for b in range(2):
                        sn = snv[:, a, b, :].unsqueeze(1).to_broadcast([128, 8, 16])
                        P.op("dve", lambda e, a=a, b=b, psv=psv, t2v=t2v, sn=sn: e.tensor_tensor(
                            t2v[:, :, a, b, :], psv[:, :, a, 1 - b, :], sn, ALU.mult),
                            reads=[ps_t, rsin_t], writes=[t2_t])
                sb_, sb_t = st.next()
                P.op("pool", lambda e, t1=t1, t2=t2, sb_=sb_: e.tensor_tensor(sb_[:], t1[:], t2[:], ALU.add),
                     reads=[t1_t, t2_t], writes=[sb_t])
                store(t, ocol, 512, sb_, sb_t)
            elif kind == "D_b":
                P.op("act", lambda e, ps=ps, t=t: e.copy(UB[:, t, :], ps[:, :]), reads=[ps_t], writes=[UB_t[t]])
            elif kind == "D_c":
                if t < NT:
                    P.op("act", lambda e, ps=ps, t=t: e.copy(CG[:, t, :], ps[:, :]), reads=[ps_t], writes=[CG_t[t]])
                else:
                    P.op("act", lambda e, ps=ps: e.copy(ZH[:, :], ps[:4, :]), reads=[ps_t], writes=[ZH_t])
            elif kind == "D_x":
                if t < NT:
                    P.op("dve", lambda e, ps=ps, t=t: e.tensor_tensor(CG[:, t, :], CG[:, t, :], ps[:, :], ALU.mult),
                         reads=[ps_t, CG_t[t]], writes=[CG_t[t]])
                else:
                    P.op("dve", lambda e, ps=ps: e.scalar_tensor_tensor(ZH[:, :], ps[:4, :], hmask[:, 0:1], ZH[:, :], ALU.mult, ALU.mult),
                         reads=[ps_t, ZH_t, hmask_t], writes=[ZH_t])
            elif kind == "D_g":
                sg, sg_t = tA.next()
                P.op("act", lambda e, ps=ps, sg=sg: e.activation(sg[:], ps[:, :], AF.Silu), reads=[ps_t], writes=[sg_t])
                P.op("dve", lambda e, sg=sg, t=t: e.tensor_tensor(UB[:, t, :], UB[:, t, :], sg[:], ALU.mult),
                     reads=[sg_t, UB_t[t]], writes=[UB_t[t]])

            if hook is not None:
                hook(ci, t)

    hTf = hT[:].rearrange("p a b -> p (a b)").bitcast(F32)
    ZMv = hTf[:, 0:NT * WG].rearrange("p (t c) -> p t c", t=NT)
    ZPv = hTf[:, NT * WG:2 * NT * WG].rearrange("p (t c) -> p t c", t=NT)
    zm_t = Tok(); zp_t = Tok()
    P.dma("sp", ZMv[1:128, :, :], CG[0:127, :, :], reads=CG_t, writes=[zm_t] + hT_t)
    P.dma("sp", ZMv[0:1, 1:NT - 1, :], CG[127:128, 0:NT - 2, :], reads=CG_t, writes=[zm_t], semtok=zm_t)
    P.dma("sp", ZMv[0:1, 0, :], ZH[0:1, :], reads=[ZH_t], writes=[zm_t], semtok=zm_t)
    P.dma("sp", ZMv[0:1, NT - 1, :], ZH[2:3, :], reads=[ZH_t], writes=[zm_t], semtok=zm_t)
    P.dma("sp", ZPv[0:127, :, :], CG[1:128, :, :], reads=CG_t, writes=[zp_t] + hT_t)
    P.dma("sp", ZPv[127:128, 0:NT - 2, :], CG[0:1, 1:NT - 1, :], reads=CG_t, writes=[zp_t], semtok=zp_t)
    P.dma("sp", ZPv[127:128, NT - 2, :], ZH[1:2, :], reads=[ZH_t], writes=[zp_t], semtok=zp_t)
    P.dma("sp", ZPv[127:128, NT - 1, :], ZH[3:4, :], reads=[ZH_t], writes=[zp_t], semtok=zp_t)
    for t in range(NT):
        t1, t1_t = tA.next()
        t2, t2_t = tB.next()
        P.op("dve", lambda e, t=t, t1=t1: e.tensor_tensor(t1[:], CG[:, t, :], convw[:, 1, :], ALU.mult),
             reads=[CG_t[t], convw_t], writes=[t1_t])
        P.op("pool", lambda e, t=t, t2=t2: e.tensor_tensor(t2[:], ZMv[:, t, :], convw[:, 0, :], ALU.mult),
             reads=[zm_t, convw_t], writes=[t2_t])
        P.op("dve", lambda e, t1=t1, t2=t2: e.tensor_tensor(t1[:], t1[:], t2[:], ALU.add), reads=[t1_t, t2_t], writes=[t1_t])
        P.op("pool", lambda e, t=t, t2=t2: e.tensor_tensor(t2[:], ZPv[:, t, :], convw[:, 2, :], ALU.mult),
             reads=[zp_t, convw_t], writes=[t2_t])
        P.op("dve", lambda e, t1=t1, t2=t2: e.tensor_tensor(t1[:], t1[:], t2[:], ALU.add), reads=[t1_t, t2_t], writes=[t1_t])
        sb_, sb_t = st.next()
        P.op("dve", lambda e, t=t, t1=t1, sb_=sb_: e.tensor_tensor(sb_[:], t1[:], UB[:, t, :], ALU.mult),
             reads=[t1_t, UB_t[t]], writes=[sb_t])
        store(t, OD, 512, sb_, sb_t)
    if hook is not None:
        hook(None, None)
    P.close_scope()


def emit_C(P, l, ntile, pa_l, obc_loc, x_d, mod_all, gpost_d, wo_d, ident_d, out_d, edge_d=None, stage=None):
    P.open_scope()
    ident = P.sbuf("ident", [128, 128], F32); ident_t = Tok()
    P.dma("sp", ident[:], ident_d[:, :], writes=[ident_t])
    wo = P.sbuf("wo", [128, 16, D], BF16)
    wo_t = [Tok() for _ in range(4)]
    for j in range(4):
        P.dma("pool", wo[:, j * 4:(j + 1) * 4, :], wo_d[j * 512:(j + 1) * 512, :].rearrange("(kc p) n -> p kc n", p=128),
              writes=[wo_t[j]])
    stoks = stage() if stage is not None else [[Tok(), Tok()], [Tok(), Tok()]]
    gp = P.sbuf("gp", [128, D], F32); gp_t = Tok()
    P.dma("sp", gp[:], gpost_d.broadcast_to([128, D]), writes=[gp_t])
    GG = P.sbuf("GG", [128, 2, D], F32); GG_t = Tok()
    for j in range(2):
        for (r, off, ln, dst) in mod_pieces(4096):
            P.dma("sp", GG[:, j, dst:dst + ln], mod_all[r, l, j, off:off + ln].unsqueeze(0).broadcast_to([128, ln]), writes=[GG_t], semtok=GG_t)
    for j in range(2):
        P.op("pool", lambda e, j=j: e.tensor_tensor(GG[:, j, :], GG[:, j, :], gp[:], ALU.mult), reads=[GG_t, gp_t], writes=[GG_t])
    os_ = Rot([P.sbuf("os%d" % i, [128, D], F32) for i in range(2)])
    xs = Rot([P.sbuf("xs%d" % i, [128, D], F32) for i in range(2)])
    oT = Rot([P.sbuf("oT%d" % i, [128, 16, 128], BF16) for i in range(2)])
    ol = Rot([P.sbuf("ol%d" % i, [128, D], F32) for i in range(2)])
    junk = P.sbuf("junk", [128, D], BF16); junk_t = Tok()
    sm = P.sbuf("sm", [128, ntile, 4], F32); sm_t = [Tok() for _ in range(ntile)]
    pt = Rot([P.psum("pt%d" % i, [128, 4, 128], F32) for i in range(3)], psum=True)
    pm = Rot([P.psum("pm%d" % i, [128, 512], F32) for i in range(4)], psum=True)
    for t in range(ntile):
        ob, ob_t = os_.next()
        P.dma("sp", ob[:, 0:512], pa_l[t, :, 0:512], writes=[ob_t])
        P.dma("sp", ob[:, 1536:2048], pa_l[t, :, 512:1024], writes=[ob_t], semtok=ob_t)
        for j in range(2):
            P.dma("sp", ob[:, 512 + j * 512:1024 + j * 512].rearrange("p (h c) -> p h c", h=4),
                  obc_loc[:, j, t * 128:(t + 1) * 128, :].rearrange("h p c -> p h c"), reads=[stoks[j][0 if t < 8 else 1]],
                  writes=[ob_t], semtok=ob_t)
        xb, xb_t = xs.next()
        P.dma("sp", xb[:], x_d[t], writes=[xb_t])
        ot, ot_t = oT.next()
        for g in range(4):
            pb, pb_t = pt.next()
            for j in range(4):
                kc = g * 4 + j
                P.op("pe", lambda e, pb=pb, j=j, kc=kc, ob=ob: e.transpose(pb[:, j, :], ob[:, kc * 128:(kc + 1) * 128], ident[:]),
                     reads=[ob_t, ident_t], writes=[pb_t])
            if g % 2 == 0:
                P.op("act", lambda e, pb=pb, g=g, ot=ot: e.copy(ot[:, g * 4:(g + 1) * 4, :], pb[:]), reads=[pb_t], writes=[ot_t])
            else:
                P.op("dve", lambda e, pb=pb, g=g, ot=ot: e.tensor_copy(ot[:, g * 4:(g + 1) * 4, :], pb[:]), reads=[pb_t], writes=[ot_t])
        olb, olb_t = ol.next()
        for cb in range(4):
            ps, ps_t = pm.next()
            for kc in range(16):
                P.op("pe", lambda e, ps=ps, kc=kc, cb=cb, ot=ot: e.matmul(
                    ps[:, :], ot[:, kc, :], wo[:, kc, cb * 512:(cb + 1) * 512], start=(kc == 0), stop=(kc == 15)),
                    reads=[ot_t, wo_t[kc // 4]], writes=[ps_t])
            if cb % 2 == 0:
                P.op("dve", lambda e, ps=ps, cb=cb, olb=olb: e.tensor_copy(olb[:, cb * 512:(cb + 1) * 512], ps[:, :]), reads=[ps_t], writes=[olb_t])
            else:
                P.op("act", lambda e, ps=ps, cb=cb, olb=olb: e.copy(olb[:, cb * 512:(cb + 1) * 512], ps[:, :]), reads=[ps_t], writes=[olb_t])
        s_ = sm[:, t, :]
        P.op("act", lambda e, olb=olb, s_=s_: e.activation(junk[:], olb[:], AF.Square, accum_out=s_[:, 0:1]),
             reads=[olb_t], writes=[junk_t, sm_t[t]])
        P.op("act", lambda e, s_=s_: e.activation(s_[:, 1:2], s_[:, 0:1], AF.Ln, bias=EPS, scale=1.0 / D), reads=[sm_t[t]], writes=[sm_t[t]])
        P.op("act", lambda e, s_=s_: e.activation(s_[:, 2:3], s_[:, 1:2], AF.Exp, scale=-0.5), reads=[sm_t[t]], writes=[sm_t[t]])
        gi = 0 if t < 8 else 1
        P.op("dve", lambda e, olb=olb, s_=s_, gi=gi: e.scalar_tensor_tensor(olb[:], olb[:], s_[:, 2:3], GG[:, gi, :], ALU.mult, ALU.mult),
             reads=[olb_t, sm_t[t], GG_t], writes=[olb_t])
        P.op("pool", lambda e, olb=olb, xb=xb: e.tensor_tensor(xb[:], xb[:], olb[:], ALU.add), reads=[olb_t, xb_t], writes=[xb_t])
        P.dma("sp", out_d[t], xb[:], reads=[xb_t], semtok=xb_t)
        if edge_d is not None:
            for (tt, prow, er) in ((0, 0, 0), (7, 127, 1), (8, 0, 2), (8, 127, 3)):
                if tt == t:
                    P.dma("sp", edge_d[er:er + 1, :], xb[prow:prow + 1, :], reads=[xb_t], semtok=xb_t)
    P.close_scope()


NTOK = CTX + SEQ
NKB = NTOK // 128


def emit_attn(P, l, has_ctxq, at_loc, lam_d, subg_d, ident_d, out_d, pre=None):
    lam_init = 0.8 - 0.6 * math.exp(-0.3 * l)
    P.open_scope()
    ident = P.sbuf("ident", [128, 128], F32); ident_t = Tok()
    P.dma("sp", ident[:], ident_d[:, :], writes=[ident_t])
    qT = P.sbuf("qT", [128, NTOK], BF16); qT_t = Tok()
    KT = [P.sbuf("KT%d" % m, [128, NTOK], BF16) for m in range(2)]
    KT_t = [Tok(), Tok()]
    for m in range(2):
        P.op("dve", lambda e, m=m: e.memset(KT[m][(1 - m) * 64:(2 - m) * 64, :], 0.0), writes=[KT_t[m]])
    QK = P.sbuf("QKtm", [128, 2, NKB, 128], F32); QK_t = [Tok(), Tok()]
    for j in range(2):
        P.dma("sp", QK[:, j, :, :], at_loc[:, j * 128:(j + 1) * 128].rearrange("(t p) d -> p t d", p=128), writes=[QK_t[j]])
    ptr = P.psum("ptr", [128, 4, 128], F32); ptr_t = Tok(psum=True)
    for g in range(0, NKB, 2):
        for j in range(2):
            for i in range(2):
                P.op("pe", lambda e, j=j, i=i, g=g: e.transpose(ptr[:, j * 2 + i, :], QK[:, j, g + i, :], ident[:]),
                     reads=[QK_t[j], ident_t], writes=[ptr_t])
        P.op("act", lambda e, g=g: e.copy(qT[:, g * 128:(g + 2) * 128], ptr[:, 0:2, :].rearrange("p a b -> p (a b)")), reads=[ptr_t], writes=[qT_t])
        P.op("dve", lambda e, g=g: e.tensor_copy(KT[0][0:64, g * 128:(g + 2) * 128], ptr[0:64, 2:4, :].rearrange("p a b -> p (a b)")), reads=[ptr_t], writes=[KT_t[0]])
        P.op("act", lambda e, g=g: e.copy(KT[1][64:128, g * 128:(g + 2) * 128], ptr[64:128, 2:4, :].rearrange("p a b -> p (a b)")), reads=[ptr_t], writes=[KT_t[1]])
    V = P.sbuf("V", [128, NKB, 129], BF16); V_t = Tok()
    P.op("dve", lambda e: e.memset(V[:, :, 128:129], 1.0), writes=[V_t])
    P.dma("pool", V[:, :, 0:128], at_loc[:, 256:384].rearrange("(t p) d -> p t d", p=128), writes=[V_t])
    GS = P.sbuf("GS", [128, NKB, 128], F32); GS_t = Tok()
    P.dma("sp", GS[:], at_loc[:, 384:512].rearrange("(t p) d -> p t d", p=128), writes=[GS_t])
    subg = P.sbuf("subg", [128, 128], F32); subg_t = Tok()
    P.dma("sp", subg[:], subg_d.broadcast_to([128, 128]), writes=[subg_t])
    lamv = P.sbuf("lamv", [128, 256], F32); lamv_t = Tok()
    P.dma("sp", lamv[:], lam_d.broadcast_to([128, 256]), writes=[lamv_t])
    if pre is not None:
        pre()
    lsm = P.sbuf("lsm", [128, 8], F32); lsm_t = Tok()
    ljunk = P.sbuf("ljunk", [128, 64], F32); ljunk_t = Tok()
    for j in range(2):
        P.op("dve", lambda e, j=j: e.scalar_tensor_tensor(ljunk[:], lamv[:, j * 128:j * 128 + 64], 1.0, lamv[:, j * 128 + 64:j * 128 + 128],
                                                         ALU.mult, ALU.mult, accum_out=lsm[:, j:j + 1]),
             reads=[lamv_t], writes=[ljunk_t, lsm_t])
    P.op("act", lambda e: e.activation(lsm[:, 2:4], lsm[:, 0:2], AF.Exp), reads=[lsm_t], writes=[lsm_t])
    P.op("dve", lambda e: e.tensor_tensor(lsm[:, 4:5], lsm[:, 3:4], lsm[:, 2:3], ALU.subtract), reads=[lsm_t], writes=[lsm_t])
    P.op("dve", lambda e: e.tensor_scalar(lsm[:, 4:5], lsm[:, 4:5], -lam_init, None, ALU.add), reads=[lsm_t], writes=[lsm_t])
    P.op("dve", lambda e: e.scalar_tensor_tensor(GS[:], GS[:], 1.0 - lam_init, subg[:].unsqueeze(1).to_broadcast([128, NKB, 128]),
                                                ALU.mult, ALU.mult), reads=[GS_t, subg_t], writes=[GS_t])

    pss = Rot([P.psum("pss%d" % i, [128, 512], F32) for i in range(3)], psum=True)
    po = [P.psum("po%d" % i, [128, 512], F32) for i in range(4)]
    po_t = [Tok(psum=True) for _ in range(4)]
    pT = Rot([P.sbuf("pT%d" % i, [128, 512], BF16) for i in range(3)])
    o0 = P.sbuf("o0", [128, 4, 128], F32); o0_t = [Tok() for _ in range(4)]
    osb = Rot([P.sbuf("osb%d" % i, [128, 128], F32) for i in range(2)])
    ost = Rot([P.sbuf("ost%d" % i, [128, 128], F32) for i in range(3)])
    sjunk = P.sbuf("sjunk", [128, 128], F32); sjunk_t = Tok()
    rs = Rot([P.sbuf("rs%d" % i, [128, 8], F32) for i in range(4)])

    groups = []
    for i in range(8):
        for m in range(2):
            groups.append((CTX + i * 512, 512, list(range(NKB)), m))
    if has_ctxq:
        for m in range(2):
            groups.append((0, 256, [0, 1], m))
    steps = []
    for gi, (q0, nq, kbs, m) in enumerate(groups):
        for kb in kbs:
            steps.append((gi, kb))
    held = {}

    def emit_qk(si):
        gi, kb = steps[si]
        q0, nq, kbs, m = groups[gi]
        ps, ps_t = pss.next()
        P.op("pe", lambda e, ps=ps, nq=nq, m=m, kb=kb, q0=q0: e.matmul(
            ps[:, :nq], KT[m][:, kb * 128:(kb + 1) * 128], qT[:, q0:q0 + nq], start=True, stop=True),
            reads=[KT_t[m], qT_t], writes=[ps_t])
        held[si] = (ps, ps_t)

    LOOK = 2
    for si in range(min(LOOK, len(steps))):
        emit_qk(si)
    for si, (gi, kb) in enumerate(steps):
        q0, nq, kbs, m = groups[gi]
        if si + LOOK < len(steps):
            emit_qk(si + LOOK)
        ps, ps_t = held.pop(si)
        pt_, pt_t = pT.next()
        P.op("act", lambda e, ps=ps, pt_=pt_, nq=nq: e.activation(pt_[:, :nq], ps[:, :nq], AF.Exp, scale=0.125),
             reads=[ps_t], writes=[pt_t])
        nqs = nq // 128
        for qs in range(nqs):
            P.op("pe", lambda e, qs=qs, pt_=pt_, kb=kb, kbs=kbs: e.matmul(
                po[qs][:, 0:129], pt_[:, qs * 128:(qs + 1) * 128], V[:, kb, :], start=(kb == kbs[0]), stop=(kb == kbs[-1])),
                reads=[pt_t, V_t], writes=[po_t[qs]])
        if kb == kbs[-1]:
            for qs in range(nqs):
                r, r_t = rs.next()
                tile = (q0 // 128) + qs
                P.op("dve", lambda e, r=r, qs=qs: e.reciprocal(r[:, 0:1], po[qs][:, 128:129]), reads=[po_t[qs]], writes=[r_t])
                if m == 0:
                    P.op("dve", lambda e, r=r, qs=qs: e.tensor_scalar(o0[:, qs, :], po[qs][:, 0:128], r[:, 0:1], None, ALU.mult),
                         reads=[po_t[qs], r_t], writes=[o0_t[qs]])
                else:
                    ob, ob_t = osb.next()
                    P.op("dve", lambda e, r=r: e.tensor_tensor(r[:, 1:2], r[:, 0:1], lsm[:, 4:5], ALU.mult), reads=[r_t, lsm_t], writes=[r_t])
                    P.op("dve", lambda e, r=r, qs=qs, ob=ob: e.scalar_tensor_tensor(ob[:], po[qs][:, 0:128], r[:, 1:2], o0[:, qs, :], ALU.mult, ALU.add),
                         reads=[po_t[qs], r_t, o0_t[qs]], writes=[ob_t])
                    P.op("dve", lambda e, r=r, ob=ob: e.scalar_tensor_tensor(sjunk[:], ob[:], 1.0, ob[:], ALU.mult, ALU.mult, accum_out=r[:, 2:3]),
                         reads=[ob_t], writes=[sjunk_t, r_t])
                    P.op("act", lambda e, r=r: e.activation(r[:, 3:4], r[:, 2:3], AF.Ln, bias=EPS, scale=1.0 / 128), reads=[r_t], writes=[r_t])
                    P.op("act", lambda e, r=r: e.activation(r[:, 4:5], r[:, 3:4], AF.Exp, scale=-0.5), reads=[r_t], writes=[r_t])
                    st_, st_t = ost.next()
                    P.op("dve", lambda e, r=r, ob=ob, st_=st_, tile=tile: e.scalar_tensor_tensor(st_[:], ob[:], r[:, 4:5], GS[:, tile, :], ALU.mult, ALU.mult),
                         reads=[ob_t, r_t, GS_t], writes=[st_t])
                    P.dma("sp", out_d[tile, :, :], st_[:], reads=[st_t], semtok=st_t)
    P.close_scope()


RW_F = 480


def rw_row0(c):
    return 1 + c * 128 if c < 2 else 259 + (c - 2) * 128


RW_ROWS = NTOK + 4
RW_COLS = 704


def emit_rwkv(P, nch, order_f, order_b, rw_loc, mu_d, w2e_d, a2e_d, vecs_d, cst_d, out_d):
    ntok = nch * 128
    RB = BF16
    P.open_scope()
    cst = P.sbuf("cst", [128, 8, 128], F32); cst_t = Tok()
    P.dma("sp", cst[:], cst_d.rearrange("c p q -> p c q"), writes=[cst_t])
    ident = cst[:, 0, :]; ones = cst[:, 1, :]
    TRI = [cst[:, 2, :], cst[:, 3, :]]
    MS = [cst[:, 4, :], cst[:, 6, :]]
    MI = [cst[:, 5, :], cst[:, 7, :]]
    MST = [cst[:, 6, :], cst[:, 4, :]]
    identb = P.sbuf("identb", [128, 128], RB); identb_t = Tok()
    P.op("dve", lambda e: e.tensor_copy(identb[:], ident), reads=[cst_t], writes=[identb_t])
    mu = P.sbuf("mu", [128, 2, RW_F], F32); mu_t = Tok()
    P.dma("sp", mu[:].rearrange("p a b -> p (a b)"), mu_d.rearrange("a b -> (a b)").unsqueeze(0).broadcast_to([128, 2 * RW_F]), writes=[mu_t])
    vecs = P.sbuf("vecs", [128, 8, 128], F32); vecs_t = Tok()
    P.dma("sp", vecs[:].rearrange("p a b -> p (a b)"), vecs_d.rearrange("a b -> (a b)").unsqueeze(0).broadcast_to([128, 8 * 128]), writes=[vecs_t])
    w2e = P.sbuf("w2e", [65, 2, 128], F32); w2e_t = Tok()
    P.dma("sp", w2e[:], w2e_d.rearrange("d k n -> k d n"), writes=[w2e_t])
    a2e = P.sbuf("a2e", [33, 2, 128], F32); a2e_t = Tok()
    P.dma("sp", a2e[:], a2e_d.rearrange("d k n -> k d n"), writes=[a2e_t])
    G = P.sbuf("G", [128, nch, 128], F32); G_t = Tok()
    P.dma("sp", G[:, 0:2, :], rw_loc[1:257, 384:512].rearrange("(c p) n -> p c n", p=128), writes=[G_t])
    P.dma("sp", G[:, 2:nch, :], rw_loc[259:259 + (nch - 2) * 128, 384:512].rearrange("(c p) n -> p c n", p=128), writes=[G_t], semtok=G_t)
    Y = [P.sbuf("Y%d" % d, [128, nch, 128], F32) for d in range(2)]
    Y_t = [[Tok() for _ in range(nch)] for d in range(2)]
    BON1 = P.sbuf("BON", [128, nch, 128], F32)
    BON = [BON1, BON1]
    BON1_t = [Tok() for _ in range(nch)]
    BON_t = [BON1_t, BON1_t]
    P.op("pool", lambda e: e.memset(BON1[:], 0.0), writes=BON1_t)
    ST32 = [P.sbuf("ST32_%d" % d, [128, 64], F32) for d in range(2)]
    STb = [P.sbuf("STb_%d" % d, [128, 64], RB) for d in range(2)]
    ST_t = [Tok(), Tok()]; STb_t = [Tok(), Tok()]
    for d in range(2):
        P.op("pool", lambda e, d=d: e.memset(ST32[d][:], 0.0), writes=[ST_t[d]])
        P.op("pool", lambda e, d=d: e.memset(STb[d][:], 0.0), writes=[STb_t[d]])

    NB_ = 2

    def dbuf(name, shape, dt, zero=False, one_rows=None):
        bufs = []
        for d in range(2):
            lst = []
            for i in range(NB_):
                b = P.sbuf("%s_%d_%d" % (name, d, i), shape, dt)
                t = Tok()
                if zero:
                    P.op("pool", lambda e, b=b: e.memset(b[:], 0.0), writes=[t])
                if one_rows is not None:
                    P.op("pool", lambda e, b=b: e.memset(b[one_rows[0]:one_rows[1], :], 1.0), writes=[t])
                lst.append((b, t))
            bufs.append(lst)
        return bufs

    X = dbuf("X", [128, RW_F], F32); XP = dbuf("XP", [128, RW_F], F32); Z = dbuf("Z", [128, RW_F], F32)
    TW = dbuf("TW", [128, 64], F32)
    TWT = dbuf("TWT", [65, 128], F32, one_rows=(64, 65)); ZAT = dbuf("ZAT", [33, 128], F32, one_rows=(32, 33))
    LW = dbuf("LW", [128, 128], F32); ETA = dbuf("ETA", [128, 128], F32)
    KK = dbuf("KK", [128, 128], F32); KP = dbuf("KP", [128, 128], F32); BB = dbuf("BB", [128, 128], F32)
    TMP = dbuf("TMP", [128, 128], F32); TMP2 = dbuf("TMP2", [128, 128], F32)
    SM = dbuf("SM", [128, 16], F32)
    CSs = dbuf("CSs", [128, 128], F32); D3 = dbuf("D3", [128, 128], F32); D4 = dbuf("D4", [128, 128], F32)
    E1 = dbuf("E1", [128, 128], F32); E2 = dbuf("E2", [128, 128], F32); E3 = dbuf("E3", [128, 128], F32); E4 = dbuf("E4", [128, 128], F32)
    GCOL = dbuf("GCOL", [128, 1], F32)
    AH = dbuf("AH", [128, 128], RB); BH = dbuf("BH", [128, 128], RB); KH = dbuf("KH", [128, 128], RB); RH = dbuf("RH", [128, 128], RB)
    BTz = [dbuf("BTz%d" % h, [128, 128], RB, zero=True) for h in range(2)]
    KTz = [dbuf("KTz%d" % h, [128, 128], RB, zero=True) for h in range(2)]
    VB = dbuf("VB", [128, 128], RB)
    BHT = dbuf("BHT", [128, 128], RB); KHT = dbuf("KHT", [128, 128], RB)
    AHTz = [dbuf("AHTz%d" % h, [128, 128], RB, zero=True) for h in range(2)]
    RHTz = [dbuf("RHTz%d" % h, [128, 128], RB, zero=True) for h in range(2)]
    Nm = [[dbuf("Nm%d_%d" % (h, i), [128, 128], F32) for i in range(2)] for h in range(2)]
    Mm = [[dbuf("Mm%d_%d" % (h, i), [128, 128], F32) for i in range(2)] for h in range(2)]
    Pm = [[dbuf("Pm%d_%d" % (h, i), [128, 128], F32) for i in range(2)] for h in range(2)]
    AH32 = dbuf("AH32", [128, 128], F32)
    AAK = [dbuf("AAK%d" % h, [128, 128], RB) for h in range(2)]
    ARB = [dbuf("ARB%d" % h, [128, 128], RB) for h in range(2)]
    ARK = [dbuf("ARK%d" % h, [128, 128], RB) for h in range(2)]
    X2 = [dbuf("X2%d" % h, [128, 64], F32) for h in range(2)]
    WTz = [dbuf("WTz%d" % h, [128, 128], RB, zero=True) for h in range(2)]
    US = [dbuf("US%d" % h, [128, 64], RB) for h in range(2)]

    pa_ = Rot([P.psum("pa%d" % i, [128, 512], F32) for i in range(2)], psum=True)
    pb_ = Rot([P.psum("pb%d" % i, [128, 512], F32) for i in range(1)], psum=True)
    ptb = P.psum("ptb", [128, 4, 128], RB); ptb_t = Tok(psum=True)
    pdbl = BankPool([P.psum("pdbl%d" % i, [128, 512], F32) for i in range(4)])

    def vec(i):
        return vecs[:, i, :]

    def unit(d, c, it):
        par = it % NB_
        g = lambda buf: buf[d][par]
        x, x_t = g(X); xp, xp_t = g(XP); z, z_t = g(Z)
        r0_ = rw_row0(c)
        rp_ = r0_ - 1 if d == 0 else r0_ + 1
        P.dma("sp", x[:, 0:384], rw_loc[r0_:r0_ + 128, 0:384], writes=[x_t])
        yield
        P.dma("sp", x[:, 384:480], rw_loc[r0_:r0_ + 128, 512 + d * 96:608 + d * 96], writes=[x_t], semtok=x_t)
        yield
        P.dma("sp", xp[:, 0:384], rw_loc[rp_:rp_ + 128, 0:384], writes=[xp_t])
        yield
        P.dma("sp", xp[:, 384:480], rw_loc[rp_:rp_ + 128, 512 + d * 96:608 + d * 96], writes=[xp_t], semtok=xp_t)
        yield
        P.op("dve", lambda e: e.tensor_tensor(xp[:], xp[:], x[:], ALU.subtract), reads=[x_t, xp_t], writes=[xp_t])
        yield
        P.op("pool", lambda e: e.tensor_tensor(xp[:], xp[:], mu[:, d, :], ALU.mult), reads=[xp_t, mu_t], writes=[xp_t])
        yield
        P.op("dve", lambda e: e.tensor_tensor(z[:], x[:], xp[:], ALU.add), reads=[x_t, xp_t], writes=[z_t])
        yield
        zr = z[:, 0:128]; zk = z[:, 128:256]; zv = z[:, 256:384]; zw = z[:, 384:448]; za = z[:, 448:480]
        yield
        tw, tw_t = g(TW); twt, twt_t = g(TWT); zat, zat_t = g(ZAT)
        P.op("act", lambda e: e.activation(tw[:], zw, AF.Tanh), reads=[z_t], writes=[tw_t])
        yield
        p1, p1_t = pa_.next()
        P.op("pe", lambda e: e.transpose(p1[0:64, 0:128], tw[:], ident), reads=[tw_t, cst_t], writes=[p1_t])
        P.op("pe", lambda e: e.transpose(p1[0:32, 128:256], za, ident), reads=[z_t, cst_t], writes=[p1_t])
        P.op("act", lambda e: e.copy(twt[0:64, :], p1[0:64, 0:128]), reads=[p1_t], writes=[twt_t])
        P.op("act", lambda e: e.copy(zat[0:32, :], p1[0:32, 128:256]), reads=[p1_t], writes=[zat_t])
        p2, p2_t = pa_.next()
        P.op("pe", lambda e: e.matmul(p2[:, 0:128], twt[:, :], w2e[:, d, :], start=True, stop=True), reads=[twt_t, w2e_t], writes=[p2_t])
        P.op("pe", lambda e: e.matmul(p2[:, 128:256], zat[:, :], a2e[:, d, :], start=True, stop=True), reads=[zat_t, a2e_t], writes=[p2_t])
        lw, lw_t = g(LW); eta, eta_t = g(ETA)
        P.op("act", lambda e: e.activation(lw[:], p2[:, 0:128], AF.Tanh, scale=0.5), reads=[p2_t], writes=[lw_t])
        P.op("act", lambda e: e.activation(eta[:], p2[:, 128:256], AF.Tanh, scale=0.5), reads=[p2_t], writes=[eta_t])
        P.op("pool", lambda e: e.tensor_scalar(lw[:], lw[:], 1.0, -0.5 * math.exp(-0.5), ALU.add, ALU.mult), reads=[lw_t], writes=[lw_t])
        yield
        P.op("pool", lambda e: e.tensor_scalar(eta[:], eta[:], 1.0, 0.5, ALU.add, ALU.mult), reads=[eta_t], writes=[eta_t])
        yield
        kk, kk_t = g(KK); kp, kp_t = g(KP); bb, bb_t = g(BB); tmp, tmp_t = g(TMP); tmp2, tmp2_t = g(TMP2); sm, sm_t = g(SM)
        P.op("dve", lambda e: e.tensor_tensor(kk[:], zk, vec(3 * d + 0), ALU.mult), reads=[z_t, vecs_t], writes=[kk_t])
        yield
        P.op("dve", lambda e: e.tensor_tensor(tmp[:], kk[:], kk[:], ALU.mult), reads=[kk_t], writes=[tmp_t])
        yield
        P.op("dve", lambda e: e.tensor_reduce(sm[:, 0:2], tmp[:].rearrange("p (h k) -> p h k", h=2), AX.X, ALU.add), reads=[tmp_t], writes=[sm_t])
        yield
        P.op("dve", lambda e: e.tensor_scalar(sm[:, 0:2], sm[:, 0:2], 1e-19, None, ALU.max), reads=[sm_t], writes=[sm_t])
        yield
        P.op("act", lambda e: e.activation(sm[:, 2:4], sm[:, 0:2], AF.Ln), reads=[sm_t], writes=[sm_t])
        yield
        P.op("act", lambda e: e.activation(sm[:, 4:6], sm[:, 2:4], AF.Exp, scale=-0.5), reads=[sm_t], writes=[sm_t])
        yield
        for h in range(2):
            P.op("dve", lambda e, h=h: e.tensor_scalar(kk[:, h * 64:(h + 1) * 64], kk[:, h * 64:(h + 1) * 64], sm[:, 4 + h:5 + h], None, ALU.mult),
                 reads=[kk_t, sm_t], writes=[kk_t])
        P.op("dve", lambda e: e.scalar_tensor_tensor(tmp[:], eta[:], -1.0, vec(3 * d + 1), ALU.add, ALU.mult), reads=[eta_t, vecs_t, tmp_t], writes=[tmp_t])
        yield
        P.op("dve", lambda e: e.scalar_tensor_tensor(kp[:], tmp[:], 1.0, zk, ALU.add, ALU.mult), reads=[tmp_t, z_t], writes=[kp_t])
        yield
        P.op("pool", lambda e: e.tensor_tensor(bb[:], kk[:], eta[:], ALU.mult), reads=[kk_t, eta_t], writes=[bb_t])
        yield
        P.op("pool", lambda e: e.tensor_tensor(tmp2[:], zr, kp[:], ALU.mult), reads=[z_t, kp_t], writes=[tmp2_t])
        P.op("pool", lambda e: e.tensor_tensor(tmp2[:], tmp2[:], vec(3 * d + 2), ALU.mult), reads=[tmp2_t, vecs_t], writes=[tmp2_t])
        P.op("dve", lambda e: e.tensor_reduce(sm[:, 6:8], tmp2[:].rearrange("p (h k) -> p h k", h=2), AX.X, ALU.add), reads=[tmp2_t, sm_t], writes=[sm_t])
        for h in range(2):
            P.op("dve", lambda e, h=h: e.scalar_tensor_tensor(BON[d][:, c, h * 64:(h + 1) * 64], z[:, 256 + h * 64:256 + (h + 1) * 64], sm[:, 6 + h:7 + h],
                                                             BON[d][:, c, h * 64:(h + 1) * 64], ALU.mult, ALU.add),
                 reads=[z_t, sm_t, BON_t[d][c]], writes=[BON_t[d][c]])
        yield
        p3, p3_t = pa_.next()
        P.op("pe", lambda e: e.matmul(p3[:, 0:128], TRI[d], lw[:], start=True, stop=True), reads=[cst_t, lw_t], writes=[p3_t])
        P.op("pe", lambda e: e.matmul(p3[:, 128:256], ones, lw[:], start=True, stop=True), reads=[cst_t, lw_t], writes=[p3_t])
        P.op("pe", lambda e: e.matmul(p3[:, 256:257], lw[:], ones[:, 0:1], start=True, stop=True), reads=[cst_t, lw_t], writes=[p3_t])
        css, css_t = g(CSs); d3, d3_t = g(D3); d4, d4_t = g(D4)
        e1, e1_t = g(E1); e2, e2_t = g(E2); e3, e3_t = g(E3); e4, e4_t = g(E4); gcol, gcol_t = g(GCOL)
        P.op("act", lambda e: e.copy(css[:], p3[:, 0:128]), reads=[p3_t], writes=[css_t])
        P.op("act", lambda e: e.activation(gcol[:], p3[:, 256:257], AF.Exp), reads=[p3_t], writes=[gcol_t])
        P.op("dve", lambda e: e.tensor_tensor(d4[:], p3[:, 128:256], css[:], ALU.subtract), reads=[p3_t, css_t], writes=[d4_t])
        P.op("pool", lambda e: e.tensor_tensor(d3[:], css[:], lw[:], ALU.subtract), reads=[css_t, lw_t], writes=[d3_t])
        yield
        P.op("act", lambda e: e.activation(e1[:], css[:], AF.Exp), reads=[css_t], writes=[e1_t])
        yield
        P.op("act", lambda e: e.activation(e2[:], css[:], AF.Exp, scale=-1.0), reads=[css_t], writes=[e2_t])
        yield
        P.op("act", lambda e: e.activation(e3[:], d3[:], AF.Exp), reads=[d3_t], writes=[e3_t])
        yield
        P.op("act", lambda e: e.activation(e4[:], d4[:], AF.Exp), reads=[d4_t], writes=[e4_t])
        yield
        ah, ah_t = g(AH); bh, bh_t = g(BH); kh, kh_t = g(KH); rh, rh_t = g(RH); vb, vb_t = g(VB)
        ah32, ah32_t = g(AH32)
        P.op("dve", lambda e: e.scalar_tensor_tensor(ah32[:], kk[:], -1.0, e3[:], ALU.mult, ALU.mult), reads=[kk_t, e3_t], writes=[ah32_t])
        yield
        P.op("pool", lambda e: e.tensor_copy(ah[:], ah32[:]), reads=[ah32_t], writes=[ah_t])
        yield
        P.op("pool", lambda e: e.tensor_tensor(bh[:], bb[:], e2[:], ALU.mult), reads=[bb_t, e2_t], writes=[bh_t])
        yield
        P.op("dve", lambda e: e.tensor_tensor(kh[:], kp[:], e2[:], ALU.mult), reads=[kp_t, e2_t], writes=[kh_t])
        yield
        P.op("pool", lambda e: e.tensor_tensor(rh[:], zr, e1[:], ALU.mult), reads=[z_t, e1_t], writes=[rh_t])
        yield
        P.op("act", lambda e: e.copy(vb[:], zv), reads=[z_t], writes=[vb_t])
        yield
        btz = [g(BTz[h]) for h in range(2)]; ktz = [g(KTz[h]) for h in range(2)]
        for h in range(2):
            hs = slice(h * 64, (h + 1) * 64)
            P.op("dve", lambda e, h=h, hs=hs: e.tensor_tensor(btz[h][0][:, hs], bb[:, hs], e4[:, hs], ALU.mult), reads=[bb_t, e4_t], writes=[btz[h][1]])
            P.op("pool", lambda e, h=h, hs=hs: e.tensor_tensor(ktz[h][0][:, hs], kp[:, hs], e4[:, hs], ALU.mult), reads=[kp_t, e4_t], writes=[ktz[h][1]])
        yield
        bht, bht_t = g(BHT); kht, kht_t = g(KHT)
        ahtz = [g(AHTz[h]) for h in range(2)]; rhtz = [g(RHTz[h]) for h in range(2)]
        for j, (src, src_t) in enumerate(((ah, ah_t), (bh, bh_t), (kh, kh_t), (rh, rh_t))):
            P.op("pe", lambda e, j=j, src=src: e.transpose(ptb[:, j, :], src[:], identb[:]), reads=[src_t, identb_t], writes=[ptb_t])
        P.op("act", lambda e: e.copy(bht[:], ptb[:, 1, :]), reads=[ptb_t], writes=[bht_t])
        P.op("act", lambda e: e.copy(kht[:], ptb[:, 2, :]), reads=[ptb_t], writes=[kht_t])
        for h in range(2):
            hs = slice(h * 64, (h + 1) * 64)
            P.op("dve", lambda e, h=h, hs=hs: e.tensor_copy(ahtz[h][0][hs, :], ptb[hs, 0, :]), reads=[ptb_t], writes=[ahtz[h][1]])
            P.op("act", lambda e, h=h, hs=hs: e.copy(rhtz[h][0][hs, :], ptb[hs, 3, :]), reads=[ptb_t], writes=[rhtz[h][1]])
        yield
        hd = []
        for h in range(2):
            n0, n0_t = g(Nm[h][0]); m0, m0_t = g(Mm[h][0]); pm0, pm0_t = g(Pm[h][0])
            aak, aak_t = g(AAK[h]); arb, arb_t = g(ARB[h]); ark, ark_t = g(ARK[h])
            specs = [
                (bht, bht_t, ahtz[h][0], ahtz[h][1], MS[d], n0, n0_t),
                (ahtz[h][0], ahtz[h][1], bht, bht_t, MST[d], m0, m0_t),
                (kht, kht_t, ahtz[h][0], ahtz[h][1], MS[d], aak, aak_t),
                (bht, bht_t, rhtz[h][0], rhtz[h][1], MI[d], arb, arb_t),
                (kht, kht_t, rhtz[h][0], rhtz[h][1], MI[d], ark, ark_t),
            ]
            for (lt, lt_t, rt, rt_t, msk, dst, dst_t) in specs:
                pp, pp_t = pa_.next()
                P.op("pe", lambda e, pp=pp, lt=lt, rt=rt: e.matmul(pp[:, 0:128], lt[:], rt[:], start=True, stop=True), reads=[lt_t, rt_t], writes=[pp_t])
                P.op("dve", lambda e, pp=pp, msk=msk, dst=dst: e.tensor_tensor(dst[:], pp[:, 0:128], msk, ALU.mult), reads=[pp_t, cst_t], writes=[dst_t])
            P.op("pool", lambda e, pm0=pm0, n0=n0: e.tensor_tensor(pm0[:], n0[:], ident, ALU.add), reads=[n0_t, cst_t], writes=[pm0_t])
            hd.append(dict(n=(n0, n0_t), m=(m0, m0_t), p=(pm0, pm0_t), aak=(aak, aak_t), arb=(arb, arb_t), ark=(ark, ark_t)))
        yield
        yield "dbl"
        for k in range(7):
            do_sq = k <= 5
            need_n = k <= 4
            do_p = k >= 1
            while len(pdbl.free) < 2:
                yield
            cur = []
            for h in range(2):
                st = hd[h]
                (n, n_t), (m, m_t), (pm, pm_t) = st["n"], st["m"], st["p"]
                m2, m2_t = g(Mm[h][(k + 1) % 2]); n2, n2_t = g(Nm[h][(k + 1) % 2]); p2_, p2__t = g(Pm[h][k % 2])
                bi = pdbl.try_get()
                pp, pp_t = pdbl.items[bi]
                if do_sq:
                    P.op("pe", lambda e, pp=pp, n=n, m=m: e.matmul(pp[:, 0:128], n[:], m[:], start=True, stop=True), reads=[n_t, m_t], writes=[pp_t])
                if need_n:
                    P.op("pe", lambda e, pp=pp, n=n, m=m: e.matmul(pp[:, 128:256], m[:], n[:], start=True, stop=True), reads=[n_t, m_t], writes=[pp_t])
                if do_p:
                    P.op("pe", lambda e, pp=pp, m=m, pm=pm: e.matmul(pp[:, 256:384], m[:], pm[:], start=True, stop=True), reads=[m_t, pm_t], writes=[pp_t])
                cur.append((pp, pp_t, m2, m2_t, n2, n2_t, p2_, p2__t, pm, pm_t, bi))
            yield
            for h in range(2):
                pp, pp_t, m2, m2_t, n2, n2_t, p2_, p2__t, pm, pm_t, bi = cur[h]
                if do_sq:
                    P.op("act", lambda e, pp=pp, m2=m2: e.copy(m2[:], pp[:, 0:128]), reads=[pp_t], writes=[m2_t])
                    hd[h]["m"] = (m2, m2_t)
                if do_p:
                    P.op("dve", lambda e, pp=pp, pm=pm, p2_=p2_: e.tensor_tensor(p2_[:], pp[:, 256:384], pm[:], ALU.add), reads=[pp_t, pm_t], writes=[p2__t])
                    hd[h]["p"] = (p2_, p2__t)
                if need_n:
                    P.op("act", lambda e, pp=pp, n2=n2: e.copy(n2[:], pp[:, 128:256]), reads=[pp_t], writes=[n2_t])
                    hd[h]["n"] = (n2, n2_t)
                pdbl.put(bi)
            yield
        for h in range(2):
            st = hd[h]
            tt, tt_t = st["p"]
            aak, aak_t = st["aak"]
            x2, x2_t = g(X2[h]); wtz, wtz_t = g(WTz[h])
            hs = slice(h * 64, (h + 1) * 64)
            pp, pp_t = pa_.next()
            P.op("pe", lambda e, pp=pp, aak=aak, hs=hs: e.matmul(pp[:, 0:64], aak[:], vb[:, hs], start=True, stop=True), reads=[aak_t, vb_t], writes=[pp_t])
            P.op("pe", lambda e, pp=pp, tt=tt: e.matmul(pp[:, 128:256], ah32[:], tt[:], start=True, stop=True), reads=[ah32_t, tt_t], writes=[pp_t])
            P.op("act", lambda e, pp=pp, x2=x2: e.copy(x2[:], pp[:, 0:64]), reads=[pp_t], writes=[x2_t])
            P.op("dve", lambda e, pp=pp, wtz=wtz, hs=hs: e.tensor_copy(wtz[hs, :], pp[hs, 128:256]), reads=[pp_t], writes=[wtz_t])
            st["x2"] = (x2, x2_t); st["wtz"] = (wtz, wtz_t)
        yield
        us = []
        for h in range(2):
            st = hd[h]
            tt, tt_t = st["p"]; x2, x2_t = st["x2"]; wtz, wtz_t = st["wtz"]
            u, u_t = g(US[h])
            pu, pu_t = pb_.next()
            P.op("pe", lambda e, pu=pu, tt=tt, x2=x2: e.matmul(pu[:, 0:64], tt[:], x2[:], start=True, stop=False), reads=[tt_t, x2_t], writes=[pu_t])
            P.op("pe", lambda e, pu=pu, wtz=wtz: e.matmul(pu[:, 0:64], wtz[:], STb[d][:], start=False, stop=True), reads=[wtz_t, STb_t[d]], writes=[pu_t])
            if h == 0:
                P.op("act", lambda e, pu=pu, u=u: e.copy(u[:], pu[:, 0:64]), reads=[pu_t], writes=[u_t])
            else:
                P.op("dve", lambda e, pu=pu, u=u: e.tensor_copy(u[:], pu[:, 0:64]), reads=[pu_t], writes=[u_t])
            us.append((u, u_t))
        py, py_t = pb_.next()
        for h in range(2):
            st = hd[h]
            arb, arb_t = st["arb"]; ark, ark_t = st["ark"]
            u, u_t = us[h]
            hs = slice(h * 64, (h + 1) * 64)
            P.op("pe", lambda e, h=h, hs=hs: e.matmul(py[:, hs], rhtz[h][0][:], STb[d][:], start=True, stop=False), reads=[rhtz[h][1], STb_t[d]], writes=[py_t])
            P.op("pe", lambda e, hs=hs, arb=arb, u=u: e.matmul(py[:, hs], arb[:], u[:], start=False, stop=False), reads=[arb_t, u_t], writes=[py_t])
            P.op("pe", lambda e, hs=hs, ark=ark: e.matmul(py[:, hs], ark[:], vb[:, hs], start=False, stop=True), reads=[ark_t, vb_t], writes=[py_t])
        P.op("act", lambda e: e.copy(Y[d][:, c, :], py[:, 0:128]), reads=[py_t], writes=[Y_t[d][c]])
        pd, pd_t = pb_.next()
        for h in range(2):
            u, u_t = us[h]
            hs = slice(h * 64, (h + 1) * 64)
            P.op("pe", lambda e, h=h, u=u: e.matmul(pd[:, 0:64], btz[h][0][:], u[:], start=(h == 0), stop=False), reads=[btz[h][1], u_t], writes=[pd_t])
            P.op("pe", lambda e, h=h, hs=hs: e.matmul(pd[:, 0:64], ktz[h][0][:], vb[:, hs], start=False, stop=(h == 1)), reads=[ktz[h][1], vb_t], writes=[pd_t])
        P.op("dve", lambda e: e.scalar_tensor_tensor(ST32[d][:], ST32[d][:], gcol[:, 0:1], pd[:, 0:64], ALU.mult, ALU.add),
             reads=[ST_t[d], gcol_t, pd_t], writes=[ST_t[d]])
        P.op("act", lambda e: e.copy(STb[d][:], ST32[d][:]), reads=[ST_t[d]], writes=[STb_t[d]])
        yield

    nsteps = len(order_f)
    active = []
    nxt = [0]

    def admit():
        it = nxt[0]
        nxt[0] += 1
        active.append([it, unit(0, order_f[it], it), False])
        active.append([it, unit(1, order_b[it], it), False])

    admit()
    while active:
        for ent in list(active):
            try:
                for _ in range(1 if ent[2] else 4):
                    if next(ent[1]) == "dbl":
                        ent[2] = True
                        break
            except StopIteration:
                active.remove(ent)
        if nxt[0] < nsteps:
            live_steps = {e_[0] for e_ in active}
            newest = [e_ for e_ in active if e_[0] == nxt[0] - 1]
            if len(live_steps) < 2 and all(e_[2] for e_ in newest):
                admit()
    ysum = Rot([P.sbuf("ysum%d" % i, [128, 128], F32) for i in range(2)])
    ysq = Rot([P.sbuf("ysq%d" % i, [128, 128], F32) for i in range(2)])
    msm = Rot([P.sbuf("msm%d" % i, [128, 16], F32) for i in range(2)])
    ost = Rot([P.sbuf("ost%d" % i, [128, 128], F32) for i in range(3)])
    for c in range(nch):
        ys, ys_t = ysum.next(); yq, yq_t = ysq.next(); ms, ms_t = msm.next(); o, o_t = ost.next()
        P.op("dve", lambda e, ys=ys, c=c: e.tensor_tensor(ys[:], Y[0][:, c, :], Y[1][:, c, :], ALU.add), reads=[Y_t[0][c], Y_t[1][c]], writes=[ys_t])
        P.op("act", lambda e, ys=ys, yq=yq: e.activation(yq[:], ys[:], AF.Square), reads=[ys_t], writes=[yq_t])
        P.op("dve", lambda e, ys=ys, ms=ms: e.tensor_reduce(ms[:, 0:2], ys[:].rearrange("p (h k) -> p h k", h=2), AX.X, ALU.add), reads=[ys_t], writes=[ms_t])
        P.op("dve", lambda e, yq=yq, ms=ms: e.tensor_reduce(ms[:, 2:4], yq[:].rearrange("p (h k) -> p h k", h=2), AX.X, ALU.add), reads=[yq_t, ms_t], writes=[ms_t])
        P.op("dve", lambda e, ms=ms: e.tensor_scalar(ms[:, 0:2], ms[:, 0:2], 1.0 / 64, None, ALU.mult), reads=[ms_t], writes=[ms_t])
        P.op("dve", lambda e, ms=ms: e.tensor_tensor(ms[:, 4:6], ms[:, 0:2], ms[:, 0:2], ALU.mult), reads=[ms_t], writes=[ms_t])
        P.op("dve", lambda e, ms=ms: e.scalar_tensor_tensor(ms[:, 4:6], ms[:, 2:4], 1.0 / 64, ms[:, 4:6], ALU.mult, ALU.subtract), reads=[ms_t], writes=[ms_t])
        P.op("act", lambda e, ms=ms: e.activation(ms[:, 6:8], ms[:, 4:6], AF.Ln, bias=GN_EPS), reads=[ms_t], writes=[ms_t])
        P.op("act", lambda e, ms=ms: e.activation(ms[:, 8:10], ms[:, 6:8], AF.Exp, scale=-0.5), reads=[ms_t], writes=[ms_t])
        for h in range(2):
            hs = slice(h * 64, (h + 1) * 64)
            P.op("dve", lambda e, ys=ys, ms=ms, h=h, hs=hs: e.tensor_scalar(ys[:, hs], ys[:, hs], ms[:, h:h + 1], ms[:, 8 + h:9 + h], ALU.subtract, ALU.mult),
                 reads=[ys_t, ms_t], writes=[ys_t])
        P.op("pool", lambda e, ys=ys: e.tensor_tensor(ys[:], ys[:], vec(6), ALU.mult), reads=[ys_t, vecs_t], writes=[ys_t])
        P.op("pool", lambda e, ys=ys: e.tensor_tensor(ys[:], ys[:], vec(7), ALU.add), reads=[ys_t, vecs_t], writes=[ys_t])
        P.op("dve", lambda e, ys=ys, c=c: e.tensor_tensor(ys[:], ys[:], BON[0][:, c, :], ALU.add), reads=[ys_t, BON_t[0][c]], writes=[ys_t])
        P.op("dve", lambda e, ys=ys, o=o, c=c: e.tensor_tensor(o[:], ys[:], G[:, c, :], ALU.mult), reads=[ys_t, G_t], writes=[o_t])
        P.dma("sp", out_d[c, :, :], o[:], reads=[o_t], semtok=o_t)
    P.close_scope()


def rwkv_orders(nch_ctx, nch_lat):
    order_f = list(range(nch_ctx + nch_lat))
    order_b = list(range(nch_ctx - 1, -1, -1)) + list(range(nch_ctx + nch_lat - 1, nch_ctx - 1, -1))
    return order_f, order_b


def rwkv_consts():
    i = np.arange(128)
    s, t = i[:, None], i[None, :]
    c = np.stack([np.eye(128), np.ones((128, 128)), s <= t, s >= t, s < t, s <= t, s > t, s >= t]).astype(np.float32)
    return c


EXG = {"A": 2048, "R": 2048, "L": 192}


def ex_dst(exl, exl_t, oad_l):
    def fn(t, col, n):
        if col < OQ:
            return oad_l[t, :, col:col + n], None
        if col < ORKV:
            return exl[t]["A"][:, col - OQ:col - OQ + n], exl_t[t]["A"]
        if col < OLF:
            return exl[t]["R"][:, col - ORKV:col - ORKV + n], exl_t[t]["R"]
        if col < OGC:
            return exl[t]["L"][:, col - OLF:col - OLF + n], exl_t[t]["L"]
        return exl[t]["R"][:, 1536 + col - OGC:1536 + col - OGC + n], exl_t[t]["R"]
    return fn


def emit_compact(P, exg, exg_t, at_loc, eng):
    P.open_scope()
    hv = P.pid4(eng) * 128
    toks = Rot([None] * 16)
    for t in range(NT):
        for s_ in range(4 if t < 8 else 2):
            tok0 = CTX + s_ * 1024 + t * 128 if t < 8 else s_ * 128
            rows = slice(s_ * 128, (s_ + 1) * 128)
            _, tk = toks.next()
            P.dma(eng, at_loc[tok0:tok0 + 128, :].rearrange("n (f c) -> n f c", f=4),
                  exg[t]["A"][rows, :].rearrange("n (f w) -> n f w", f=4)[:, :, bass.ds(hv, 128)], reads=[exg_t[t]["A"]], writes=[tk], semtok=tk)
    P.close_scope()


def compact_rw(P, exg, exg_t, rw_loc, eng):
    hv = P.pid4(eng) * 128
    zt = P.sbuf("zt", [4, RW_COLS], F32); zt_t = Tok()
    P.op("pool", lambda e: e.memset(zt[:], 0.0), writes=[zt_t])
    for r in (0, 257, 258, RW_ROWS - 1):
        P.dma(eng, rw_loc[r:r + 1, :], zt[0:1, :], reads=[zt_t], semtok=zt_t)
    toks = Rot([None] * 12)
    for t in range(NT):
        for s_ in range(4 if t < 8 else 2):
            tok0 = CTX + s_ * 1024 + t * 128 if t < 8 else s_ * 128
            rrow = 259 + (tok0 - CTX) if t < 8 else 1 + tok0
            rows = slice(s_ * 128, (s_ + 1) * 128)
            _, tk = toks.next()
            P.dma(eng, rw_loc[rrow:rrow + 128, 0:512].rearrange("n (f c) -> n f c", f=4),
                  exg[t]["R"][rows, :].rearrange("n (f w) -> n f w", f=4)[:, :, bass.ds(hv, 128)], reads=[exg_t[t]["R"]], writes=[tk], semtok=tk)
            _, tk = toks.next()
            P.dma(eng, rw_loc[rrow:rrow + 128, 512:704], exg[t]["L"][rows, :], reads=[exg_t[t]["L"]], writes=[tk], semtok=tk)


def emit_obc_stage(P, gath, gath_t, obc_loc, eng):
    q = P.pid4(eng)
    toks = []
    for j in range(2):
        g, g_c = gath[j]
        tk = [Tok(), Tok()]
        P.dma(eng, obc_loc[:, j, 0:1024, :],
              g.rearrange("j h n c -> (j h n) c")[bass.ds(q * 4096, 4096), :].rearrange("(h n) c -> h n c", h=4), reads=[gath_t[j]], writes=[tk[0]], semtok=tk[0])
        P.dma(eng, obc_loc[:, j, 1024:1152, :], g_c[:, bass.ds((q % 2) * 128, 128), :], reads=[gath_t[j]], writes=[tk[1]], semtok=tk[1])
        toks.append(tk)
    return toks


def build_fused():
    P = Prog()
    di = P.dram_in
    x_d = di("x", [NT, 128, D], F32); xh_d = di("xh", [4, D], F32); hmask_d = di("hmask", [4, 1], F32)
    cT_d = di("cT", [D, 2], F32); wm_d = di("wm", [2, D, MC], F32); bm_d = di("bm", [2, MC], F32)
    gpre_d = di("gpreT", [2, 128, 16], F32); win_d = di("w_in", [2, D, N_IN], F32)
    sguw_d = di("sgu_wT", [2, 128, 4, 128], F32); sgub_d = di("sgu_bT", [2, 128, 4], F32); convw_d = di("conv_w", [2, 1, 3 * WG], F32)
    cos_d = di("rcos", [128, NT, 64], F32); sin_d = di("rsin", [128, NT, 64], F32); ident_d = di("ident", [128, 128], F32)
    wo_d = di("w_out", [2, D, D], F32); gpost_d = di("gpost", [2, 1, D], F32)
    lam_d = di("lamv", [2, 1, 256], F32); subg_d = di("subg", [2, 1, 128], F32)
    mu_d = di("rw_mu", [2, 2, RW_F], F32); w2e_d = di("rw_w2e", [2, 2, 65, 128], F32); a2e_d = di("rw_a2e", [2, 2, 33, 128], F32)
    vecs_d = di("rw_vecs", [2, 8, 128], F32); cst_d = di("rw_cst", [8, 128, 128], F32)
    xo_d = P.dram_out("xo", [8, 128, D], F32)
    tmp = P.dram_tmp
    stage = [0]
    STOP = DEBUG.get("stop", 999)

    def done():
        stage[0] += 1
        return stage[0] >= STOP

    mod_l = tmp("mod_l", [2, 2, MC], F32); mod_all = tmp("mod_all", [4, 2, 2, MC], F32)
    emit_M(P, cT_d, wm_d, bm_d, mod_l)
    P.allgather(mod_l.rearrange("l j n -> (l j) n"), mod_all.rearrange("r l j n -> (r l j) n"), GROUPS)
    if done():
        return P.finish()
    of, ob = rwkv_orders(2, 32)
    xcur, xhcur = x_d, xh_d
    for l in range(2):
        oad_l = tmp("oad_l%d" % l, [NT, 128, 1024], F32)
        exl = [{g: tmp("exl%d_%d%s" % (l, t, g), [128, n], F32) for g, n in EXG.items()} for t in range(NT)]
        exg = [{g: tmp("exg%d_%d%s" % (l, t, g), [4 * 128, n], F32) for g, n in EXG.items()} for t in range(NT)]
        exl_t = [{g: Tok(multi=True) for g in EXG} for t in range(NT)]
        exg_t = [{g: Tok() for g in EXG} for t in range(NT)]

        pend = []
        st_ = {"step": 0, "last": -99}

        def hook(pos, t, exl=exl, exg=exg, exl_t=exl_t, exg_t=exg_t, pend=pend, st_=st_):
            def issue():
                tt, g = pend.pop(0)
                P.allgather_async(exl[tt][g], exg[tt][g], GROUPS, reads=[exl_t[tt][g]], out_tok=exg_t[tt][g])
            if pos is None:
                while pend:
                    issue()
                return
            if t == NT - 1 and pos == 4:
                pend.extend((tt, "A") for tt in range(NT))
            if t == NT - 1 and pos == 9:
                pend.extend((tt, "R") for tt in range(NT))
                pend.extend((tt, "L") for tt in range(NT))
            st_["step"] += 1
            if pend and st_["step"] - st_["last"] >= 5:
                issue()
                st_["last"] = st_["step"]

        emit_A(P, l, xcur, xhcur, hmask_d, mod_all, gpre_d[l], win_d[l], sguw_d[l], sgub_d[l], convw_d[l], cos_d, sin_d, ident_d,
               ex_dst(exl, exl_t, oad_l), hook)
        if done():
            return P.finish()
        at_loc = tmp("at_loc%d" % l, [NTOK, 512], F32); rw_loc = tmp("rw_loc%d" % l, [RW_ROWS, RW_COLS], F32)
        emit_compact(P, exg, exg_t, at_loc, "sp")
        if done():
            return P.finish()
        ob_l = tmp("ob_l%d" % l, [NKB, 128, 128], F32); oc_l = tmp("oc_l%d" % l, [NKB, 128, 128], F32)
        gath = [(tmp("obg%d_%d" % (l, j), [4, 4, 1024, 128], F32), tmp("obgc%d_%d" % (l, j), [4, 256, 128], F32)) for j in range(2)]
        gath_t = [Tok(), Tok()]
        emit_attn(P, l, l == 0, at_loc, lam_d[l], subg_d[l], ident_d, ob_l,
                  pre=lambda exg=exg, exg_t=exg_t, rw_loc=rw_loc: compact_rw(P, exg, exg_t, rw_loc, "pool"))
        if done():
            return P.finish()

        def gather_o(j, src):
            g, g_c = gath[j]
            for c in range(4):
                P.allgather_async(src[2 + 8 * c:2 + 8 * (c + 1)].rearrange("t p c -> (t p) c"), g[c].rearrange("h n c -> (h n) c"), GROUPS)
            P.allgather_async(src[0:2].rearrange("t p c -> (t p) c"), g_c.rearrange("h n c -> (h n) c"), GROUPS, out_tok=gath_t[j])

        gather_o(0, ob_l)
        emit_rwkv(P, NKB, of, ob, rw_loc, mu_d[l], w2e_d[l], a2e_d[l], vecs_d[l], cst_d, oc_l)
        if done():
            return P.finish()
        gather_o(1, oc_l)
        obc_loc = tmp("obc_loc%d" % l, [4, 2, NT * 128, 128], F32)
        stage_fn = lambda gath=gath, gath_t=gath_t, obc_loc=obc_loc: emit_obc_stage(P, gath, gath_t, obc_loc, "pool")
        if l == 0:
            x1 = tmp("x1", [NT, 128, D], F32); edge_l = tmp("edge_l", [4, D], F32); edge_all = tmp("edge_all", [16, D], F32)
            xh1 = tmp("xh1", [4, D], F32)
            emit_C(P, 0, NT, oad_l, obc_loc, xcur, mod_all, gpost_d[0], wo_d[0], ident_d, x1, edge_l, stage=stage_fn)
            P.allgather(edge_l, edge_all, GROUPS)
            q = P.pid4("pool")
            etk = [Tok() for _ in range(4)]
            for i, (dq, er) in enumerate(((3, 1), (1, 0), (3, 3), (1, 2))):
                P.dma("pool", xh1[i:i + 1, :], edge_all[bass.ds(((q + dq) % 4) * 4 + er, 1), :], writes=[etk[i]])
            P.barrier()
            xcur, xhcur = x1, xh1
            if done():
                return P.finish()
        else:
            emit_C(P, 1, 8, oad_l, obc_loc, xcur, mod_all, gpost_d[1], wo_d[1], ident_d, xo_d, stage=stage_fn)
    return P.finish()


_NC_CACHE = {}


def _f32(a):
    return np.ascontiguousarray(a, dtype=np.float32)


def _rope_tables():
    inv = (10000.0 ** (-np.arange(0, 32, 2, dtype=np.float32) / 32)).astype(np.float32)
    tabs = []
    for q in range(4):
        cos = np.ones((128, NT, 2, 2, 16), np.float32)
        sin = np.zeros((128, NT, 2, 2, 16), np.float32)
        for t in range(NT - 1):
            tok = q * 1024 + t * 128 + np.arange(128)
            for a, pos in enumerate((tok // 64, tok % 64)):
                ang = pos.astype(np.float32)[:, None] * inv[None, :]
                cos[:, t, a, 0, :] = np.cos(ang); cos[:, t, a, 1, :] = np.cos(ang)
                sin[:, t, a, 0, :] = -np.sin(ang); sin[:, t, a, 1, :] = np.sin(ang)
        tabs.append((cos.reshape(128, NT, 64), sin.reshape(128, NT, 64)))
    return tabs


def _to_pk(v):
    return np.ascontiguousarray(v.reshape(16, 128).T)


def rwkv_consts():
    i = np.arange(128)
    s, t = i[:, None], i[None, :]
    c = np.stack([np.eye(128), np.ones((128, 128)), s <= t, s >= t, s < t, s <= t, s > t, s >= t]).astype(np.float32)
    return c


def _core_inputs(p):
    tabs = _rope_tables()
    ident = np.eye(128, dtype=np.float32)
    cst = rwkv_consts()
    x, xc = p['x'], p['ctx']
    shared = {
        "gpreT": _f32(np.stack([_to_pk(p['g_pre'][l]) for l in range(2)])),
        "w_in": _f32(p['w_in']), "sgu_wT": _f32(p['sgu_w'].transpose(0, 3, 1, 2)), "sgu_bT": _f32(p['sgu_b'].transpose(0, 2, 1)),
        "conv_w": _f32(p['conv_w'].reshape(2, 1, 3 * WG)), "ident": ident, "w_out": _f32(p['w_out']),
        "gpost": _f32(p['g_post'][:, None, :]),
        "lamv": _f32(np.concatenate([p['lam_q1'], p['lam_k1'], p['lam_q2'], p['lam_k2']], 1)[:, None, :]),
        "subg": _f32(p['subln_g'][:, None, :]), "rw_cst": cst,
    }
    ins = []
    for k in range(NCORES):
        b, q = k // 4, k % 4
        ct = q % 2
        xt = np.concatenate([x[b, q * 1024:(q + 1) * 1024].reshape(8, 128, D), xc[b, ct * 128:(ct + 1) * 128][None]], 0)
        xh = np.zeros((4, D), np.float32); hm = np.zeros((4, 1), np.float32)
        if q > 0:
            xh[0] = x[b, q * 1024 - 1]; hm[0] = 1
        if q < 3:
            xh[1] = x[b, (q + 1) * 1024]; hm[1] = 1
        if ct > 0:
            xh[2] = xc[b, ct * 128 - 1]; hm[2] = 1
        if ct < 1:
            xh[3] = xc[b, (ct + 1) * 128]; hm[3] = 1
        cols = slice(q * 128, (q + 1) * 128)
        c0 = q * 128
        mus, w2e, a2e, vecs = [], [], [], []
        for l in range(2):
            mus.append([]); w2e.append([]); a2e.append([])
            for d in range(2):
                m = p['rwkv_mu'][l, d]
                mus[l].append(np.concatenate([m[c0:c0 + 128], m[512 + c0:512 + c0 + 128], m[1024 + c0:1024 + c0 + 128], m[1536 + d * 0:1632]]))
                w2e[l].append(np.concatenate([p['rwkv_w2'][l, d][:, cols], p['rwkv_w0'][l, d][None, cols]], 0))
                a2e[l].append(np.concatenate([p['rwkv_a2'][l, d][:, cols], p['rwkv_a0'][l, d][None, cols]], 0))
            vecs.append(np.stack([p['rwkv_kk'][l, 0][cols], p['rwkv_ka'][l, 0][cols], p['rwkv_rk'][l, 0].reshape(-1)[cols],
                                  p['rwkv_kk'][l, 1][cols], p['rwkv_ka'][l, 1][cols], p['rwkv_rk'][l, 1].reshape(-1)[cols],
                                  p['rwkv_ln_w'][l][cols], p['rwkv_ln_b'][l][cols]]))
        dct = dict(shared)
        dct.update({
            "x": _f32(xt), "xh": xh, "hmask": hm,
            "cT": _f32(np.stack([p['c'][b], p['c_ctx']], 1)),
            "wm": _f32(p['w_mod'][:, :, q * MC:(q + 1) * MC]), "bm": _f32(p['b_mod'][:, q * MC:(q + 1) * MC]),
            "rcos": tabs[q][0], "rsin": tabs[q][1],
            "rw_mu": _f32(np.array(mus)), "rw_w2e": _f32(np.array(w2e)), "rw_a2e": _f32(np.array(a2e)), "rw_vecs": _f32(np.array(vecs)),
        })
        ins.append(dct)
    return ins


def kernel(**inputs):
    p = {k: np.asarray(v) for k, v in inputs.items()}
    if "fused" not in _NC_CACHE:
        _NC_CACHE["fused"] = build_fused()
    ins = _core_inputs(p)
    res = run_bass_kernel_spmd(_NC_CACHE["fused"], ins, core_ids=list(range(NCORES))).results
    out = np.zeros((NB, SEQ, D), np.float32)
    for k in range(NCORES):
        b, q = k // 4, k % 4
        out[b, q * 1024:(q + 1) * 1024] = res[k]["xo"].reshape(1024, D)
    return out
```

```python
import math
import numpy as np
import concourse.bass as bass
import concourse.mybir as mybir
from concourse.bass_utils import run_bass_kernel_spmd

F32 = mybir.dt.float32
BF16 = mybir.dt.bfloat16
AF = mybir.ActivationFunctionType
ALU = mybir.AluOpType
AX = mybir.AxisListType

D = 2048
SEQ = 4096
CTX = 256
NB = 2
WG = 512
N_IN = 7872
EPS = 1e-6
GN_EPS = 64e-5
NCORES = 8

ENGS = ("pe", "act", "dve", "pool", "sp")
SAME_ENGINE_SYNC = {"pe": False, "act": True, "dve": True, "pool": True, "sp": False}


class Tok:
    __slots__ = ("name", "lastw", "readers", "sem", "semcnt", "psum", "multi", "wlist")

    def __init__(self, name="", psum=False, multi=False):
        self.name = name
        self.psum = psum
        self.multi = multi
        self.wlist = []
        self.lastw = None
        self.readers = []
        self.sem = None
        self.semcnt = 0


NSEM_POOL = 92


class Prog:
    def __init__(self):
        self.nc = bass.Bass("TRN2", target_bir_lowering=False)
        nc = self.nc
        self.eng = {"pe": nc.tensor, "act": nc.scalar, "dve": nc.vector, "pool": nc.gpsimd, "sp": nc.sync}
        self.cnt = {e: 0 for e in ENGS}
        self.seen = {e: {} for e in ENGS}
        self.esem = {}
        self._ctx = []
        for e in ENGS:
            cm = nc.semaphore("s_" + e)
            self.esem[e] = cm.__enter__()
            self._ctx.append(cm)
        cm = nc.semaphore("s_cc")
        self.ccsem = cm.__enter__(); self._ctx.append(cm)
        self.cccnt = 0
        self.sem_free = []
        self.sem_base = {}
        for i in range(NSEM_POOL):
            cm = nc.semaphore("d_%d" % i)
            sm = cm.__enter__(); self._ctx.append(cm)
            self.sem_free.append(sm)
            self.sem_base[id(sm)] = 0
        self.live = []
        self.scopes = []
        self.out_events = []
        self.nalloc = 0

    def _push(self, cm):
        t = cm.__enter__()
        (self.scopes[-1]["ctx"] if self.scopes else self._ctx).append(cm)
        return t

    def sbuf(self, name, shape, dt):
        self.nalloc += 1
        return self._push(self.nc.sbuf_tensor("sb%d_%s" % (self.nalloc, name), list(shape), dt))

    def psum(self, name, shape, dt=F32):
        self.nalloc += 1
        return self._push(self.nc.psum_tensor("pp%d_%s" % (self.nalloc, name), list(shape), dt))

    def dram_in(self, name, shape, dt):
        return self.nc.dram_tensor(name, list(shape), dt, kind="ExternalInput").ap()

    def dram_out(self, name, shape, dt):
        return self.nc.dram_tensor(name, list(shape), dt, kind="ExternalOutput").ap()

    def dram_tmp(self, name, shape, dt):
        return self.nc.dram_tensor(name, list(shape), dt, kind="Internal").ap()

    def pid4(self, eng):
        if not hasattr(self, "_pid4"):
            self._pid4 = {}
        if eng not in self._pid4:
            self._pid4[eng] = self.eng[eng].partition_id() % 4
        return self._pid4[eng]

    def open_scope(self):
        self.scopes.append({"ctx": [], "toks": []})

    def close_scope(self):
        self.barrier()
        sc = self.scopes.pop()
        for cm in reversed(sc["ctx"]):
            cm.__exit__(None, None, None)
        for t in sc["toks"]:
            self.sem_base[id(t.sem)] = t.semcnt
            self.sem_free.append(t.sem)
            self.live.remove(t)
            t.sem = None

    def _tsem(self, tok):
        if tok.sem is None:
            tok.sem = self.sem_free.pop(0)
            tok.semcnt = self.sem_base[id(tok.sem)]
            self.live.append(tok)
            if self.scopes:
                self.scopes[-1]["toks"].append(tok)
        return tok.sem

    def _need(self, eng, deps):
        waits = []
        best = {}
        for d in deps:
            if d is None:
                continue
            key, val = d
            if key == eng and not SAME_ENGINE_SYNC[eng]:
                continue
            kk = id(key) if not isinstance(key, str) else key
            if best.get(kk, (None, 0))[1] < val:
                best[kk] = (key, val)
        for kk, (key, val) in best.items():
            if self.seen[eng].get(kk, 0) >= val:
                continue
            self.seen[eng][kk] = val
            sem = key if not isinstance(key, str) else self.esem[key]
            waits.append((sem, val))
        return waits

    @staticmethod
    def _deps(eng, reads, writes):
        deps = []
        for r in reads:
            deps.append(r.lastw)
            if r.multi:
                deps.extend(r.wlist)
            if r.psum:
                deps.extend(x for x in r.readers if x[0] != eng)
        for w in writes:
            if w.multi:
                continue
            deps.append(w.lastw)
            deps.extend(w.readers)
        return deps

    @staticmethod
    def _commit(ev, reads, writes):
        for r in reads:
            r.readers.append(ev)
        for w in writes:
            if w.multi:
                w.wlist.append(ev)
            else:
                w.lastw = ev
                w.readers = []

    def op(self, eng, fn, reads=(), writes=()):
        waits = self._need(eng, self._deps(eng, reads, writes))
        e = self.eng[eng]
        for (s, v) in waits:
            e.wait_ge(s, v)
        self.cnt[eng] += 1
        ev = (eng, self.cnt[eng])
        fn(e).then_inc(self.esem[eng], 1)
        self._commit(ev, reads, writes)
        return ev

    def dma(self, eng, out, in_, reads=(), writes=(), semtok=None, is_output=False, **kw):
        if semtok is None:
            semtok = writes[0] if writes else reads[0]
        sem = self._tsem(semtok)
        waits = self._need(eng, self._deps(eng, reads, writes))
        e = self.eng[eng]
        for (s, v) in waits:
            e.wait_ge(s, v)
        semtok.semcnt += 16
        ev = (sem, semtok.semcnt)
        e.dma_start(out=out, in_=in_, **kw).then_inc(sem, 16)
        self._commit(ev, reads, writes)
        return ev

    def barrier(self):
        for e in ENGS:
            eo = self.eng[e]
            for e2 in ENGS:
                if e2 != e and self.seen[e].get(e2, 0) < self.cnt[e2]:
                    eo.wait_ge(self.esem[e2], self.cnt[e2])
                    self.seen[e][e2] = self.cnt[e2]
            for t in self.live:
                if self.seen[e].get(id(t.sem), 0) < t.semcnt:
                    eo.wait_ge(t.sem, t.semcnt)
                    self.seen[e][id(t.sem)] = t.semcnt
            if self.cccnt and self.seen[e].get(id(self.ccsem), 0) < self.cccnt:
                eo.wait_ge(self.ccsem, self.cccnt)
                self.seen[e][id(self.ccsem)] = self.cccnt

    def allgather(self, in_ap, out_ap, groups):
        self.barrier()
        self.cccnt += 1
        self.eng["pool"].collective_compute("AllGather", ALU.bypass, replica_groups=groups, ins=[in_ap], outs=[out_ap]).then_inc(self.ccsem, 1)
        self.barrier()

    def allgather_async(self, in_ap, out_ap, groups, reads=(), out_tok=None):
        waits = self._need("pool", self._deps("pool", reads, []))
        e = self.eng["pool"]
        for (s_, v) in waits:
            e.wait_ge(s_, v)
        if self.cccnt and self.seen["pool"].get(id(self.ccsem), 0) < self.cccnt:
            e.wait_ge(self.ccsem, self.cccnt)
            self.seen["pool"][id(self.ccsem)] = self.cccnt
        self.cccnt += 1
        e.collective_compute("AllGather", ALU.bypass, replica_groups=groups, ins=[in_ap], outs=[out_ap]).then_inc(self.ccsem, 1)
        if out_tok is not None:
            out_tok.lastw = (self.ccsem, self.cccnt)
            out_tok.readers = []

    def finish(self):
        self.barrier()
        for cm in reversed(self._ctx):
            cm.__exit__(None, None, None)
        return self.nc


class BankPool:
    def __init__(self, bufs):
        self.items = [(b, Tok(psum=True)) for b in bufs]
        self.free = list(range(len(bufs)))

    def try_get(self):
        if not self.free:
            return None
        i = self.free.pop(0)
        return i

    def put(self, i):
        self.free.append(i)


class Rot:
    def __init__(self, bufs, psum=False):
        self.bufs = bufs
        self.toks = [Tok(psum=psum) for _ in bufs]
        self.i = 0

    def next(self):
        j = self.i % len(self.bufs)
        self.i += 1
        return self.bufs[j], self.toks[j]


MC = 1536
GROUPS = [[0, 1, 2, 3], [4, 5, 6, 7]]


def emit_M(P, cT_d, wm_d, bm_d, out_d):
    P.open_scope()
    cs = P.sbuf("cs", [128, 16, 2], F32); cs_t = Tok()
    sb = P.sbuf("sb", [128, 16, 2], BF16); sb_t = Tok()
    bs = P.sbuf("bs", [2, 2, MC], F32); bs_t = Tok()
    os_ = P.sbuf("os", [2, 2, MC], F32); os_t = Tok()
    ps = [P.psum("ps%d" % i, [2, 512], F32) for i in range(6)]
    ps_t = [Tok(psum=True) for _ in range(6)]
    wb = [P.sbuf("wb%d" % i, [128, 16, 512], BF16) for i in range(3)]
    wb_t = [Tok() for _ in range(3)]
    P.dma("sp", cs[:], cT_d.rearrange("(kc p) j -> p kc j", p=128), writes=[cs_t])
    for l in range(2):
        P.dma("sp", bs[:, l, :], bm_d[l:l + 1, :].broadcast_to([2, MC]), writes=[bs_t])
    P.op("act", lambda e: e.activation(sb[:], cs[:], AF.Silu), reads=[cs_t], writes=[sb_t])
    for l in range(2):
        for j in range(3):
            pi = l * 3 + j
            w, w_t = wb[pi % 3], wb_t[pi % 3]
            P.dma("pool", w[:], wm_d[l][:, j * 512:(j + 1) * 512].rearrange("(kc p) n -> p kc n", p=128), writes=[w_t])
            for kc in range(16):
                P.op("pe", lambda e, pi=pi, w=w, kc=kc: e.matmul(ps[pi][:, :], sb[:, kc, :], w[:, kc, :], start=(kc == 0), stop=(kc == 15)),
                     reads=[sb_t, w_t], writes=[ps_t[pi]])
            P.op("dve", lambda e, pi=pi, l=l, j=j: e.tensor_tensor(os_[:, l, j * 512:(j + 1) * 512], ps[pi][:, :], bs[:, l, j * 512:(j + 1) * 512], ALU.add),
                 reads=[ps_t[pi], bs_t], writes=[os_t])
    P.dma("sp", out_d.rearrange("l j n -> j l n"), os_[:], reads=[os_t])
    P.close_scope()


def mod_pieces(c0, n=D):
    out = []
    c = c0
    while c < c0 + n:
        r, off = c // MC, c % MC
        ln = min(MC - off, c0 + n - c)
        out.append((r, off, ln, c - c0))
        c += ln
    return out


NT = 9
NTT = 10
OA, OD, OQ, OK_, OV, OGB, ORKV, OLF, OLB, OGC = 0, 512, 1024, 1536, 2048, 2560, 3072, 4608, 4704, 4800
NOUT_A = 5312
CBS = [
    (0, 512, "A_u", None), (512, 512, "A_v", None), (1024, 512, "A_g", OA),
    (1536, 512, "rope", OQ), (2048, 512, "rope", OK_), (2560, 512, "copy", OV), (3072, 512, "silu", OGB),
    (3584, 512, "copy", ORKV), (4096, 512, "copy", ORKV + 512), (4608, 512, "copy", ORKV + 1024),
    (5120, 192, "copy", OLF), (5312, 512, "silu", OGC),
    (5824, 512, "D_b", None), (6336, 512, "D_c", None), (6848, 512, "D_x", None), (7360, 512, "D_g", None),
]


DEBUG = {"ncb": 16, "conv": True, "halo": True}


def emit_A(P, l, x_d, xh_d, hmask_d, mod_all, gpre_d, win_d, sguw_d, sgub_d, convw_d, cos_d, sin_d, ident_d, dst_fn, hook=None):
    P.open_scope()
    ident = P.sbuf("ident", [128, 128], F32); ident_t = Tok()
    P.dma("sp", ident[:], ident_d[:, :], writes=[ident_t])
    gpre = P.sbuf("gpre", [128, 16], F32); gpre_t = Tok()
    P.dma("sp", gpre[:], gpre_d, writes=[gpre_t])
    MV = P.sbuf("MV", [64, 128], F32); MV_t = Tok()
    for v, (row, c0) in enumerate(((0, 2048), (0, 0), (1, 2048), (1, 0))):
        for (r, off, ln, dst) in mod_pieces(c0):
            P.dma("sp", MV[v * 16 + dst // 128:v * 16 + (dst + ln) // 128, :],
                  mod_all[r, l, row, off:off + ln].rearrange("(k c) -> k c", c=128), writes=[MV_t], semtok=MV_t)
    pmv = P.psum("pmv", [128, 512], F32); pmv_t = Tok(psum=True)
    P.op("pe", lambda e: e.transpose(pmv[:, 0:64], MV[:, :], ident[0:64, 0:64]), reads=[MV_t, ident_t], writes=[pmv_t])
    modT = P.sbuf("modT", [128, 4, 16], F32); modT_t = Tok()
    P.op("dve", lambda e: e.tensor_copy(modT[:].rearrange("p a b -> p (a b)"), pmv[:, 0:64]), reads=[pmv_t], writes=[modT_t])
    sc = P.sbuf("sc", [128, 2, 16], F32); sc_t = Tok()
    sh = P.sbuf("sh", [128, 2, 16], F32); sh_t = Tok()
    for j in range(2):
        P.op("dve", lambda e, j=j: e.scalar_tensor_tensor(sc[:, j, :], modT[:, 2 * j, :], 1.0, gpre[:], ALU.add, ALU.mult),
             reads=[modT_t, gpre_t], writes=[sc_t])
        P.op("dve", lambda e, j=j: e.tensor_copy(sh[:, j, :], modT[:, 2 * j + 1, :]), reads=[modT_t], writes=[sh_t])
    sguw32 = P.sbuf("sguw32", [128, 4, 128], F32); sguw32_t = Tok()
    P.dma("sp", sguw32[:], sguw_d, writes=[sguw32_t])
    sguw = P.sbuf("sguw", [128, 4, 128], BF16); sguw_t = Tok()
    P.op("dve", lambda e: e.tensor_copy(sguw[:], sguw32[:]), reads=[sguw32_t], writes=[sguw_t])
    sgub = P.sbuf("sgub", [128, 4], F32); sgub_t = Tok()
    P.dma("sp", sgub[:], sgub_d, writes=[sgub_t])
    convw = P.sbuf("convw", [128, 3, WG], F32); convw_t = Tok()
    P.dma("sp", convw[:].rearrange("p a b -> p (a b)"), convw_d.broadcast_to([128, 3 * WG]), writes=[convw_t])
    rcos = P.sbuf("rcos", [128, NT, 64], F32); rcos_t = Tok()
    rsin = P.sbuf("rsin", [128, NT, 64], F32); rsin_t = Tok()
    P.dma("sp", rcos[:], cos_d[:, :, :], writes=[rcos_t])
    P.dma("sp", rsin[:], sin_d[:, :, :], writes=[rsin_t])
    hmask = P.sbuf("hmask", [4, 1], F32); hmask_t = Tok()
    P.dma("sp", hmask[:], hmask_d[:, :], writes=[hmask_t])

    hT = P.sbuf("hT", [128, 16, NTT * 128], BF16)
    hT_t = [Tok() for _ in range(NTT)]
    xs = Rot([P.sbuf("xs%d" % i, [128, D], F32) for i in range(2)])
    junk = P.sbuf("junk", [128, D], BF16); junk_t = Tok()
    small = P.sbuf("small", [128, NTT, 4], F32)
    small_t = [Tok() for _ in range(NTT)]
    wbuf = [P.sbuf("wbuf%d" % i, [128, 16, 512], BF16) for i in range(2)]
    wbuf_t = [[Tok(), Tok()] for _ in range(2)]
    UB = P.sbuf("UB", [128, NT, WG], F32); UB_t = [Tok() for _ in range(NT)]
    VN = P.sbuf("VN", [128, NT, WG], BF16); VN_t = [Tok() for _ in range(NT)]
    CG = P.sbuf("CG", [128, NT, WG], F32); CG_t = [Tok() for _ in range(NT)]
    ZH = P.sbuf("ZH", [4, WG], F32); ZH_t = Tok()
    st = Rot([P.sbuf("st%d" % i, [128, 512], F32) for i in range(4)])
    stb = Rot([P.sbuf("stb%d" % i, [128, 512], BF16) for i in range(4)])
    tA = Rot([P.sbuf("tA%d" % i, [128, 512], F32) for i in range(2)])
    tB = Rot([P.sbuf("tB%d" % i, [128, 512], F32) for i in range(2)])
    lnst = Rot([P.sbuf("lnst%d" % i, [128, 16], F32) for i in range(2)])
    pt = Rot([P.psum("pt%d" % i, [128, 4, 128], F32) for i in range(2)], psum=True)
    pm = Rot([P.psum("pm%d" % i, [128, 512], F32) for i in range(3)], psum=True)
    pmix = Rot([P.psum("pmix%d" % i, [128, 512], F32) for i in range(2)], psum=True)

    P.op("pool", lambda e: e.memset(hT[:, :, NT * 128:NTT * 128], 0.0), writes=[hT_t[NT]])

    for t in range(NTT):
        rows = 128 if t < NT else 4
        xb, xb_t = xs.next()
        src = x_d[t] if t < NT else xh_d[:, :]
        P.dma("sp", xb[:rows, :], src, writes=[xb_t])
        sm = small[:rows, t, :]
        P.op("act", lambda e, xb=xb, rows=rows, sm=sm: e.activation(junk[:rows, :], xb[:rows, :], AF.Square, accum_out=sm[:, 0:1]),
             reads=[xb_t], writes=[junk_t, small_t[t]])
        P.op("act", lambda e, sm=sm: e.activation(sm[:, 1:2], sm[:, 0:1], AF.Ln, bias=EPS, scale=1.0 / D),
             reads=[small_t[t]], writes=[small_t[t]])
        P.op("act", lambda e, sm=sm: e.activation(sm[:, 2:3], sm[:, 1:2], AF.Exp, scale=-0.5),
             reads=[small_t[t]], writes=[small_t[t]])
        P.op("dve", lambda e, xb=xb, rows=rows, sm=sm: e.tensor_scalar(xb[:rows, :], xb[:rows, :], sm[:, 2:3], None, ALU.mult),
             reads=[xb_t, small_t[t]], writes=[xb_t])
        for g in range(4):
            pb, pb_t = pt.next()
            for j in range(4):
                kc = g * 4 + j
                P.op("pe", lambda e, pb=pb, j=j, kc=kc, xb=xb, rows=rows: e.transpose(
                    pb[:, j, :rows], xb[:rows, kc * 128:(kc + 1) * 128], ident[:rows, :rows]),
                    reads=[xb_t, ident_t], writes=[pb_t])
            for j in range(4):
                kc = g * 4 + j
                if t < NT:
                    m = 0 if t < NT - 1 else 1
                    parts = [(0, 128, m)]
                else:
                    parts = [(0, 2, 0), (2, 4, 1)]
                for (a, b, m) in parts:
                    P.op("act", lambda e, pb=pb, j=j, kc=kc, t=t, a=a, b=b, m=m: e.activation(
                        hT[:, kc, t * 128 + a:t * 128 + b], pb[:, j, a:b], AF.Identity,
                        scale=sc[:, m, kc:kc + 1], bias=sh[:, m, kc:kc + 1]),
                        reads=[pb_t, sc_t, sh_t], writes=[hT_t[t]])

    oq = ["sp"]

    def store(t, col, n, buf, buf_t):
        dap, dtok = dst_fn(t, col, n)
        P.dma("sp", dap, buf[:, :n], reads=[buf_t], writes=[dtok] if dtok is not None else [], semtok=buf_t)

    order = [3, 4, 5, 6, 7, 8, 9, 10, 11, 0, 1, 2, 12, 13, 14, 15]
    for ci, (c0, ncol, kind, ocol) in enumerate([CBS[i] for i in order]):
        wb = wbuf[ci % 2]; wb_t = wbuf_t[ci % 2]
        for hf in range(2):
            P.dma("pool", wb[:, hf * 8:(hf + 1) * 8, :ncol],
                  win_d[hf * 1024:(hf + 1) * 1024, c0:c0 + ncol].rearrange("(kc p) n -> p kc n", p=128),
                  writes=[wb_t[hf]])
        tiles = list(range(NT)) + ([NT] if kind in ("D_c", "D_x") else [])
        for t in tiles:
            rows = 128 if t < NT else 4
            ps, ps_t = pm.next()
            for kc in range(16):
                P.op("pe", lambda e, ps=ps, rows=rows, ncol=ncol, kc=kc, t=t, wb=wb: e.matmul(
                    ps[:rows, :ncol], hT[:, kc, t * 128:t * 128 + rows], wb[:, kc, :ncol], start=(kc == 0), stop=(kc == 15)),
                    reads=[hT_t[t], wb_t[kc // 8]], writes=[ps_t])
            if kind == "A_u":
                P.op("act", lambda e, ps=ps, t=t: e.copy(UB[:, t, :], ps[:, :]), reads=[ps_t], writes=[UB_t[t]])
            elif kind == "A_v":
                ls, ls_t = lnst.next()
                sq, sq_t = tA.next()
                P.op("act", lambda e, ps=ps, sq=sq: e.activation(sq[:], ps[:, :], AF.Square), reads=[ps_t], writes=[sq_t])
                P.op("dve", lambda e, ps=ps, ls=ls: e.tensor_reduce(ls[:, 0:4], ps[:, :].rearrange("p (h d) -> p h d", h=4), AX.X, ALU.add),
                     reads=[ps_t], writes=[ls_t])
                P.op("dve", lambda e, sq=sq, ls=ls: e.tensor_reduce(ls[:, 4:8], sq[:].rearrange("p (h d) -> p h d", h=4), AX.X, ALU.add),
                     reads=[sq_t, ls_t], writes=[ls_t])
                P.op("dve", lambda e, ls=ls: e.tensor_scalar(ls[:, 0:4], ls[:, 0:4], 1.0 / 128, None, ALU.mult), reads=[ls_t], writes=[ls_t])
                P.op("dve", lambda e, ls=ls: e.tensor_tensor(ls[:, 8:12], ls[:, 0:4], ls[:, 0:4], ALU.mult), reads=[ls_t], writes=[ls_t])
                P.op("dve", lambda e, ls=ls: e.scalar_tensor_tensor(ls[:, 8:12], ls[:, 4:8], 1.0 / 128, ls[:, 8:12], ALU.mult, ALU.subtract),
                     reads=[ls_t], writes=[ls_t])
                P.op("act", lambda e, ls=ls: e.activation(ls[:, 8:12], ls[:, 8:12], AF.Ln, bias=EPS), reads=[ls_t], writes=[ls_t])
                P.op("act", lambda e, ls=ls: e.activation(ls[:, 12:16], ls[:, 8:12], AF.Exp, scale=-0.5), reads=[ls_t], writes=[ls_t])
                for h in range(4):
                    P.op("dve", lambda e, ps=ps, ls=ls, h=h, t=t: e.tensor_scalar(
                        VN[:, t, h * 128:(h + 1) * 128], ps[:, h * 128:(h + 1) * 128], ls[:, h:h + 1], ls[:, 12 + h:13 + h],
                        ALU.subtract, ALU.mult), reads=[ps_t, ls_t], writes=[VN_t[t]])
            elif kind == "A_g":
                sg, sg_t = tA.next()
                P.op("act", lambda e, ps=ps, sg=sg: e.activation(sg[:], ps[:, :], AF.Silu), reads=[ps_t], writes=[sg_t])
                px, px_t = pmix.next()
                for h in range(4):
                    P.op("pe", lambda e, px=px, h=h, t=t: e.matmul(
                        px[:, h * 128:(h + 1) * 128], sguw[:, h, :], VN[:, t, h * 128:(h + 1) * 128], start=True, stop=True),
                        reads=[sguw_t, VN_t[t]], writes=[px_t])
                tb, tb_t = tB.next()
                for h in range(4):
                    P.op("dve", lambda e, px=px, h=h, t=t, tb=tb: e.scalar_tensor_tensor(
                        tb[:, h * 128:(h + 1) * 128], px[:, h * 128:(h + 1) * 128], sgub[:, h:h + 1], UB[:, t, h * 128:(h + 1) * 128],
                        ALU.add, ALU.mult), reads=[px_t, sgub_t, UB_t[t]], writes=[tb_t])
                sb_, sb_t = st.next()
                P.op("pool", lambda e, tb=tb, sg=sg, sb_=sb_: e.tensor_tensor(sb_[:], tb[:], sg[:], ALU.mult),
                     reads=[tb_t, sg_t], writes=[sb_t])
                store(t, ocol, 512, sb_, sb_t)
            elif kind == "copy":
                sb_, sb_t = (stb if ocol == OV else st).next()
                if t % 2 == 0:
                    P.op("act", lambda e, ps=ps, sb_=sb_, ncol=ncol: e.copy(sb_[:, :ncol], ps[:, :ncol]), reads=[ps_t], writes=[sb_t])
                else:
                    P.op("dve", lambda e, ps=ps, sb_=sb_, ncol=ncol: e.tensor_copy(sb_[:, :ncol], ps[:, :ncol]), reads=[ps_t], writes=[sb_t])
                store(t, ocol, ncol, sb_, sb_t)
            elif kind == "silu":
                sb_, sb_t = st.next()
                P.op("act", lambda e, ps=ps, sb_=sb_: e.activation(sb_[:], ps[:, :], AF.Silu), reads=[ps_t], writes=[sb_t])
                store(t, ocol, 512, sb_, sb_t)
            elif kind == "rope":
                t1, t1_t = tA.next()
                t2, t2_t = tB.next()
                cosb = rcos[:, t, :].unsqueeze(1).to_broadcast([128, 8, 64])
                P.op("dve", lambda e, ps=ps, t1=t1, cosb=cosb: e.tensor_tensor(
                    t1[:].rearrange("p (g d) -> p g d", g=8), ps[:, :].rearrange("p (g d) -> p g d", g=8), cosb, ALU.mult),
                    reads=[ps_t, rcos_t], writes=[t1_t])
                psv = ps[:, :].rearrange("p (g a b d) -> p g a b d", g=8, a=2, b=2)
                t2v = t2[:].rearrange("p (g a b d) -> p g a b d", g=8, a=2, b=2)
                snv = rsin[:, t, :].rearrange("p (a b d) -> p a b d", a=2, b=2)
                for a in range(2):
                    for b in range(2):
                        sn = snv[:, a, b, :].unsqueeze(1).to_broadcast([128, 8, 16])
                        P.op("dve", lambda e, a=a, b=b, psv=psv, t2v=t2v, sn=sn: e.tensor_tensor(
                            t2v[:, :, a, b, :], psv[:, :, a, 1 - b, :], sn, ALU.mult),
                            reads=[ps_t, rsin_t], writes=[t2_t])
                sb_, sb_t = stb.next()
                P.op("pool", lambda e, t1=t1, t2=t2, sb_=sb_: e.tensor_tensor(sb_[:], t1[:], t2[:], ALU.add),
                     reads=[t1_t, t2_t], writes=[sb_t])
                store(t, ocol, 512, sb_, sb_t)
            elif kind == "D_b":
                P.op("act", lambda e, ps=ps, t=t: e.copy(UB[:, t, :], ps[:, :]), reads=[ps_t], writes=[UB_t[t]])
            elif kind == "D_c":
                if t < NT:
                    P.op("act", lambda e, ps=ps, t=t: e.copy(CG[:, t, :], ps[:, :]), reads=[ps_t], writes=[CG_t[t]])
                else:
                    P.op("act", lambda e, ps=ps: e.copy(ZH[:, :], ps[:4, :]), reads=[ps_t], writes=[ZH_t])
            elif kind == "D_x":
                if t < NT:
                    P.op("dve", lambda e, ps=ps, t=t: e.tensor_tensor(CG[:, t, :], CG[:, t, :], ps[:, :], ALU.mult),
                         reads=[ps_t, CG_t[t]], writes=[CG_t[t]])
                else:
                    P.op("dve", lambda e, ps=ps: e.scalar_tensor_tensor(ZH[:, :], ps[:4, :], hmask[:, 0:1], ZH[:, :], ALU.mult, ALU.mult),
                         reads=[ps_t, ZH_t, hmask_t], writes=[ZH_t])
            elif kind == "D_g":
                sg, sg_t = tA.next()
                P.op("act", lambda e, ps=ps, sg=sg: e.activation(sg[:], ps[:, :], AF.Silu), reads=[ps_t], writes=[sg_t])
                P.op("dve", lambda e, sg=sg, t=t: e.tensor_tensor(UB[:, t, :], UB[:, t, :], sg[:], ALU.mult),
                     reads=[sg_t, UB_t[t]], writes=[UB_t[t]])

            if hook is not None:
                hook(ci, t)

    hTf = hT[:].rearrange("p a b -> p (a b)").bitcast(F32)
    ZMv = hTf[:, 0:NT * WG].rearrange("p (t c) -> p t c", t=NT)
    ZPv = hTf[:, NT * WG:2 * NT * WG].rearrange("p (t c) -> p t c", t=NT)
    zm_t = Tok(); zp_t = Tok()
    P.dma("sp", ZMv[1:128, :, :], CG[0:127, :, :], reads=CG_t, writes=[zm_t] + hT_t)
    P.dma("sp", ZMv[0:1, 1:NT - 1, :], CG[127:128, 0:NT - 2, :], reads=CG_t, writes=[zm_t], semtok=zm_t)
    P.dma("sp", ZMv[0:1, 0, :], ZH[0:1, :], reads=[ZH_t], writes=[zm_t], semtok=zm_t)
    P.dma("sp", ZMv[0:1, NT - 1, :], ZH[2:3, :], reads=[ZH_t], writes=[zm_t], semtok=zm_t)
    P.dma("sp", ZPv[0:127, :, :], CG[1:128, :, :], reads=CG_t, writes=[zp_t] + hT_t)
    P.dma("sp", ZPv[127:128, 0:NT - 2, :], CG[0:1, 1:NT - 1, :], reads=CG_t, writes=[zp_t], semtok=zp_t)
    P.dma("sp", ZPv[127:128, NT - 2, :], ZH[1:2, :], reads=[ZH_t], writes=[zp_t], semtok=zp_t)
    P.dma("sp", ZPv[127:128, NT - 1, :], ZH[3:4, :], reads=[ZH_t], writes=[zp_t], semtok=zp_t)
    for t in range(NT):
        t1, t1_t = tA.next()
        t2, t2_t = tB.next()
        P.op("dve", lambda e, t=t, t1=t1: e.tensor_tensor(t1[:], CG[:, t, :], convw[:, 1, :], ALU.mult),
             reads=[CG_t[t], convw_t], writes=[t1_t])
        P.op("pool", lambda e, t=t, t2=t2: e.tensor_tensor(t2[:], ZMv[:, t, :], convw[:, 0, :], ALU.mult),
             reads=[zm_t, convw_t], writes=[t2_t])
        P.op("dve", lambda e, t1=t1, t2=t2: e.tensor_tensor(t1[:], t1[:], t2[:], ALU.add), reads=[t1_t, t2_t], writes=[t1_t])
        P.op("pool", lambda e, t=t, t2=t2: e.tensor_tensor(t2[:], ZPv[:, t, :], convw[:, 2, :], ALU.mult),
             reads=[zp_t, convw_t], writes=[t2_t])
        P.op("dve", lambda e, t1=t1, t2=t2: e.tensor_tensor(t1[:], t1[:], t2[:], ALU.add), reads=[t1_t, t2_t], writes=[t1_t])
        sb_, sb_t = st.next()
        P.op("dve", lambda e, t=t, t1=t1, sb_=sb_: e.tensor_tensor(sb_[:], t1[:], UB[:, t, :], ALU.mult),
             reads=[t1_t, UB_t[t]], writes=[sb_t])
        store(t, OD, 512, sb_, sb_t)
    if hook is not None:
        hook(None, None)
    P.close_scope()


def emit_C(P, l, ntile, pa_l, obc_loc, x_d, mod_all, gpost_d, wo_d, ident_d, out_d, edge_d=None, stage=None):
    P.open_scope()
    ident = P.sbuf("ident", [128, 128], F32); ident_t = Tok()
    P.dma("sp", ident[:], ident_d[:, :], writes=[ident_t])
    wo = P.sbuf("wo", [128, 16, D], BF16)
    wo_t = [Tok() for _ in range(4)]
    for j in range(4):
        P.dma("pool", wo[:, j * 4:(j + 1) * 4, :], wo_d[j * 512:(j + 1) * 512, :].rearrange("(kc p) n -> p kc n", p=128),
              writes=[wo_t[j]])
    stoks = stage() if stage is not None else [[Tok(), Tok()], [Tok(), Tok()]]
    gp = P.sbuf("gp", [128, D], F32); gp_t = Tok()
    P.dma("sp", gp[:], gpost_d.broadcast_to([128, D]), writes=[gp_t])
    GG = P.sbuf("GG", [128, 2, D], F32); GG_t = Tok()
    for j in range(2):
        for (r, off, ln, dst) in mod_pieces(4096):
            P.dma("sp", GG[:, j, dst:dst + ln], mod_all[r, l, j, off:off + ln].unsqueeze(0).broadcast_to([128, ln]), writes=[GG_t], semtok=GG_t)
    for j in range(2):
        P.op("pool", lambda e, j=j: e.tensor_tensor(GG[:, j, :], GG[:, j, :], gp[:], ALU.mult), reads=[GG_t, gp_t], writes=[GG_t])
    os_ = Rot([P.sbuf("os%d" % i, [128, D], F32) for i in range(2)])
    xs = Rot([P.sbuf("xs%d" % i, [128, D], F32) for i in range(2)])
    oT = Rot([P.sbuf("oT%d" % i, [128, 16, 128], BF16) for i in range(2)])
    ol = Rot([P.sbuf("ol%d" % i, [128, D], F32) for i in range(2)])
    junk = P.sbuf("junk", [128, D], BF16); junk_t = Tok()
    sm = P.sbuf("sm", [128, ntile, 4], F32); sm_t = [Tok() for _ in range(ntile)]
    pt = Rot([P.psum("pt%d" % i, [128, 4, 128], F32) for i in range(3)], psum=True)
    pm = Rot([P.psum("pm%d" % i, [128, 512], F32) for i in range(4)], psum=True)
    for t in range(ntile):
        ob, ob_t = os_.next()
        P.dma("sp", ob[:, 0:512], pa_l[t, :, 0:512], writes=[ob_t])
        P.dma("sp", ob[:, 1536:2048], pa_l[t, :, 512:1024], writes=[ob_t], semtok=ob_t)
        for j in range(2):
            P.dma("sp", ob[:, 512 + j * 512:1024 + j * 512].rearrange("p (h c) -> p h c", h=4),
                  obc_loc[:, j, t * 128:(t + 1) * 128, :].rearrange("h p c -> p h c"), reads=[stoks[j][0 if t < 8 else 1]],
                  writes=[ob_t], semtok=ob_t)
        xb, xb_t = xs.next()
        P.dma("sp", xb[:], x_d[t], writes=[xb_t])
        ot, ot_t = oT.next()
        for g in range(4):
            pb, pb_t = pt.next()
            for j in range(4):
                kc = g * 4 + j
                P.op("pe", lambda e, pb=pb, j=j, kc=kc, ob=ob: e.transpose(pb[:, j, :], ob[:, kc * 128:(kc + 1) * 128], ident[:]),
                     reads=[ob_t, ident_t], writes=[pb_t])
            if g % 2 == 0:
                P.op("act", lambda e, pb=pb, g=g, ot=ot: e.copy(ot[:, g * 4:(g + 1) * 4, :], pb[:]), reads=[pb_t], writes=[ot_t])
            else:
                P.op("dve", lambda e, pb=pb, g=g, ot=ot: e.tensor_copy(ot[:, g * 4:(g + 1) * 4, :], pb[:]), reads=[pb_t], writes=[ot_t])
        olb, olb_t = ol.next()
        for cb in range(4):
            ps, ps_t = pm.next()
            for kc in range(16):
                P.op("pe", lambda e, ps=ps, kc=kc, cb=cb, ot=ot: e.matmul(
                    ps[:, :], ot[:, kc, :], wo[:, kc, cb * 512:(cb + 1) * 512], start=(kc == 0), stop=(kc == 15)),
                    reads=[ot_t, wo_t[kc // 4]], writes=[ps_t])
            if cb % 2 == 0:
                P.op("dve", lambda e, ps=ps, cb=cb, olb=olb: e.tensor_copy(olb[:, cb * 512:(cb + 1) * 512], ps[:, :]), reads=[ps_t], writes=[olb_t])
            else:
                P.op("act", lambda e, ps=ps, cb=cb, olb=olb: e.copy(olb[:, cb * 512:(cb + 1) * 512], ps[:, :]), reads=[ps_t], writes=[olb_t])
        s_ = sm[:, t, :]
        P.op("act", lambda e, olb=olb, s_=s_: e.activation(junk[:], olb[:], AF.Square, accum_out=s_[:, 0:1]),
             reads=[olb_t], writes=[junk_t, sm_t[t]])
        P.op("act", lambda e, s_=s_: e.activation(s_[:, 1:2], s_[:, 0:1], AF.Ln, bias=EPS, scale=1.0 / D), reads=[sm_t[t]], writes=[sm_t[t]])
        P.op("act", lambda e, s_=s_: e.activation(s_[:, 2:3], s_[:, 1:2], AF.Exp, scale=-0.5), reads=[sm_t[t]], writes=[sm_t[t]])
        gi = 0 if t < 8 else 1
        P.op("dve", lambda e, olb=olb, s_=s_, gi=gi: e.scalar_tensor_tensor(olb[:], olb[:], s_[:, 2:3], GG[:, gi, :], ALU.mult, ALU.mult),
             reads=[olb_t, sm_t[t], GG_t], writes=[olb_t])
        P.op("pool", lambda e, olb=olb, xb=xb: e.tensor_tensor(xb[:], xb[:], olb[:], ALU.add), reads=[olb_t, xb_t], writes=[xb_t])
        P.dma("sp", out_d[t], xb[:], reads=[xb_t], semtok=xb_t)
        if edge_d is not None:
            for (tt, prow, er) in ((0, 0, 0), (7, 127, 1), (8, 0, 2), (8, 127, 3)):
                if tt == t:
                    P.dma("sp", edge_d[er:er + 1, :], xb[prow:prow + 1, :], reads=[xb_t], semtok=xb_t)
    P.close_scope()


NTOK = CTX + SEQ
NKB = NTOK // 128


def emit_attn(P, l, has_ctxq, at_loc, at_g, lam_d, subg_d, ident_d, out_d, pre=None):
    lam_init = 0.8 - 0.6 * math.exp(-0.3 * l)
    P.open_scope()
    ident32 = P.sbuf("ident32", [128, 128], F32); ident32_t = Tok()
    P.dma("sp", ident32[:], ident_d[:, :], writes=[ident32_t])
    ident = P.sbuf("ident", [128, 128], BF16); ident_t = Tok()
    P.op("dve", lambda e: e.tensor_copy(ident[:], ident32[:]), reads=[ident32_t], writes=[ident_t])
    qT = P.sbuf("qT", [128, NTOK], BF16); qT_t = Tok()
    KT = [P.sbuf("KT%d" % m, [128, NTOK], BF16) for m in range(2)]
    KT_t = [Tok(), Tok()]
    for m in range(2):
        P.op("dve", lambda e, m=m: e.memset(KT[m][(1 - m) * 64:(2 - m) * 64, :], 0.0), writes=[KT_t[m]])
    QK = P.sbuf("QKtm", [128, 2, NKB, 128], BF16); QK_t = [Tok(), Tok()]
    for j in range(2):
        P.dma("sp", QK[:, j, :, :], at_loc[:, j * 128:(j + 1) * 128].rearrange("(t p) d -> p t d", p=128), writes=[QK_t[j]])
    ptr = P.psum("ptr", [128, 4, 128], BF16); ptr_t = Tok(psum=True)
    for g in range(0, NKB, 2):
        for j in range(2):
            for i in range(2):
                P.op("pe", lambda e, j=j, i=i, g=g: e.transpose(ptr[:, j * 2 + i, :], QK[:, j, g + i, :], ident[:]),
                     reads=[QK_t[j], ident_t], writes=[ptr_t])
        P.op("act", lambda e, g=g: e.copy(qT[:, g * 128:(g + 2) * 128], ptr[:, 0:2, :].rearrange("p a b -> p (a b)")), reads=[ptr_t], writes=[qT_t])
        P.op("dve", lambda e, g=g: e.tensor_copy(KT[0][0:64, g * 128:(g + 2) * 128], ptr[0:64, 2:4, :].rearrange("p a b -> p (a b)")), reads=[ptr_t], writes=[KT_t[0]])
        P.op("act", lambda e, g=g: e.copy(KT[1][64:128, g * 128:(g + 2) * 128], ptr[64:128, 2:4, :].rearrange("p a b -> p (a b)")), reads=[ptr_t], writes=[KT_t[1]])
    V = P.sbuf("V", [128, NKB, 129], BF16); V_t = Tok()
    P.op("dve", lambda e: e.memset(V[:, :, 128:129], 1.0), writes=[V_t])
    P.dma("sp", V[:, :, 0:128], at_loc[:, 256:384].rearrange("(t p) d -> p t d", p=128), writes=[V_t])
    GS = P.sbuf("GS", [128, NKB, 128], F32); GS_t = Tok()
    P.dma("sp", GS[:], at_g.rearrange("(t p) d -> p t d", p=128), writes=[GS_t])
    subg = P.sbuf("subg", [128, 128], F32); subg_t = Tok()
    P.dma("sp", subg[:], subg_d.broadcast_to([128, 128]), writes=[subg_t])
    lamv = P.sbuf("lamv", [128, 256], F32); lamv_t = Tok()
    P.dma("sp", lamv[:], lam_d.broadcast_to([128, 256]), writes=[lamv_t])
    if pre is not None:
        pre()
    lsm = P.sbuf("lsm", [128, 8], F32); lsm_t = Tok()
    ljunk = P.sbuf("ljunk", [128, 64], F32); ljunk_t = Tok()
    for j in range(2):
        P.op("dve", lambda e, j=j: e.scalar_tensor_tensor(ljunk[:], lamv[:, j * 128:j * 128 + 64], 1.0, lamv[:, j * 128 + 64:j * 128 + 128],
                                                         ALU.mult, ALU.mult, accum_out=lsm[:, j:j + 1]),
             reads=[lamv_t], writes=[ljunk_t, lsm_t])
    P.op("act", lambda e: e.activation(lsm[:, 2:4], lsm[:, 0:2], AF.Exp), reads=[lsm_t], writes=[lsm_t])
    P.op("dve", lambda e: e.tensor_tensor(lsm[:, 4:5], lsm[:, 3:4], lsm[:, 2:3], ALU.subtract), reads=[lsm_t], writes=[lsm_t])
    P.op("dve", lambda e: e.tensor_scalar(lsm[:, 4:5], lsm[:, 4:5], -lam_init, None, ALU.add), reads=[lsm_t], writes=[lsm_t])
    P.op("dve", lambda e: e.scalar_tensor_tensor(GS[:], GS[:], 1.0 - lam_init, subg[:].unsqueeze(1).to_broadcast([128, NKB, 128]),
                                                ALU.mult, ALU.mult), reads=[GS_t, subg_t], writes=[GS_t])

    pss = Rot([P.psum("pss%d" % i, [128, 512], F32) for i in range(3)], psum=True)
    po = [P.psum("po%d" % i, [128, 512], F32) for i in range(4)]
    po_t = [Tok(psum=True) for _ in range(4)]
    pT = Rot([P.sbuf("pT%d" % i, [128, 512], BF16) for i in range(3)])
    o0 = P.sbuf("o0", [128, 4, 128], F32); o0_t = [Tok() for _ in range(4)]
    osb = Rot([P.sbuf("osb%d" % i, [128, 128], F32) for i in range(2)])
    ost = Rot([P.sbuf("ost%d" % i, [128, 128], F32) for i in range(3)])
    sjunk = P.sbuf("sjunk", [128, 128], F32); sjunk_t = Tok()
    rs = Rot([P.sbuf("rs%d" % i, [128, 8], F32) for i in range(4)])

    groups = []
    for i in range(8):
        for m in range(2):
            groups.append((CTX + i * 512, 512, list(range(NKB)), m))
    if has_ctxq:
        for m in range(2):
            groups.append((0, 256, [0, 1], m))
    steps = []
    for gi, (q0, nq, kbs, m) in enumerate(groups):
        for kb in kbs:
            steps.append((gi, kb))
    held = {}

    def emit_qk(si):
        gi, kb = steps[si]
        q0, nq, kbs, m = groups[gi]
        ps, ps_t = pss.next()
        P.op("pe", lambda e, ps=ps, nq=nq, m=m, kb=kb, q0=q0: e.matmul(
            ps[:, :nq], KT[m][:, kb * 128:(kb + 1) * 128], qT[:, q0:q0 + nq], start=True, stop=True),
            reads=[KT_t[m], qT_t], writes=[ps_t])
        held[si] = (ps, ps_t)

    LOOK = 2
    for si in range(min(LOOK, len(steps))):
        emit_qk(si)
    for si, (gi, kb) in enumerate(steps):
        q0, nq, kbs, m = groups[gi]
        if si + LOOK < len(steps):
            emit_qk(si + LOOK)
        ps, ps_t = held.pop(si)
        pt_, pt_t = pT.next()
        P.op("act", lambda e, ps=ps, pt_=pt_, nq=nq: e.activation(pt_[:, :nq], ps[:, :nq], AF.Exp, scale=0.125),
             reads=[ps_t], writes=[pt_t])
        nqs = nq // 128
        for qs in range(nqs):
            P.op("pe", lambda e, qs=qs, pt_=pt_, kb=kb, kbs=kbs: e.matmul(
                po[qs][:, 0:129], pt_[:, qs * 128:(qs + 1) * 128], V[:, kb, :], start=(kb == kbs[0]), stop=(kb == kbs[-1])),
                reads=[pt_t, V_t], writes=[po_t[qs]])
        if kb == kbs[-1]:
            for qs in range(nqs):
                r, r_t = rs.next()
                tile = (q0 // 128) + qs
                P.op("dve", lambda e, r=r, qs=qs: e.reciprocal(r[:, 0:1], po[qs][:, 128:129]), reads=[po_t[qs]], writes=[r_t])
                if m == 0:
                    P.op("dve", lambda e, r=r, qs=qs: e.tensor_scalar(o0[:, qs, :], po[qs][:, 0:128], r[:, 0:1], None, ALU.mult),
                         reads=[po_t[qs], r_t], writes=[o0_t[qs]])
                else:
                    ob, ob_t = osb.next()
                    P.op("dve", lambda e, r=r: e.tensor_tensor(r[:, 1:2], r[:, 0:1], lsm[:, 4:5], ALU.mult), reads=[r_t, lsm_t], writes=[r_t])
                    P.op("dve", lambda e, r=r, qs=qs, ob=ob: e.scalar_tensor_tensor(ob[:], po[qs][:, 0:128], r[:, 1:2], o0[:, qs, :], ALU.mult, ALU.add),
                         reads=[po_t[qs], r_t, o0_t[qs]], writes=[ob_t])
                    P.op("dve", lambda e, r=r, ob=ob: e.scalar_tensor_tensor(sjunk[:], ob[:], 1.0, ob[:], ALU.mult, ALU.mult, accum_out=r[:, 2:3]),
                         reads=[ob_t], writes=[sjunk_t, r_t])
                    P.op("act", lambda e, r=r: e.activation(r[:, 3:4], r[:, 2:3], AF.Ln, bias=EPS, scale=1.0 / 128), reads=[r_t], writes=[r_t])
                    P.op("act", lambda e, r=r: e.activation(r[:, 4:5], r[:, 3:4], AF.Exp, scale=-0.5), reads=[r_t], writes=[r_t])
                    st_, st_t = ost.next()
                    P.op("dve", lambda e, r=r, ob=ob, st_=st_, tile=tile: e.scalar_tensor_tensor(st_[:], ob[:], r[:, 4:5], GS[:, tile, :], ALU.mult, ALU.mult),
                         reads=[ob_t, r_t, GS_t], writes=[st_t])
                    P.dma("sp", out_d[tile, :, :], st_[:], reads=[st_t], semtok=st_t)
    P.close_scope()


RW_F = 480


def rw_row0(c):
    return 1 + c * 128 if c < 2 else 259 + (c - 2) * 128


RW_ROWS = NTOK + 4
RW_COLS = 704


def emit_rwkv(P, nch, order_f, order_b, rw_loc, mu_d, w2e_d, a2e_d, vecs_d, cst_d, out_d):
    ntok = nch * 128
    RB = BF16
    P.open_scope()
    cst = P.sbuf("cst", [128, 8, 128], F32); cst_t = Tok()
    P.dma("sp", cst[:], cst_d.rearrange("c p q -> p c q"), writes=[cst_t])
    ident = cst[:, 0, :]; ones = cst[:, 1, :]
    TRI = [cst[:, 2, :], cst[:, 3, :]]
    MS = [cst[:, 4, :], cst[:, 6, :]]
    MI = [cst[:, 5, :], cst[:, 7, :]]
    MST = [cst[:, 6, :], cst[:, 4, :]]
    identb = P.sbuf("identb", [128, 128], RB); identb_t = Tok()
    P.op("dve", lambda e: e.tensor_copy(identb[:], ident), reads=[cst_t], writes=[identb_t])
    mu = P.sbuf("mu", [128, 2, RW_F], F32); mu_t = Tok()
    P.dma("sp", mu[:].rearrange("p a b -> p (a b)"), mu_d.rearrange("a b -> (a b)").unsqueeze(0).broadcast_to([128, 2 * RW_F]), writes=[mu_t])
    vecs = P.sbuf("vecs", [128, 8, 128], F32); vecs_t = Tok()
    P.dma("sp", vecs[:].rearrange("p a b -> p (a b)"), vecs_d.rearrange("a b -> (a b)").unsqueeze(0).broadcast_to([128, 8 * 128]), writes=[vecs_t])
    w2e = P.sbuf("w2e", [65, 2, 128], F32); w2e_t = Tok()
    P.dma("sp", w2e[:], w2e_d.rearrange("d k n -> k d n"), writes=[w2e_t])
    a2e = P.sbuf("a2e", [33, 2, 128], F32); a2e_t = Tok()
    P.dma("sp", a2e[:], a2e_d.rearrange("d k n -> k d n"), writes=[a2e_t])
    G = P.sbuf("G", [128, nch, 128], F32); G_t = Tok()
    P.dma("sp", G[:, 0:2, :], rw_loc[1:257, 384:512].rearrange("(c p) n -> p c n", p=128), writes=[G_t])
    P.dma("sp", G[:, 2:nch, :], rw_loc[259:259 + (nch - 2) * 128, 384:512].rearrange("(c p) n -> p c n", p=128), writes=[G_t], semtok=G_t)
    Y = [P.sbuf("Y%d" % d, [128, nch, 128], F32) for d in range(2)]
    Y_t = [[Tok() for _ in range(nch)] for d in range(2)]
    BON1 = P.sbuf("BON", [128, nch, 128], F32)
    BON = [BON1, BON1]
    BON1_t = [Tok() for _ in range(nch)]
    BON_t = [BON1_t, BON1_t]
    P.op("pool", lambda e: e.memset(BON1[:], 0.0), writes=BON1_t)
    ST32 = [P.sbuf("ST32_%d" % d, [128, 64], F32) for d in range(2)]
    STb = [P.sbuf("STb_%d" % d, [128, 64], RB) for d in range(2)]
    ST_t = [Tok(), Tok()]; STb_t = [Tok(), Tok()]
    for d in range(2):
        P.op("pool", lambda e, d=d: e.memset(ST32[d][:], 0.0), writes=[ST_t[d]])
        P.op("pool", lambda e, d=d: e.memset(STb[d][:], 0.0), writes=[STb_t[d]])

    NB_ = 2

    def dbuf(name, shape, dt, zero=False, one_rows=None):
        bufs = []
        for d in range(2):
            lst = []
            for i in range(NB_):
                b = P.sbuf("%s_%d_%d" % (name, d, i), shape, dt)
                t = Tok()
                if zero:
                    P.op("pool", lambda e, b=b: e.memset(b[:], 0.0), writes=[t])
                if one_rows is not None:
                    P.op("pool", lambda e, b=b: e.memset(b[one_rows[0]:one_rows[1], :], 1.0), writes=[t])
                lst.append((b, t))
            bufs.append(lst)
        return bufs

    X = dbuf("X", [128, RW_F], F32); XP = dbuf("XP", [128, RW_F], F32); Z = dbuf("Z", [128, RW_F], F32)
    TW = dbuf("TW", [128, 64], F32)
    TWT = dbuf("TWT", [65, 128], F32, one_rows=(64, 65)); ZAT = dbuf("ZAT", [33, 128], F32, one_rows=(32, 33))
    LW = dbuf("LW", [128, 128], F32); ETA = dbuf("ETA", [128, 128], F32)
    KK = dbuf("KK", [128, 128], F32); KP = dbuf("KP", [128, 128], F32); BB = dbuf("BB", [128, 128], F32)
    TMP = dbuf("TMP", [128, 128], F32); TMP2 = dbuf("TMP2", [128, 128], F32)
    SM = dbuf("SM", [128, 16], F32)
    CSs = dbuf("CSs", [128, 128], F32); D3 = dbuf("D3", [128, 128], F32); D4 = dbuf("D4", [128, 128], F32)
    E1 = dbuf("E1", [128, 128], F32); E2 = dbuf("E2", [128, 128], F32); E3 = dbuf("E3", [128, 128], F32); E4 = dbuf("E4", [128, 128], F32)
    GCOL = dbuf("GCOL", [128, 1], F32)
    AH = dbuf("AH", [128, 128], RB); BH = dbuf("BH", [128, 128], RB); KH = dbuf("KH", [128, 128], RB); RH = dbuf("RH", [128, 128], RB)
    BTz = [dbuf("BTz%d" % h, [128, 128], RB, zero=True) for h in range(2)]
    KTz = [dbuf("KTz%d" % h, [128, 128], RB, zero=True) for h in range(2)]
    VB = dbuf("VB", [128, 128], RB)
    BHT = dbuf("BHT", [128, 128], RB); KHT = dbuf("KHT", [128, 128], RB)
    AHTz = [dbuf("AHTz%d" % h, [128, 128], RB, zero=True) for h in range(2)]
    RHTz = [dbuf("RHTz%d" % h, [128, 128], RB, zero=True) for h in range(2)]
    Nm = [[dbuf("Nm%d_%d" % (h, i), [128, 128], F32) for i in range(2)] for h in range(2)]
    Mm = [[dbuf("Mm%d_%d" % (h, i), [128, 128], F32) for i in range(2)] for h in range(2)]
    Pm = [[dbuf("Pm%d_%d" % (h, i), [128, 128], F32) for i in range(2)] for h in range(2)]
    AH32 = dbuf("AH32", [128, 128], F32)
    AAK = [dbuf("AAK%d" % h, [128, 128], RB) for h in range(2)]
    ARB = [dbuf("ARB%d" % h, [128, 128], RB) for h in range(2)]
    ARK = [dbuf("ARK%d" % h, [128, 128], RB) for h in range(2)]
    X2 = [dbuf("X2%d" % h, [128, 64], F32) for h in range(2)]
    WTz = [dbuf("WTz%d" % h, [128, 128], RB, zero=True) for h in range(2)]
    US = [dbuf("US%d" % h, [128, 64], RB) for h in range(2)]

    pa_ = Rot([P.psum("pa%d" % i, [128, 512], F32) for i in range(2)], psum=True)
    pb_ = Rot([P.psum("pb%d" % i, [128, 512], F32) for i in range(1)], psum=True)
    ptb = P.psum("ptb", [128, 4, 128], RB); ptb_t = Tok(psum=True)
    pdbl = BankPool([P.psum("pdbl%d" % i, [128, 512], F32) for i in range(4)])

    def vec(i):
        return vecs[:, i, :]

    def unit(d, c, it):
        par = it % NB_
        g = lambda buf: buf[d][par]
        x, x_t = g(X); xp, xp_t = g(XP); z, z_t = g(Z)
        r0_ = rw_row0(c)
        rp_ = r0_ - 1 if d == 0 else r0_ + 1
        P.dma("sp", x[:, 0:384], rw_loc[r0_:r0_ + 128, 0:384], writes=[x_t])
        yield
        P.dma("sp", x[:, 384:480], rw_loc[r0_:r0_ + 128, 512 + d * 96:608 + d * 96], writes=[x_t], semtok=x_t)
        yield
        P.dma("sp", xp[:, 0:384], rw_loc[rp_:rp_ + 128, 0:384], writes=[xp_t])
        yield
        P.dma("sp", xp[:, 384:480], rw_loc[rp_:rp_ + 128, 512 + d * 96:608 + d * 96], writes=[xp_t], semtok=xp_t)
        yield
        P.op("dve", lambda e: e.tensor_tensor(xp[:], xp[:], x[:], ALU.subtract), reads=[x_t, xp_t], writes=[xp_t])
        yield
        P.op("pool", lambda e: e.tensor_tensor(xp[:], xp[:], mu[:, d, :], ALU.mult), reads=[xp_t, mu_t], writes=[xp_t])
        yield
        P.op("dve", lambda e: e.tensor_tensor(z[:], x[:], xp[:], ALU.add), reads=[x_t, xp_t], writes=[z_t])
        yield
        zr = z[:, 0:128]; zk = z[:, 128:256]; zv = z[:, 256:384]; zw = z[:, 384:448]; za = z[:, 448:480]
        yield
        tw, tw_t = g(TW); twt, twt_t = g(TWT); zat, zat_t = g(ZAT)
        P.op("act", lambda e: e.activation(tw[:], zw, AF.Tanh), reads=[z_t], writes=[tw_t])
        yield
        p1, p1_t = pa_.next()
        P.op("pe", lambda e: e.transpose(p1[0:64, 0:128], tw[:], ident), reads=[tw_t, cst_t], writes=[p1_t])
        P.op("pe", lambda e: e.transpose(p1[0:32, 128:256], za, ident), reads=[z_t, cst_t], writes=[p1_t])
        P.op("act", lambda e: e.copy(twt[0:64, :], p1[0:64, 0:128]), reads=[p1_t], writes=[twt_t])
        P.op("act", lambda e: e.copy(zat[0:32, :], p1[0:32, 128:256]), reads=[p1_t], writes=[zat_t])
        p2, p2_t = pa_.next()
        P.op("pe", lambda e: e.matmul(p2[:, 0:128], twt[:, :], w2e[:, d, :], start=True, stop=True), reads=[twt_t, w2e_t], writes=[p2_t])
        P.op("pe", lambda e: e.matmul(p2[:, 128:256], zat[:, :], a2e[:, d, :], start=True, stop=True), reads=[zat_t, a2e_t], writes=[p2_t])
        lw, lw_t = g(LW); eta, eta_t = g(ETA)
        P.op("act", lambda e: e.activation(lw[:], p2[:, 0:128], AF.Tanh, scale=0.5), reads=[p2_t], writes=[lw_t])
        P.op("act", lambda e: e.activation(eta[:], p2[:, 128:256], AF.Tanh, scale=0.5), reads=[p2_t], writes=[eta_t])
        P.op("pool", lambda e: e.tensor_scalar(lw[:], lw[:], 1.0, -0.5 * math.exp(-0.5), ALU.add, ALU.mult), reads=[lw_t], writes=[lw_t])
        yield
        P.op("pool", lambda e: e.tensor_scalar(eta[:], eta[:], 1.0, 0.5, ALU.add, ALU.mult), reads=[eta_t], writes=[eta_t])
        yield
        kk, kk_t = g(KK); kp, kp_t = g(KP); bb, bb_t = g(BB); tmp, tmp_t = g(TMP); tmp2, tmp2_t = g(TMP2); sm, sm_t = g(SM)
        P.op("dve", lambda e: e.tensor_tensor(kk[:], zk, vec(3 * d + 0), ALU.mult), reads=[z_t, vecs_t], writes=[kk_t])
        yield
        P.op("dve", lambda e: e.tensor_tensor(tmp[:], kk[:], kk[:], ALU.mult), reads=[kk_t], writes=[tmp_t])
        yield
        P.op("dve", lambda e: e.tensor_reduce(sm[:, 0:2], tmp[:].rearrange("p (h k) -> p h k", h=2), AX.X, ALU.add), reads=[tmp_t], writes=[sm_t])
        yield
        P.op("dve", lambda e: e.tensor_scalar(sm[:, 0:2], sm[:, 0:2], 1e-19, None, ALU.max), reads=[sm_t], writes=[sm_t])
        yield
        P.op("act", lambda e: e.activation(sm[:, 2:4], sm[:, 0:2], AF.Ln), reads=[sm_t], writes=[sm_t])
        yield
        P.op("act", lambda e: e.activation(sm[:, 4:6], sm[:, 2:4], AF.Exp, scale=-0.5), reads=[sm_t], writes=[sm_t])
        yield
        for h in range(2):
            P.op("dve", lambda e, h=h: e.tensor_scalar(kk[:, h * 64:(h + 1) * 64], kk[:, h * 64:(h + 1) * 64], sm[:, 4 + h:5 + h], None, ALU.mult),
                 reads=[kk_t, sm_t], writes=[kk_t])
        P.op("dve", lambda e: e.scalar_tensor_tensor(tmp[:], eta[:], -1.0, vec(3 * d + 1), ALU.add, ALU.mult), reads=[eta_t, vecs_t, tmp_t], writes=[tmp_t])
        yield
        P.op("dve", lambda e: e.scalar_tensor_tensor(kp[:], tmp[:], 1.0, zk, ALU.add, ALU.mult), reads=[tmp_t, z_t], writes=[kp_t])
        yield
        P.op("pool", lambda e: e.tensor_tensor(bb[:], kk[:], eta[:], ALU.mult), reads=[kk_t, eta_t], writes=[bb_t])
        yield
        P.op("pool", lambda e: e.tensor_tensor(tmp2[:], zr, kp[:], ALU.mult), reads=[z_t, kp_t], writes=[tmp2_t])
        P.op("pool", lambda e: e.tensor_tensor(tmp2[:], tmp2[:], vec(3 * d + 2), ALU.mult), reads=[tmp2_t, vecs_t], writes=[tmp2_t])
        P.op("dve", lambda e: e.tensor_reduce(sm[:, 6:8], tmp2[:].rearrange("p (h k) -> p h k", h=2), AX.X, ALU.add), reads=[tmp2_t, sm_t], writes=[sm_t])
        for h in range(2):
            P.op("dve", lambda e, h=h: e.scalar_tensor_tensor(BON[d][:, c, h * 64:(h + 1) * 64], z[:, 256 + h * 64:256 + (h + 1) * 64], sm[:, 6 + h:7 + h],
                                                             BON[d][:, c, h * 64:(h + 1) * 64], ALU.mult, ALU.add),
                 reads=[z_t, sm_t, BON_t[d][c]], writes=[BON_t[d][c]])
        yield
        p3, p3_t = pa_.next()
        P.op("pe", lambda e: e.matmul(p3[:, 0:128], TRI[d], lw[:], start=True, stop=True), reads=[cst_t, lw_t], writes=[p3_t])
        P.op("pe", lambda e: e.matmul(p3[:, 128:256], ones, lw[:], start=True, stop=True), reads=[cst_t, lw_t], writes=[p3_t])
        P.op("pe", lambda e: e.matmul(p3[:, 256:257], lw[:], ones[:, 0:1], start=True, stop=True), reads=[cst_t, lw_t], writes=[p3_t])
        css, css_t = g(CSs); d3, d3_t = g(D3); d4, d4_t = g(D4)
        e1, e1_t = g(E1); e2, e2_t = g(E2); e3, e3_t = g(E3); e4, e4_t = g(E4); gcol, gcol_t = g(GCOL)
        P.op("act", lambda e: e.copy(css[:], p3[:, 0:128]), reads=[p3_t], writes=[css_t])
        P.op("act", lambda e: e.activation(gcol[:], p3[:, 256:257], AF.Exp), reads=[p3_t], writes=[gcol_t])
        P.op("dve", lambda e: e.tensor_tensor(d4[:], p3[:, 128:256], css[:], ALU.subtract), reads=[p3_t, css_t], writes=[d4_t])
        P.op("pool", lambda e: e.tensor_tensor(d3[:], css[:], lw[:], ALU.subtract), reads=[css_t, lw_t], writes=[d3_t])
        yield
        P.op("act", lambda e: e.activation(e1[:], css[:], AF.Exp), reads=[css_t], writes=[e1_t])
        yield
        P.op("act", lambda e: e.activation(e2[:], css[:], AF.Exp, scale=-1.0), reads=[css_t], writes=[e2_t])
        yield
        P.op("act", lambda e: e.activation(e3[:], d3[:], AF.Exp), reads=[d3_t], writes=[e3_t])
        yield
        P.op("act", lambda e: e.activation(e4[:], d4[:], AF.Exp), reads=[d4_t], writes=[e4_t])
        yield
        ah, ah_t = g(AH); bh, bh_t = g(BH); kh, kh_t = g(KH); rh, rh_t = g(RH); vb, vb_t = g(VB)
        ah32, ah32_t = g(AH32)
        P.op("dve", lambda e: e.scalar_tensor_tensor(ah32[:], kk[:], -1.0, e3[:], ALU.mult, ALU.mult), reads=[kk_t, e3_t], writes=[ah32_t])
        yield
        P.op("pool", lambda e: e.tensor_copy(ah[:], ah32[:]), reads=[ah32_t], writes=[ah_t])
        yield
        P.op("pool", lambda e: e.tensor_tensor(bh[:], bb[:], e2[:], ALU.mult), reads=[bb_t, e2_t], writes=[bh_t])
        yield
        P.op("dve", lambda e: e.tensor_tensor(kh[:], kp[:], e2[:], ALU.mult), reads=[kp_t, e2_t], writes=[kh_t])
        yield
        P.op("pool", lambda e: e.tensor_tensor(rh[:], zr, e1[:], ALU.mult), reads=[z_t, e1_t], writes=[rh_t])
        yield
        P.op("act", lambda e: e.copy(vb[:], zv), reads=[z_t], writes=[vb_t])
        yield
        btz = [g(BTz[h]) for h in range(2)]; ktz = [g(KTz[h]) for h in range(2)]
        for h in range(2):
            hs = slice(h * 64, (h + 1) * 64)
            P.op("dve", lambda e, h=h, hs=hs: e.tensor_tensor(btz[h][0][:, hs], bb[:, hs], e4[:, hs], ALU.mult), reads=[bb_t, e4_t], writes=[btz[h][1]])
            P.op("pool", lambda e, h=h, hs=hs: e.tensor_tensor(ktz[h][0][:, hs], kp[:, hs], e4[:, hs], ALU.mult), reads=[kp_t, e4_t], writes=[ktz[h][1]])
        yield
        bht, bht_t = g(BHT); kht, kht_t = g(KHT)
        ahtz = [g(AHTz[h]) for h in range(2)]; rhtz = [g(RHTz[h]) for h in range(2)]
        for j, (src, src_t) in enumerate(((ah, ah_t), (bh, bh_t), (kh, kh_t), (rh, rh_t))):
            P.op("pe", lambda e, j=j, src=src: e.transpose(ptb[:, j, :], src[:], identb[:]), reads=[src_t, identb_t], writes=[ptb_t])
        P.op("act", lambda e: e.copy(bht[:], ptb[:, 1, :]), reads=[ptb_t], writes=[bht_t])
        P.op("act", lambda e: e.copy(kht[:], ptb[:, 2, :]), reads=[ptb_t], writes=[kht_t])
        for h in range(2):
            hs = slice(h * 64, (h + 1) * 64)
            P.op("dve", lambda e, h=h, hs=hs: e.tensor_copy(ahtz[h][0][hs, :], ptb[hs, 0, :]), reads=[ptb_t], writes=[ahtz[h][1]])
            P.op("act", lambda e, h=h, hs=hs: e.copy(rhtz[h][0][hs, :], ptb[hs, 3, :]), reads=[ptb_t], writes=[rhtz[h][1]])
        yield
        hd = []
        for h in range(2):
            n0, n0_t = g(Nm[h][0]); m0, m0_t = g(Mm[h][0]); pm0, pm0_t = g(Pm[h][0])
            aak, aak_t = g(AAK[h]); arb, arb_t = g(ARB[h]); ark, ark_t = g(ARK[h])
            specs = [
                (bht, bht_t, ahtz[h][0], ahtz[h][1], MS[d], n0, n0_t),
                (ahtz[h][0], ahtz[h][1], bht, bht_t, MST[d], m0, m0_t),
                (kht, kht_t, ahtz[h][0], ahtz[h][1], MS[d], aak, aak_t),
                (bht, bht_t, rhtz[h][0], rhtz[h][1], MI[d], arb, arb_t),
                (kht, kht_t, rhtz[h][0], rhtz[h][1], MI[d], ark, ark_t),
            ]
            for (lt, lt_t, rt, rt_t, msk, dst, dst_t) in specs:
                pp, pp_t = pa_.next()
                P.op("pe", lambda e, pp=pp, lt=lt, rt=rt: e.matmul(pp[:, 0:128], lt[:], rt[:], start=True, stop=True), reads=[lt_t, rt_t], writes=[pp_t])
                P.op("dve", lambda e, pp=pp, msk=msk, dst=dst: e.tensor_tensor(dst[:], pp[:, 0:128], msk, ALU.mult), reads=[pp_t, cst_t], writes=[dst_t])
            P.op("pool", lambda e, pm0=pm0, n0=n0: e.tensor_tensor(pm0[:], n0[:], ident, ALU.add), reads=[n0_t, cst_t], writes=[pm0_t])
            hd.append(dict(n=(n0, n0_t), m=(m0, m0_t), p=(pm0, pm0_t), aak=(aak, aak_t), arb=(arb, arb_t), ark=(ark, ark_t)))
        yield
        for i in range(6):
            cur = []
            while len(pdbl.free) < 2:
                yield
            for h in range(2):
                st = hd[h]
                (n, n_t), (m, m_t), (pm, pm_t) = st["n"], st["m"], st["p"]
                m2, m2_t = g(Mm[h][(i + 1) % 2]); n2, n2_t = g(Nm[h][(i + 1) % 2]); p2_, p2__t = g(Pm[h][(i + 1) % 2])
                bi = pdbl.try_get()
                pp, pp_t = pdbl.items[bi]
                P.op("pe", lambda e, pp=pp, n=n, m=m: e.matmul(pp[:, 0:128], n[:], m[:], start=True, stop=True), reads=[n_t, m_t], writes=[pp_t])
                cur.append((pp, pp_t, m2, m2_t, n2, n2_t, p2_, p2__t, pm, pm_t, bi))
            yield
            for h in range(2):
                pp, pp_t, m2, m2_t, n2, n2_t, p2_, p2__t, pm, pm_t, bi = cur[h]
                P.op("act", lambda e, pp=pp, m2=m2: e.copy(m2[:], pp[:, 0:128]), reads=[pp_t], writes=[m2_t])
            yield
            for h in range(2):
                pp, pp_t, m2, m2_t, n2, n2_t, p2_, p2__t, pm, pm_t, bi = cur[h]
                P.op("pe", lambda e, pp=pp, m2=m2, pm=pm: e.matmul(pp[:, 256:384], m2[:], pm[:], start=True, stop=True), reads=[m2_t, pm_t], writes=[pp_t])
                if i < 5:
                    P.op("pe", lambda e, pp=pp, m2=m2: e.transpose(pp[:, 128:256], m2[:], ident), reads=[m2_t, cst_t], writes=[pp_t])
            yield
            for h in range(2):
                pp, pp_t, m2, m2_t, n2, n2_t, p2_, p2__t, pm, pm_t, bi = cur[h]
                P.op("dve", lambda e, pp=pp, pm=pm, p2_=p2_: e.tensor_tensor(p2_[:], pp[:, 256:384], pm[:], ALU.add), reads=[pp_t, pm_t], writes=[p2__t])
                if i < 5:
                    P.op("act", lambda e, pp=pp, n2=n2: e.copy(n2[:], pp[:, 128:256]), reads=[pp_t], writes=[n2_t])
                hd[h]["n"], hd[h]["m"], hd[h]["p"] = (n2, n2_t), (m2, m2_t), (p2_, p2__t)
                pdbl.put(bi)
            yield
        for h in range(2):
            st = hd[h]
            tt, tt_t = st["p"]
            aak, aak_t = st["aak"]
            x2, x2_t = g(X2[h]); wtz, wtz_t = g(WTz[h])
            hs = slice(h * 64, (h + 1) * 64)
            pp, pp_t = pa_.next()
            P.op("pe", lambda e, pp=pp, aak=aak, hs=hs: e.matmul(pp[:, 0:64], aak[:], vb[:, hs], start=True, stop=True), reads=[aak_t, vb_t], writes=[pp_t])
            P.op("pe", lambda e, pp=pp, tt=tt: e.matmul(pp[:, 128:256], ah32[:], tt[:], start=True, stop=True), reads=[ah32_t, tt_t], writes=[pp_t])
            P.op("act", lambda e, pp=pp, x2=x2: e.copy(x2[:], pp[:, 0:64]), reads=[pp_t], writes=[x2_t])
            P.op("dve", lambda e, pp=pp, wtz=wtz, hs=hs: e.tensor_copy(wtz[hs, :], pp[hs, 128:256]), reads=[pp_t], writes=[wtz_t])
            st["x2"] = (x2, x2_t); st["wtz"] = (wtz, wtz_t)
        yield
        us = []
        for h in range(2):
            st = hd[h]
            tt, tt_t = st["p"]; x2, x2_t = st["x2"]; wtz, wtz_t = st["wtz"]
            u, u_t = g(US[h])
            pu, pu_t = pb_.next()
            P.op("pe", lambda e, pu=pu, tt=tt, x2=x2: e.matmul(pu[:, 0:64], tt[:], x2[:], start=True, stop=False), reads=[tt_t, x2_t], writes=[pu_t])
            P.op("pe", lambda e, pu=pu, wtz=wtz: e.matmul(pu[:, 0:64], wtz[:], STb[d][:], start=False, stop=True), reads=[wtz_t, STb_t[d]], writes=[pu_t])
            if h == 0:
                P.op("act", lambda e, pu=pu, u=u: e.copy(u[:], pu[:, 0:64]), reads=[pu_t], writes=[u_t])
            else:
                P.op("dve", lambda e, pu=pu, u=u: e.tensor_copy(u[:], pu[:, 0:64]), reads=[pu_t], writes=[u_t])
            us.append((u, u_t))
        py, py_t = pb_.next()
        for h in range(2):
            st = hd[h]
            arb, arb_t = st["arb"]; ark, ark_t = st["ark"]
            u, u_t = us[h]
            hs = slice(h * 64, (h + 1) * 64)
            P.op("pe", lambda e, h=h, hs=hs: e.matmul(py[:, hs], rhtz[h][0][:], STb[d][:], start=True, stop=False), reads=[rhtz[h][1], STb_t[d]], writes=[py_t])
            P.op("pe", lambda e, hs=hs, arb=arb, u=u: e.matmul(py[:, hs], arb[:], u[:], start=False, stop=False), reads=[arb_t, u_t], writes=[py_t])
            P.op("pe", lambda e, hs=hs, ark=ark: e.matmul(py[:, hs], ark[:], vb[:, hs], start=False, stop=True), reads=[ark_t, vb_t], writes=[py_t])
        P.op("act", lambda e: e.copy(Y[d][:, c, :], py[:, 0:128]), reads=[py_t], writes=[Y_t[d][c]])
        pd, pd_t = pb_.next()
        for h in range(2):
            u, u_t = us[h]
            hs = slice(h * 64, (h + 1) * 64)
            P.op("pe", lambda e, h=h, u=u: e.matmul(pd[:, 0:64], btz[h][0][:], u[:], start=(h == 0), stop=False), reads=[btz[h][1], u_t], writes=[pd_t])
            P.op("pe", lambda e, h=h, hs=hs: e.matmul(pd[:, 0:64], ktz[h][0][:], vb[:, hs], start=False, stop=(h == 1)), reads=[ktz[h][1], vb_t], writes=[pd_t])
        P.op("dve", lambda e: e.scalar_tensor_tensor(ST32[d][:], ST32[d][:], gcol[:, 0:1], pd[:, 0:64], ALU.mult, ALU.add),
             reads=[ST_t[d], gcol_t, pd_t], writes=[ST_t[d]])
        P.op("act", lambda e: e.copy(STb[d][:], ST32[d][:]), reads=[ST_t[d]], writes=[STb_t[d]])
        yield

    nsteps = len(order_f)
    active = []
    nxt = [0]

    def admit():
        it = nxt[0]
        nxt[0] += 1
        active.append([it, unit(0, order_f[it], it), 0])
        active.append([it, unit(1, order_b[it], it), 0])

    admit()
    while active:
        for ent in list(active):
            try:
                next(ent[1])
                ent[2] += 1
            except StopIteration:
                active.remove(ent)
        if nxt[0] < nsteps:
            live_steps = {e_[0] for e_ in active}
            newest = [e_ for e_ in active if e_[0] == nxt[0] - 1]
            if len(live_steps) < 2 and all(e_[2] >= 6 for e_ in newest):
                admit()
    ysum = Rot([P.sbuf("ysum%d" % i, [128, 128], F32) for i in range(2)])
    ysq = Rot([P.sbuf("ysq%d" % i, [128, 128], F32) for i in range(2)])
    msm = Rot([P.sbuf("msm%d" % i, [128, 16], F32) for i in range(2)])
    ost = Rot([P.sbuf("ost%d" % i, [128, 128], F32) for i in range(3)])
    for c in range(nch):
        ys, ys_t = ysum.next(); yq, yq_t = ysq.next(); ms, ms_t = msm.next(); o, o_t = ost.next()
        P.op("dve", lambda e, ys=ys, c=c: e.tensor_tensor(ys[:], Y[0][:, c, :], Y[1][:, c, :], ALU.add), reads=[Y_t[0][c], Y_t[1][c]], writes=[ys_t])
        P.op("act", lambda e, ys=ys, yq=yq: e.activation(yq[:], ys[:], AF.Square), reads=[ys_t], writes=[yq_t])
        P.op("dve", lambda e, ys=ys, ms=ms: e.tensor_reduce(ms[:, 0:2], ys[:].rearrange("p (h k) -> p h k", h=2), AX.X, ALU.add), reads=[ys_t], writes=[ms_t])
        P.op("dve", lambda e, yq=yq, ms=ms: e.tensor_reduce(ms[:, 2:4], yq[:].rearrange("p (h k) -> p h k", h=2), AX.X, ALU.add), reads=[yq_t, ms_t], writes=[ms_t])
        P.op("dve", lambda e, ms=ms: e.tensor_scalar(ms[:, 0:2], ms[:, 0:2], 1.0 / 64, None, ALU.mult), reads=[ms_t], writes=[ms_t])
        P.op("dve", lambda e, ms=ms: e.tensor_tensor(ms[:, 4:6], ms[:, 0:2], ms[:, 0:2], ALU.mult), reads=[ms_t], writes=[ms_t])
        P.op("dve", lambda e, ms=ms: e.scalar_tensor_tensor(ms[:, 4:6], ms[:, 2:4], 1.0 / 64, ms[:, 4:6], ALU.mult, ALU.subtract), reads=[ms_t], writes=[ms_t])
        P.op("act", lambda e, ms=ms: e.activation(ms[:, 6:8], ms[:, 4:6], AF.Ln, bias=GN_EPS), reads=[ms_t], writes=[ms_t])
        P.op("act", lambda e, ms=ms: e.activation(ms[:, 8:10], ms[:, 6:8], AF.Exp, scale=-0.5), reads=[ms_t], writes=[ms_t])
        for h in range(2):
            hs = slice(h * 64, (h + 1) * 64)
            P.op("dve", lambda e, ys=ys, ms=ms, h=h, hs=hs: e.tensor_scalar(ys[:, hs], ys[:, hs], ms[:, h:h + 1], ms[:, 8 + h:9 + h], ALU.subtract, ALU.mult),
                 reads=[ys_t, ms_t], writes=[ys_t])
        P.op("pool", lambda e, ys=ys: e.tensor_tensor(ys[:], ys[:], vec(6), ALU.mult), reads=[ys_t, vecs_t], writes=[ys_t])
        P.op("pool", lambda e, ys=ys: e.tensor_tensor(ys[:], ys[:], vec(7), ALU.add), reads=[ys_t, vecs_t], writes=[ys_t])
        P.op("dve", lambda e, ys=ys, c=c: e.tensor_tensor(ys[:], ys[:], BON[0][:, c, :], ALU.add), reads=[ys_t, BON_t[0][c]], writes=[ys_t])
        P.op("dve", lambda e, ys=ys, o=o, c=c: e.tensor_tensor(o[:], ys[:], G[:, c, :], ALU.mult), reads=[ys_t, G_t], writes=[o_t])
        P.dma("sp", out_d[c, :, :], o[:], reads=[o_t], semtok=o_t)
    P.close_scope()


def rwkv_orders(nch_ctx, nch_lat):
    order_f = list(range(nch_ctx + nch_lat))
    order_b = list(range(nch_ctx - 1, -1, -1)) + list(range(nch_ctx + nch_lat - 1, nch_ctx - 1, -1))
    return order_f, order_b


def rwkv_consts():
    i = np.arange(128)
    s, t = i[:, None], i[None, :]
    c = np.stack([np.eye(128), np.ones((128, 128)), s <= t, s >= t, s < t, s <= t, s > t, s >= t]).astype(np.float32)
    return c


EXG = {"A": 1536, "R": 2048, "L": 704}
EXG_DT = {"A": BF16, "R": F32, "L": F32}


def ex_dst(exl, exl_t, oad_l):
    def fn(t, col, n):
        if col < OQ:
            return oad_l[t, :, col:col + n], None
        if col < OGB:
            return exl[t]["A"][:, col - OQ:col - OQ + n], exl_t[t]["A"]
        if col < ORKV:
            return exl[t]["L"][:, 192 + col - OGB:192 + col - OGB + n], exl_t[t]["L"]
        if col < OLF:
            return exl[t]["R"][:, col - ORKV:col - ORKV + n], exl_t[t]["R"]
        if col < OGC:
            return exl[t]["L"][:, col - OLF:col - OLF + n], exl_t[t]["L"]
        return exl[t]["R"][:, 1536 + col - OGC:1536 + col - OGC + n], exl_t[t]["R"]
    return fn


def emit_compact(P, exg, exg_t, at_loc, at_g, eng):
    P.open_scope()
    hv = P.pid4(eng) * 128
    toks = Rot([None] * 16)
    for t in range(NT):
        for s_ in range(4 if t < 8 else 2):
            tok0 = CTX + s_ * 1024 + t * 128 if t < 8 else s_ * 128
            rows = slice(s_ * 128, (s_ + 1) * 128)
            _, tk = toks.next()
            P.dma(eng, at_loc[tok0:tok0 + 128, :].rearrange("n (f c) -> n f c", f=3),
                  exg[t]["A"][rows, :].rearrange("n (f w) -> n f w", f=3)[:, :, bass.ds(hv, 128)], reads=[exg_t[t]["A"]], writes=[tk], semtok=tk)
            _, tk = toks.next()
            P.dma(eng, at_g[tok0:tok0 + 128, :], exg[t]["L"][rows, 192:704][:, bass.ds(hv, 128)], reads=[exg_t[t]["L"]], writes=[tk], semtok=tk)
    P.close_scope()


def compact_rw(P, exg, exg_t, rw_loc, eng):
    hv = P.pid4(eng) * 128
    zt = P.sbuf("zt", [4, RW_COLS], F32); zt_t = Tok()
    P.op("pool", lambda e: e.memset(zt[:], 0.0), writes=[zt_t])
    for r in (0, 257, 258, RW_ROWS - 1):
        P.dma(eng, rw_loc[r:r + 1, :], zt[0:1, :], reads=[zt_t], semtok=zt_t)
    toks = Rot([None] * 12)
    for t in range(NT):
        for s_ in range(4 if t < 8 else 2):
            tok0 = CTX + s_ * 1024 + t * 128 if t < 8 else s_ * 128
            rrow = 259 + (tok0 - CTX) if t < 8 else 1 + tok0
            rows = slice(s_ * 128, (s_ + 1) * 128)
            _, tk = toks.next()
            P.dma(eng, rw_loc[rrow:rrow + 128, 0:512].rearrange("n (f c) -> n f c", f=4),
                  exg[t]["R"][rows, :].rearrange("n (f w) -> n f w", f=4)[:, :, bass.ds(hv, 128)], reads=[exg_t[t]["R"]], writes=[tk], semtok=tk)
            _, tk = toks.next()
            P.dma(eng, rw_loc[rrow:rrow + 128, 512:704], exg[t]["L"][rows, 0:192], reads=[exg_t[t]["L"]], writes=[tk], semtok=tk)


def emit_obc_stage(P, gath, gath_t, obc_loc, eng):
    q = P.pid4(eng)
    toks = []
    for j in range(2):
        g, g_c = gath[j]
        tk = [Tok(), Tok()]
        P.dma(eng, obc_loc[:, j, 0:1024, :],
              g.rearrange("j h n c -> (j h n) c")[bass.ds(q * 4096, 4096), :].rearrange("(h n) c -> h n c", h=4), reads=[gath_t[j]], writes=[tk[0]], semtok=tk[0])
        P.dma(eng, obc_loc[:, j, 1024:1152, :], g_c[:, bass.ds((q % 2) * 128, 128), :], reads=[gath_t[j]], writes=[tk[1]], semtok=tk[1])
        toks.append(tk)
    return toks


def build_fused():
    P = Prog()
    di = P.dram_in
    x_d = di("x", [NT, 128, D], F32); xh_d = di("xh", [4, D], F32); hmask_d = di("hmask", [4, 1], F32)
    cT_d = di("cT", [D, 2], F32); wm_d = di("wm", [2, D, MC], F32); bm_d = di("bm", [2, MC], F32)
    gpre_d = di("gpreT", [2, 128, 16], F32); win_d = di("w_in", [2, D, N_IN], F32)
    sguw_d = di("sgu_wT", [2, 128, 4, 128], F32); sgub_d = di("sgu_bT", [2, 128, 4], F32); convw_d = di("conv_w", [2, 1, 3 * WG], F32)
    cos_d = di("rcos", [128, NT, 64], F32); sin_d = di("rsin", [128, NT, 64], F32); ident_d = di("ident", [128, 128], F32)
    wo_d = di("w_out", [2, D, D], F32); gpost_d = di("gpost", [2, 1, D], F32)
    lam_d = di("lamv", [2, 1, 256], F32); subg_d = di("subg", [2, 1, 128], F32)
    mu_d = di("rw_mu", [2, 2, RW_F], F32); w2e_d = di("rw_w2e", [2, 2, 65, 128], F32); a2e_d = di("rw_a2e", [2, 2, 33, 128], F32)
    vecs_d = di("rw_vecs", [2, 8, 128], F32); cst_d = di("rw_cst", [8, 128, 128], F32)
    xo_d = P.dram_out("xo", [8, 128, D], F32)
    tmp = P.dram_tmp
    stage = [0]
    STOP = DEBUG.get("stop", 999)

    def done():
        stage[0] += 1
        return stage[0] >= STOP

    mod_l = tmp("mod_l", [2, 2, MC], F32); mod_all = tmp("mod_all", [4, 2, 2, MC], F32)
    emit_M(P, cT_d, wm_d, bm_d, mod_l)
    P.allgather(mod_l.rearrange("l j n -> (l j) n"), mod_all.rearrange("r l j n -> (r l j) n"), GROUPS)
    if done():
        return P.finish()
    of, ob = rwkv_orders(2, 32)
    xcur, xhcur = x_d, xh_d
    for l in range(2):
        oad_l = tmp("oad_l%d" % l, [NT, 128, 1024], F32)
        exl = [{g: tmp("exl%d_%d%s" % (l, t, g), [128, n], EXG_DT[g]) for g, n in EXG.items()} for t in range(NT)]
        exg = [{g: tmp("exg%d_%d%s" % (l, t, g), [4 * 128, n], EXG_DT[g]) for g, n in EXG.items()} for t in range(NT)]
        exl_t = [{g: Tok(multi=True) for g in EXG} for t in range(NT)]
        exg_t = [{g: Tok() for g in EXG} for t in range(NT)]

        pend = []
        st_ = {"step": 0, "last": -99}

        def hook(pos, t, exl=exl, exg=exg, exl_t=exl_t, exg_t=exg_t, pend=pend, st_=st_):
            def issue():
                tt, g = pend.pop(0)
                P.allgather_async(exl[tt][g], exg[tt][g], GROUPS, reads=[exl_t[tt][g]], out_tok=exg_t[tt][g])
            if pos is None:
                while pend:
                    issue()
                return
            if t == NT - 1 and pos == 4:
                pend.extend((tt, "A") for tt in range(NT))
            if t == NT - 1 and pos == 9:
                pend.extend((tt, "R") for tt in range(NT))
                pend.extend((tt, "L") for tt in range(NT))
            st_["step"] += 1
            if pend and st_["step"] - st_["last"] >= 5:
                issue()
                st_["last"] = st_["step"]

        emit_A(P, l, xcur, xhcur, hmask_d, mod_all, gpre_d[l], win_d[l], sguw_d[l], sgub_d[l], convw_d[l], cos_d, sin_d, ident_d,
               ex_dst(exl, exl_t, oad_l), hook)
        if done():
            return P.finish()
        at_loc = tmp("at_loc%d" % l, [NTOK, 384], BF16); at_g = tmp("at_g%d" % l, [NTOK, 128], F32)
        rw_loc = tmp("rw_loc%d" % l, [RW_ROWS, RW_COLS], F32)
        emit_compact(P, exg, exg_t, at_loc, at_g, "sp")
        if done():
            return P.finish()
        ob_l = tmp("ob_l%d" % l, [NKB, 128, 128], F32); oc_l = tmp("oc_l%d" % l, [NKB, 128, 128], F32)
        gath = [(tmp("obg%d_%d" % (l, j), [4, 4, 1024, 128], F32), tmp("obgc%d_%d" % (l, j), [4, 256, 128], F32)) for j in range(2)]
        gath_t = [Tok(), Tok()]
        emit_attn(P, l, l == 0, at_loc, at_g, lam_d[l], subg_d[l], ident_d, ob_l,
                  pre=lambda exg=exg, exg_t=exg_t, rw_loc=rw_loc: compact_rw(P, exg, exg_t, rw_loc, "pool"))
        if done():
            return P.finish()

        def gather_o(j, src):
            g, g_c = gath[j]
            for c in range(4):
                P.allgather_async(src[2 + 8 * c:2 + 8 * (c + 1)].rearrange("t p c -> (t p) c"), g[c].rearrange("h n c -> (h n) c"), GROUPS)
            P.allgather_async(src[0:2].rearrange("t p c -> (t p) c"), g_c.rearrange("h n c -> (h n) c"), GROUPS, out_tok=gath_t[j])

        gather_o(0, ob_l)
        emit_rwkv(P, NKB, of, ob, rw_loc, mu_d[l], w2e_d[l], a2e_d[l], vecs_d[l], cst_d, oc_l)
        if done():
            return P.finish()
        gather_o(1, oc_l)
        obc_loc = tmp("obc_loc%d" % l, [4, 2, NT * 128, 128], F32)
        stage_fn = lambda gath=gath, gath_t=gath_t, obc_loc=obc_loc: emit_obc_stage(P, gath, gath_t, obc_loc, "pool")
        if l == 0:
            x1 = tmp("x1", [NT, 128, D], F32); edge_l = tmp("edge_l", [4, D], F32); edge_all = tmp("edge_all", [16, D], F32)
            xh1 = tmp("xh1", [4, D], F32)
            emit_C(P, 0, NT, oad_l, obc_loc, xcur, mod_all, gpost_d[0], wo_d[0], ident_d, x1, edge_l, stage=stage_fn)
            P.allgather(edge_l, edge_all, GROUPS)
            q = P.pid4("pool")
            etk = [Tok() for _ in range(4)]
            for i, (dq, er) in enumerate(((3, 1), (1, 0), (3, 3), (1, 2))):
                P.dma("pool", xh1[i:i + 1, :], edge_all[bass.ds(((q + dq) % 4) * 4 + er, 1), :], writes=[etk[i]])
            P.barrier()
            xcur, xhcur = x1, xh1
            if done():
                return P.finish()
        else:
            emit_C(P, 1, 8, oad_l, obc_loc, xcur, mod_all, gpost_d[1], wo_d[1], ident_d, xo_d, stage=stage_fn)
    return P.finish()


_NC_CACHE = {}


def _f32(a):
    return np.ascontiguousarray(a, dtype=np.float32)


def _rope_tables():
    inv = (10000.0 ** (-np.arange(0, 32, 2, dtype=np.float32) / 32)).astype(np.float32)
    tabs = []
    for q in range(4):
        cos = np.ones((128, NT, 2, 2, 16), np.float32)
        sin = np.zeros((128, NT, 2, 2, 16), np.float32)
        for t in range(NT - 1):
            tok = q * 1024 + t * 128 + np.arange(128)
            for a, pos in enumerate((tok // 64, tok % 64)):
                ang = pos.astype(np.float32)[:, None] * inv[None, :]
                cos[:, t, a, 0, :] = np.cos(ang); cos[:, t, a, 1, :] = np.cos(ang)
                sin[:, t, a, 0, :] = -np.sin(ang); sin[:, t, a, 1, :] = np.sin(ang)
        tabs.append((cos.reshape(128, NT, 64), sin.reshape(128, NT, 64)))
    return tabs


def _to_pk(v):
    return np.ascontiguousarray(v.reshape(16, 128).T)


def rwkv_consts():
    i = np.arange(128)
    s, t = i[:, None], i[None, :]
    c = np.stack([np.eye(128), np.ones((128, 128)), s <= t, s >= t, s < t, s <= t, s > t, s >= t]).astype(np.float32)
    return c


def _core_inputs(p):
    tabs = _rope_tables()
    ident = np.eye(128, dtype=np.float32)
    cst = rwkv_consts()
    x, xc = p['x'], p['ctx']
    shared = {
        "gpreT": _f32(np.stack([_to_pk(p['g_pre'][l]) for l in range(2)])),
        "w_in": _f32(p['w_in']), "sgu_wT": _f32(p['sgu_w'].transpose(0, 3, 1, 2)), "sgu_bT": _f32(p['sgu_b'].transpose(0, 2, 1)),
        "conv_w": _f32(p['conv_w'].reshape(2, 1, 3 * WG)), "ident": ident, "w_out": _f32(p['w_out']),
        "gpost": _f32(p['g_post'][:, None, :]),
        "lamv": _f32(np.concatenate([p['lam_q1'], p['lam_k1'], p['lam_q2'], p['lam_k2']], 1)[:, None, :]),
        "subg": _f32(p['subln_g'][:, None, :]), "rw_cst": cst,
    }
    ins = []
    for k in range(NCORES):
        b, q = k // 4, k % 4
        ct = q % 2
        xt = np.concatenate([x[b, q * 1024:(q + 1) * 1024].reshape(8, 128, D), xc[b, ct * 128:(ct + 1) * 128][None]], 0)
        xh = np.zeros((4, D), np.float32); hm = np.zeros((4, 1), np.float32)
        if q > 0:
            xh[0] = x[b, q * 1024 - 1]; hm[0] = 1
        if q < 3:
            xh[1] = x[b, (q + 1) * 1024]; hm[1] = 1
        if ct > 0:
            xh[2] = xc[b, ct * 128 - 1]; hm[2] = 1
        if ct < 1:
            xh[3] = xc[b, (ct + 1) * 128]; hm[3] = 1
        cols = slice(q * 128, (q + 1) * 128)
        c0 = q * 128
        mus, w2e, a2e, vecs = [], [], [], []
        for l in range(2):
            mus.append([]); w2e.append([]); a2e.append([])
            for d in range(2):
                m = p['rwkv_mu'][l, d]
                mus[l].append(np.concatenate([m[c0:c0 + 128], m[512 + c0:512 + c0 + 128], m[1024 + c0:1024 + c0 + 128], m[1536 + d * 0:1632]]))
                w2e[l].append(np.concatenate([p['rwkv_w2'][l, d][:, cols], p['rwkv_w0'][l, d][None, cols]], 0))
                a2e[l].append(np.concatenate([p['rwkv_a2'][l, d][:, cols], p['rwkv_a0'][l, d][None, cols]], 0))
            vecs.append(np.stack([p['rwkv_kk'][l, 0][cols], p['rwkv_ka'][l, 0][cols], p['rwkv_rk'][l, 0].reshape(-1)[cols],
                                  p['rwkv_kk'][l, 1][cols], p['rwkv_ka'][l, 1][cols], p['rwkv_rk'][l, 1].reshape(-1)[cols],
                                  p['rwkv_ln_w'][l][cols], p['rwkv_ln_b'][l][cols]]))
        dct = dict(shared)
        dct.update({
            "x": _f32(xt), "xh": xh, "hmask": hm,
            "cT": _f32(np.stack([p['c'][b], p['c_ctx']], 1)),
            "wm": _f32(p['w_mod'][:, :, q * MC:(q + 1) * MC]), "bm": _f32(p['b_mod'][:, q * MC:(q + 1) * MC]),
            "rcos": tabs[q][0], "rsin": tabs[q][1],
            "rw_mu": _f32(np.array(mus)), "rw_w2e": _f32(np.array(w2e)), "rw_a2e": _f32(np.array(a2e)), "rw_vecs": _f32(np.array(vecs)),
        })
        ins.append(dct)
    return ins


def kernel(**inputs):
    p = {k: np.asarray(v) for k, v in inputs.items()}
    if "fused" not in _NC_CACHE:
        _NC_CACHE["fused"] = build_fused()
    ins = _core_inputs(p)
    res = run_bass_kernel_spmd(_NC_CACHE["fused"], ins, core_ids=list(range(NCORES))).results
    out = np.zeros((NB, SEQ, D), np.float32)
    for k in range(NCORES):
        b, q = k // 4, k % 4
        out[b, q * 1024:(q + 1) * 1024] = res[k]["xo"].reshape(1024, D)
    return out
```

```python
import math
import numpy as np
import concourse.bass as bass
import concourse.mybir as mybir
from concourse.bass_utils import run_bass_kernel_spmd

F32 = mybir.dt.float32
BF16 = mybir.dt.bfloat16
AF = mybir.ActivationFunctionType
ALU = mybir.AluOpType
AX = mybir.AxisListType

D = 2048
SEQ = 4096
CTX = 256
NB = 2
WG = 512
N_IN = 7872
EPS = 1e-6
GN_EPS = 64e-5
NCORES = 8

ENGS = ("pe", "act", "dve", "pool", "sp")
SAME_ENGINE_SYNC = {"pe": False, "act": True, "dve": True, "pool": True, "sp": False}


class Tok:
    __slots__ = ("name", "lastw", "readers", "sem", "semcnt", "psum", "multi", "wlist")

    def __init__(self, name="", psum=False, multi=False):
        self.name = name
        self.psum = psum
        self.multi = multi
        self.wlist = []
        self.lastw = None
        self.readers = []
        self.sem = None
        self.semcnt = 0


NSEM_POOL = 92


class Prog:
    def __init__(self):
        self.nc = bass.Bass("TRN2", target_bir_lowering=False)
        nc = self.nc
        self.eng = {"pe": nc.tensor, "act": nc.scalar, "dve": nc.vector, "pool": nc.gpsimd, "sp": nc.sync}
        self.cnt = {e: 0 for e in ENGS}
        self.seen = {e: {} for e in ENGS}
        self.esem = {}
        self._ctx = []
        for e in ENGS:
            cm = nc.semaphore("s_" + e)
            self.esem[e] = cm.__enter__()
            self._ctx.append(cm)
        cm = nc.semaphore("s_cc")
        self.ccsem = cm.__enter__(); self._ctx.append(cm)
        self.cccnt = 0
        self.sem_free = []
        self.sem_base = {}
        for i in range(NSEM_POOL):
            cm = nc.semaphore("d_%d" % i)
            sm = cm.__enter__(); self._ctx.append(cm)
            self.sem_free.append(sm)
            self.sem_base[id(sm)] = 0
        self.live = []
        self.scopes = []
        self.out_events = []
        self.nalloc = 0

    def _push(self, cm):
        t = cm.__enter__()
        (self.scopes[-1]["ctx"] if self.scopes else self._ctx).append(cm)
        return t

    def sbuf(self, name, shape, dt):
        self.nalloc += 1
        return self._push(self.nc.sbuf_tensor("sb%d_%s" % (self.nalloc, name), list(shape), dt))

    def psum(self, name, shape, dt=F32):
        self.nalloc += 1
        return self._push(self.nc.psum_tensor("pp%d_%s" % (self.nalloc, name), list(shape), dt))

    def dram_in(self, name, shape, dt):
        return self.nc.dram_tensor(name, list(shape), dt, kind="ExternalInput").ap()

    def dram_out(self, name, shape, dt):
        return self.nc.dram_tensor(name, list(shape), dt, kind="ExternalOutput").ap()

    def dram_tmp(self, name, shape, dt):
        return self.nc.dram_tensor(name, list(shape), dt, kind="Internal").ap()

    def pid4(self, eng):
        if not hasattr(self, "_pid4"):
            self._pid4 = {}
        if eng not in self._pid4:
            self._pid4[eng] = self.eng[eng].partition_id() % 4
        return self._pid4[eng]

    def open_scope(self):
        self.scopes.append({"ctx": [], "toks": []})

    def close_scope(self):
        self.barrier()
        sc = self.scopes.pop()
        for cm in reversed(sc["ctx"]):
            cm.__exit__(None, None, None)
        for t in sc["toks"]:
            self.sem_base[id(t.sem)] = t.semcnt
            self.sem_free.append(t.sem)
            self.live.remove(t)
            t.sem = None

    def _tsem(self, tok):
        if tok.sem is None:
            tok.sem = self.sem_free.pop(0)
            tok.semcnt = self.sem_base[id(tok.sem)]
            self.live.append(tok)
            if self.scopes:
                self.scopes[-1]["toks"].append(tok)
        return tok.sem

    def _need(self, eng, deps):
        waits = []
        best = {}
        for d in deps:
            if d is None:
                continue
            key, val = d
            if key == eng and not SAME_ENGINE_SYNC[eng]:
                continue
            kk = id(key) if not isinstance(key, str) else key
            if best.get(kk, (None, 0))[1] < val:
                best[kk] = (key, val)
        for kk, (key, val) in best.items():
            if self.seen[eng].get(kk, 0) >= val:
                continue
            self.seen[eng][kk] = val
            sem = key if not isinstance(key, str) else self.esem[key]
            waits.append((sem, val))
        return waits

    @staticmethod
    def _deps(eng, reads, writes):
        deps = []
        for r in reads:
            deps.append(r.lastw)
            if r.multi:
                deps.extend(r.wlist)
            if r.psum:
                deps.extend(x for x in r.readers if x[0] != eng)
        for w in writes:
            if w.multi:
                continue
            deps.append(w.lastw)
            deps.extend(w.readers)
        return deps

    @staticmethod
    def _commit(ev, reads, writes):
        for r in reads:
            r.readers.append(ev)
        for w in writes:
            if w.multi:
                w.wlist.append(ev)
            else:
                w.lastw = ev
                w.readers = []

    def op(self, eng, fn, reads=(), writes=()):
        waits = self._need(eng, self._deps(eng, reads, writes))
        e = self.eng[eng]
        for (s, v) in waits:
            e.wait_ge(s, v)
        self.cnt[eng] += 1
        ev = (eng, self.cnt[eng])
        fn(e).then_inc(self.esem[eng], 1)
        self._commit(ev, reads, writes)
        return ev

    def dma(self, eng, out, in_, reads=(), writes=(), semtok=None, is_output=False, **kw):
        if semtok is None:
            semtok = writes[0] if writes else reads[0]
        sem = self._tsem(semtok)
        waits = self._need(eng, self._deps(eng, reads, writes))
        e = self.eng[eng]
        for (s, v) in waits:
            e.wait_ge(s, v)
        semtok.semcnt += 16
        ev = (sem, semtok.semcnt)
        e.dma_start(out=out, in_=in_, **kw).then_inc(sem, 16)
        self._commit(ev, reads, writes)
        return ev

    def barrier(self):
        for e in ENGS:
            eo = self.eng[e]
            for e2 in ENGS:
                if e2 != e and self.seen[e].get(e2, 0) < self.cnt[e2]:
                    eo.wait_ge(self.esem[e2], self.cnt[e2])
                    self.seen[e][e2] = self.cnt[e2]
            for t in self.live:
                if self.seen[e].get(id(t.sem), 0) < t.semcnt:
                    eo.wait_ge(t.sem, t.semcnt)
                    self.seen[e][id(t.sem)] = t.semcnt
            if self.cccnt and self.seen[e].get(id(self.ccsem), 0) < self.cccnt:
                eo.wait_ge(self.ccsem, self.cccnt)
                self.seen[e][id(self.ccsem)] = self.cccnt

    def allgather(self, in_ap, out_ap, groups):
        self.barrier()
        self.cccnt += 1
        self.eng["pool"].collective_compute("AllGather", ALU.bypass, replica_groups=groups, ins=[in_ap], outs=[out_ap]).then_inc(self.ccsem, 1)
        self.barrier()

    def allgather_async(self, in_ap, out_ap, groups, reads=(), out_tok=None):
        waits = self._need("pool", self._deps("pool", reads, []))
        e = self.eng["pool"]
        for (s_, v) in waits:
            e.wait_ge(s_, v)
        if self.cccnt and self.seen["pool"].get(id(self.ccsem), 0) < self.cccnt:
            e.wait_ge(self.ccsem, self.cccnt)
            self.seen["pool"][id(self.ccsem)] = self.cccnt
        self.cccnt += 1
        e.collective_compute("AllGather", ALU.bypass, replica_groups=groups, ins=[in_ap], outs=[out_ap]).then_inc(self.ccsem, 1)
        if out_tok is not None:
            out_tok.lastw = (self.ccsem, self.cccnt)
            out_tok.readers = []

    def finish(self):
        self.barrier()
        for cm in reversed(self._ctx):
            cm.__exit__(None, None, None)
        return self.nc


class BankPool:
    def __init__(self, bufs):
        self.items = [(b, Tok(psum=True)) for b in bufs]
        self.free = list(range(len(bufs)))

    def try_get(self):
        if not self.free:
            return None
        i = self.free.pop(0)
        return i

    def put(self, i):
        self.free.append(i)


class Rot:
    def __init__(self, bufs, psum=False):
        self.bufs = bufs
        self.toks = [Tok(psum=psum) for _ in bufs]
        self.i = 0

    def next(self):
        j = self.i % len(self.bufs)
        self.i += 1
        return self.bufs[j], self.toks[j]


MC = 1536
GROUPS = [[0, 1, 2, 3], [4, 5, 6, 7]]


def emit_M(P, cT_d, wm_d, bm_d, out_d):
    P.open_scope()
    cs = P.sbuf("cs", [128, 16, 2], F32); cs_t = Tok()
    sb = P.sbuf("sb", [128, 16, 2], BF16); sb_t = Tok()
    bs = P.sbuf("bs", [2, 2, MC], F32); bs_t = Tok()
    os_ = P.sbuf("os", [2, 2, MC], F32); os_t = Tok()
    ps = [P.psum("ps%d" % i, [2, 512], F32) for i in range(6)]
    ps_t = [Tok(psum=True) for _ in range(6)]
    wb = [P.sbuf("wb%d" % i, [128, 16, 512], BF16) for i in range(3)]
    wb_t = [Tok() for _ in range(3)]
    P.dma("sp", cs[:], cT_d.rearrange("(kc p) j -> p kc j", p=128), writes=[cs_t])
    for l in range(2):
        P.dma("sp", bs[:, l, :], bm_d[l:l + 1, :].broadcast_to([2, MC]), writes=[bs_t])
    P.op("act", lambda e: e.activation(sb[:], cs[:], AF.Silu), reads=[cs_t], writes=[sb_t])
    for l in range(2):
        for j in range(3):
            pi = l * 3 + j
            w, w_t = wb[pi % 3], wb_t[pi % 3]
            P.dma("pool", w[:], wm_d[l][:, j * 512:(j + 1) * 512].rearrange("(kc p) n -> p kc n", p=128), writes=[w_t])
            for kc in range(16):
                P.op("pe", lambda e, pi=pi, w=w, kc=kc: e.matmul(ps[pi][:, :], sb[:, kc, :], w[:, kc, :], start=(kc == 0), stop=(kc == 15)),
                     reads=[sb_t, w_t], writes=[ps_t[pi]])
            P.op("dve", lambda e, pi=pi, l=l, j=j: e.tensor_tensor(os_[:, l, j * 512:(j + 1) * 512], ps[pi][:, :], bs[:, l, j * 512:(j + 1) * 512], ALU.add),
                 reads=[ps_t[pi], bs_t], writes=[os_t])
    P.dma("sp", out_d.rearrange("l j n -> j l n"), os_[:], reads=[os_t])
    P.close_scope()


def mod_pieces(c0, n=D):
    out = []
    c = c0
    while c < c0 + n:
        r, off = c // MC, c % MC
        ln = min(MC - off, c0 + n - c)
        out.append((r, off, ln, c - c0))
        c += ln
    return out


NT = 9
NTT = 10
OA, OD, OQ, OK_, OV, OGB, ORKV, OLF, OLB, OGC = 0, 512, 1024, 1536, 2048, 2560, 3072, 4608, 4704, 4800
NOUT_A = 5312
CBS = [
    (0, 512, "A_u", None), (512, 512, "A_v", None), (1024, 512, "A_g", OA),
    (1536, 512, "rope", OQ), (2048, 512, "rope", OK_), (2560, 512, "copy", OV), (3072, 512, "silu", OGB),
    (3584, 512, "copy", ORKV), (4096, 512, "copy", ORKV + 512), (4608, 512, "copy", ORKV + 1024),
    (5120, 192, "copy", OLF), (5312, 512, "silu", OGC),
    (5824, 512, "D_b", None), (6336, 512, "D_c", None), (6848, 512, "D_x", None), (7360, 512, "D_g", None),
]


DEBUG = {"ncb": 16, "conv": True, "halo": True}


def emit_A(P, l, x_d, xh_d, hmask_d, mod_all, gpre_d, win_d, sguw_d, sgub_d, convw_d, cos_d, sin_d, ident_d, dst_fn, hook=None):
    P.open_scope()
    ident = P.sbuf("ident", [128, 128], F32); ident_t = Tok()
    P.dma("sp", ident[:], ident_d[:, :], writes=[ident_t])
    gpre = P.sbuf("gpre", [128, 16], F32); gpre_t = Tok()
    P.dma("sp", gpre[:], gpre_d, writes=[gpre_t])
    MV = P.sbuf("MV", [64, 128], F32); MV_t = Tok()
    for v, (row, c0) in enumerate(((0, 2048), (0, 0), (1, 2048), (1, 0))):
        for (r, off, ln, dst) in mod_pieces(c0):
            P.dma("sp", MV[v * 16 + dst // 128:v * 16 + (dst + ln) // 128, :],
                  mod_all[r, l, row, off:off + ln].rearrange("(k c) -> k c", c=128), writes=[MV_t], semtok=MV_t)
    pmv = P.psum("pmv", [128, 512], F32); pmv_t = Tok(psum=True)
    P.op("pe", lambda e: e.transpose(pmv[:, 0:64], MV[:, :], ident[0:64, 0:64]), reads=[MV_t, ident_t], writes=[pmv_t])
    modT = P.sbuf("modT", [128, 4, 16], F32); modT_t = Tok()
    P.op("dve", lambda e: e.tensor_copy(modT[:].rearrange("p a b -> p (a b)"), pmv[:, 0:64]), reads=[pmv_t], writes=[modT_t])
    sc = P.sbuf("sc", [128, 2, 16], F32); sc_t = Tok()
    sh = P.sbuf("sh", [128, 2, 16], F32); sh_t = Tok()
    for j in range(2):
        P.op("dve", lambda e, j=j: e.scalar_tensor_tensor(sc[:, j, :], modT[:, 2 * j, :], 1.0, gpre[:], ALU.add, ALU.mult),
             reads=[modT_t, gpre_t], writes=[sc_t])
        P.op("dve", lambda e, j=j: e.tensor_copy(sh[:, j, :], modT[:, 2 * j + 1, :]), reads=[modT_t], writes=[sh_t])
    sguw32 = P.sbuf("sguw32", [128, 4, 128], F32); sguw32_t = Tok()
    P.dma("sp", sguw32[:], sguw_d, writes=[sguw32_t])
    sguw = P.sbuf("sguw", [128, 4, 128], BF16); sguw_t = Tok()
    P.op("dve", lambda e: e.tensor_copy(sguw[:], sguw32[:]), reads=[sguw32_t], writes=[sguw_t])
    sgub = P.sbuf("sgub", [128, 4], F32); sgub_t = Tok()
    P.dma("sp", sgub[:], sgub_d, writes=[sgub_t])
    convw = P.sbuf("convw", [128, 3, WG], F32); convw_t = Tok()
    P.dma("sp", convw[:].rearrange("p a b -> p (a b)"), convw_d.broadcast_to([128, 3 * WG]), writes=[convw_t])
    rcos = P.sbuf("rcos", [128, NT, 64], F32); rcos_t = Tok()
    rsin = P.sbuf("rsin", [128, NT, 64], F32); rsin_t = Tok()
    P.dma("sp", rcos[:], cos_d[:, :, :], writes=[rcos_t])
    P.dma("sp", rsin[:], sin_d[:, :, :], writes=[rsin_t])
    hmask = P.sbuf("hmask", [4, 1], F32); hmask_t = Tok()
    P.dma("sp", hmask[:], hmask_d[:, :], writes=[hmask_t])

    hT = P.sbuf("hT", [128, 16, NTT * 128], BF16)
    hT_t = [Tok() for _ in range(NTT)]
    xs = Rot([P.sbuf("xs%d" % i, [128, D], F32) for i in range(2)])
    junk = P.sbuf("junk", [128, D], BF16); junk_t = Tok()
    small = P.sbuf("small", [128, NTT, 4], F32)
    small_t = [Tok() for _ in range(NTT)]
    wbuf = [P.sbuf("wbuf%d" % i, [128, 16, 512], BF16) for i in range(2)]
    wbuf_t = [[Tok(), Tok()] for _ in range(2)]
    UB = P.sbuf("UB", [128, NT, WG], F32); UB_t = [Tok() for _ in range(NT)]
    VN = P.sbuf("VN", [128, NT, WG], BF16); VN_t = [Tok() for _ in range(NT)]
    CG = P.sbuf("CG", [128, NT, WG], F32); CG_t = [Tok() for _ in range(NT)]
    ZH = P.sbuf("ZH", [4, WG], F32); ZH_t = Tok()
    st = Rot([P.sbuf("st%d" % i, [128, 512], F32) for i in range(4)])
    stb = Rot([P.sbuf("stb%d" % i, [128, 512], BF16) for i in range(4)])
    tA = Rot([P.sbuf("tA%d" % i, [128, 512], F32) for i in range(2)])
    tB = Rot([P.sbuf("tB%d" % i, [128, 512], F32) for i in range(2)])
    lnst = Rot([P.sbuf("lnst%d" % i, [128, 16], F32) for i in range(2)])
    pt = Rot([P.psum("pt%d" % i, [128, 4, 128], F32) for i in range(2)], psum=True)
    pm = Rot([P.psum("pm%d" % i, [128, 512], F32) for i in range(3)], psum=True)
    pmix = Rot([P.psum("pmix%d" % i, [128, 512], F32) for i in range(2)], psum=True)

    P.op("pool", lambda e: e.memset(hT[:, :, NT * 128:NTT * 128], 0.0), writes=[hT_t[NT]])

    for t in range(NTT):
        rows = 128 if t < NT else 4
        xb, xb_t = xs.next()
        src = x_d[t] if t < NT else xh_d[:, :]
        P.dma("sp", xb[:rows, :], src, writes=[xb_t])
        sm = small[:rows, t, :]
        P.op("act", lambda e, xb=xb, rows=rows, sm=sm: e.activation(junk[:rows, :], xb[:rows, :], AF.Square, accum_out=sm[:, 0:1]),
             reads=[xb_t], writes=[junk_t, small_t[t]])
        P.op("act", lambda e, sm=sm: e.activation(sm[:, 1:2], sm[:, 0:1], AF.Ln, bias=EPS, scale=1.0 / D),
             reads=[small_t[t]], writes=[small_t[t]])
        P.op("act", lambda e, sm=sm: e.activation(sm[:, 2:3], sm[:, 1:2], AF.Exp, scale=-0.5),
             reads=[small_t[t]], writes=[small_t[t]])
        P.op("dve", lambda e, xb=xb, rows=rows, sm=sm: e.tensor_scalar(xb[:rows, :], xb[:rows, :], sm[:, 2:3], None, ALU.mult),
             reads=[xb_t, small_t[t]], writes=[xb_t])
        for g in range(4):
            pb, pb_t = pt.next()
            for j in range(4):
                kc = g * 4 + j
                P.op("pe", lambda e, pb=pb, j=j, kc=kc, xb=xb, rows=rows: e.transpose(
                    pb[:, j, :rows], xb[:rows, kc * 128:(kc + 1) * 128], ident[:rows, :rows]),
                    reads=[xb_t, ident_t], writes=[pb_t])
            for j in range(4):
                kc = g * 4 + j
                if t < NT:
                    m = 0 if t < NT - 1 else 1
                    parts = [(0, 128, m)]
                else:
                    parts = [(0, 2, 0), (2, 4, 1)]
                for (a, b, m) in parts:
                    P.op("act", lambda e, pb=pb, j=j, kc=kc, t=t, a=a, b=b, m=m: e.activation(
                        hT[:, kc, t * 128 + a:t * 128 + b], pb[:, j, a:b], AF.Identity,
                        scale=sc[:, m, kc:kc + 1], bias=sh[:, m, kc:kc + 1]),
                        reads=[pb_t, sc_t, sh_t], writes=[hT_t[t]])

    oq = ["sp"]

    def store(t, col, n, buf, buf_t):
        dap, dtok = dst_fn(t, col, n)
        P.dma("sp", dap, buf[:, :n], reads=[buf_t], writes=[dtok] if dtok is not None else [], semtok=buf_t)

    order = [3, 4, 5, 6, 7, 8, 9, 10, 11, 0, 1, 2, 12, 13, 14, 15]
    for ci, (c0, ncol, kind, ocol) in enumerate([CBS[i] for i in order]):
        wb = wbuf[ci % 2]; wb_t = wbuf_t[ci % 2]
        for hf in range(2):
            P.dma("pool", wb[:, hf * 8:(hf + 1) * 8, :ncol],
                  win_d[hf * 1024:(hf + 1) * 1024, c0:c0 + ncol].rearrange("(kc p) n -> p kc n", p=128),
                  writes=[wb_t[hf]])
        tiles = list(range(NT)) + ([NT] if kind in ("D_c", "D_x") else [])
        for t in tiles:
            rows = 128 if t < NT else 4
            ps, ps_t = pm.next()
            for kc in range(16):
                P.op("pe", lambda e, ps=ps, rows=rows, ncol=ncol, kc=kc, t=t, wb=wb: e.matmul(
                    ps[:rows, :ncol], hT[:, kc, t * 128:t * 128 + rows], wb[:, kc, :ncol], start=(kc == 0), stop=(kc == 15)),
                    reads=[hT_t[t], wb_t[kc // 8]], writes=[ps_t])
            if kind == "A_u":
                P.op("act", lambda e, ps=ps, t=t: e.copy(UB[:, t, :], ps[:, :]), reads=[ps_t], writes=[UB_t[t]])
            elif kind == "A_v":
                ls, ls_t = lnst.next()
                sq, sq_t = tA.next()
                P.op("act", lambda e, ps=ps, sq=sq: e.activation(sq[:], ps[:, :], AF.Square), reads=[ps_t], writes=[sq_t])
                P.op("dve", lambda e, ps=ps, ls=ls: e.tensor_reduce(ls[:, 0:4], ps[:, :].rearrange("p (h d) -> p h d", h=4), AX.X, ALU.add),
                     reads=[ps_t], writes=[ls_t])
                P.op("dve", lambda e, sq=sq, ls=ls: e.tensor_reduce(ls[:, 4:8], sq[:].rearrange("p (h d) -> p h d", h=4), AX.X, ALU.add),
                     reads=[sq_t, ls_t], writes=[ls_t])
                P.op("dve", lambda e, ls=ls: e.tensor_scalar(ls[:, 0:4], ls[:, 0:4], 1.0 / 128, None, ALU.mult), reads=[ls_t], writes=[ls_t])
                P.op("dve", lambda e, ls=ls: e.tensor_tensor(ls[:, 8:12], ls[:, 0:4], ls[:, 0:4], ALU.mult), reads=[ls_t], writes=[ls_t])
                P.op("dve", lambda e, ls=ls: e.scalar_tensor_tensor(ls[:, 8:12], ls[:, 4:8], 1.0 / 128, ls[:, 8:12], ALU.mult, ALU.subtract),
                     reads=[ls_t], writes=[ls_t])
                P.op("act", lambda e, ls=ls: e.activation(ls[:, 8:12], ls[:, 8:12], AF.Ln, bias=EPS), reads=[ls_t], writes=[ls_t])
                P.op("act", lambda e, ls=ls: e.activation(ls[:, 12:16], ls[:, 8:12], AF.Exp, scale=-0.5), reads=[ls_t], writes=[ls_t])
                for h in range(4):
                    P.op("dve", lambda e, ps=ps, ls=ls, h=h, t=t: e.tensor_scalar(
                        VN[:, t, h * 128:(h + 1) * 128], ps[:, h * 128:(h + 1) * 128], ls[:, h:h + 1], ls[:, 12 + h:13 + h],
                        ALU.subtract, ALU.mult), reads=[ps_t, ls_t], writes=[VN_t[t]])
            elif kind == "A_g":
                sg, sg_t = tA.next()
                P.op("act", lambda e, ps=ps, sg=sg: e.activation(sg[:], ps[:, :], AF.Silu), reads=[ps_t], writes=[sg_t])
                px, px_t = pmix.next()
                for h in range(4):
                    P.op("pe", lambda e, px=px, h=h, t=t: e.matmul(
                        px[:, h * 128:(h + 1) * 128], sguw[:, h, :], VN[:, t, h * 128:(h + 1) * 128], start=True, stop=True),
                        reads=[sguw_t, VN_t[t]], writes=[px_t])
                tb, tb_t = tB.next()
                for h in range(4):
                    P.op("dve", lambda e, px=px, h=h, t=t, tb=tb: e.scalar_tensor_tensor(
                        tb[:, h * 128:(h + 1) * 128], px[:, h * 128:(h + 1) * 128], sgub[:, h:h + 1], UB[:, t, h * 128:(h + 1) * 128],
                        ALU.add, ALU.mult), reads=[px_t, sgub_t, UB_t[t]], writes=[tb_t])
                sb_, sb_t = st.next()
                P.op("pool", lambda e, tb=tb, sg=sg, sb_=sb_: e.tensor_tensor(sb_[:], tb[:], sg[:], ALU.mult),
                     reads=[tb_t, sg_t], writes=[sb_t])
                store(t, ocol, 512, sb_, sb_t)
            elif kind == "copy":
                sb_, sb_t = (stb if ocol == OV else st).next()
                if t % 2 == 0:
                    P.op("act", lambda e, ps=ps, sb_=sb_, ncol=ncol: e.copy(sb_[:, :ncol], ps[:, :ncol]), reads=[ps_t], writes=[sb_t])
                else:
                    P.op("dve", lambda e, ps=ps, sb_=sb_, ncol=ncol: e.tensor_copy(sb_[:, :ncol], ps[:, :ncol]), reads=[ps_t], writes=[sb_t])
                store(t, ocol, ncol, sb_, sb_t)
            elif kind == "silu":
                sb_, sb_t = st.next()
                P.op("act", lambda e, ps=ps, sb_=sb_: e.activation(sb_[:], ps[:, :], AF.Silu), reads=[ps_t], writes=[sb_t])
                store(t, ocol, 512, sb_, sb_t)
            elif kind == "rope":
                t1, t1_t = tA.next()
                t2, t2_t = tB.next()
                cosb = rcos[:, t, :].unsqueeze(1).to_broadcast([128, 8, 64])
                P.op("dve", lambda e, ps=ps, t1=t1, cosb=cosb: e.tensor_tensor(
                    t1[:].rearrange("p (g d) -> p g d", g=8), ps[:, :].rearrange("p (g d) -> p g d", g=8), cosb, ALU.mult),
                    reads=[ps_t, rcos_t], writes=[t1_t])
                psv = ps[:, :].rearrange("p (g a b d) -> p g a b d", g=8, a=2, b=2)
                t2v = t2[:].rearrange("p (g a b d) -> p g a b d", g=8, a=2, b=2)
                snv = rsin[:, t, :].rearrange("p (a b d) -> p a b d", a=2, b=2)
                for a in range(2):
                    for b in range(2):
                        sn = snv[:, a, b, :].unsqueeze(1).to_broadcast([128, 8, 16])
                        P.op("dve", lambda e, a=a, b=b, psv=psv, t2v=t2v, sn=sn: e.tensor_tensor(
                            t2v[:, :, a, b, :], psv[:, :, a, 1 - b, :], sn, ALU.mult),
                            reads=[ps_t, rsin_t], writes=[t2_t])
                sb_, sb_t = stb.next()
                P.op("pool", lambda e, t1=t1, t2=t2, sb_=sb_: e.tensor_tensor(sb_[:], t1[:], t2[:], ALU.add),
                     reads=[t1_t, t2_t], writes=[sb_t])
                store(t, ocol, 512, sb_, sb_t)
            elif kind == "D_b":
                P.op("act", lambda e, ps=ps, t=t: e.copy(UB[:, t, :], ps[:, :]), reads=[ps_t], writes=[UB_t[t]])
            elif kind == "D_c":
                if t < NT:
                    P.op("act", lambda e, ps=ps, t=t: e.copy(CG[:, t, :], ps[:, :]), reads=[ps_t], writes=[CG_t[t]])
                else:
                    P.op("act", lambda e, ps=ps: e.copy(ZH[:, :], ps[:4, :]), reads=[ps_t], writes=[ZH_t])
            elif kind == "D_x":
                if t < NT:
                    P.op("dve", lambda e, ps=ps, t=t: e.tensor_tensor(CG[:, t, :], CG[:, t, :], ps[:, :], ALU.mult),
                         reads=[ps_t, CG_t[t]], writes=[CG_t[t]])
                else:
                    P.op("dve", lambda e, ps=ps: e.scalar_tensor_tensor(ZH[:, :], ps[:4, :], hmask[:, 0:1], ZH[:, :], ALU.mult, ALU.mult),
                         reads=[ps_t, ZH_t, hmask_t], writes=[ZH_t])
            elif kind == "D_g":
                sg, sg_t = tA.next()
                P.op("act", lambda e, ps=ps, sg=sg: e.activation(sg[:], ps[:, :], AF.Silu), reads=[ps_t], writes=[sg_t])
                P.op("dve", lambda e, sg=sg, t=t: e.tensor_tensor(UB[:, t, :], UB[:, t, :], sg[:], ALU.mult),
                     reads=[sg_t, UB_t[t]], writes=[UB_t[t]])

            if hook is not None:
                hook(ci, t)

    hTf = hT[:].rearrange("p a b -> p (a b)").bitcast(F32)
    ZMv = hTf[:, 0:NT * WG].rearrange("p (t c) -> p t c", t=NT)
    ZPv = hTf[:, NT * WG:2 * NT * WG].rearrange("p (t c) -> p t c", t=NT)
    zm_t = Tok(); zp_t = Tok()
    P.dma("sp", ZMv[1:128, :, :], CG[0:127, :, :], reads=CG_t, writes=[zm_t] + hT_t)
    P.dma("sp", ZMv[0:1, 1:NT - 1, :], CG[127:128, 0:NT - 2, :], reads=CG_t, writes=[zm_t], semtok=zm_t)
    P.dma("sp", ZMv[0:1, 0, :], ZH[0:1, :], reads=[ZH_t], writes=[zm_t], semtok=zm_t)
    P.dma("sp", ZMv[0:1, NT - 1, :], ZH[2:3, :], reads=[ZH_t], writes=[zm_t], semtok=zm_t)
    P.dma("sp", ZPv[0:127, :, :], CG[1:128, :, :], reads=CG_t, writes=[zp_t] + hT_t)
    P.dma("sp", ZPv[127:128, 0:NT - 2, :], CG[0:1, 1:NT - 1, :], reads=CG_t, writes=[zp_t], semtok=zp_t)
    P.dma("sp", ZPv[127:128, NT - 2, :], ZH[1:2, :], reads=[ZH_t], writes=[zp_t], semtok=zp_t)
    P.dma("sp", ZPv[127:128, NT - 1, :], ZH[3:4, :], reads=[ZH_t], writes=[zp_t], semtok=zp_t)
    for t in range(NT):
        t1, t1_t = tA.next()
        t2, t2_t = tB.next()
        P.op("dve", lambda e, t=t, t1=t1: e.tensor_tensor(t1[:], CG[:, t, :], convw[:, 1, :], ALU.mult),
             reads=[CG_t[t], convw_t], writes=[t1_t])
        P.op("pool", lambda e, t=t, t2=t2: e.tensor_tensor(t2[:], ZMv[:, t, :], convw[:, 0, :], ALU.mult),
             reads=[zm_t, convw_t], writes=[t2_t])
        P.op("dve", lambda e, t1=t1, t2=t2: e.tensor_tensor(t1[:], t1[:], t2[:], ALU.add), reads=[t1_t, t2_t], writes=[t1_t])
        P.op("pool", lambda e, t=t, t2=t2: e.tensor_tensor(t2[:], ZPv[:, t, :], convw[:, 2, :], ALU.mult),
             reads=[zp_t, convw_t], writes=[t2_t])
        P.op("dve", lambda e, t1=t1, t2=t2: e.tensor_tensor(t1[:], t1[:], t2[:], ALU.add), reads=[t1_t, t2_t], writes=[t1_t])
        sb_, sb_t = st.next()
        P.op("dve", lambda e, t=t, t1=t1, sb_=sb_: e.tensor_tensor(sb_[:], t1[:], UB[:, t, :], ALU.mult),
             reads=[t1_t, UB_t[t]], writes=[sb_t])
        store(t, OD, 512, sb_, sb_t)
    if hook is not None:
        hook(None, None)
    P.close_scope()


def emit_C(P, l, ntile, pa_l, obc_loc, x_d, mod_all, gpost_d, wo_d, ident_d, out_d, edge_d=None, stage=None):
    P.open_scope()
    ident = P.sbuf("ident", [128, 128], F32); ident_t = Tok()
    P.dma("sp", ident[:], ident_d[:, :], writes=[ident_t])
    wo = P.sbuf("wo", [128, 16, D], BF16)
    wo_t = [Tok() for _ in range(4)]
    for j in range(4):
        P.dma("pool", wo[:, j * 4:(j + 1) * 4, :], wo_d[j * 512:(j + 1) * 512, :].rearrange("(kc p) n -> p kc n", p=128),
              writes=[wo_t[j]])
    stoks = stage() if stage is not None else [[Tok(), Tok()], [Tok(), Tok()]]
    gp = P.sbuf("gp", [128, D], F32); gp_t = Tok()
    P.dma("sp", gp[:], gpost_d.broadcast_to([128, D]), writes=[gp_t])
    GG = P.sbuf("GG", [128, 2, D], F32); GG_t = Tok()
    for j in range(2):
        for (r, off, ln, dst) in mod_pieces(4096):
            P.dma("sp", GG[:, j, dst:dst + ln], mod_all[r, l, j, off:off + ln].unsqueeze(0).broadcast_to([128, ln]), writes=[GG_t], semtok=GG_t)
    for j in range(2):
        P.op("pool", lambda e, j=j: e.tensor_tensor(GG[:, j, :], GG[:, j, :], gp[:], ALU.mult), reads=[GG_t, gp_t], writes=[GG_t])
    os_ = Rot([P.sbuf("os%d" % i, [128, D], F32) for i in range(2)])
    xs = Rot([P.sbuf("xs%d" % i, [128, D], F32) for i in range(2)])
    oT = Rot([P.sbuf("oT%d" % i, [128, 16, 128], BF16) for i in range(2)])
    ol = Rot([P.sbuf("ol%d" % i, [128, D], F32) for i in range(2)])
    junk = P.sbuf("junk", [128, D], BF16); junk_t = Tok()
    sm = P.sbuf("sm", [128, ntile, 4], F32); sm_t = [Tok() for _ in range(ntile)]
    pt = Rot([P.psum("pt%d" % i, [128, 4, 128], F32) for i in range(3)], psum=True)
    pm = Rot([P.psum("pm%d" % i, [128, 512], F32) for i in range(4)], psum=True)
    for t in range(ntile):
        ob, ob_t = os_.next()
        P.dma("sp", ob[:, 0:512], pa_l[t, :, 0:512], writes=[ob_t])
        P.dma("sp", ob[:, 1536:2048], pa_l[t, :, 512:1024], writes=[ob_t], semtok=ob_t)
        for j in range(2):
            P.dma("sp", ob[:, 512 + j * 512:1024 + j * 512].rearrange("p (h c) -> p h c", h=4),
                  obc_loc[:, j, t * 128:(t + 1) * 128, :].rearrange("h p c -> p h c"), reads=[stoks[j][0 if t < 8 else 1]],
                  writes=[ob_t], semtok=ob_t)
        xb, xb_t = xs.next()
        P.dma("sp", xb[:], x_d[t], writes=[xb_t])
        ot, ot_t = oT.next()
        for g in range(4):
            pb, pb_t = pt.next()
            for j in range(4):
                kc = g * 4 + j
                P.op("pe", lambda e, pb=pb, j=j, kc=kc, ob=ob: e.transpose(pb[:, j, :], ob[:, kc * 128:(kc + 1) * 128], ident[:]),
                     reads=[ob_t, ident_t], writes=[pb_t])
            if g % 2 == 0:
                P.op("act", lambda e, pb=pb, g=g, ot=ot: e.copy(ot[:, g * 4:(g + 1) * 4, :], pb[:]), reads=[pb_t], writes=[ot_t])
            else:
                P.op("dve", lambda e, pb=pb, g=g, ot=ot: e.tensor_copy(ot[:, g * 4:(g + 1) * 4, :], pb[:]), reads=[pb_t], writes=[ot_t])
        olb, olb_t = ol.next()
        for cb in range(4):
            ps, ps_t = pm.next()
            for kc in range(16):
                P.op("pe", lambda e, ps=ps, kc=kc, cb=cb, ot=ot: e.matmul(
                    ps[:, :], ot[:, kc, :], wo[:, kc, cb * 512:(cb + 1) * 512], start=(kc == 0), stop=(kc == 15)),
                    reads=[ot_t, wo_t[kc // 4]], writes=[ps_t])
            if cb % 2 == 0:
                P.op("dve", lambda e, ps=ps, cb=cb, olb=olb: e.tensor_copy(olb[:, cb * 512:(cb + 1) * 512], ps[:, :]), reads=[ps_t], writes=[olb_t])
            else:
                P.op("act", lambda e, ps=ps, cb=cb, olb=olb: e.copy(olb[:, cb * 512:(cb + 1) * 512], ps[:, :]), reads=[ps_t], writes=[olb_t])
        s_ = sm[:, t, :]
        P.op("act", lambda e, olb=olb, s_=s_: e.activation(junk[:], olb[:], AF.Square, accum_out=s_[:, 0:1]),
             reads=[olb_t], writes=[junk_t, sm_t[t]])
        P.op("act", lambda e, s_=s_: e.activation(s_[:, 1:2], s_[:, 0:1], AF.Ln, bias=EPS, scale=1.0 / D), reads=[sm_t[t]], writes=[sm_t[t]])
        P.op("act", lambda e, s_=s_: e.activation(s_[:, 2:3], s_[:, 1:2], AF.Exp, scale=-0.5), reads=[sm_t[t]], writes=[sm_t[t]])
        gi = 0 if t < 8 else 1
        P.op("dve", lambda e, olb=olb, s_=s_, gi=gi: e.scalar_tensor_tensor(olb[:], olb[:], s_[:, 2:3], GG[:, gi, :], ALU.mult, ALU.mult),
             reads=[olb_t, sm_t[t], GG_t], writes=[olb_t])
        P.op("pool", lambda e, olb=olb, xb=xb: e.tensor_tensor(xb[:], xb[:], olb[:], ALU.add), reads=[olb_t, xb_t], writes=[xb_t])
        P.dma("sp", out_d[t], xb[:], reads=[xb_t], semtok=xb_t)
        if edge_d is not None:
            for (tt, prow, er) in ((0, 0, 0), (7, 127, 1), (8, 0, 2), (8, 127, 3)):
                if tt == t:
                    P.dma("sp", edge_d[er:er + 1, :], xb[prow:prow + 1, :], reads=[xb_t], semtok=xb_t)
    P.close_scope()


NTOK = CTX + SEQ
NKB = NTOK // 128


def emit_attn(P, l, has_ctxq, at_loc, at_g, lam_d, subg_d, ident_d, out_d, pre=None):
    lam_init = 0.8 - 0.6 * math.exp(-0.3 * l)
    P.open_scope()
    ident32 = P.sbuf("ident32", [128, 128], F32); ident32_t = Tok()
    P.dma("sp", ident32[:], ident_d[:, :], writes=[ident32_t])
    ident = P.sbuf("ident", [128, 128], BF16); ident_t = Tok()
    P.op("dve", lambda e: e.tensor_copy(ident[:], ident32[:]), reads=[ident32_t], writes=[ident_t])
    qT = P.sbuf("qT", [128, NTOK], BF16); qT_t = Tok()
    KT = [P.sbuf("KT%d" % m, [128, NTOK], BF16) for m in range(2)]
    KT_t = [Tok(), Tok()]
    for m in range(2):
        P.op("dve", lambda e, m=m: e.memset(KT[m][(1 - m) * 64:(2 - m) * 64, :], 0.0), writes=[KT_t[m]])
    QK = P.sbuf("QKtm", [128, 2, NKB, 128], BF16); QK_t = [Tok(), Tok()]
    for j in range(2):
        P.dma("sp", QK[:, j, :, :], at_loc[:, j * 128:(j + 1) * 128].rearrange("(t p) d -> p t d", p=128), writes=[QK_t[j]])
    ptr = P.psum("ptr", [128, 4, 128], BF16); ptr_t = Tok(psum=True)
    for g in range(0, NKB, 2):
        for j in range(2):
            for i in range(2):
                P.op("pe", lambda e, j=j, i=i, g=g: e.transpose(ptr[:, j * 2 + i, :], QK[:, j, g + i, :], ident[:]),
                     reads=[QK_t[j], ident_t], writes=[ptr_t])
        P.op("act", lambda e, g=g: e.copy(qT[:, g * 128:(g + 2) * 128], ptr[:, 0:2, :].rearrange("p a b -> p (a b)")), reads=[ptr_t], writes=[qT_t])
        P.op("dve", lambda e, g=g: e.tensor_copy(KT[0][0:64, g * 128:(g + 2) * 128], ptr[0:64, 2:4, :].rearrange("p a b -> p (a b)")), reads=[ptr_t], writes=[KT_t[0]])
        P.op("act", lambda e, g=g: e.copy(KT[1][64:128, g * 128:(g + 2) * 128], ptr[64:128, 2:4, :].rearrange("p a b -> p (a b)")), reads=[ptr_t], writes=[KT_t[1]])
    V = P.sbuf("V", [128, NKB, 129], BF16); V_t = Tok()
    P.op("dve", lambda e: e.memset(V[:, :, 128:129], 1.0), writes=[V_t])
    P.dma("sp", V[:, :, 0:128], at_loc[:, 256:384].rearrange("(t p) d -> p t d", p=128), writes=[V_t])
    GS = P.sbuf("GS", [128, NKB, 128], F32); GS_t = Tok()
    P.dma("sp", GS[:], at_g.rearrange("(t p) d -> p t d", p=128), writes=[GS_t])
    subg = P.sbuf("subg", [128, 128], F32); subg_t = Tok()
    P.dma("sp", subg[:], subg_d.broadcast_to([128, 128]), writes=[subg_t])
    lamv = P.sbuf("lamv", [128, 256], F32); lamv_t = Tok()
    P.dma("sp", lamv[:], lam_d.broadcast_to([128, 256]), writes=[lamv_t])
    if pre is not None:
        pre()
    lsm = P.sbuf("lsm", [128, 8], F32); lsm_t = Tok()
    ljunk = P.sbuf("ljunk", [128, 64], F32); ljunk_t = Tok()
    for j in range(2):
        P.op("dve", lambda e, j=j: e.scalar_tensor_tensor(ljunk[:], lamv[:, j * 128:j * 128 + 64], 1.0, lamv[:, j * 128 + 64:j * 128 + 128],
                                                         ALU.mult, ALU.mult, accum_out=lsm[:, j:j + 1]),
             reads=[lamv_t], writes=[ljunk_t, lsm_t])
    P.op("act", lambda e: e.activation(lsm[:, 2:4], lsm[:, 0:2], AF.Exp), reads=[lsm_t], writes=[lsm_t])
    P.op("dve", lambda e: e.tensor_tensor(lsm[:, 4:5], lsm[:, 3:4], lsm[:, 2:3], ALU.subtract), reads=[lsm_t], writes=[lsm_t])
    P.op("dve", lambda e: e.tensor_scalar(lsm[:, 4:5], lsm[:, 4:5], -lam_init, None, ALU.add), reads=[lsm_t], writes=[lsm_t])
    P.op("dve", lambda e: e.scalar_tensor_tensor(GS[:], GS[:], 1.0 - lam_init, subg[:].unsqueeze(1).to_broadcast([128, NKB, 128]),
                                                ALU.mult, ALU.mult), reads=[GS_t, subg_t], writes=[GS_t])

    pss = Rot([P.psum("pss%d" % i, [128, 512], F32) for i in range(3)], psum=True)
    po = [P.psum("po%d" % i, [128, 512], F32) for i in range(4)]
    po_t = [Tok(psum=True) for _ in range(4)]
    pT = Rot([P.sbuf("pT%d" % i, [128, 512], BF16) for i in range(3)])
    o0 = P.sbuf("o0", [128, 4, 128], F32); o0_t = [Tok() for _ in range(4)]
    osb = Rot([P.sbuf("osb%d" % i, [128, 128], F32) for i in range(2)])
    ost = Rot([P.sbuf("ost%d" % i, [128, 128], F32) for i in range(3)])
    sjunk = P.sbuf("sjunk", [128, 128], F32); sjunk_t = Tok()
    rs = Rot([P.sbuf("rs%d" % i, [128, 8], F32) for i in range(4)])

    groups = []
    for i in range(8):
        for m in range(2):
            groups.append((CTX + i * 512, 512, list(range(NKB)), m))
    if has_ctxq:
        for m in range(2):
            groups.append((0, 256, [0, 1], m))
    steps = []
    for gi, (q0, nq, kbs, m) in enumerate(groups):
        for kb in kbs:
            steps.append((gi, kb))
    held = {}

    def emit_qk(si):
        gi, kb = steps[si]
        q0, nq, kbs, m = groups[gi]
        ps, ps_t = pss.next()
        P.op("pe", lambda e, ps=ps, nq=nq, m=m, kb=kb, q0=q0: e.matmul(
            ps[:, :nq], KT[m][:, kb * 128:(kb + 1) * 128], qT[:, q0:q0 + nq], start=True, stop=True),
            reads=[KT_t[m], qT_t], writes=[ps_t])
        held[si] = (ps, ps_t)

    LOOK = 2
    for si in range(min(LOOK, len(steps))):
        emit_qk(si)
    for si, (gi, kb) in enumerate(steps):
        q0, nq, kbs, m = groups[gi]
        if si + LOOK < len(steps):
            emit_qk(si + LOOK)
        ps, ps_t = held.pop(si)
        pt_, pt_t = pT.next()
        P.op("act", lambda e, ps=ps, pt_=pt_, nq=nq: e.activation(pt_[:, :nq], ps[:, :nq], AF.Exp, scale=0.125),
             reads=[ps_t], writes=[pt_t])
        nqs = nq // 128
        for qs in range(nqs):
            P.op("pe", lambda e, qs=qs, pt_=pt_, kb=kb, kbs=kbs: e.matmul(
                po[qs][:, 0:129], pt_[:, qs * 128:(qs + 1) * 128], V[:, kb, :], start=(kb == kbs[0]), stop=(kb == kbs[-1])),
                reads=[pt_t, V_t], writes=[po_t[qs]])
        if kb == kbs[-1]:
            for qs in range(nqs):
                r, r_t = rs.next()
                tile = (q0 // 128) + qs
                P.op("dve", lambda e, r=r, qs=qs: e.reciprocal(r[:, 0:1], po[qs][:, 128:129]), reads=[po_t[qs]], writes=[r_t])
                if m == 0:
                    P.op("dve", lambda e, r=r, qs=qs: e.tensor_scalar(o0[:, qs, :], po[qs][:, 0:128], r[:, 0:1], None, ALU.mult),
                         reads=[po_t[qs], r_t], writes=[o0_t[qs]])
                else:
                    ob, ob_t = osb.next()
                    P.op("dve", lambda e, r=r: e.tensor_tensor(r[:, 1:2], r[:, 0:1], lsm[:, 4:5], ALU.mult), reads=[r_t, lsm_t], writes=[r_t])
                    P.op("dve", lambda e, r=r, qs=qs, ob=ob: e.scalar_tensor_tensor(ob[:], po[qs][:, 0:128], r[:, 1:2], o0[:, qs, :], ALU.mult, ALU.add),
                         reads=[po_t[qs], r_t, o0_t[qs]], writes=[ob_t])
                    P.op("dve", lambda e, r=r, ob=ob: e.scalar_tensor_tensor(sjunk[:], ob[:], 1.0, ob[:], ALU.mult, ALU.mult, accum_out=r[:, 2:3]),
                         reads=[ob_t], writes=[sjunk_t, r_t])
                    P.op("act", lambda e, r=r: e.activation(r[:, 3:4], r[:, 2:3], AF.Ln, bias=EPS, scale=1.0 / 128), reads=[r_t], writes=[r_t])
                    P.op("act", lambda e, r=r: e.activation(r[:, 4:5], r[:, 3:4], AF.Exp, scale=-0.5), reads=[r_t], writes=[r_t])
                    st_, st_t = ost.next()
                    P.op("dve", lambda e, r=r, ob=ob, st_=st_, tile=tile: e.scalar_tensor_tensor(st_[:], ob[:], r[:, 4:5], GS[:, tile, :], ALU.mult, ALU.mult),
                         reads=[ob_t, r_t, GS_t], writes=[st_t])
                    P.dma("sp", out_d[tile, :, :], st_[:], reads=[st_t], semtok=st_t)
    P.close_scope()


RW_F = 480


def rw_row0(c):
    return 1 + c * 128 if c < 2 else 259 + (c - 2) * 128


RW_ROWS = NTOK + 4
RW_COLS = 704


def emit_rwkv(P, nch, order_f, order_b, rw_loc, mu_d, w2e_d, a2e_d, vecs_d, cst_d, out_d):
    ntok = nch * 128
    RB = BF16
    P.open_scope()
    cst = P.sbuf("cst", [128, 8, 128], F32); cst_t = Tok()
    P.dma("sp", cst[:], cst_d.rearrange("c p q -> p c q"), writes=[cst_t])
    ident = cst[:, 0, :]; ones = cst[:, 1, :]
    TRI = [cst[:, 2, :], cst[:, 3, :]]
    MS = [cst[:, 4, :], cst[:, 6, :]]
    MI = [cst[:, 5, :], cst[:, 7, :]]
    MST = [cst[:, 6, :], cst[:, 4, :]]
    mhalf = P.sbuf("mhalf", [128, 2], F32); mhalf_t = Tok()
    P.op("pool", lambda e: e.memset(mhalf[:], -0.5), writes=[mhalf_t])
    identb = P.sbuf("identb", [128, 128], RB); identb_t = Tok()
    P.op("dve", lambda e: e.tensor_copy(identb[:], ident), reads=[cst_t], writes=[identb_t])
    mu = P.sbuf("mu", [128, 2, RW_F], F32); mu_t = Tok()
    P.dma("sp", mu[:].rearrange("p a b -> p (a b)"), mu_d.rearrange("a b -> (a b)").unsqueeze(0).broadcast_to([128, 2 * RW_F]), writes=[mu_t])
    vecs = P.sbuf("vecs", [128, 8, 128], F32); vecs_t = Tok()
    P.dma("sp", vecs[:].rearrange("p a b -> p (a b)"), vecs_d.rearrange("a b -> (a b)").unsqueeze(0).broadcast_to([128, 8 * 128]), writes=[vecs_t])
    w2e = P.sbuf("w2e", [65, 2, 128], F32); w2e_t = Tok()
    P.dma("sp", w2e[:], w2e_d.rearrange("d k n -> k d n"), writes=[w2e_t])
    a2e = P.sbuf("a2e", [33, 2, 128], F32); a2e_t = Tok()
    P.dma("sp", a2e[:], a2e_d.rearrange("d k n -> k d n"), writes=[a2e_t])
    G = P.sbuf("G", [128, nch, 128], F32); G_t = Tok()
    P.dma("sp", G[:, 0:2, :], rw_loc[1:257, 384:512].rearrange("(c p) n -> p c n", p=128), writes=[G_t])
    P.dma("sp", G[:, 2:nch, :], rw_loc[259:259 + (nch - 2) * 128, 384:512].rearrange("(c p) n -> p c n", p=128), writes=[G_t], semtok=G_t)
    Y = [P.sbuf("Y%d" % d, [128, nch, 128], F32) for d in range(2)]
    Y_t = [[Tok() for _ in range(nch)] for d in range(2)]
    BON1 = P.sbuf("BON", [128, nch, 128], F32)
    BON = [BON1, BON1]
    BON1_t = [Tok() for _ in range(nch)]
    BON_t = [BON1_t, BON1_t]
    P.op("pool", lambda e: e.memset(BON1[:], 0.0), writes=BON1_t)
    ST32 = [P.sbuf("ST32_%d" % d, [128, 64], F32) for d in range(2)]
    STb = [P.sbuf("STb_%d" % d, [128, 64], RB) for d in range(2)]
    ST_t = [Tok(), Tok()]; STb_t = [Tok(), Tok()]
    for d in range(2):
        P.op("pool", lambda e, d=d: e.memset(ST32[d][:], 0.0), writes=[ST_t[d]])
        P.op("pool", lambda e, d=d: e.memset(STb[d][:], 0.0), writes=[STb_t[d]])

    NB_ = 2

    def dbuf(name, shape, dt, zero=False, one_rows=None):
        bufs = []
        for d in range(2):
            lst = []
            for i in range(NB_):
                b = P.sbuf("%s_%d_%d" % (name, d, i), shape, dt)
                t = Tok()
                if zero:
                    P.op("pool", lambda e, b=b: e.memset(b[:], 0.0), writes=[t])
                if one_rows is not None:
                    P.op("pool", lambda e, b=b: e.memset(b[one_rows[0]:one_rows[1], :], 1.0), writes=[t])
                lst.append((b, t))
            bufs.append(lst)
        return bufs

    X = dbuf("X", [128, RW_F], F32); XP = dbuf("XP", [128, RW_F], F32); Z = dbuf("Z", [128, RW_F], F32)
    TW = dbuf("TW", [128, 64], F32)
    TWT = dbuf("TWT", [65, 128], F32, one_rows=(64, 65)); ZAT = dbuf("ZAT", [33, 128], F32, one_rows=(32, 33))
    LW = dbuf("LW", [128, 128], F32); ETA = dbuf("ETA", [128, 128], F32)
    KK = dbuf("KK", [128, 128], F32); KP = dbuf("KP", [128, 128], F32); BB = dbuf("BB", [128, 128], F32)
    TMP = dbuf("TMP", [128, 128], F32); TMP2 = dbuf("TMP2", [128, 128], F32)
    SM = dbuf("SM", [128, 16], F32)
    CSs = dbuf("CSs", [128, 128], F32); D3 = dbuf("D3", [128, 128], F32); D4 = dbuf("D4", [128, 128], F32)
    E1 = dbuf("E1", [128, 128], F32); E2 = dbuf("E2", [128, 128], F32); E3 = dbuf("E3", [128, 128], F32); E4 = dbuf("E4", [128, 128], F32)
    GCOL = dbuf("GCOL", [128, 1], F32)
    AH = dbuf("AH", [128, 128], RB); BH = dbuf("BH", [128, 128], RB); KH = dbuf("KH", [128, 128], RB); RH = dbuf("RH", [128, 128], RB)
    BTz = [dbuf("BTz%d" % h, [128, 128], RB, zero=True) for h in range(2)]
    KTz = [dbuf("KTz%d" % h, [128, 128], RB, zero=True) for h in range(2)]
    VB = dbuf("VB", [128, 128], RB)
    BHT = dbuf("BHT", [128, 128], RB); KHT = dbuf("KHT", [128, 128], RB)
    AHTz = [dbuf("AHTz%d" % h, [128, 128], RB, zero=True) for h in range(2)]
    RHTz = [dbuf("RHTz%d" % h, [128, 128], RB, zero=True) for h in range(2)]
    Nm = [[dbuf("Nm%d_%d" % (h, i), [128, 128], F32) for i in range(2)] for h in range(2)]
    Mm = [[dbuf("Mm%d_%d" % (h, i), [128, 128], F32) for i in range(2)] for h in range(2)]
    Pm = [[dbuf("Pm%d_%d" % (h, i), [128, 128], F32) for i in range(2)] for h in range(2)]
    AH32 = dbuf("AH32", [128, 128], F32)
    AAK = [dbuf("AAK%d" % h, [128, 128], RB) for h in range(2)]
    ARB = [dbuf("ARB%d" % h, [128, 128], RB) for h in range(2)]
    ARK = [dbuf("ARK%d" % h, [128, 128], RB) for h in range(2)]
    X2 = [dbuf("X2%d" % h, [128, 64], F32) for h in range(2)]
    WTz = [dbuf("WTz%d" % h, [128, 128], RB, zero=True) for h in range(2)]
    US = [dbuf("US%d" % h, [128, 64], RB) for h in range(2)]

    pa_ = Rot([P.psum("pa%d" % i, [128, 512], F32) for i in range(2)], psum=True)
    pb_ = Rot([P.psum("pb%d" % i, [128, 512], F32) for i in range(1)], psum=True)
    ptb = P.psum("ptb", [128, 4, 128], RB); ptb_t = Tok(psum=True)
    pdbl = BankPool([P.psum("pdbl%d" % i, [128, 512], F32) for i in range(4)])

    def vec(i):
        return vecs[:, i, :]

    def unit(d, c, it):
        par = it % NB_
        g = lambda buf: buf[d][par]
        x, x_t = g(X); xp, xp_t = g(XP); z, z_t = g(Z)
        r0_ = rw_row0(c)
        rp_ = r0_ - 1 if d == 0 else r0_ + 1
        P.dma("sp", x[:, 0:384], rw_loc[r0_:r0_ + 128, 0:384], writes=[x_t])
        yield
        P.dma("sp", x[:, 384:480], rw_loc[r0_:r0_ + 128, 512 + d * 96:608 + d * 96], writes=[x_t], semtok=x_t)
        yield
        P.dma("sp", xp[:, 0:384], rw_loc[rp_:rp_ + 128, 0:384], writes=[xp_t])
        yield
        P.dma("sp", xp[:, 384:480], rw_loc[rp_:rp_ + 128, 512 + d * 96:608 + d * 96], writes=[xp_t], semtok=xp_t)
        yield
        P.op("dve", lambda e: e.tensor_tensor(xp[:], xp[:], x[:], ALU.subtract), reads=[x_t, xp_t], writes=[xp_t])
        yield
        P.op("pool", lambda e: e.tensor_tensor(xp[:], xp[:], mu[:, d, :], ALU.mult), reads=[xp_t, mu_t], writes=[xp_t])
        yield
        P.op("dve", lambda e: e.tensor_tensor(z[:], x[:], xp[:], ALU.add), reads=[x_t, xp_t], writes=[z_t])
        yield
        zr = z[:, 0:128]; zk = z[:, 128:256]; zv = z[:, 256:384]; zw = z[:, 384:448]; za = z[:, 448:480]
        yield
        tw, tw_t = g(TW); twt, twt_t = g(TWT); zat, zat_t = g(ZAT)
        P.op("act", lambda e: e.activation(tw[:], zw, AF.Tanh), reads=[z_t], writes=[tw_t])
        yield
        p1, p1_t = pa_.next()
        P.op("pe", lambda e: e.transpose(p1[0:64, 0:128], tw[:], ident), reads=[tw_t, cst_t], writes=[p1_t])
        P.op("pe", lambda e: e.transpose(p1[0:32, 128:256], za, ident), reads=[z_t, cst_t], writes=[p1_t])
        P.op("act", lambda e: e.copy(twt[0:64, :], p1[0:64, 0:128]), reads=[p1_t], writes=[twt_t])
        P.op("act", lambda e: e.copy(zat[0:32, :], p1[0:32, 128:256]), reads=[p1_t], writes=[zat_t])
        p2, p2_t = pa_.next()
        P.op("pe", lambda e: e.matmul(p2[:, 0:128], twt[:, :], w2e[:, d, :], start=True, stop=True), reads=[twt_t, w2e_t], writes=[p2_t])
        P.op("pe", lambda e: e.matmul(p2[:, 128:256], zat[:, :], a2e[:, d, :], start=True, stop=True), reads=[zat_t, a2e_t], writes=[p2_t])
        lw, lw_t = g(LW); eta, eta_t = g(ETA)
        P.op("act", lambda e: e.activation(lw[:], p2[:, 0:128], AF.Tanh, scale=0.5), reads=[p2_t], writes=[lw_t])
        P.op("act", lambda e: e.activation(eta[:], p2[:, 128:256], AF.Tanh, scale=0.5), reads=[p2_t], writes=[eta_t])
        P.op("pool", lambda e: e.tensor_scalar(lw[:], lw[:], 1.0, -0.5 * math.exp(-0.5), ALU.add, ALU.mult), reads=[lw_t], writes=[lw_t])
        yield
        P.op("pool", lambda e: e.tensor_scalar(eta[:], eta[:], 1.0, 0.5, ALU.add, ALU.mult), reads=[eta_t], writes=[eta_t])
        yield
        kk, kk_t = g(KK); kp, kp_t = g(KP); bb, bb_t = g(BB); tmp, tmp_t = g(TMP); tmp2, tmp2_t = g(TMP2); sm, sm_t = g(SM)
        P.op("dve", lambda e: e.tensor_tensor(kk[:], zk, vec(3 * d + 0), ALU.mult), reads=[z_t, vecs_t], writes=[kk_t])
        yield
        P.op("dve", lambda e: e.tensor_tensor(tmp[:], kk[:], kk[:], ALU.mult), reads=[kk_t], writes=[tmp_t])
        yield
        P.op("dve", lambda e: e.tensor_reduce(sm[:, 0:2], tmp[:].rearrange("p (h k) -> p h k", h=2), AX.X, ALU.add), reads=[tmp_t], writes=[sm_t])
        yield
        P.op("dve", lambda e: e.tensor_scalar(sm[:, 0:2], sm[:, 0:2], 1e-19, None, ALU.max), reads=[sm_t], writes=[sm_t])
        yield
        P.op("pool", lambda e: e.tensor_tensor(sm[:, 4:6], sm[:, 0:2], mhalf[:, 0:2], ALU.pow), reads=[sm_t, mhalf_t], writes=[sm_t])
        yield
        for h in range(2):
            P.op("dve", lambda e, h=h: e.tensor_scalar(kk[:, h * 64:(h + 1) * 64], kk[:, h * 64:(h + 1) * 64], sm[:, 4 + h:5 + h], None, ALU.mult),
                 reads=[kk_t, sm_t], writes=[kk_t])
        P.op("dve", lambda e: e.scalar_tensor_tensor(tmp[:], eta[:], -1.0, vec(3 * d + 1), ALU.add, ALU.mult), reads=[eta_t, vecs_t, tmp_t], writes=[tmp_t])
        yield
        P.op("dve", lambda e: e.scalar_tensor_tensor(kp[:], tmp[:], 1.0, zk, ALU.add, ALU.mult), reads=[tmp_t, z_t], writes=[kp_t])
        yield
        P.op("pool", lambda e: e.tensor_tensor(bb[:], kk[:], eta[:], ALU.mult), reads=[kk_t, eta_t], writes=[bb_t])
        yield
        P.op("pool", lambda e: e.tensor_tensor(tmp2[:], zr, kp[:], ALU.mult), reads=[z_t, kp_t], writes=[tmp2_t])
        P.op("pool", lambda e: e.tensor_tensor(tmp2[:], tmp2[:], vec(3 * d + 2), ALU.mult), reads=[tmp2_t, vecs_t], writes=[tmp2_t])
        P.op("dve", lambda e: e.tensor_reduce(sm[:, 6:8], tmp2[:].rearrange("p (h k) -> p h k", h=2), AX.X, ALU.add), reads=[tmp2_t, sm_t], writes=[sm_t])
        for h in range(2):
            P.op("dve", lambda e, h=h: e.scalar_tensor_tensor(BON[d][:, c, h * 64:(h + 1) * 64], z[:, 256 + h * 64:256 + (h + 1) * 64], sm[:, 6 + h:7 + h],
                                                             BON[d][:, c, h * 64:(h + 1) * 64], ALU.mult, ALU.add),
                 reads=[z_t, sm_t, BON_t[d][c]], writes=[BON_t[d][c]])
        yield
        p3, p3_t = pa_.next()
        P.op("pe", lambda e: e.matmul(p3[:, 0:128], TRI[d], lw[:], start=True, stop=True), reads=[cst_t, lw_t], writes=[p3_t])
        P.op("pe", lambda e: e.matmul(p3[:, 128:256], ones, lw[:], start=True, stop=True), reads=[cst_t, lw_t], writes=[p3_t])
        P.op("pe", lambda e: e.matmul(p3[:, 256:257], lw[:], ones[:, 0:1], start=True, stop=True), reads=[cst_t, lw_t], writes=[p3_t])
        css, css_t = g(CSs); d3, d3_t = g(D3); d4, d4_t = g(D4)
        e1, e1_t = g(E1); e2, e2_t = g(E2); e3, e3_t = g(E3); e4, e4_t = g(E4); gcol, gcol_t = g(GCOL)
        P.op("act", lambda e: e.copy(css[:], p3[:, 0:128]), reads=[p3_t], writes=[css_t])
        P.op("act", lambda e: e.activation(gcol[:], p3[:, 256:257], AF.Exp), reads=[p3_t], writes=[gcol_t])
        P.op("dve", lambda e: e.tensor_tensor(d4[:], p3[:, 128:256], css[:], ALU.subtract), reads=[p3_t, css_t], writes=[d4_t])
        P.op("pool", lambda e: e.tensor_tensor(d3[:], css[:], lw[:], ALU.subtract), reads=[css_t, lw_t], writes=[d3_t])
        yield
        P.op("act", lambda e: e.activation(e1[:], css[:], AF.Exp), reads=[css_t], writes=[e1_t])
        yield
        P.op("act", lambda e: e.activation(e2[:], css[:], AF.Exp, scale=-1.0), reads=[css_t], writes=[e2_t])
        yield
        P.op("act", lambda e: e.activation(e3[:], d3[:], AF.Exp), reads=[d3_t], writes=[e3_t])
        yield
        P.op("act", lambda e: e.activation(e4[:], d4[:], AF.Exp), reads=[d4_t], writes=[e4_t])
        yield
        ah, ah_t = g(AH); bh, bh_t = g(BH); kh, kh_t = g(KH); rh, rh_t = g(RH); vb, vb_t = g(VB)
        ah32, ah32_t = g(AH32)
        P.op("dve", lambda e: e.scalar_tensor_tensor(ah32[:], kk[:], -1.0, e3[:], ALU.mult, ALU.mult), reads=[kk_t, e3_t], writes=[ah32_t])
        yield
        P.op("pool", lambda e: e.tensor_copy(ah[:], ah32[:]), reads=[ah32_t], writes=[ah_t])
        yield
        P.op("pool", lambda e: e.tensor_tensor(bh[:], bb[:], e2[:], ALU.mult), reads=[bb_t, e2_t], writes=[bh_t])
        yield
        P.op("dve", lambda e: e.tensor_tensor(kh[:], kp[:], e2[:], ALU.mult), reads=[kp_t, e2_t], writes=[kh_t])
        yield
        P.op("pool", lambda e: e.tensor_tensor(rh[:], zr, e1[:], ALU.mult), reads=[z_t, e1_t], writes=[rh_t])
        yield
        P.op("act", lambda e: e.copy(vb[:], zv), reads=[z_t], writes=[vb_t])
        yield
        btz = [g(BTz[h]) for h in range(2)]; ktz = [g(KTz[h]) for h in range(2)]
        for h in range(2):
            hs = slice(h * 64, (h + 1) * 64)
            P.op("dve", lambda e, h=h, hs=hs: e.tensor_tensor(btz[h][0][:, hs], bb[:, hs], e4[:, hs], ALU.mult), reads=[bb_t, e4_t], writes=[btz[h][1]])
            P.op("pool", lambda e, h=h, hs=hs: e.tensor_tensor(ktz[h][0][:, hs], kp[:, hs], e4[:, hs], ALU.mult), reads=[kp_t, e4_t], writes=[ktz[h][1]])
        yield
        bht, bht_t = g(BHT); kht, kht_t = g(KHT)
        ahtz = [g(AHTz[h]) for h in range(2)]; rhtz = [g(RHTz[h]) for h in range(2)]
        for j, (src, src_t) in enumerate(((ah, ah_t), (bh, bh_t), (kh, kh_t), (rh, rh_t))):
            P.op("pe", lambda e, j=j, src=src: e.transpose(ptb[:, j, :], src[:], identb[:]), reads=[src_t, identb_t], writes=[ptb_t])
        P.op("act", lambda e: e.copy(bht[:], ptb[:, 1, :]), reads=[ptb_t], writes=[bht_t])
        P.op("act", lambda e: e.copy(kht[:], ptb[:, 2, :]), reads=[ptb_t], writes=[kht_t])
        for h in range(2):
            hs = slice(h * 64, (h + 1) * 64)
            P.op("dve", lambda e, h=h, hs=hs: e.tensor_copy(ahtz[h][0][hs, :], ptb[hs, 0, :]), reads=[ptb_t], writes=[ahtz[h][1]])
            P.op("act", lambda e, h=h, hs=hs: e.copy(rhtz[h][0][hs, :], ptb[hs, 3, :]), reads=[ptb_t], writes=[rhtz[h][1]])
        yield
        hd = []
        for h in range(2):
            n0, n0_t = g(Nm[h][0]); m0, m0_t = g(Mm[h][0]); pm0, pm0_t = g(Pm[h][0])
            aak, aak_t = g(AAK[h]); arb, arb_t = g(ARB[h]); ark, ark_t = g(ARK[h])
            specs = [
                (bht, bht_t, ahtz[h][0], ahtz[h][1], MS[d], n0, n0_t),
                (ahtz[h][0], ahtz[h][1], bht, bht_t, MST[d], m0, m0_t),
                (kht, kht_t, ahtz[h][0], ahtz[h][1], MS[d], aak, aak_t),
                (bht, bht_t, rhtz[h][0], rhtz[h][1], MI[d], arb, arb_t),
                (kht, kht_t, rhtz[h][0], rhtz[h][1], MI[d], ark, ark_t),
            ]
            for (lt, lt_t, rt, rt_t, msk, dst, dst_t) in specs:
                pp, pp_t = pa_.next()
                P.op("pe", lambda e, pp=pp, lt=lt, rt=rt: e.matmul(pp[:, 0:128], lt[:], rt[:], start=True, stop=True), reads=[lt_t, rt_t], writes=[pp_t])
                P.op("dve", lambda e, pp=pp, msk=msk, dst=dst: e.tensor_tensor(dst[:], pp[:, 0:128], msk, ALU.mult), reads=[pp_t, cst_t], writes=[dst_t])
            P.op("pool", lambda e, pm0=pm0, n0=n0: e.tensor_tensor(pm0[:], n0[:], ident, ALU.add), reads=[n0_t, cst_t], writes=[pm0_t])
            hd.append(dict(n=(n0, n0_t), m=(m0, m0_t), p=(pm0, pm0_t), aak=(aak, aak_t), arb=(arb, arb_t), ark=(ark, ark_t)))
        yield
        for i in range(6):
            cur = []
            while len(pdbl.free) < 2:
                yield
            for h in range(2):
                st = hd[h]
                (n, n_t), (m, m_t), (pm, pm_t) = st["n"], st["m"], st["p"]
                m2, m2_t = g(Mm[h][(i + 1) % 2]); n2, n2_t = g(Nm[h][(i + 1) % 2]); p2_, p2__t = g(Pm[h][(i + 1) % 2])
                bi = pdbl.try_get()
                pp, pp_t = pdbl.items[bi]
                P.op("pe", lambda e, pp=pp, n=n, m=m: e.matmul(pp[:, 0:128], n[:], m[:], start=True, stop=True), reads=[n_t, m_t], writes=[pp_t])
                cur.append((pp, pp_t, m2, m2_t, n2, n2_t, p2_, p2__t, pm, pm_t, bi))
            yield
            for h in range(2):
                pp, pp_t, m2, m2_t, n2, n2_t, p2_, p2__t, pm, pm_t, bi = cur[h]
                P.op("act", lambda e, pp=pp, m2=m2: e.copy(m2[:], pp[:, 0:128]), reads=[pp_t], writes=[m2_t])
            yield
            for h in range(2):
                pp, pp_t, m2, m2_t, n2, n2_t, p2_, p2__t, pm, pm_t, bi = cur[h]
                P.op("pe", lambda e, pp=pp, m2=m2, pm=pm: e.matmul(pp[:, 256:384], m2[:], pm[:], start=True, stop=True), reads=[m2_t, pm_t], writes=[pp_t])
                if i < 5:
                    P.op("pe", lambda e, pp=pp, m2=m2: e.transpose(pp[:, 128:256], m2[:], ident), reads=[m2_t, cst_t], writes=[pp_t])
            yield
            for h in range(2):
                pp, pp_t, m2, m2_t, n2, n2_t, p2_, p2__t, pm, pm_t, bi = cur[h]
                P.op("dve", lambda e, pp=pp, pm=pm, p2_=p2_: e.tensor_tensor(p2_[:], pp[:, 256:384], pm[:], ALU.add), reads=[pp_t, pm_t], writes=[p2__t])
                if i < 5:
                    P.op("act", lambda e, pp=pp, n2=n2: e.copy(n2[:], pp[:, 128:256]), reads=[pp_t], writes=[n2_t])
                hd[h]["n"], hd[h]["m"], hd[h]["p"] = (n2, n2_t), (m2, m2_t), (p2_, p2__t)
                pdbl.put(bi)
            yield
        for h in range(2):
            st = hd[h]
            tt, tt_t = st["p"]
            aak, aak_t = st["aak"]
            x2, x2_t = g(X2[h]); wtz, wtz_t = g(WTz[h])
            hs = slice(h * 64, (h + 1) * 64)
            pp, pp_t = pa_.next()
            P.op("pe", lambda e, pp=pp, aak=aak, hs=hs: e.matmul(pp[:, 0:64], aak[:], vb[:, hs], start=True, stop=True), reads=[aak_t, vb_t], writes=[pp_t])
            P.op("pe", lambda e, pp=pp, tt=tt: e.matmul(pp[:, 128:256], ah32[:], tt[:], start=True, stop=True), reads=[ah32_t, tt_t], writes=[pp_t])
            P.op("act", lambda e, pp=pp, x2=x2: e.copy(x2[:], pp[:, 0:64]), reads=[pp_t], writes=[x2_t])
            P.op("dve", lambda e, pp=pp, wtz=wtz, hs=hs: e.tensor_copy(wtz[hs, :], pp[hs, 128:256]), reads=[pp_t], writes=[wtz_t])
            st["x2"] = (x2, x2_t); st["wtz"] = (wtz, wtz_t)
        yield
        us = []
        for h in range(2):
            st = hd[h]
            tt, tt_t = st["p"]; x2, x2_t = st["x2"]; wtz, wtz_t = st["wtz"]
            u, u_t = g(US[h])
            pu, pu_t = pb_.next()
            P.op("pe", lambda e, pu=pu, tt=tt, x2=x2: e.matmul(pu[:, 0:64], tt[:], x2[:], start=True, stop=False), reads=[tt_t, x2_t], writes=[pu_t])
            P.op("pe", lambda e, pu=pu, wtz=wtz: e.matmul(pu[:, 0:64], wtz[:], STb[d][:], start=False, stop=True), reads=[wtz_t, STb_t[d]], writes=[pu_t])
            if h == 0:
                P.op("act", lambda e, pu=pu, u=u: e.copy(u[:], pu[:, 0:64]), reads=[pu_t], writes=[u_t])
            else:
                P.op("dve", lambda e, pu=pu, u=u: e.tensor_copy(u[:], pu[:, 0:64]), reads=[pu_t], writes=[u_t])
            us.append((u, u_t))
        py, py_t = pb_.next()
        for h in range(2):
            st = hd[h]
            arb, arb_t = st["arb"]; ark, ark_t = st["ark"]
            u, u_t = us[h]
            hs = slice(h * 64, (h + 1) * 64)
            P.op("pe", lambda e, h=h, hs=hs: e.matmul(py[:, hs], rhtz[h][0][:], STb[d][:], start=True, stop=False), reads=[rhtz[h][1], STb_t[d]], writes=[py_t])
            P.op("pe", lambda e, hs=hs, arb=arb, u=u: e.matmul(py[:, hs], arb[:], u[:], start=False, stop=False), reads=[arb_t, u_t], writes=[py_t])
            P.op("pe", lambda e, hs=hs, ark=ark: e.matmul(py[:, hs], ark[:], vb[:, hs], start=False, stop=True), reads=[ark_t, vb_t], writes=[py_t])
        P.op("act", lambda e: e.copy(Y[d][:, c, :], py[:, 0:128]), reads=[py_t], writes=[Y_t[d][c]])
        pd, pd_t = pb_.next()
        for h in range(2):
            u, u_t = us[h]
            hs = slice(h * 64, (h + 1) * 64)
            P.op("pe", lambda e, h=h, u=u: e.matmul(pd[:, 0:64], btz[h][0][:], u[:], start=(h == 0), stop=False), reads=[btz[h][1], u_t], writes=[pd_t])
            P.op("pe", lambda e, h=h, hs=hs: e.matmul(pd[:, 0:64], ktz[h][0][:], vb[:, hs], start=False, stop=(h == 1)), reads=[ktz[h][1], vb_t], writes=[pd_t])
        P.op("dve", lambda e: e.scalar_tensor_tensor(ST32[d][:], ST32[d][:], gcol[:, 0:1], pd[:, 0:64], ALU.mult, ALU.add),
             reads=[ST_t[d], gcol_t, pd_t], writes=[ST_t[d]])
        P.op("act", lambda e: e.copy(STb[d][:], ST32[d][:]), reads=[ST_t[d]], writes=[STb_t[d]])
        yield

    nsteps = len(order_f)
    active = []
    nxt = [0]

    def admit():
        it = nxt[0]
        nxt[0] += 1
        active.append([it, unit(0, order_f[it], it), 0])
        active.append([it, unit(1, order_b[it], it), 0])

    admit()
    while active:
        for ent in list(active):
            try:
                next(ent[1])
                ent[2] += 1
            except StopIteration:
                active.remove(ent)
        if nxt[0] < nsteps:
            live_steps = {e_[0] for e_ in active}
            newest = [e_ for e_ in active if e_[0] == nxt[0] - 1]
            if len(live_steps) < 2 and all(e_[2] >= 6 for e_ in newest):
                admit()
    ysum = Rot([P.sbuf("ysum%d" % i, [128, 128], F32) for i in range(2)])
    ysq = Rot([P.sbuf("ysq%d" % i, [128, 128], F32) for i in range(2)])
    msm = Rot([P.sbuf("msm%d" % i, [128, 16], F32) for i in range(2)])
    ost = Rot([P.sbuf("ost%d" % i, [128, 128], F32) for i in range(3)])
    for c in range(nch):
        ys, ys_t = ysum.next(); yq, yq_t = ysq.next(); ms, ms_t = msm.next(); o, o_t = ost.next()
        P.op("dve", lambda e, ys=ys, c=c: e.tensor_tensor(ys[:], Y[0][:, c, :], Y[1][:, c, :], ALU.add), reads=[Y_t[0][c], Y_t[1][c]], writes=[ys_t])
        P.op("act", lambda e, ys=ys, yq=yq: e.activation(yq[:], ys[:], AF.Square), reads=[ys_t], writes=[yq_t])
        P.op("dve", lambda e, ys=ys, ms=ms: e.tensor_reduce(ms[:, 0:2], ys[:].rearrange("p (h k) -> p h k", h=2), AX.X, ALU.add), reads=[ys_t], writes=[ms_t])
        P.op("dve", lambda e, yq=yq, ms=ms: e.tensor_reduce(ms[:, 2:4], yq[:].rearrange("p (h k) -> p h k", h=2), AX.X, ALU.add), reads=[yq_t, ms_t], writes=[ms_t])
        P.op("dve", lambda e, ms=ms: e.tensor_scalar(ms[:, 0:2], ms[:, 0:2], 1.0 / 64, None, ALU.mult), reads=[ms_t], writes=[ms_t])
        P.op("dve", lambda e, ms=ms: e.tensor_tensor(ms[:, 4:6], ms[:, 0:2], ms[:, 0:2], ALU.mult), reads=[ms_t], writes=[ms_t])
        P.op("dve", lambda e, ms=ms: e.scalar_tensor_tensor(ms[:, 4:6], ms[:, 2:4], 1.0 / 64, ms[:, 4:6], ALU.mult, ALU.subtract), reads=[ms_t], writes=[ms_t])
        P.op("act", lambda e, ms=ms: e.activation(ms[:, 6:8], ms[:, 4:6], AF.Ln, bias=GN_EPS), reads=[ms_t], writes=[ms_t])
        P.op("act", lambda e, ms=ms: e.activation(ms[:, 8:10], ms[:, 6:8], AF.Exp, scale=-0.5), reads=[ms_t], writes=[ms_t])
        for h in range(2):
            hs = slice(h * 64, (h + 1) * 64)
            P.op("dve", lambda e, ys=ys, ms=ms, h=h, hs=hs: e.tensor_scalar(ys[:, hs], ys[:, hs], ms[:, h:h + 1], ms[:, 8 + h:9 + h], ALU.subtract, ALU.mult),
                 reads=[ys_t, ms_t], writes=[ys_t])
        P.op("pool", lambda e, ys=ys: e.tensor_tensor(ys[:], ys[:], vec(6), ALU.mult), reads=[ys_t, vecs_t], writes=[ys_t])
        P.op("pool", lambda e, ys=ys: e.tensor_tensor(ys[:], ys[:], vec(7), ALU.add), reads=[ys_t, vecs_t], writes=[ys_t])
        P.op("dve", lambda e, ys=ys, c=c: e.tensor_tensor(ys[:], ys[:], BON[0][:, c, :], ALU.add), reads=[ys_t, BON_t[0][c]], writes=[ys_t])
        P.op("dve", lambda e, ys=ys, o=o, c=c: e.tensor_tensor(o[:], ys[:], G[:, c, :], ALU.mult), reads=[ys_t, G_t], writes=[o_t])
        P.dma("sp", out_d[c, :, :], o[:], reads=[o_t], semtok=o_t)
    P.close_scope()


def rwkv_orders(nch_ctx, nch_lat):
    order_f = list(range(nch_ctx + nch_lat))
    order_b = list(range(nch_ctx - 1, -1, -1)) + list(range(nch_ctx + nch_lat - 1, nch_ctx - 1, -1))
    return order_f, order_b


def rwkv_consts():
    i = np.arange(128)
    s, t = i[:, None], i[None, :]
    c = np.stack([np.eye(128), np.ones((128, 128)), s <= t, s >= t, s < t, s <= t, s > t, s >= t]).astype(np.float32)
    return c


EXG = {"A": 1536, "R": 2048, "L": 704}
EXG_DT = {"A": BF16, "R": F32, "L": F32}


def ex_dst(exl, exl_t, oad_l):
    def fn(t, col, n):
        if col < OQ:
            return oad_l[t, :, col:col + n], None
        if col < OGB:
            return exl[t]["A"][:, col - OQ:col - OQ + n], exl_t[t]["A"]
        if col < ORKV:
            return exl[t]["L"][:, 192 + col - OGB:192 + col - OGB + n], exl_t[t]["L"]
        if col < OLF:
            return exl[t]["R"][:, col - ORKV:col - ORKV + n], exl_t[t]["R"]
        if col < OGC:
            return exl[t]["L"][:, col - OLF:col - OLF + n], exl_t[t]["L"]
        return exl[t]["R"][:, 1536 + col - OGC:1536 + col - OGC + n], exl_t[t]["R"]
    return fn


def emit_compact(P, exg, exg_t, at_loc, at_g, eng):
    P.open_scope()
    hv = P.pid4(eng) * 128
    toks = Rot([None] * 16)
    for t in range(NT):
        for s_ in range(4 if t < 8 else 2):
            tok0 = CTX + s_ * 1024 + t * 128 if t < 8 else s_ * 128
            rows = slice(s_ * 128, (s_ + 1) * 128)
            _, tk = toks.next()
            P.dma(eng, at_loc[tok0:tok0 + 128, :].rearrange("n (f c) -> n f c", f=3),
                  exg[t]["A"][rows, :].rearrange("n (f w) -> n f w", f=3)[:, :, bass.ds(hv, 128)], reads=[exg_t[t]["A"]], writes=[tk], semtok=tk)
            _, tk = toks.next()
            P.dma(eng, at_g[tok0:tok0 + 128, :], exg[t]["L"][rows, 192:704][:, bass.ds(hv, 128)], reads=[exg_t[t]["L"]], writes=[tk], semtok=tk)
    P.close_scope()


def compact_rw(P, exg, exg_t, rw_loc, eng):
    hv = P.pid4(eng) * 128
    zt = P.sbuf("zt", [4, RW_COLS], F32); zt_t = Tok()
    P.op("pool", lambda e: e.memset(zt[:], 0.0), writes=[zt_t])
    for r in (0, 257, 258, RW_ROWS - 1):
        P.dma(eng, rw_loc[r:r + 1, :], zt[0:1, :], reads=[zt_t], semtok=zt_t)
    toks = Rot([None] * 12)
    for t in range(NT):
        for s_ in range(4 if t < 8 else 2):
            tok0 = CTX + s_ * 1024 + t * 128 if t < 8 else s_ * 128
            rrow = 259 + (tok0 - CTX) if t < 8 else 1 + tok0
            rows = slice(s_ * 128, (s_ + 1) * 128)
            _, tk = toks.next()
            P.dma(eng, rw_loc[rrow:rrow + 128, 0:512].rearrange("n (f c) -> n f c", f=4),
                  exg[t]["R"][rows, :].rearrange("n (f w) -> n f w", f=4)[:, :, bass.ds(hv, 128)], reads=[exg_t[t]["R"]], writes=[tk], semtok=tk)
            _, tk = toks.next()
            P.dma(eng, rw_loc[rrow:rrow + 128, 512:704], exg[t]["L"][rows, 0:192], reads=[exg_t[t]["L"]], writes=[tk], semtok=tk)


def emit_obc_stage(P, gath, gath_t, obc_loc, eng):
    q = P.pid4(eng)
    toks = []
    for j in range(2):
        g, g_c = gath[j]
        tk = [Tok(), Tok()]
        P.dma(eng, obc_loc[:, j, 0:1024, :],
              g.rearrange("j h n c -> (j h n) c")[bass.ds(q * 4096, 4096), :].rearrange("(h n) c -> h n c", h=4), reads=[gath_t[j]], writes=[tk[0]], semtok=tk[0])
        P.dma(eng, obc_loc[:, j, 1024:1152, :], g_c[:, bass.ds((q % 2) * 128, 128), :], reads=[gath_t[j]], writes=[tk[1]], semtok=tk[1])
        toks.append(tk)
    return toks


def build_fused():
    P = Prog()
    di = P.dram_in
    x_d = di("x", [NT, 128, D], F32); xh_d = di("xh", [4, D], F32); hmask_d = di("hmask", [4, 1], F32)
    cT_d = di("cT", [D, 2], F32); wm_d = di("wm", [2, D, MC], F32); bm_d = di("bm", [2, MC], F32)
    gpre_d = di("gpreT", [2, 128, 16], F32); win_d = di("w_in", [2, D, N_IN], F32)
    sguw_d = di("sgu_wT", [2, 128, 4, 128], F32); sgub_d = di("sgu_bT", [2, 128, 4], F32); convw_d = di("conv_w", [2, 1, 3 * WG], F32)
    cos_d = di("rcos", [128, NT, 64], F32); sin_d = di("rsin", [128, NT, 64], F32); ident_d = di("ident", [128, 128], F32)
    wo_d = di("w_out", [2, D, D], F32); gpost_d = di("gpost", [2, 1, D], F32)
    lam_d = di("lamv", [2, 1, 256], F32); subg_d = di("subg", [2, 1, 128], F32)
    mu_d = di("rw_mu", [2, 2, RW_F], F32); w2e_d = di("rw_w2e", [2, 2, 65, 128], F32); a2e_d = di("rw_a2e", [2, 2, 33, 128], F32)
    vecs_d = di("rw_vecs", [2, 8, 128], F32); cst_d = di("rw_cst", [8, 128, 128], F32)
    xo_d = P.dram_out("xo", [8, 128, D], F32)
    tmp = P.dram_tmp
    stage = [0]
    STOP = DEBUG.get("stop", 999)

    def done():
        stage[0] += 1
        return stage[0] >= STOP

    mod_l = tmp("mod_l", [2, 2, MC], F32); mod_all = tmp("mod_all", [4, 2, 2, MC], F32)
    emit_M(P, cT_d, wm_d, bm_d, mod_l)
    P.allgather(mod_l.rearrange("l j n -> (l j) n"), mod_all.rearrange("r l j n -> (r l j) n"), GROUPS)
    if done():
        return P.finish()
    of, ob = rwkv_orders(2, 32)
    xcur, xhcur = x_d, xh_d
    for l in range(2):
        oad_l = tmp("oad_l%d" % l, [NT, 128, 1024], F32)
        exl = [{g: tmp("exl%d_%d%s" % (l, t, g), [128, n], EXG_DT[g]) for g, n in EXG.items()} for t in range(NT)]
        exg = [{g: tmp("exg%d_%d%s" % (l, t, g), [4 * 128, n], EXG_DT[g]) for g, n in EXG.items()} for t in range(NT)]
        exl_t = [{g: Tok(multi=True) for g in EXG} for t in range(NT)]
        exg_t = [{g: Tok() for g in EXG} for t in range(NT)]

        pend = []
        st_ = {"step": 0, "last": -99}

        def hook(pos, t, exl=exl, exg=exg, exl_t=exl_t, exg_t=exg_t, pend=pend, st_=st_):
            def issue():
                tt, g = pend.pop(0)
                P.allgather_async(exl[tt][g], exg[tt][g], GROUPS, reads=[exl_t[tt][g]], out_tok=exg_t[tt][g])
            if pos is None:
                while pend:
                    issue()
                return
            if t == NT - 1 and pos == 4:
                pend.extend((tt, "A") for tt in range(NT))
            if t == NT - 1 and pos == 9:
                pend.extend((tt, "R") for tt in range(NT))
                pend.extend((tt, "L") for tt in range(NT))
            st_["step"] += 1
            if pend and st_["step"] - st_["last"] >= 5:
                issue()
                st_["last"] = st_["step"]

        emit_A(P, l, xcur, xhcur, hmask_d, mod_all, gpre_d[l], win_d[l], sguw_d[l], sgub_d[l], convw_d[l], cos_d, sin_d, ident_d,
               ex_dst(exl, exl_t, oad_l), hook)
        if done():
            return P.finish()
        at_loc = tmp("at_loc%d" % l, [NTOK, 384], BF16); at_g = tmp("at_g%d" % l, [NTOK, 128], F32)
        rw_loc = tmp("rw_loc%d" % l, [RW_ROWS, RW_COLS], F32)
        emit_compact(P, exg, exg_t, at_loc, at_g, "sp")
        if done():
            return P.finish()
        ob_l = tmp("ob_l%d" % l, [NKB, 128, 128], F32); oc_l = tmp("oc_l%d" % l, [NKB, 128, 128], F32)
        gath = [(tmp("obg%d_%d" % (l, j), [4, 4, 1024, 128], F32), tmp("obgc%d_%d" % (l, j), [4, 256, 128], F32)) for j in range(2)]
        gath_t = [Tok(), Tok()]
        emit_attn(P, l, l == 0, at_loc, at_g, lam_d[l], subg_d[l], ident_d, ob_l,
                  pre=lambda exg=exg, exg_t=exg_t, rw_loc=rw_loc: compact_rw(P, exg, exg_t, rw_loc, "pool"))
        if done():
            return P.finish()

        def gather_o(j, src):
            g, g_c = gath[j]
            for c in range(4):
                P.allgather_async(src[2 + 8 * c:2 + 8 * (c + 1)].rearrange("t p c -> (t p) c"), g[c].rearrange("h n c -> (h n) c"), GROUPS)
            P.allgather_async(src[0:2].rearrange("t p c -> (t p) c"), g_c.rearrange("h n c -> (h n) c"), GROUPS, out_tok=gath_t[j])

        gather_o(0, ob_l)
        emit_rwkv(P, NKB, of, ob, rw_loc, mu_d[l], w2e_d[l], a2e_d[l], vecs_d[l], cst_d, oc_l)
        if done():
            return P.finish()
        gather_o(1, oc_l)
        obc_loc = tmp("obc_loc%d" % l, [4, 2, NT * 128, 128], F32)
        stage_fn = lambda gath=gath, gath_t=gath_t, obc_loc=obc_loc: emit_obc_stage(P, gath, gath_t, obc_loc, "pool")
        if l == 0:
            x1 = tmp("x1", [NT, 128, D], F32); edge_l = tmp("edge_l", [4, D], F32); edge_all = tmp("edge_all", [16, D], F32)
            xh1 = tmp("xh1", [4, D], F32)
            emit_C(P, 0, NT, oad_l, obc_loc, xcur, mod_all, gpost_d[0], wo_d[0], ident_d, x1, edge_l, stage=stage_fn)
            P.allgather(edge_l, edge_all, GROUPS)
            q = P.pid4("pool")
            etk = [Tok() for _ in range(4)]
            for i, (dq, er) in enumerate(((3, 1), (1, 0), (3, 3), (1, 2))):
                P.dma("pool", xh1[i:i + 1, :], edge_all[bass.ds(((q + dq) % 4) * 4 + er, 1), :], writes=[etk[i]])
            P.barrier()
            xcur, xhcur = x1, xh1
            if done():
                return P.finish()
        else:
            emit_C(P, 1, 8, oad_l, obc_loc, xcur, mod_all, gpost_d[1], wo_d[1], ident_d, xo_d, stage=stage_fn)
    return P.finish()


_NC_CACHE = {}


def _f32(a):
    return np.ascontiguousarray(a, dtype=np.float32)


def _rope_tables():
    inv = (10000.0 ** (-np.arange(0, 32, 2, dtype=np.float32) / 32)).astype(np.float32)
    tabs = []
    for q in range(4):
        cos = np.ones((128, NT, 2, 2, 16), np.float32)
        sin = np.zeros((128, NT, 2, 2, 16), np.float32)
        for t in range(NT - 1):
            tok = q * 1024 + t * 128 + np.arange(128)
            for a, pos in enumerate((tok // 64, tok % 64)):
                ang = pos.astype(np.float32)[:, None] * inv[None, :]
                cos[:, t, a, 0, :] = np.cos(ang); cos[:, t, a, 1, :] = np.cos(ang)
                sin[:, t, a, 0, :] = -np.sin(ang); sin[:, t, a, 1, :] = np.sin(ang)
        tabs.append((cos.reshape(128, NT, 64), sin.reshape(128, NT, 64)))
    return tabs


def _to_pk(v):
    return np.ascontiguousarray(v.reshape(16, 128).T)


def rwkv_consts():
    i = np.arange(128)
    s, t = i[:, None], i[None, :]
    c = np.stack([np.eye(128), np.ones((128, 128)), s <= t, s >= t, s < t, s <= t, s > t, s >= t]).astype(np.float32)
    return c


def _core_inputs(p):
    tabs = _rope_tables()
    ident = np.eye(128, dtype=np.float32)
    cst = rwkv_consts()
    x, xc = p['x'], p['ctx']
    shared = {
        "gpreT": _f32(np.stack([_to_pk(p['g_pre'][l]) for l in range(2)])),
        "w_in": _f32(p['w_in']), "sgu_wT": _f32(p['sgu_w'].transpose(0, 3, 1, 2)), "sgu_bT": _f32(p['sgu_b'].transpose(0, 2, 1)),
        "conv_w": _f32(p['conv_w'].reshape(2, 1, 3 * WG)), "ident": ident, "w_out": _f32(p['w_out']),
        "gpost": _f32(p['g_post'][:, None, :]),
        "lamv": _f32(np.concatenate([p['lam_q1'], p['lam_k1'], p['lam_q2'], p['lam_k2']], 1)[:, None, :]),
        "subg": _f32(p['subln_g'][:, None, :]), "rw_cst": cst,
    }
    ins = []
    for k in range(NCORES):
        b, q = k // 4, k % 4
        ct = q % 2
        xt = np.concatenate([x[b, q * 1024:(q + 1) * 1024].reshape(8, 128, D), xc[b, ct * 128:(ct + 1) * 128][None]], 0)
        xh = np.zeros((4, D), np.float32); hm = np.zeros((4, 1), np.float32)
        if q > 0:
            xh[0] = x[b, q * 1024 - 1]; hm[0] = 1
        if q < 3:
            xh[1] = x[b, (q + 1) * 1024]; hm[1] = 1
        if ct > 0:
            xh[2] = xc[b, ct * 128 - 1]; hm[2] = 1
        if ct < 1:
            xh[3] = xc[b, (ct + 1) * 128]; hm[3] = 1
        cols = slice(q * 128, (q + 1) * 128)
        c0 = q * 128
        mus, w2e, a2e, vecs = [], [], [], []
        for l in range(2):
            mus.append([]); w2e.append([]); a2e.append([])
            for d in range(2):
                m = p['rwkv_mu'][l, d]
                mus[l].append(np.concatenate([m[c0:c0 + 128], m[512 + c0:512 + c0 + 128], m[1024 + c0:1024 + c0 + 128], m[1536 + d * 0:1632]]))
                w2e[l].append(np.concatenate([p['rwkv_w2'][l, d][:, cols], p['rwkv_w0'][l, d][None, cols]], 0))
                a2e[l].append(np.concatenate([p['rwkv_a2'][l, d][:, cols], p['rwkv_a0'][l, d][None, cols]], 0))
            vecs.append(np.stack([p['rwkv_kk'][l, 0][cols], p['rwkv_ka'][l, 0][cols], p['rwkv_rk'][l, 0].reshape(-1)[cols],
                                  p['rwkv_kk'][l, 1][cols], p['rwkv_ka'][l, 1][cols], p['rwkv_rk'][l, 1].reshape(-1)[cols],
                                  p['rwkv_ln_w'][l][cols], p['rwkv_ln_b'][l][cols]]))
        dct = dict(shared)
        dct.update({
            "x": _f32(xt), "xh": xh, "hmask": hm,
            "cT": _f32(np.stack([p['c'][b], p['c_ctx']], 1)),
            "wm": _f32(p['w_mod'][:, :, q * MC:(q + 1) * MC]), "bm": _f32(p['b_mod'][:, q * MC:(q + 1) * MC]),
            "rcos": tabs[q][0], "rsin": tabs[q][1],
            "rw_mu": _f32(np.array(mus)), "rw_w2e": _f32(np.array(w2e)), "rw_a2e": _f32(np.array(a2e)), "rw_vecs": _f32(np.array(vecs)),
        })
        ins.append(dct)
    return ins


def kernel(**inputs):
    p = {k: np.asarray(v) for k, v in inputs.items()}
    if "fused" not in _NC_CACHE:
        _NC_CACHE["fused"] = build_fused()
    ins = _core_inputs(p)
    res = run_bass_kernel_spmd(_NC_CACHE["fused"], ins, core_ids=list(range(NCORES))).results
    out = np.zeros((NB, SEQ, D), np.float32)
    for k in range(NCORES):
        b, q = k // 4, k % 4
        out[b, q * 1024:(q + 1) * 1024] = res[k]["xo"].reshape(1024, D)
    return out
```
